# Optimizing a Trainium2 kernel written in Bass

```python
import jax, jax.numpy as jnp
from jax import lax
import numpy as np

D_MODEL = 1024
BATCH = 8
SEQ = 4096
DEPTH = 2

GRID_W = 64
CTX_LEN = 256
N_EVEN = (DEPTH + 1) // 2
N_ODD = DEPTH // 2
RMS_EPS = 1e-6

HEAD_DIM = 64
CONV_CH = D_MODEL // 2
CONV_K = 3
ATT_Q_HEADS = (D_MODEL // 2) // HEAD_DIM
ATT_KV_HEADS = ATT_Q_HEADS // 4
ATT_GROUP = ATT_Q_HEADS // ATT_KV_HEADS
ATT_Q_DIM = ATT_Q_HEADS * HEAD_DIM
ATT_KV_DIM = ATT_KV_HEADS * HEAD_DIM
IN_SPLITS = (CONV_CH, 2 * CONV_CH, 3 * CONV_CH, 3 * CONV_CH + ATT_Q_DIM, 3 * CONV_CH + ATT_Q_DIM + ATT_KV_DIM)
IN_PROJ_DIM = 3 * CONV_CH + ATT_Q_DIM + 2 * ATT_KV_DIM
Q_BLOCK = 128
ATT_SCALE = HEAD_DIM ** -0.5
ROPE_THETA = 10000.0
ROPE_AXIS_DIM = HEAD_DIM // 2

RWKV_HEAD = 64
RWKV_HEADS = D_MODEL // RWKV_HEAD
DECAY_LORA = 64
ICLR_LORA = 64
GATE_LORA = 128
GN_EPS = 64e-5
L2_EPS = 1e-12

PEER_HEADS = 8
PEER_NKEYS = 128
PEER_EXPERTS = PEER_NKEYS * PEER_NKEYS
PEER_TOPK = 16
PEER_QDIM = 256
PEER_HALF = PEER_QDIM // 2
PEER_CHUNK = 128

kernel_name = 'hybrid_conv_gqa_rwkv7_peer_dit'


def rms_norm(x, gain):
    xf = x.astype(jnp.float32)
    y = xf * lax.rsqrt(jnp.mean(xf * xf, axis=-1, keepdims=True) + RMS_EPS)
    return (y * gain.astype(jnp.float32)).astype(x.dtype)


def modulate(x, shift, scale):
    return x * (1 + scale) + shift


def rope_2d_tables(seq_len):
    t = jnp.arange(seq_len, dtype=jnp.int32)
    row = (t // GRID_W).astype(jnp.float32)
    col = (t % GRID_W).astype(jnp.float32)
    inv_freq = ROPE_THETA ** (-jnp.arange(0, ROPE_AXIS_DIM, 2, dtype=jnp.float32) / ROPE_AXIS_DIM)
    ang = jnp.concatenate([row[:, None] * inv_freq, col[:, None] * inv_freq], axis=-1)
    return jnp.cos(ang), jnp.sin(ang)


def apply_rope(x, cos, sin):
    shp = x.shape
    xf = x.astype(jnp.float32).reshape(shp[:-1] + (shp[-1] // 2, 2))
    x1, x2 = xf[..., 0], xf[..., 1]
    bshape = (cos.shape[0],) + (1,) * (x.ndim - 3) + (cos.shape[1],)
    c, s = cos.reshape(bshape), sin.reshape(bshape)
    out = jnp.stack([x1 * c - x2 * s, x1 * s + x2 * c], axis=-1)
    return out.reshape(shp).astype(x.dtype)


def short_conv(x, w):
    T = x.shape[1]
    pad = CONV_K // 2
    xp = jnp.pad(x, ((0, 0), (pad, pad), (0, 0)))
    return sum(xp[:, j:j + T] * w[j] for j in range(CONV_K))


def block_attention(q, k, v):
    B, T = q.shape[:2]
    nb = T // Q_BLOCK
    qb = jnp.moveaxis(q.reshape(B, nb, Q_BLOCK, ATT_KV_HEADS, ATT_GROUP, HEAD_DIM), 1, 0)

    def one_block(qblk):
        s = jnp.einsum('bqkgd,bskd->bkgqs', qblk, k, preferred_element_type=jnp.float32) * ATT_SCALE
        p = jax.nn.softmax(s, axis=-1).astype(v.dtype)
        return jnp.einsum('bkgqs,bskd->bqkgd', p, v)

    out = lax.map(one_block, qb)
    return jnp.moveaxis(out, 0, 1).reshape(B, T, ATT_Q_DIM)


def conv_attn_mixer(xn_c, xn_l, w_in, conv_w, q_gain, k_gain, w_out, rope_cos, rope_sin, need_ctx_out):
    B, S, _ = xn_l.shape
    L = xn_c.shape[1]
    h_l, gb_l, gc_l, q_l, k_l, v_l = jnp.split(xn_l @ w_in, IN_SPLITS, axis=-1)
    h_c, gb_c, gc_c, q_c, k_c, v_c = jnp.split(xn_c @ w_in, IN_SPLITS, axis=-1)
    k_l = apply_rope(rms_norm(k_l.reshape(B, S, ATT_KV_HEADS, HEAD_DIM), k_gain), rope_cos, rope_sin)
    k_c = rms_norm(k_c.reshape(B, L, ATT_KV_HEADS, HEAD_DIM), k_gain)
    v_l = v_l.reshape(B, S, ATT_KV_HEADS, HEAD_DIM)
    v_c = v_c.reshape(B, L, ATT_KV_HEADS, HEAD_DIM)
    q_l = apply_rope(rms_norm(q_l.reshape(B, S, ATT_Q_HEADS, HEAD_DIM), q_gain), rope_cos, rope_sin)
    att_l = block_attention(q_l.reshape(B, S, ATT_KV_HEADS, ATT_GROUP, HEAD_DIM),
                            jnp.concatenate([k_c, k_l], axis=1), jnp.concatenate([v_c, v_l], axis=1))
    conv_l = gb_l * short_conv(gc_l * h_l, conv_w)
    y_l = jnp.concatenate([conv_l, att_l], axis=-1) @ w_out
    y_c = None
    if need_ctx_out:
        q_c = rms_norm(q_c.reshape(B, L, ATT_Q_HEADS, HEAD_DIM), q_gain)
        att_c = block_attention(q_c.reshape(B, L, ATT_KV_HEADS, ATT_GROUP, HEAD_DIM), k_c, v_c)
        conv_c = gb_c * short_conv(gc_c * h_c, conv_w)
        y_c = jnp.concatenate([conv_c, att_c], axis=-1) @ w_out
    return y_l, y_c


def shift_latent(x):
    B, S, D = x.shape
    rows = S // GRID_W
    q = D // 4
    g = x.reshape(B, rows, GRID_W, D)
    left = jnp.pad(g[:, :, :-1, 0:q], ((0, 0), (0, 0), (1, 0), (0, 0)))
    right = jnp.pad(g[:, :, 1:, q:2 * q], ((0, 0), (0, 0), (0, 1), (0, 0)))
    up = jnp.pad(g[:, :-1, :, 2 * q:3 * q], ((0, 0), (1, 0), (0, 0), (0, 0)))
    down = jnp.pad(g[:, 1:, :, 3 * q:], ((0, 0), (0, 1), (0, 0), (0, 0)))
    return jnp.concatenate([left, right, up, down], axis=-1).reshape(B, S, D)


def shift_context(x):
    h = x.shape[-1] // 2
    prev = jnp.pad(x[:, :-1, :h], ((0, 0), (1, 0), (0, 0)))
    nxt = jnp.pad(x[:, 1:, h:], ((0, 0), (0, 1), (0, 0)))
    return jnp.concatenate([prev, nxt], axis=-1)


def _heads(t):
    return t.reshape(t.shape[0], t.shape[1], RWKV_HEADS, RWKV_HEAD)


def rwkv_project(xn, shifted, mu, w_r, w_k, w_v, g1, g2, k_k, k_a, w0, w1, w2, a0, a1, a2):
    f32 = jnp.float32
    xx = shifted - xn
    xr, xw, xk, xv, xa, xg = [xn + xx * mu[m] for m in range(6)]
    r = xr @ w_r
    k = xk @ w_k
    v = xv @ w_v
    g = jax.nn.sigmoid(xg @ g1) @ g2
    kk = _heads((k * k_k).astype(f32))
    kk = kk * lax.rsqrt(jnp.sum(kk * kk, axis=-1, keepdims=True) + L2_EPS)
    per_dir = []
    for d in range(2):
        logw = -jax.nn.softplus(-(w0[d] + jnp.tanh(xw @ w1[d]) @ w2[d]).astype(f32)) - 0.5
        decay = jnp.exp(-jnp.exp(logw))
        a = jax.nn.sigmoid((a0[d] + (xa @ a1[d]) @ a2[d]).astype(f32))
        kd = k.astype(f32) * (1 + (a - 1) * k_a)
        per_dir.append((_heads(decay), _heads(a), _heads(kd)))
    return _heads(r.astype(f32)), _heads(v.astype(f32)), g, kk, per_dir


def wkv_scan(r, decay, k, v, kk, a, state0, reverse, need_out):
    tm = lambda t: jnp.swapaxes(t, 0, 1)

    def step(S, inp):
        r_t, w_t, k_t, v_t, kk_t, a_t = inp
        sa = jnp.einsum('bhij,bhj->bhi', S, -kk_t)
        S = S * w_t[:, :, None, :] + sa[..., :, None] * (kk_t * a_t)[..., None, :] + v_t[..., :, None] * k_t[..., None, :]
        y = jnp.einsum('bhij,bhj->bhi', S, r_t) if need_out else None
        return S, y

    s_fin, ys = lax.scan(step, state0, (tm(r), tm(decay), tm(k), tm(v), tm(kk), tm(a)), reverse=reverse)
    return s_fin, (tm(ys) if need_out else None)


def rwkv7_bidir(xn_c, xn_l, mu, w_r, w_k, w_v, w_o, g1, g2, k_k, k_a, r_k, w0, w1, w2, a0, a1, a2, gn_w, gn_b, need_ctx_out):
    r_c, v_c, g_c, kk_c, dirs_c = rwkv_project(xn_c, shift_context(xn_c), mu, w_r, w_k, w_v, g1, g2, k_k, k_a, w0, w1, w2, a0, a1, a2)
    r_l, v_l, g_l, kk_l, dirs_l = rwkv_project(xn_l, shift_latent(xn_l), mu, w_r, w_k, w_v, g1, g2, k_k, k_a, w0, w1, w2, a0, a1, a2)
    B = xn_l.shape[0]
    state0 = jnp.zeros((B, RWKV_HEADS, RWKV_HEAD, RWKV_HEAD), jnp.float32)
    ys_l, ys_c = [], []
    for d, rev in enumerate((False, True)):
        dec_c, a_c, kd_c = dirs_c[d]
        s_ctx, y_cd = wkv_scan(r_c, dec_c, kd_c, v_c, kk_c, a_c, state0, rev, need_ctx_out)
        dec_l, a_l, kd_l = dirs_l[d]
        _, y_ld = wkv_scan(r_l, dec_l, kd_l, v_l, kk_l, a_l, s_ctx, rev, True)
        ys_l.append(y_ld)
        ys_c.append(y_cd)

    def readout(y_sum, r, v, g, k_sum):
        Bq, Tq = y_sum.shape[:2]
        mean = jnp.mean(y_sum, axis=-1, keepdims=True)
        var = jnp.mean(jnp.square(y_sum - mean), axis=-1, keepdims=True)
        yn = ((y_sum - mean) * lax.rsqrt(var + GN_EPS)).reshape(Bq, Tq, D_MODEL) * gn_w + gn_b
        bonus = (jnp.sum(r * k_sum * r_k, axis=-1, keepdims=True) * v).reshape(Bq, Tq, D_MODEL)
        return ((yn + bonus).astype(g.dtype) * g) @ w_o

    y_l = readout(ys_l[0] + ys_l[1], r_l, v_l, g_l, dirs_l[0][2] + dirs_l[1][2])
    y_c = None
    if need_ctx_out:
        y_c = readout(ys_c[0] + ys_c[1], r_c, v_c, g_c, dirs_c[0][2] + dirs_c[1][2])
    return y_l, y_c


def peer_ffn(xn, w_q, sub_keys, u_tab, v_tab):
    B, T, D = xn.shape
    chunks = xn.reshape(-1, PEER_CHUNK, D)

    def chunk_fn(xc):
        n = xc.shape[0]
        q = (xc @ w_q).reshape(n, PEER_HEADS, 2, PEER_HALF)
        s = jnp.einsum('nhpd,hpkd->nhpk', q, sub_keys, preferred_element_type=jnp.float32)
        s1, i1 = lax.top_k(s[:, :, 0], PEER_TOPK)
        s2, i2 = lax.top_k(s[:, :, 1], PEER_TOPK)
        cand_s = (s1[..., :, None] + s2[..., None, :]).reshape(n, PEER_HEADS, PEER_TOPK * PEER_TOPK)
        cand_i = (i1[..., :, None] * PEER_NKEYS + i2[..., None, :]).reshape(n, PEER_HEADS, PEER_TOPK * PEER_TOPK)
        top_s, pos = lax.top_k(cand_s, PEER_TOPK)
        idx = jnp.take_along_axis(cand_i, pos, axis=-1)
        gate = jax.nn.softmax(top_s, axis=-1)
        u = u_tab[idx]
        v = v_tab[idx]
        act = jax.nn.gelu(jnp.einsum('nd,nhkd->nhk', xc, u, preferred_element_type=jnp.float32), approximate=False)
        return jnp.einsum('nhk,nhkd->nd', (gate * act).astype(v.dtype), v)

    return lax.map(chunk_fn, chunks).reshape(B, T, D)


def setup_inputs(seed: int = 0) -> dict:
    key = jax.random.key(seed)
    ks = list(jax.random.split(key, 40))
    it = iter(ks)
    D = D_MODEL

    def nrm(shape, scale):
        return scale * jax.random.normal(next(it), shape, jnp.float32)

    inp = {}
    inp['x'] = nrm((BATCH, SEQ, D), 1.0)
    inp['c'] = nrm((BATCH, D), 1.0)
    inp['ctx'] = nrm((BATCH, CTX_LEN, D), 1.0)
    inp['c_ctx'] = nrm((D,), 1.0)
    inp['ada_w'] = nrm((DEPTH, D, 6 * D), 0.5 * D ** -0.5)
    inp['ada_b'] = nrm((DEPTH, 6 * D), 0.02)
    inp['norm1_g'] = 1.0 + nrm((DEPTH, D), 0.02)
    inp['norm2_g'] = 1.0 + nrm((DEPTH, D), 0.02)
    inp['ev_w_in'] = nrm((N_EVEN, D, IN_PROJ_DIM), D ** -0.5)
    inp['ev_conv_w'] = nrm((N_EVEN, CONV_K, CONV_CH), CONV_K ** -0.5)
    inp['ev_q_gain'] = 1.0 + nrm((N_EVEN, HEAD_DIM), 0.02)
    inp['ev_k_gain'] = 1.0 + nrm((N_EVEN, HEAD_DIM), 0.02)
    inp['ev_w_out'] = nrm((N_EVEN, D, D), D ** -0.5)
    inp['od_mu'] = jax.random.uniform(next(it), (N_ODD, 6, D), jnp.float32)
    inp['od_w_r'] = nrm((N_ODD, D, D), D ** -0.5)
    inp['od_w_k'] = nrm((N_ODD, D, D), D ** -0.5)
    inp['od_w_v'] = nrm((N_ODD, D, D), D ** -0.5)
    inp['od_w_o'] = nrm((N_ODD, D, D), D ** -0.5)
    inp['od_g1'] = nrm((N_ODD, D, GATE_LORA), D ** -0.5)
    inp['od_g2'] = nrm((N_ODD, GATE_LORA, D), GATE_LORA ** -0.5)
    inp['od_k_k'] = 0.85 + nrm((N_ODD, D), 0.02)
    inp['od_k_a'] = 1.0 + nrm((N_ODD, D), 0.02)
    inp['od_r_k'] = nrm((N_ODD, RWKV_HEADS, RWKV_HEAD), 0.1)
    inp['od_w0'] = -0.6 + nrm((N_ODD, 2, D), 0.3)
    inp['od_w1'] = nrm((N_ODD, 2, D, DECAY_LORA), D ** -0.5)
    inp['od_w2'] = nrm((N_ODD, 2, DECAY_LORA, D), 0.5 * DECAY_LORA ** -0.5)
    inp['od_a0'] = nrm((N_ODD, 2, D), 0.02)
    inp['od_a1'] = nrm((N_ODD, 2, D, ICLR_LORA), D ** -0.5)
    inp['od_a2'] = nrm((N_ODD, 2, ICLR_LORA, D), ICLR_LORA ** -0.5)
    inp['od_gn_w'] = 1.0 + nrm((N_ODD, D), 0.02)
    inp['od_gn_b'] = nrm((N_ODD, D), 0.02)
    inp['peer_wq'] = nrm((DEPTH, D, PEER_HEADS * PEER_QDIM), D ** -0.5)
    inp['peer_keys'] = nrm((DEPTH, PEER_HEADS, 2, PEER_NKEYS, PEER_HALF), PEER_HALF ** -0.5)
    inp['peer_u'] = nrm((DEPTH, PEER_EXPERTS, D), D ** -0.5)
    inp['peer_v'] = nrm((DEPTH, PEER_EXPERTS, D), PEER_HEADS ** -0.5)
    return inp


def reference(x, c, ctx, c_ctx, ada_w, ada_b, norm1_g, norm2_g, ev_w_in, ev_conv_w, ev_q_gain, ev_k_gain, ev_w_out,
              od_mu, od_w_r, od_w_k, od_w_v, od_w_o, od_g1, od_g2, od_k_k, od_k_a, od_r_k, od_w0, od_w1, od_w2,
              od_a0, od_a1, od_a2, od_gn_w, od_gn_b, peer_wq, peer_keys, peer_u, peer_v):
    rope_cos, rope_sin = rope_2d_tables(x.shape[1])
    silu_c = jax.nn.silu(c)
    silu_cc = jax.nn.silu(c_ctx)
    hx, hc = x, ctx
    for i in range(DEPTH):
        last = i == DEPTH - 1
        j = i // 2
        mod_l = jnp.split((silu_c @ ada_w[i] + ada_b[i])[:, None, :], 6, axis=-1)
        mod_c = jnp.split(silu_cc @ ada_w[i] + ada_b[i], 6, axis=-1)
        xn_l = modulate(rms_norm(hx, norm1_g[i]), mod_l[0], mod_l[1])
        xn_c = modulate(rms_norm(hc, norm1_g[i]), mod_c[0], mod_c[1])
        if i % 2 == 0:
            y_l, y_c = conv_attn_mixer(xn_c, xn_l, ev_w_in[j], ev_conv_w[j], ev_q_gain[j], ev_k_gain[j], ev_w_out[j],
                                       rope_cos, rope_sin, not last)
        else:
            y_l, y_c = rwkv7_bidir(xn_c, xn_l, od_mu[j], od_w_r[j], od_w_k[j], od_w_v[j], od_w_o[j], od_g1[j], od_g2[j],
                                   od_k_k[j], od_k_a[j], od_r_k[j], od_w0[j], od_w1[j], od_w2[j], od_a0[j], od_a1[j],
                                   od_a2[j], od_gn_w[j], od_gn_b[j], not last)
        hx = hx + mod_l[2] * y_l.astype(hx.dtype)
        hx = hx + mod_l[5] * peer_ffn(modulate(rms_norm(hx, norm2_g[i]), mod_l[3], mod_l[4]),
                                      peer_wq[i], peer_keys[i], peer_u[i], peer_v[i])
        if not last:
            hc = hc + mod_c[2] * y_c.astype(hc.dtype)
            hc = hc + mod_c[5] * peer_ffn(modulate(rms_norm(hc, norm2_g[i]), mod_c[3], mod_c[4]),
                                          peer_wq[i], peer_keys[i], peer_u[i], peer_v[i])
    return hx
```

```python
import numpy as np
from contextlib import ExitStack
import concourse.bass as bass
import concourse.mybir as mybir
from concourse.bass_utils import run_bass_kernel_spmd

F32 = mybir.dt.float32
BF16 = mybir.dt.bfloat16
ALU = mybir.AluOpType
AF = mybir.ActivationFunctionType
AX = mybir.AxisListType

D = 1024
NCTX = 256
NLAT = 4096
TOK = NCTX + NLAT
NT = TOK // 128
TOKP = TOK + 4
EPS = 1e-6
INPUT_SHAPES = {
    "x": [NLAT, D],
    "ctx": [NCTX, D],
    "cvec": [2, D],
    "ada_w": [2, D, 6 * D],
    "ada_b": [2, 6 * D],
    "norm1_g": [2, D],
    "norm2_g": [2, D],
    "w_in": [D, 2432],
    "conv_w": [3, 512],
    "qg2": [128],
    "kg2": [128],
    "w_out": [D, D],
    "od_mu": [6, D],
    "od_w_r": [D, D],
    "od_w_k": [D, D],
    "od_w_v": [D, D],
    "od_w_o": [D, D],
    "od_g1": [D, 128],
    "od_g2": [128, D],
    "od_k_k": [D],
    "od_k_a": [D],
    "od_r_k": [D],
    "od_w0": [2, D],
    "od_w1": [2, D, 64],
    "od_w2": [2, 64, D],
    "od_a0": [2, D],
    "od_a1": [2, D, 64],
    "od_a2": [2, 64, D],
    "od_gn_w": [D],
    "od_gn_b": [D],
    "peer_wq": [2, D, 2048],
    "peer_keys": [2, 16, 128, 128],
    "peer_u": [2, 16384, D],
    "peer_v": [2, 16384, D],
    "ident": [128, 128],
    "ropeC": [128, TOK],
    "ropeS": [128, TOK],
    "ropeP": [128, 128],
    "blk64": [128, 128],
    "trimask": [128, 4, 128],
}


class T:
    __slots__ = ("t", "w", "r", "sem", "dcount", "name")

    def __init__(self, t, name):
        self.t = t
        self.name = name
        self.w = {}
        self.r = {}
        self.sem = None
        self.dcount = 0

    def __getitem__(self, k):
        return self.t[k]

    def ap(self):
        return self.t.ap()


class FW:
    def __init__(self, nc, es):
        self.nc = nc
        self.es = es
        self.root_es = es
        self.stack = []
        self.live = []
        self.drams = []
        self.sempool = []
        self.uid = 0
        self.eng = {"pe": nc.tensor, "act": nc.scalar, "dve": nc.vector, "pool": nc.gpsimd, "sp": nc.sync}
        self.esem = {}
        self.ecount = {}
        self.known = {}
        self.nsem = 0
        for k in self.eng:
            self.esem[k] = self.newsem("e_" + k)
            self.ecount[k] = 0
            self.known[k] = {}
        self.selfwait = {"pe": False, "act": True, "dve": True, "pool": True, "sp": False}
        self.ninst = 0

    def newsem(self, name):
        self.nsem += 1
        assert self.nsem < 200, "too many semaphores"
        return self.root_es.enter_context(self.nc.semaphore(name))

    def push(self):
        self.stack.append((self.es, self.live))
        self.es = ExitStack()
        self.live = []

    def pop(self):
        self.barrier()
        for t in self.live:
            if t.sem is not None:
                self.sempool.append((t.sem, t.dcount))
                t.sem = None
        self.es.close()
        self.es, self.live = self.stack.pop()

    def barrier(self):
        toks = {}
        for k in self.eng:
            if self.ecount[k] > 0:
                toks[self.esem[k]] = self.ecount[k]
        for _, live in self.stack + [(None, self.live)]:
            for t in live:
                if t.sem is not None and t.dcount > 0:
                    toks[t.sem] = 16 * t.dcount
        for t in self.drams:
            if t.sem is not None and t.dcount > 0:
                toks[t.sem] = 16 * t.dcount
        for e in self.eng:
            kn = self.known[e]
            for s_, v in toks.items():
                if s_ is self.esem[e]:
                    continue
                if kn.get(s_, 0) < v:
                    self.eng[e].wait_ge(s_, v)
                    kn[s_] = v
                    self.ninst += 1

    def sb(self, name, shape, dt=F32):
        self.uid += 1
        t = T(self.es.enter_context(self.nc.sbuf_tensor("%s_%d" % (name, self.uid), list(shape), dt)), name)
        self.live.append(t)
        return t

    def ps(self, name, shape, dt=F32):
        t = T(self.es.enter_context(self.nc.psum_tensor(name, list(shape), dt)), name)
        self.live.append(t)
        return t

    def dram(self, name, shape, dt=F32, kind="Internal"):
        t = T(self.nc.dram_tensor(name, list(shape), dt, kind=kind), name)
        self.drams.append(t)
        return t

    def _waits(self, e, reads, writes):
        toks = {}

        def add(tok):
            if tok is None:
                return
            s, v = tok
            if toks.get(s, 0) < v:
                toks[s] = v
        for r in reads:
            for s, v in r.w.items():
                add((s, v))
        for w in writes:
            for s, v in w.w.items():
                add((s, v))
            for s, v in w.r.items():
                add((s, v))
        eng = self.eng[e]
        kn = self.known[e]
        for s, v in toks.items():
            if s is self.esem[e] and not self.selfwait[e]:
                continue
            if kn.get(s, 0) < v:
                eng.wait_ge(s, v)
                kn[s] = v
                self.ninst += 1

    def op(self, e, fn, reads=(), writes=()):
        self._waits(e, reads, writes)
        inst = fn(self.eng[e])
        self.ecount[e] += 1
        self.ninst += 1
        s = self.esem[e]
        inst.then_inc(s, 1)
        tok = (s, self.ecount[e])
        for w in writes:
            w.w = {s: tok[1]}
            w.r = {}
        for r in reads:
            if r.r.get(s, 0) < tok[1]:
                r.r[s] = tok[1]
        return inst

    def dma(self, e, outT, out_ap, inT, in_ap, **kw):
        self._waits(e, [inT], [outT])
        own = inT if (outT in self.drams and inT not in self.drams) else outT
        if own.sem is None:
            if self.sempool:
                own.sem, own.dcount = self.sempool.pop()
            else:
                own.sem = self.newsem("d%d" % self.nsem)
        inst = self.eng[e].dma_start(out=out_ap, in_=in_ap, **kw)
        inst.then_inc(own.sem, 16)
        self.ninst += 1
        own.dcount += 1
        tok = (own.sem, 16 * own.dcount)
        if own is outT:
            outT.w = {tok[0]: tok[1]}
        else:
            outT.w[tok[0]] = tok[1]
        outT.r = {}
        if inT.r.get(tok[0], 0) < tok[1]:
            inT.r[tok[0]] = tok[1]
        return inst

    def finish(self, e, tiles):
        self._waits(e, tiles, [])


class Rot:
    def __init__(self, items):
        self.items = items
        self.i = -1

    def next(self):
        self.i = (self.i + 1) % len(self.items)
        return self.items[self.i]


def tokcol(tile):
    return 1 + tile * 128 if tile < 2 else 259 + (tile - 2) * 128


class Prog:
    def __init__(self, dbg=False, stop=None):
        self.dbg = dbg
        self.stop = stop
        self.nc = bass.Bass("TRN2", target_bir_lowering=False)
        self.es = ExitStack()
        self.fw = FW(self.nc, self.es)
        fw = self.fw

        prog = self

        class Lazy(dict):
            def __missing__(d, name):
                t = fw.dram(name, INPUT_SHAPES[name], F32, kind="ExternalInput")
                d[name] = t
                return t
        self.I = Lazy()
        self.out = fw.dram("out", [NLAT, D], F32, kind="ExternalOutput")
        k = "ExternalOutput" if dbg else "Internal"
        self.modrow = fw.dram("modrow", [2, 2, 6 * D], F32, kind=k)
        self.h1 = fw.dram("h1", [TOK, D], F32, kind=k)
        self.h2 = fw.dram("h2", [TOK, D], F32, kind=("ExternalInput" if stop and stop.startswith("L1") else k))
        self.h3 = fw.dram("h3", [TOK, D], F32, kind=k)
        self.uT = fw.dram("uT", [512, TOKP], BF16)
        if dbg:
            self.dza = fw.dram("dza", [TOK, 512], BF16, kind="ExternalOutput")
            self.dzc = fw.dram("dzc", [512, TOK], BF16, kind="ExternalOutput")
        self.gbT = fw.dram("gbT", [512, TOKP], BF16)
        self.UT = fw.dram("UT", [128, 128, 8, 128], BF16)
        self.Vb = fw.dram("Vb", [128, 128, D], BF16)
        self.banks = [fw.ps("bank%d" % i, [128, 512]) for i in range(8)]
        self.bank = Rot(self.banks)
        self.identf = fw.sb("identf", [128, 128])
        self.identb = fw.sb("identb", [128, 128], BF16)
        fw.dma("sp", self.identf, self.identf[:], self.I["ident"], self.I["ident"].ap())
        fw.dma("pool", self.identb, self.identb[:], self.I["ident"], self.I["ident"].ap())

    def bview(self, bank):
        return bank.ap().bitcast(BF16)

    def load_colT(self, dst, dst_ap, src, src_ap1d):
        self.fw.dma("sp", dst, dst_ap, src, src_ap1d.rearrange("(k p) -> p k", p=128), allow_slow_non_contiguous=True)

    def modvecs(self, layer, which):
        fw = self.fw
        I = self.I
        gname = "norm1_g" if which == 0 else "norm2_g"
        g = fw.sb("g", [128, 8])
        self.load_colT(g, g[:], I[gname], I[gname].ap()[layer])
        GT, shT = [], []
        for s in range(2):
            sc = fw.sb("scl", [128, 8]); sh = fw.sb("shf", [128, 8]); G = fw.sb("G", [128, 8])
            base = which * 3 * D
            self.load_colT(sh, sh[:], self.modrow, self.modrow.ap()[layer, s, base:base + D])
            self.load_colT(sc, sc[:], self.modrow, self.modrow.ap()[layer, s, base + D:base + 2 * D])
            fw.op("dve", lambda e: e.scalar_tensor_tensor(out=G[:], in0=sc[:], scalar=1.0, in1=g[:], op0=ALU.add, op1=ALU.mult), [sc, g], [G])
            GT.append(G); shT.append(sh)
        return GT, shT

    def gate_bcast(self, layer, which):
        fw = self.fw
        res = []
        for s in range(2):
            gt = fw.sb("gate", [128, D])
            off = which * 3 * D + 2 * D
            fw.dma("sp", gt, gt[:], self.modrow, self.modrow.ap()[layer, s, off:off + D].partition_broadcast(128))
            res.append(gt)
        return res

    def make_normer(self):
        fw = self.fw
        self.nb_h = Rot([fw.sb("nh", [128, D]) for _ in range(2)])
        self.nb_sq = fw.sb("nsq", [128, D], BF16)
        self.nb_ss = Rot([fw.sb("nss", [128, 1]) for _ in range(2)])
        self.nb_rs = Rot([fw.sb("nrs", [128, 1]) for _ in range(2)])
        self.nb_hn = Rot([fw.sb("nhn", [128, D], BF16) for _ in range(2)])

    def norm_tile(self, src, src_ap, G, sh, xnT, xnT_cols):
        fw = self.fw
        h = self.nb_h.next(); ss = self.nb_ss.next(); rs = self.nb_rs.next(); hn = self.nb_hn.next(); sq = self.nb_sq
        fw.dma("sp", h, h[:], src, src_ap)
        fw.op("act", lambda e: e.activation(out=sq[:], in_=h[:], func=AF.Square, scale=1.0 / 32, accum_out=ss[:]), [h], [sq, ss])
        fw.op("act", lambda e: e.activation(out=rs[:], in_=ss[:], func=AF.Sqrt, bias=EPS, scale=1.0), [ss], [rs])
        fw.op("dve", lambda e: e.reciprocal(out=rs[:], in_=rs[:]), [rs], [rs])
        fw.op("act", lambda e: e.activation(out=hn[:], in_=h[:], func=AF.Copy, scale=rs[:]), [h, rs], [hn])
        pb = self.bank.next()
        pv = self.bview(pb)
        for k in range(8):
            fw.op("pe", lambda e: e.transpose(out=pv[:, k * 128:(k + 1) * 128], in_=hn[:, k * 128:(k + 1) * 128], identity=self.identb[:]), [hn, self.identb], [pb])
        for k in range(8):
            fw.op("act", lambda e: e.activation(out=xnT[:, k, xnT_cols], in_=pv[:, k * 128:(k + 1) * 128], func=AF.Identity,
                                                scale=G[:, k:k + 1], bias=sh[:, k:k + 1]), [pb, G, sh], [xnT])
        return h

    def src_of(self, tile, hsrc):
        if hsrc is None:
            if tile < 2:
                return self.I["ctx"], self.I["ctx"].ap()[tile * 128:(tile + 1) * 128, :]
            return self.I["x"], self.I["x"].ap()[(tile - 2) * 128:(tile - 1) * 128, :]
        return hsrc, hsrc.ap()[tile * 128:(tile + 1) * 128, :]

    def phase0(self):
        fw = self.fw
        I = self.I
        fw.push()
        sc = fw.sb("sc", [128, 2, 8])
        for s_ in range(2):
            self.load_colT(sc, sc[:, s_, :], I["cvec"], I["cvec"].ap()[s_])
        fw.op("act", lambda e: e.activation(out=sc[:], in_=sc[:], func=AF.Silu), [sc], [sc])
        wb = Rot([fw.sb("adaw", [128, 8, 512]) for _ in range(2)])
        for i in range(2):
            bias = fw.sb("adab", [2, 6 * D])
            modr = fw.sb("modr", [2, 6 * D])
            fw.dma("sp", bias, bias[:], I["ada_b"], I["ada_b"].ap()[i].partition_broadcast(2))
            for cb in range(12):
                wt = wb.next()
                fw.dma("sp" if cb % 2 == 0 else "act", wt, wt[:], I["ada_w"], I["ada_w"].ap()[i, :, cb * 512:(cb + 1) * 512].rearrange("(k p) n -> p k n", p=128))
                pb = self.bank.next()
                for k in range(8):
                    fw.op("pe", lambda e: e.matmul(pb[0:2, :], lhsT=sc[:, :, k], rhs=wt[:, k, :], start=(k == 0), stop=(k == 7)), [sc, wt], [pb])
                fw.op("dve", lambda e: e.tensor_tensor(out=modr[:, cb * 512:(cb + 1) * 512], in0=pb[0:2, :], in1=bias[:, cb * 512:(cb + 1) * 512], op=ALU.add), [pb, bias], [modr])
            fw.dma("sp", self.modrow, self.modrow.ap()[i], modr, modr[:])
        fw.pop()

    def l0_phaseA(self):
        fw = self.fw
        I = self.I
        fw.push()
        self.qT = fw.sb("qT", [128, 4, TOK], BF16)
        self.kT2 = fw.sb("kT2", [128, 2, TOK], BF16)
        self.vext = fw.sb("vext", [128, NT, 2, 66], BF16)
        fw.op("pool", lambda e: e.memset(self.vext[:], 1.0), [], [self.vext])
        fw.push()
        GT, shT = self.modvecs(0, 0)
        self.make_normer()
        w_in = fw.sb("w_in", [128, 8, 2432], BF16)
        for k in range(8):
            fw.dma("pool", w_in, w_in[:, k, :], I["w_in"], I["w_in"].ap()[k * 128:(k + 1) * 128, :])
        rC = fw.sb("rC", [128, TOK], BF16); rS = fw.sb("rS", [128, TOK], BF16)
        fw.dma("pool", rC, rC[:], I["ropeC"], I["ropeC"].ap())
        fw.dma("pool", rS, rS[:], I["ropeS"], I["ropeS"].ap())
        ropeP = fw.sb("ropeP", [128, 128], BF16); blk64 = fw.sb("blk64", [128, 128], BF16)
        fw.dma("pool", ropeP, ropeP[:], I["ropeP"], I["ropeP"].ap())
        fw.dma("pool", blk64, blk64[:], I["blk64"], I["blk64"].ap())
        gains = fw.sb("gains", [128, 2])
        fw.dma("sp", gains, gains[:, 0:1], I["qg2"], I["qg2"].ap().rearrange("(p o) -> p o", o=1))
        fw.dma("sp", gains, gains[:, 1:2], I["kg2"], I["kg2"].ap().rearrange("(p o) -> p o", o=1))
        zt = fw.sb("zt", [128, 4, 2], BF16)
        fw.op("pool", lambda e: e.memset(zt[:], 0.0), [], [zt])
        for dst in (self.uT,):
            v = dst.ap().rearrange("(b p) c -> p b c", p=128)
            fw.dma("pool", dst, v[:, :, 0:1], zt, zt[:, :, 0:1], allow_slow_non_contiguous=True)
            fw.dma("pool", dst, v[:, :, 257:259], zt, zt[:, :, 0:2], allow_slow_non_contiguous=True)
            fw.dma("pool", dst, v[:, :, TOKP - 1:TOKP], zt, zt[:, :, 0:1], allow_slow_non_contiguous=True)
        xnTs = Rot([fw.sb("xnT", [128, 8, 256], BF16) for _ in range(2)])
        hTt = fw.sb("hTt", [128, 4, 256])
        gbst = Rot([fw.sb("gbst", [128, 4, 256], BF16) for _ in range(2)])
        ust = Rot([fw.sb("ust", [128, 4, 256], BF16) for _ in range(2)])
        qraw = Rot([fw.sb("qraw", [128, 256]) for _ in range(2)])
        sqb = Rot([fw.sb("sqb", [128, 256], BF16) for _ in range(2)])
        rsb = Rot([fw.sb("rsb", [128, 256]) for _ in range(2)])
        qnb = Rot([fw.sb("qnb", [128, 256], BF16) for _ in range(2)])
        t1b = Rot([fw.sb("t1b", [128, 256]) for _ in range(2)])
        t2b = Rot([fw.sb("t2b", [128, 256]) for _ in range(2)])
        uTv = self.uT.ap().rearrange("(b p) c -> p b c", p=128)
        gbTv = self.gbT.ap().rearrange("(b p) c -> p b c", p=128)
        import os
        skip = os.environ.get('L0A_SKIP', '').split(',')
        for g in range(int(os.environ.get('L0A_NG', NT // 2))):
            s = 1 if g == 0 else 0
            xnT = xnTs.next()
            for j in range(2):
                tile = 2 * g + j
                src, sap = self.src_of(tile, None)
                self.norm_tile(src, sap, GT[s], shT[s], xnT, slice(j * 128, (j + 1) * 128))
            tok0 = g * 256
            gb_s = gbst.next(); u_s = ust.next()
            for fb in range(18):
                if fb >= 12 and 'qk' in skip:
                    continue
                if fb < 12 and 'conv' in skip:
                    continue
                pb = self.bank.next()
                for k in range(8):
                    fw.op("pe", lambda e: e.matmul(pb[:, 0:256], lhsT=w_in[:, k, fb * 128:(fb + 1) * 128], rhs=xnT[:, k, :], start=(k == 0), stop=(k == 7)), [w_in, xnT], [pb])
                if fb < 4:
                    fw.op("act", lambda e: e.activation(out=hTt[:, fb, :], in_=pb[:, 0:256], func=AF.Copy), [pb], [hTt])
                elif fb < 8:
                    fw.op("act", lambda e: e.activation(out=gb_s[:, fb - 4, :], in_=pb[:, 0:256], func=AF.Copy), [pb], [gb_s])
                elif fb < 12:
                    fw.op("dve", lambda e: e.tensor_tensor(out=u_s[:, fb - 8, :], in0=pb[:, 0:256], in1=hTt[:, fb - 8, :], op=ALU.mult), [pb, hTt], [u_s])
                else:
                    isq = fb < 16
                    qr = qraw.next(); sq = sqb.next(); rs = rsb.next(); qn = qnb.next(); t1 = t1b.next(); t2 = t2b.next()
                    gcol = gains[:, 0:1] if isq else gains[:, 1:2]
                    fw.op("act", lambda e: e.activation(out=qr[:], in_=pb[:, 0:256], func=AF.Copy), [pb], [qr])
                    fw.op("act", lambda e: e.activation(out=sq[:], in_=pb[:, 0:256], func=AF.Square), [pb], [sq])
                    p2 = self.bank.next()
                    fw.op("pe", lambda e: e.matmul(p2[:, 0:256], lhsT=blk64[:], rhs=sq[:], start=True, stop=True), [blk64, sq], [p2])
                    fw.op("act", lambda e: e.activation(out=rs[:], in_=p2[:, 0:256], func=AF.Sqrt, bias=EPS, scale=1.0 / 64), [p2], [rs])
                    fw.op("dve", lambda e: e.reciprocal(out=rs[:], in_=rs[:]), [rs], [rs])
                    fw.op("dve", lambda e: e.scalar_tensor_tensor(out=qn[:], in0=qr[:], scalar=gcol, in1=rs[:], op0=ALU.mult, op1=ALU.mult), [qr, gains, rs], [qn])
                    p3 = self.bank.next()
                    fw.op("pe", lambda e: e.matmul(p3[:, 0:256], lhsT=ropeP[:], rhs=qn[:], start=True, stop=True), [ropeP, qn], [p3])
                    fw.op("dve", lambda e: e.tensor_tensor(out=t2[:], in0=p3[:, 0:256], in1=rS[:, tok0:tok0 + 256], op=ALU.mult), [p3, rS], [t2])
                    fw.op("pool", lambda e: e.tensor_tensor(out=t1[:], in0=qn[:], in1=rC[:, tok0:tok0 + 256], op=ALU.mult), [qn, rC], [t1])
                    dst = self.qT if isq else self.kT2
                    bi = fb - 12 if isq else fb - 16
                    fw.op("dve", lambda e: e.tensor_tensor(out=dst[:, bi, tok0:tok0 + 256], in0=t1[:], in1=t2[:], op=ALU.add), [t1, t2], [dst])
            for j in range(2 if 'v' not in skip else 0):
                tile = 2 * g + j
                pb = self.bank.next()
                for k in range(8):
                    fw.op("pe", lambda e: e.matmul(pb[:, 0:128], lhsT=xnT[:, k, j * 128:(j + 1) * 128], rhs=w_in[:, k, 2304:2432], start=(k == 0), stop=(k == 7)), [xnT, w_in], [pb])
                fw.op("act", lambda e: e.activation(out=self.vext[:, tile, :, 0:64], in_=pb[:, 0:128].rearrange("p (a d) -> p a d", a=2), func=AF.Copy), [pb], [self.vext])
            for j in range(2 if 'store' not in skip else 0):
                c0 = tokcol(2 * g + j)
                fw.dma("sp", self.uT, uTv[:, :, c0:c0 + 128], u_s, u_s[:, :, j * 128:(j + 1) * 128])
                fw.dma("sp", self.gbT, gbTv[:, :, c0:c0 + 128], gb_s, gb_s[:, :, j * 128:(j + 1) * 128])
        fw.pop()

    def l0_phaseB(self):
        fw = self.fw
        I = self.I
        fw.push()
        gate = self.gate_bcast(0, 0)
        w_out = fw.sb("w_out", [128, 8, D], BF16)
        for k in range(8):
            fw.dma("pool", w_out, w_out[:, k, :], I["w_out"], I["w_out"].ap()[k * 128:(k + 1) * 128, :])
        cw = fw.sb("cw", [128, 3, 4])
        for j_ in range(3):
            self.load_colT(cw, cw[:, j_, :], I["conv_w"], I["conv_w"].ap()[j_])
        PTs = Rot([fw.sb("PT", [128, NT, 512], BF16) for _ in range(2)])
        zatt = Rot([fw.sb("zatt", [128, 512], BF16) for _ in range(2)])
        rcb = Rot([fw.sb("rcb", [128, 4]) for _ in range(2)])
        zaTs = Rot([fw.sb("zaT", [128, 4, 128], BF16) for _ in range(2)])
        zcTs = Rot([fw.sb("zcT", [128, 4, 128], BF16) for _ in range(2)])
        uhs = Rot([fw.sb("uh", [128, 4, 130], BF16) for _ in range(2)])
        gbts = Rot([fw.sb("gbt", [128, 4, 128], BF16) for _ in range(2)])
        accs = Rot([fw.sb("acc", [128, 128]) for _ in range(3)])
        hts = Rot([fw.sb("ht", [128, D]) for _ in range(2)])
        tmps = Rot([fw.sb("tmp", [128, D]) for _ in range(2)])
        uTv = self.uT.ap().rearrange("(b p) c -> p b c", p=128)
        gbTv = self.gbT.ap().rearrange("(b p) c -> p b c", p=128)
        obanks = Rot(self.banks[0:2]); sbanks = Rot(self.banks[2:6]); self.bank = Rot(self.banks[6:8])
        import os
        skip = os.environ.get('L0B_SKIP', '').split(',')
        for qt in range(int(os.environ.get('L0B_NT', NT))):
            s = 1 if qt < 2 else 0
            nch = 2 if qt < 2 else NT
            ht = hts.next()
            src, sap = self.src_of(qt, None)
            fw.dma("sp", ht, ht[:], src, sap)
            uh = uhs.next(); gbt = gbts.next()
            c0 = tokcol(qt)
            fw.dma("sp", uh, uh[:], self.uT, uTv[:, :, c0 - 1:c0 + 129])
            fw.dma("sp", gbt, gbt[:], self.gbT, gbTv[:, :, c0:c0 + 128])
            za = zatt.next()
            for kv in range(2 if 'attn' not in skip else 0):
                PTall = PTs.next()
                for sc in range(nch):
                    sbs = [sbanks.next(), sbanks.next()]
                    for b in range(2):
                        for hf in range(2):
                            fw.op("pe", lambda e: e.matmul(sbs[hf][:, b * 128:(b + 1) * 128], lhsT=self.kT2[hf * 64:(hf + 1) * 64, kv, sc * 128:(sc + 1) * 128],
                                                            rhs=self.qT[hf * 64:(hf + 1) * 64, 2 * kv + b, qt * 128:(qt + 1) * 128], start=True, stop=True), [self.kT2, self.qT], [sbs[hf]])
                    PT4 = PTall[:, sc, :].rearrange("p (b h n) -> p b h n", b=2, h=2)
                    for hf in range(2):
                        fw.op("act", lambda e: e.activation(out=PT4[:, :, hf, :], in_=sbs[hf][:, 0:256].rearrange("p (b n) -> p b n", b=2), func=AF.Exp, scale=0.125), [sbs[hf]], [PTall])
                ob = obanks.next()
                O = ob[:, 0:260].rearrange("p (c e) -> p c e", e=65)
                for c in range(4):
                    for sc in range(nch):
                        fw.op("pe", lambda e: e.matmul(O[:, c, :], lhsT=PTall[:, sc, c * 128:(c + 1) * 128], rhs=self.vext[:, sc, kv, 0:65], start=(sc == 0), stop=(sc == nch - 1)), [PTall, self.vext], [ob])
                rc = rcb.next()
                fw.op("dve", lambda e: e.reciprocal(out=rc[:], in_=O[:, :, 64]), [ob], [rc])
                fw.op("dve", lambda e: e.tensor_tensor(out=za[:, kv * 256:(kv + 1) * 256].rearrange("p (c d) -> p c d", c=4), in0=O[:, :, 0:64],
                                                       in1=rc[:].unsqueeze(2).to_broadcast([128, 4, 64]), op=ALU.mult), [ob, rc], [za])
            pb = self.bank.next(); pv = self.bview(pb)
            for k in range(4):
                fw.op("pe", lambda e: e.transpose(out=pv[:, k * 128:(k + 1) * 128], in_=za[:, k * 128:(k + 1) * 128], identity=self.identb[:]), [za, self.identb], [pb])
            zaT = zaTs.next()
            fw.op("act", lambda e: e.activation(out=zaT[:].rearrange("p k n -> p (k n)"), in_=pv[:, 0:512], func=AF.Copy), [pb], [zaT])
            zcT = zcTs.next()
            for b in range(4 if 'conv' not in skip else 0):
                acc = accs.next()
                fw.op("pool", lambda e: e.tensor_scalar(out=acc[:], in0=uh[:, b, 0:128], scalar1=cw[:, 0, b:b + 1], scalar2=None, op0=ALU.mult), [uh, cw], [acc])
                for j_ in (1, 2):
                    t_ = accs.next()
                    fw.op("pool", lambda e: e.tensor_scalar(out=t_[:], in0=uh[:, b, j_:j_ + 128], scalar1=cw[:, j_, b:b + 1], scalar2=None, op0=ALU.mult), [uh, cw], [t_])
                    fw.op("pool", lambda e: e.tensor_tensor(out=acc[:], in0=acc[:], in1=t_[:], op=ALU.add), [acc, t_], [acc])
                fw.op("pool", lambda e: e.tensor_tensor(out=zcT[:, b, :], in0=acc[:], in1=gbt[:, b, :], op=ALU.mult), [acc, gbt], [zcT])
            if self.dbg:
                fw.dma("sp", self.dza, self.dza.ap()[qt * 128:(qt + 1) * 128, :], za, za[:])
                fw.dma("sp", self.dzc, self.dzc.ap().rearrange("(b p) c -> p b c", p=128)[:, :, qt * 128:(qt + 1) * 128], zcT, zcT[:])
            tmp = tmps.next()
            for half in range(2):
                pb = self.bank.next()
                for k in range(8):
                    lt = zcT[:, k, :] if k < 4 else zaT[:, k - 4, :]
                    fw.op("pe", lambda e: e.matmul(pb[:], lhsT=lt, rhs=w_out[:, k, half * 512:(half + 1) * 512], start=(k == 0), stop=(k == 7)), [zcT, zaT, w_out], [pb])
                fw.op("dve", lambda e: e.tensor_tensor(out=tmp[:, half * 512:(half + 1) * 512], in0=pb[:], in1=gate[s][:, half * 512:(half + 1) * 512], op=ALU.mult), [pb, gate[s]], [tmp])
            fw.op("pool", lambda e: e.tensor_tensor(out=tmp[:], in0=tmp[:], in1=ht[:], op=ALU.add), [tmp, ht], [tmp])
            fw.dma("act", self.h1, self.h1.ap()[qt * 128:(qt + 1) * 128, :], tmp, tmp[:])
        self.bank = Rot(self.banks)
        fw.pop()
        fw.pop()

    def peer_prep(self, layer):
        fw = self.fw
        I = self.I
        fw.push()
        banks = Rot(self.banks)
        Uv = I["peer_u"].ap()[layer].rearrange("(i j) d -> j i d", j=128)
        Vv = I["peer_v"].ap()[layer].rearrange("(i j) d -> j i d", j=128)
        UTv = self.UT.ap().rearrange("i p k j -> p i (k j)")
        Vbv = self.Vb.ap().rearrange("i j d -> j i d")
        ubs = Rot([fw.sb("ub", [128, 4, D], BF16) for _ in range(2)])
        vbs = Rot([fw.sb("vb", [128, 4, D], BF16) for _ in range(2)])
        uts = Rot([fw.sb("ut", [128, 4, 1024], BF16) for _ in range(2)])
        for c in range(32):
            ub = ubs.next(); vb = vbs.next(); ut = uts.next()
            fw.dma("pool", ub, ub[:], I["peer_u"], Uv[:, c * 4:(c + 1) * 4, :])
            fw.dma("pool", vb, vb[:], I["peer_v"], Vv[:, c * 4:(c + 1) * 4, :])
            fw.dma("sp", self.Vb, Vbv[:, c * 4:(c + 1) * 4, :], vb, vb[:])
            for ii in range(4):
                pb = banks.next(); pv = self.bview(pb)
                for k in range(8):
                    fw.op("pe", lambda e: e.transpose(out=pv[:, k * 128:(k + 1) * 128], in_=ub[:, ii, k * 128:(k + 1) * 128], identity=self.identb[:]), [ub, self.identb], [pb])
                if ii % 2 == 0:
                    fw.op("act", lambda e: e.activation(out=ut[:, ii, :], in_=pv[:, 0:1024], func=AF.Copy), [pb], [ut])
                else:
                    fw.op("dve", lambda e: e.tensor_copy(out=ut[:, ii, :], in_=pv[:, 0:1024]), [pb], [ut])
            fw.dma("sp", self.UT, UTv[:, c * 4:(c + 1) * 4, :], ut, ut[:])
        fw.pop()

    def peer(self, layer, hsrc, hdst, final=False):
        fw = self.fw
        I = self.I
        fw.push()
        GT, shT = self.modvecs(layer, 1)
        gate = self.gate_bcast(layer, 1)
        self.make_normer()
        obanks = self.banks[0:4]
        self.bank = Rot(self.banks[4:8])
        w_q = fw.sb("w_q", [128, 8, 2048], BF16)
        for k in range(8):
            fw.dma("pool", w_q, w_q[:, k, :], I["peer_wq"], I["peer_wq"].ap()[layer, k * 128:(k + 1) * 128, :])
        keyn = fw.sb("keyn", [128, 16, 128], BF16)
        fw.dma("pool", keyn, keyn[:], I["peer_keys"], I["peer_keys"].ap()[layer].rearrange("h k d -> k h d"))
        keysT = fw.sb("keysT", [128, 16, 128], BF16)
        for c in range(2):
            pb = self.bank.next(); pv = self.bview(pb)
            for q in range(8):
                fw.op("pe", lambda e: e.transpose(out=pv[:, q * 128:(q + 1) * 128], in_=keyn[:, c * 8 + q, :], identity=self.identb[:]), [keyn, self.identb], [pb])
            fw.op("act", lambda e: e.activation(out=keysT[:, c * 8:(c + 1) * 8, :].rearrange("p a b -> p (a b)"), in_=pv[:, 0:1024], func=AF.Copy), [pb], [keysT])
        xn2T = fw.sb("xn2T", [128, 8, 256], BF16)
        qT = fw.sb("pqT", [128, 16, 256], BF16)
        s_all = fw.sb("s_all", [128, 2, 16, 128])
        tau = fw.sb("tau", [128, 2, 8]); bias = fw.sb("pbias", [128, 2, 8])
        m0 = fw.sb("m0", [128, 16]); m1 = fw.sb("m1", [128, 16]); scr = fw.sb("scr", [128, 128])
        cand = fw.sb("cand", [128, 16, 16]); c24 = fw.sb("c24", [128, 24]); cscr = fw.sb("cscr", [128, 256])
        st1 = fw.sb("st1", [128, 4]); e16 = fw.sb("e16", [128, 16])
        IB = 16
        Ws = Rot([fw.sb("W", [128, 2, IB, 128], BF16) for _ in range(2)])
        sums = Rot([fw.sb("sum", [128, IB, 128]) for _ in range(2)])
        Ps = Rot([fw.sb("P", [128, IB, 128], BF16) for _ in range(2)])
        Whs = Rot([fw.sb("Wh", [128, IB, 128], BF16) for _ in range(2)])
        UTs = Rot([fw.sb("UTs", [128, 2, 1024], BF16) for _ in range(2)])
        Vs = Rot([fw.sb("Vs", [128, 2, D], BF16) for _ in range(2)])
        Gs = Rot([fw.sb("Gs", [128, 256], BF16) for _ in range(2)])
        WTs = Rot([fw.sb("WTs", [128, 256], BF16) for _ in range(2)])
        WAs = Rot([fw.sb("WAs", [128, 256], BF16) for _ in range(2)])
        tmps = Rot([fw.sb("ptmp", [128, D]) for _ in range(2)])
        hts = Rot([fw.sb("pht", [128, D]) for _ in range(2)])
        UTv = self.UT.ap().rearrange("i p k j -> p i (k j)")
        Vbv = self.Vb.ap().rearrange("i j d -> j i d")
        import os
        ng = int(os.environ.get("PEER_NG", NT // 2))
        for g in range(ng):
            if final and g == 0:
                continue
            s = 1 if g == 0 else 0
            for j in range(2):
                src, sap = self.src_of(2 * g + j, hsrc)
                self.norm_tile(src, sap, GT[s], shT[s], xn2T, slice(j * 128, (j + 1) * 128))
            for hp in range(16):
                pb = self.bank.next()
                for k in range(8):
                    fw.op("pe", lambda e: e.matmul(pb[:, 0:256], lhsT=w_q[:, k, hp * 128:(hp + 1) * 128], rhs=xn2T[:, k, :], start=(k == 0), stop=(k == 7)), [w_q, xn2T], [pb])
                fw.op("act", lambda e: e.activation(out=qT[:, hp, :], in_=pb[:, 0:256], func=AF.Copy), [pb], [qT])
            for t in range(2):
                for c in range(4):
                    pb = self.bank.next()
                    for q in range(4):
                        hp = c * 4 + q
                        fw.op("pe", lambda e: e.matmul(pb[:, q * 128:(q + 1) * 128], lhsT=qT[:, hp, t * 128:(t + 1) * 128], rhs=keysT[:, hp, :], start=True, stop=True), [qT, keysT], [pb])
                    fw.op("act", lambda e: e.activation(out=s_all[:, t, c * 4:(c + 1) * 4, :].rearrange("p a b -> p (a b)"), in_=pb[:], func=AF.Copy), [pb], [s_all])
            for t in range(2):
                for h in range(8):
                    for (mm, src) in ((m0, s_all[:, t, 2 * h, :]), (m1, s_all[:, t, 2 * h + 1, :])):
                        fw.op("dve", lambda e: e.max(out=mm[:, 0:8], in_=src), [s_all], [mm])
                        fw.op("dve", lambda e: e.match_replace(out=scr[:], in_to_replace=mm[:, 0:8], in_values=src, imm_value=-1e30), [s_all, mm], [scr])
                        fw.op("dve", lambda e: e.max(out=mm[:, 8:16], in_=scr[:]), [scr], [mm])
                    fw.op("dve", lambda e: e.tensor_tensor(out=cand[:], in0=m0[:].unsqueeze(2).to_broadcast([128, 16, 16]), in1=m1[:].unsqueeze(1).to_broadcast([128, 16, 16]), op=ALU.add), [m0, m1], [cand])
                    cf = cand[:].rearrange("p a b -> p (a b)")
                    fw.op("dve", lambda e: e.max(out=c24[:, 0:8], in_=cf), [cand], [c24])
                    fw.op("dve", lambda e: e.match_replace(out=cscr[:], in_to_replace=c24[:, 0:8], in_values=cf, imm_value=-1e30), [cand, c24], [cscr])
                    fw.op("dve", lambda e: e.max(out=c24[:, 8:16], in_=cscr[:]), [cscr], [c24])
                    fw.op("dve", lambda e: e.match_replace(out=cscr[:], in_to_replace=c24[:, 8:16], in_values=cscr[:], imm_value=-1e30), [cscr, c24], [cscr])
                    fw.op("dve", lambda e: e.max(out=c24[:, 16:24], in_=cscr[:]), [cscr], [c24])
                    fw.op("dve", lambda e: e.tensor_scalar(out=st1[:, 0:1], in0=c24[:, 16:17], scalar1=0.5, scalar2=None, op0=ALU.mult), [c24], [st1])
                    fw.op("dve", lambda e: e.scalar_tensor_tensor(out=tau[:, t, h:h + 1], in0=c24[:, 15:16], scalar=0.5, in1=st1[:, 0:1], op0=ALU.mult, op1=ALU.add), [c24, st1], [tau])
                    fw.op("dve", lambda e: e.tensor_scalar(out=st1[:, 1:2], in0=c24[:, 0:1], scalar1=-1.0, scalar2=None, op0=ALU.mult), [c24], [st1])
                    fw.op("act", lambda e: e.activation(out=e16[:], in_=c24[:, 0:16], func=AF.Exp, bias=st1[:, 1:2], scale=1.0, accum_out=st1[:, 2:3]), [c24, st1], [e16, st1])
                    fw.op("act", lambda e: e.activation(out=st1[:, 3:4], in_=st1[:, 2:3], func=AF.Ln), [st1], [st1])
                    fw.op("dve", lambda e: e.tensor_tensor(out=bias[:, t, h:h + 1], in0=st1[:, 1:2], in1=st1[:, 3:4], op=ALU.subtract), [st1], [bias])
            for ib in range(128 // IB):
                i0 = ib * IB
                W = Ws.next()
                for t in range(2):
                    for h in range(8):
                        sm = sums.next(); P = Ps.next()
                        fw.op("dve", lambda e: e.tensor_tensor(out=sm[:], in0=s_all[:, t, 2 * h, i0:i0 + IB].unsqueeze(2).to_broadcast([128, IB, 128]),
                                                               in1=s_all[:, t, 2 * h + 1, :].unsqueeze(1).to_broadcast([128, IB, 128]), op=ALU.add), [s_all], [sm])
                        fw.op("act", lambda e: e.activation(out=P[:], in_=sm[:], func=AF.Exp, bias=bias[:, t, h:h + 1], scale=1.0), [sm, bias], [P])
                        if h == 0:
                            fw.op("dve", lambda e: e.scalar_tensor_tensor(out=W[:, t, :, :], in0=sm[:], scalar=tau[:, t, h:h + 1], in1=P[:], op0=ALU.is_ge, op1=ALU.mult), [sm, tau, P], [W])
                        else:
                            Wh = Whs.next()
                            fw.op("dve", lambda e: e.scalar_tensor_tensor(out=Wh[:], in0=sm[:], scalar=tau[:, t, h:h + 1], in1=P[:], op0=ALU.is_ge, op1=ALU.mult), [sm, tau, P], [Wh])
                            fw.op("pool", lambda e: e.tensor_tensor(out=W[:, t, :, :], in0=W[:, t, :, :], in1=Wh[:], op=ALU.add), [W, Wh], [W])
                for ii in range(0, IB, 2):
                    i = i0 + ii
                    UTt = UTs.next(); Vt = Vs.next()
                    fw.dma("sp", UTt, UTt[:], self.UT, UTv[:, i:i + 2, :])
                    fw.dma("sp", Vt, Vt[:], self.Vb, Vbv[:, i:i + 2, :])
                    for i2 in range(2):
                        iw = ii + i2
                        pa = self.bank.next()
                        for k in range(8):
                            fw.op("pe", lambda e: e.matmul(pa[:, 0:256], lhsT=UTt[:, i2, k * 128:(k + 1) * 128], rhs=xn2T[:, k, :], start=(k == 0), stop=(k == 7)), [UTt, xn2T], [pa])
                        G = Gs.next()
                        fw.op("act", lambda e: e.activation(out=G[:], in_=pa[:, 0:256], func=AF.Gelu), [pa], [G])
                        pw = self.bank.next(); pwv = self.bview(pw)
                        for t in range(2):
                            fw.op("pe", lambda e: e.transpose(out=pwv[:, t * 128:(t + 1) * 128], in_=W[:, t, iw, :], identity=self.identb[:]), [W, self.identb], [pw])
                        WT = WTs.next(); WA = WAs.next()
                        fw.op("act", lambda e: e.activation(out=WT[:], in_=pwv[:, 0:256], func=AF.Copy), [pw], [WT])
                        fw.op("pool", lambda e: e.tensor_tensor(out=WA[:], in0=G[:], in1=WT[:], op=ALU.mult), [G, WT], [WA])
                        first = (i + i2 == 0); last = (i + i2 == 127)
                        for t in range(2):
                            for half in range(2):
                                ob = obanks[t * 2 + half]
                                fw.op("pe", lambda e: e.matmul(ob[:], lhsT=WA[:, t * 128:(t + 1) * 128], rhs=Vt[:, i2, half * 512:(half + 1) * 512], start=first, stop=last), [WA, Vt], [ob])
            for t in range(2):
                tile = 2 * g + t
                tmp = tmps.next(); ht = hts.next()
                src, sap = self.src_of(tile, hsrc)
                fw.dma("sp", ht, ht[:], src, sap)
                for half in range(2):
                    ob = obanks[t * 2 + half]
                    fw.op("dve", lambda e: e.tensor_tensor(out=tmp[:, half * 512:(half + 1) * 512], in0=ob[:], in1=gate[s][:, half * 512:(half + 1) * 512], op=ALU.mult), [ob, gate[s]], [tmp])
                fw.op("pool", lambda e: e.tensor_tensor(out=tmp[:], in0=tmp[:], in1=ht[:], op=ALU.add), [tmp, ht], [tmp])
                if final:
                    fw.dma("act", self.out, self.out.ap()[(tile - 2) * 128:(tile - 1) * 128, :], tmp, tmp[:])
                else:
                    fw.dma("act", hdst, hdst.ap()[tile * 128:(tile + 1) * 128, :], tmp, tmp[:])
        self.bank = Rot(self.banks)
        fw.pop()

    def l1_phaseA(self, hsrc):
        fw = self.fw
        I = self.I
        S = self.S1 = {}
        for nm in ("R", "KAP", "V", "G", "LW0", "LW1", "B0", "B1", "KD0", "KD1"):
            S[nm] = fw.dram("s1_" + nm, [TOK, D], F32, kind=("ExternalOutput" if self.dbg else "Internal"))
        S["BON"] = fw.dram("s1_BON", [TOK, 16], F32, kind=("ExternalOutput" if self.dbg else "Internal"))
        fw.push()
        GT, shT = self.modvecs(1, 0)
        self.make_normer()
        self.bank = Rot(self.banks)
        W = {}
        for nm in ("od_w_r", "od_w_k", "od_w_v"):
            W[nm] = fw.sb(nm, [128, 8, D], BF16)
            for k in range(8):
                fw.dma("pool", W[nm], W[nm][:, k, :], I[nm], I[nm].ap()[k * 128:(k + 1) * 128, :])
        g1 = fw.sb("g1", [128, 8, 128], BF16); g2 = fw.sb("g2", [128, D], BF16)
        fw.dma("pool", g1, g1[:], I["od_g1"], I["od_g1"].ap().rearrange("(k p) n -> p k n", p=128))
        fw.dma("pool", g2, g2[:], I["od_g2"], I["od_g2"].ap())
        w1c = fw.sb("w1c", [128, 8, 128], BF16); a1c = fw.sb("a1c", [128, 8, 128], BF16)
        for d in range(2):
            fw.dma("pool", w1c, w1c[:, :, d * 64:(d + 1) * 64], I["od_w1"], I["od_w1"].ap()[d].rearrange("(k p) n -> p k n", p=128))
            fw.dma("pool", a1c, a1c[:, :, d * 64:(d + 1) * 64], I["od_a1"], I["od_a1"].ap()[d].rearrange("(k p) n -> p k n", p=128))
        w2c = fw.sb("w2c", [128, D], BF16); a2c = fw.sb("a2c", [128, D], BF16)
        fw.dma("pool", w2c, w2c[:], I["od_w2"], I["od_w2"].ap().rearrange("d l n -> (d l) n"))
        fw.dma("pool", a2c, a2c[:], I["od_a2"], I["od_a2"].ap().rearrange("d l n -> (d l) n"))
        brow = fw.sb("brow", [1, 4, D], BF16)
        fw.dma("pool", brow, brow[:, 0:2, :], I["od_w0"], I["od_w0"].ap().rearrange("(o d) n -> o d n", o=1))
        fw.dma("pool", brow, brow[:, 2:4, :], I["od_a0"], I["od_a0"].ap().rearrange("(o d) n -> o d n", o=1))
        ones1 = fw.sb("ones1", [1, 128], BF16)
        fw.op("dve", lambda e: e.memset(ones1[:], 1.0), [], [ones1])
        kkb = fw.sb("kkb", [128, D]); kab = fw.sb("kab", [128, D]); rkb = fw.sb("rkb", [128, D])
        for t_, nm in ((kkb, "od_k_k"), (kab, "od_k_a"), (rkb, "od_r_k")):
            fw.dma("sp", t_, t_[:], I[nm], I[nm].ap().partition_broadcast(128))
        muT = fw.sb("muT", [128, 6, 8])
        for m in range(6):
            self.load_colT(muT, muT[:, m, :], I["od_mu"], I["od_mu"].ap()[m])
        NG = NT // 2
        win = [fw.sb("xnw", [128, 8, 256], BF16) for _ in range(4)]
        xxT = fw.sb("xxT", [128, 8, 256], BF16)
        xms = Rot([fw.sb("xm", [128, 8, 256], BF16) for _ in range(2)])
        tmpp = Rot([fw.sb("l1t", [128, D]) for _ in range(8)])
        ksb0 = fw.sb("ksb0", [128, D]); ksb1 = fw.sb("ksb1", [128, D])
        kap = fw.sb("kap", [128, D]); kka = fw.sb("kka", [128, D]); kmk = fw.sb("kmk", [128, D]); r_s = fw.sb("r_s", [128, D]); kd0 = fw.sb("kd0", [128, D])
        ss16 = Rot([fw.sb("ss16", [128, 16]) for _ in range(2)])
        lo = Rot([fw.sb("lo", [128, 128], BF16) for _ in range(3)])
        loT = Rot([fw.sb("loT", [128, 128], BF16) for _ in range(3)])

        def norm_group(g):
            s = 1 if g == 0 else 0
            for j in range(2):
                src, sap = self.src_of(2 * g + j, hsrc)
                self.norm_tile(src, sap, GT[s], shT[s], win[g % 4], slice(j * 128, (j + 1) * 128))

        def sub(out, a, b):
            fw.op("pool", lambda e: e.tensor_tensor(out=out, in0=a, in1=b, op=ALU.subtract), [win[0], win[1], win[2], win[3]], [xxT])

        def neg(out, a):
            fw.op("pool", lambda e: e.tensor_scalar(out=out, in0=a, scalar1=-1.0, scalar2=None, op0=ALU.mult), [win[0], win[1], win[2], win[3]], [xxT])

        def proj_tok(xm, t, Wt, nb=2):
            outs = []
            for half in range(nb):
                pb = self.bank.next()
                for k in range(8):
                    fw.op("pe", lambda e: e.matmul(pb[:], lhsT=xm[:, k, t * 128:(t + 1) * 128], rhs=Wt[:, k, half * 512:(half + 1) * 512], start=(k == 0), stop=(k == 7)), [xm, Wt], [pb])
                outs.append(pb)
            return outs

        def evac(dst, pbs):
            for half, pb in enumerate(pbs):
                fw.op("act", lambda e: e.activation(out=dst[:, half * 512:(half + 1) * 512], in_=pb[:], func=AF.Copy), [pb], [dst])

        def store(nm, tile, src, ap=None):
            fw.dma("sp", S[nm], S[nm].ap()[tile * 128:(tile + 1) * 128, :], src, src[:] if ap is None else ap)

        def lerp(m, cur):
            xm = xms.next()
            for k in range(8):
                fw.op("dve", lambda e: e.scalar_tensor_tensor(out=xm[:, k, :], in0=xxT[:, k, :], scalar=muT[:, m, k:k + 1], in1=cur[:, k, :], op0=ALU.mult, op1=ALU.add), [xxT, muT, cur], [xm])
            return xm

        def lora(xm, t, w1, w2, brow_i, func):
            pb = self.bank.next()
            for k in range(8):
                fw.op("pe", lambda e: e.matmul(pb[:, 0:128], lhsT=xm[:, k, t * 128:(t + 1) * 128], rhs=w1[:, k, :], start=(k == 0), stop=(k == 7)), [xm, w1], [pb])
            l_ = lo.next()
            fw.op("act", lambda e: e.activation(out=l_[:], in_=pb[:, 0:128], func=func), [pb], [l_])
            pt = self.bank.next(); pv = self.bview(pt)
            fw.op("pe", lambda e: e.transpose(out=pv[:, 0:128], in_=l_[:], identity=self.identb[:]), [l_, self.identb], [pt])
            lT = loT.next()
            fw.op("act", lambda e: e.activation(out=lT[:], in_=pv[:, 0:128], func=AF.Copy), [pt], [lT])
            res = []
            for d in range(2):
                outs = []
                for half in range(2):
                    po = self.bank.next()
                    fw.op("pe", lambda e: e.matmul(po[:], lhsT=ones1[:], rhs=brow[:, brow_i + d, half * 512:(half + 1) * 512], start=True, stop=False), [ones1, brow], [po])
                    fw.op("pe", lambda e: e.matmul(po[:], lhsT=lT[d * 64:(d + 1) * 64, :], rhs=w2[d * 64:(d + 1) * 64, half * 512:(half + 1) * 512], start=False, stop=True), [lT, w2], [po])
                    outs.append(po)
                res.append(outs)
            return res

        norm_group(0)
        import os
        for g in range(int(os.environ.get("L1A_NG", NG))):
            if g + 1 < NG:
                norm_group(g + 1)
            cur = win[g % 4]; prv = win[(g - 1) % 4]; nxt = win[(g + 1) % 4]
            for k in range(8):
                c = cur[:, k, :]; o = xxT[:, k, :]
                if g == 0:
                    if k < 4:
                        sub(o[:, 1:256], c[:, 0:255], c[:, 1:256]); neg(o[:, 0:1], c[:, 0:1])
                    else:
                        sub(o[:, 0:255], c[:, 1:256], c[:, 0:255]); neg(o[:, 255:256], c[:, 255:256])
                else:
                    c3 = c.rearrange("p (r c) -> p r c", c=64); o3 = o.rearrange("p (r c) -> p r c", c=64)
                    if k < 2:
                        sub(o3[:, :, 1:64], c3[:, :, 0:63], c3[:, :, 1:64]); neg(o3[:, :, 0:1], c3[:, :, 0:1])
                    elif k < 4:
                        sub(o3[:, :, 0:63], c3[:, :, 1:64], c3[:, :, 0:63]); neg(o3[:, :, 63:64], c3[:, :, 63:64])
                    elif k < 6:
                        sub(o[:, 64:256], c[:, 0:192], c[:, 64:256])
                        if g == 1:
                            neg(o[:, 0:64], c[:, 0:64])
                        else:
                            sub(o[:, 0:64], prv[:, k, 192:256], c[:, 0:64])
                    else:
                        sub(o[:, 0:192], c[:, 64:256], c[:, 0:192])
                        if g == NG - 1:
                            neg(o[:, 192:256], c[:, 192:256])
                        else:
                            sub(o[:, 192:256], nxt[:, k, 0:64], c[:, 192:256])
            xr = lerp(0, cur)
            for t in range(2):
                tile = 2 * g + t
                rr_ = tmpp.next()
                evac(rr_, proj_tok(xr, t, W["od_w_r"]))
                store("R", tile, rr_)
            xk = lerp(2, cur)
            ktiles = [ksb0, ksb1]
            for t in range(2):
                evac(ktiles[t], proj_tok(xk, t, W["od_w_k"]))
            xv = lerp(3, cur)
            for t in range(2):
                v_ = tmpp.next()
                evac(v_, proj_tok(xv, t, W["od_w_v"]))
                store("V", 2 * g + t, v_)
            xg = lerp(5, cur)
            for t in range(2):
                pb = self.bank.next()
                for k in range(8):
                    fw.op("pe", lambda e: e.matmul(pb[:, 0:128], lhsT=xg[:, k, t * 128:(t + 1) * 128], rhs=g1[:, k, :], start=(k == 0), stop=(k == 7)), [xg, g1], [pb])
                l_ = lo.next()
                fw.op("act", lambda e: e.activation(out=l_[:], in_=pb[:, 0:128], func=AF.Sigmoid), [pb], [l_])
                pt = self.bank.next(); pv = self.bview(pt)
                fw.op("pe", lambda e: e.transpose(out=pv[:, 0:128], in_=l_[:], identity=self.identb[:]), [l_, self.identb], [pt])
                lT = loT.next()
                fw.op("act", lambda e: e.activation(out=lT[:], in_=pv[:, 0:128], func=AF.Copy), [pt], [lT])
                g_ = tmpp.next()
                outs = []
                for half in range(2):
                    po = self.bank.next()
                    fw.op("pe", lambda e: e.matmul(po[:], lhsT=lT[:], rhs=g2[:, half * 512:(half + 1) * 512], start=True, stop=True), [lT, g2], [po])
                    outs.append(po)
                evac(g_, outs)
                store("G", 2 * g + t, g_)
            xw = lerp(1, cur)
            xa = lerp(4, cur)
            for t in range(2):
                tile = 2 * g + t
                ksb = ktiles[t]
                kkx = tmpp.next(); sq = tmpp.next()
                fw.op("pool", lambda e: e.tensor_tensor(out=kkx[:], in0=ksb[:], in1=kkb[:], op=ALU.mult), [ksb, kkb], [kkx])
                fw.op("pool", lambda e: e.tensor_tensor(out=sq[:], in0=kkx[:], in1=kkx[:], op=ALU.mult), [kkx], [sq])
                s16 = ss16.next()
                fw.op("dve", lambda e: e.reduce_sum(out=s16[:], in_=sq[:].rearrange("p (h j) -> p h j", j=64), axis=AX.X), [sq], [s16])
                fw.op("act", lambda e: e.activation(out=s16[:], in_=s16[:], func=AF.Sqrt, bias=1e-12, scale=1.0), [s16], [s16])
                fw.op("dve", lambda e: e.reciprocal(out=s16[:], in_=s16[:]), [s16], [s16])
                fw.op("dve", lambda e: e.tensor_tensor(out=kap[:].rearrange("p (h j) -> p h j", j=64), in0=kkx[:].rearrange("p (h j) -> p h j", j=64),
                                                       in1=s16[:].unsqueeze(2).to_broadcast([128, 16, 64]), op=ALU.mult), [kkx, s16], [kap])
                store("KAP", tile, kap)
                fw.op("pool", lambda e: e.tensor_tensor(out=kka[:], in0=ksb[:], in1=kab[:], op=ALU.mult), [ksb, kab], [kka])
                fw.op("pool", lambda e: e.tensor_tensor(out=kmk[:], in0=ksb[:], in1=kka[:], op=ALU.subtract), [ksb, kka], [kmk])
                wl = lora(xw, t, w1c, w2c, 0, AF.Tanh)
                for d in range(2):
                    lw = tmpp.next()
                    for half in range(2):
                        fw.op("act", lambda e: e.activation(out=lw[:, half * 512:(half + 1) * 512], in_=wl[d][half][:], func=AF.Sigmoid), [wl[d][half]], [lw])
                    fw.op("pool", lambda e: e.tensor_scalar(out=lw[:], in0=lw[:], scalar1=-0.6065306597126334, scalar2=None, op0=ALU.mult), [lw], [lw])
                    store("LW%d" % d, tile, lw)
                al = lora(xa, t, a1c, a2c, 2, AF.Copy)
                avs = []
                for d in range(2):
                    a_ = tmpp.next()
                    for half in range(2):
                        fw.op("act", lambda e: e.activation(out=a_[:, half * 512:(half + 1) * 512], in_=al[d][half][:], func=AF.Sigmoid), [al[d][half]], [a_])
                    avs.append(a_)
                kds = []
                for d in range(2):
                    a_ = avs[d]
                    b_ = tmpp.next()
                    fw.op("pool", lambda e: e.tensor_tensor(out=b_[:], in0=kap[:], in1=a_[:], op=ALU.mult), [kap, a_], [b_])
                    store("B%d" % d, tile, b_)
                    kd = kd0 if d == 0 else tmpp.next()
                    fw.op("pool", lambda e: e.tensor_tensor(out=kd[:], in0=kka[:], in1=a_[:], op=ALU.mult), [kka, a_], [kd])
                    fw.op("pool", lambda e: e.tensor_tensor(out=kd[:], in0=kd[:], in1=kmk[:], op=ALU.add), [kd, kmk], [kd])
                    store("KD%d" % d, tile, kd)
                    kds.append(kd)
                fw.dma("sp", r_s, r_s[:], S["R"], S["R"].ap()[tile * 128:(tile + 1) * 128, :])
                ks = tmpp.next()
                fw.op("pool", lambda e: e.tensor_tensor(out=ks[:], in0=kds[0][:], in1=kds[1][:], op=ALU.add), [kds[0], kds[1]], [ks])
                fw.op("pool", lambda e: e.tensor_tensor(out=ks[:], in0=ks[:], in1=rkb[:], op=ALU.mult), [ks, rkb], [ks])
                fw.op("pool", lambda e: e.tensor_tensor(out=ks[:], in0=ks[:], in1=r_s[:], op=ALU.mult), [ks, r_s], [ks])
                bon = ss16.next()
                fw.op("dve", lambda e: e.reduce_sum(out=bon[:], in_=ks[:].rearrange("p (h j) -> p h j", j=64), axis=AX.X), [ks], [bon])
                fw.dma("sp", S["BON"], S["BON"].ap()[tile * 128:(tile + 1) * 128, :], bon, bon[:])
        fw.pop()

    def l1_phaseB(self):
        fw = self.fw
        I = self.I
        S = self.S1
        k_ = "ExternalOutput" if self.dbg else "Internal"
        self.Y = [fw.dram("s1_Y%d" % d, [TOK, D], F32, kind=k_) for d in range(2)]
        fw.push()
        ybanks = self.banks[0:2]
        pbanks = Rot(self.banks[2:4])
        hb = []
        hbr = Rot([(self.banks[4 + i // 2], (i % 2) * 256) for i in range(8)])
        tm = fw.sb("tm", [128, 4, 128])
        fw.dma("sp", tm, tm[:], I["trimask"], I["trimask"].ap())
        ones = fw.sb("ones", [128, 128])
        fw.op("dve", lambda e: e.memset(ones[:], 1.0), [], [ones])
        MA = []; MB = []; MN = []; MC = []
        for d in range(2):
            mS, mI, mSn, mC = (0, 1, 2, 1) if d == 0 else (2, 3, 0, 3)
            a = fw.sb("MA", [128, 2, 128]); b = fw.sb("MB", [128, 2, 128]); n = fw.sb("MN", [128, 128])
            fw.op("dve", lambda e: e.tensor_scalar(out=a[:, 0, :], in0=tm[:, mS, :], scalar1=-1.0, scalar2=None, op0=ALU.mult), [tm], [a])
            fw.op("dve", lambda e: e.tensor_copy(out=a[:, 1, :], in_=tm[:, mI, :]), [tm], [a])
            fw.op("dve", lambda e: e.tensor_copy(out=b[:, 0, :], in_=tm[:, mS, :]), [tm], [b])
            fw.op("dve", lambda e: e.tensor_copy(out=b[:, 1, :], in_=tm[:, mI, :]), [tm], [b])
            fw.op("dve", lambda e: e.tensor_scalar(out=n[:], in0=tm[:, mSn, :], scalar1=-1.0, scalar2=None, op0=ALU.mult), [tm], [n])
            MA.append(a); MB.append(b); MN.append(n); MC.append(mC)
        H = fw.sb("H", [64, 16, 64])
        ld = {nm: Rot([fw.sb("ld" + nm, [128, D]) for _ in range(2)]) for nm in ("R", "KAP", "V", "LW", "B", "KD")}
        prep = {nm: fw.sb("pp" + nm, [128, D]) for nm in ("rt", "kt", "bt", "kdt", "bh", "kh")}
        cumS = fw.sb("cumS", [128, 512]); et = Rot([fw.sb("et", [128, 512]) for _ in range(3)])
        gC = fw.sb("gC", [64, 16])
        NH = 4
        TT = Rot([fw.sb("TT", [64, 4, 128]) for _ in range(NH)])
        SC = Rot([fw.sb("SC", [128, 4, 128]) for _ in range(NH)])
        NM = Rot([fw.sb("NM", [128, 2, 128]) for _ in range(NH * 3)])
        XS = Rot([fw.sb("XS", [128, 64]) for _ in range(NH * 3)])
        ysb = Rot([fw.sb("ysb", [128, D]) for _ in range(2)])
        import os
        nchunks = int(os.environ.get("L1B_NC", NT))

        def head_gen(h, d, T_):
            R_, KAP_, V_, LW_, B_, KD_ = T_
            hs = slice(h * 64, (h + 1) * 64)
            tt = TT.next(); sc = SC.next()
            for pair, (x0, x1) in enumerate(((prep["kt"], prep["rt"]), (prep["bt"], prep["kdt"]))):
                bk, off = hbr.next()
                fw.op("pe", lambda e: e.transpose(out=bk[0:64, off:off + 128], in_=x0[:, hs], identity=self.identf[:]), [x0, self.identf], [bk])
                fw.op("pe", lambda e: e.transpose(out=bk[0:64, off + 128:off + 256], in_=x1[:, hs], identity=self.identf[:]), [x1, self.identf], [bk])
                fw.op("act", lambda e: e.activation(out=tt[:, 2 * pair:2 * pair + 2, :].rearrange("p a b -> p (a b)"), in_=bk[0:64, off:off + 256], func=AF.Copy), [bk], [tt])
            yield
            kT = tt[:, 0, :]; rT = tt[:, 1, :]; bT = tt[:, 2, :]; kdT = tt[:, 3, :]
            kr = tt[:, 0:2, :].rearrange("p a b -> p (a b)")
            bk, off = hbr.next()
            fw.op("pe", lambda e: e.matmul(bk[:, off:off + 256], lhsT=bT, rhs=kr, start=True, stop=True), [tt], [bk])
            fw.op("dve", lambda e: e.tensor_tensor(out=sc[:, 0:2, :].rearrange("p a b -> p (a b)"), in0=bk[:, off:off + 256], in1=MA[d][:].rearrange("p a b -> p (a b)"), op=ALU.mult), [bk, MA[d]], [sc])
            bk, off = hbr.next()
            fw.op("pe", lambda e: e.matmul(bk[:, off:off + 256], lhsT=kdT, rhs=kr, start=True, stop=True), [tt], [bk])
            fw.op("dve", lambda e: e.tensor_tensor(out=sc[:, 2:4, :].rearrange("p a b -> p (a b)"), in0=bk[:, off:off + 256], in1=MB[d][:].rearrange("p a b -> p (a b)"), op=ALU.mult), [bk, MB[d]], [sc])
            nm = NM.next()
            bk, off = hbr.next()
            fw.op("pe", lambda e: e.matmul(bk[:, off:off + 128], lhsT=kT, rhs=bT, start=True, stop=True), [tt], [bk])
            fw.op("dve", lambda e: e.tensor_tensor(out=nm[:, 0, :], in0=bk[:, off:off + 128], in1=MN[d][:], op=ALU.mult), [bk, MN[d]], [nm])
            fw.op("act", lambda e: e.activation(out=nm[:, 1, :], in_=sc[:, 0, :], func=AF.Copy), [sc], [nm])
            yield
            NT_ = sc[:, 0, :]; BrbT = sc[:, 1, :]; AkT = sc[:, 2, :]; BrkT = sc[:, 3, :]
            bk, off = hbr.next()
            fw.op("pe", lambda e: e.matmul(bk[:, off:off + 64], lhsT=kT, rhs=H[:, h, :], start=True, stop=False), [tt, H], [bk])
            fw.op("pe", lambda e: e.matmul(bk[:, off:off + 64], lhsT=AkT, rhs=V_[:, hs], start=False, stop=True), [sc, V_], [bk])
            X = XS.next()
            fw.op("act", lambda e: e.activation(out=X[:], in_=bk[:, off:off + 64], func=AF.Copy, scale=-1.0), [bk], [X])
            yield
            cur = nm
            for lvl in range(7):
                bk, off = hbr.next()
                fw.op("pe", lambda e: e.matmul(bk[:, off:off + 64], lhsT=cur[:, 1, :], rhs=X[:], start=True, stop=True), [cur, X], [bk])
                X2 = XS.next()
                fw.op("dve", lambda e: e.tensor_tensor(out=X2[:], in0=bk[:, off:off + 64], in1=X[:], op=ALU.add), [bk, X], [X2])
                X = X2
                if lvl < 6:
                    nxt = NM.next()
                    bk, off = hbr.next()
                    if lvl < 5:
                        fw.op("pe", lambda e: e.matmul(bk[:, off:off + 128], lhsT=cur[:, 1, :], rhs=cur[:, 0, :], start=True, stop=True), [cur], [bk])
                    fw.op("pe", lambda e: e.matmul(bk[:, off + 128:off + 256], lhsT=cur[:, 0, :], rhs=cur[:, 1, :], start=True, stop=True), [cur], [bk])
                    if lvl < 5:
                        fw.op("act", lambda e: e.activation(out=nxt[:].rearrange("p a b -> p (a b)"), in_=bk[:, off:off + 256], func=AF.Copy), [bk], [nxt])
                    else:
                        fw.op("act", lambda e: e.activation(out=nxt[:, 1, :], in_=bk[:, off + 128:off + 256], func=AF.Copy), [bk], [nxt])
                    cur = nxt
                yield
            U = X
            yb = ybanks[h // 8]; yo = (h % 8) * 64
            fw.op("pe", lambda e: e.matmul(yb[:, yo:yo + 64], lhsT=rT, rhs=H[:, h, :], start=True, stop=False), [tt, H], [yb])
            fw.op("pe", lambda e: e.matmul(yb[:, yo:yo + 64], lhsT=BrbT, rhs=U[:], start=False, stop=False), [sc, U], [yb])
            fw.op("pe", lambda e: e.matmul(yb[:, yo:yo + 64], lhsT=BrkT, rhs=V_[:, hs], start=False, stop=True), [sc, V_], [yb])
            bk, off = hbr.next()
            fw.op("pe", lambda e: e.matmul(bk[0:64, off:off + 64], lhsT=prep["bh"][:, hs], rhs=U[:], start=True, stop=False), [prep["bh"], U], [bk])
            fw.op("pe", lambda e: e.matmul(bk[0:64, off:off + 64], lhsT=prep["kh"][:, hs], rhs=V_[:, hs], start=False, stop=True), [prep["kh"], V_], [bk])
            fw.op("dve", lambda e: e.scalar_tensor_tensor(out=H[:, h, :], in0=H[:, h, :], scalar=gC[:, h:h + 1], in1=bk[0:64, off:off + 64], op0=ALU.mult, op1=ALU.add), [H, gC, bk], [H])
            yield

        for d in range(2):
            fw.op("dve", lambda e: e.memset(H[:], 0.0), [], [H])
            order = list(range(NT)) if d == 0 else [1, 0] + list(range(NT - 1, 1, -1))
            for c in order[:nchunks]:
                rows = slice(c * 128, (c + 1) * 128)
                T_ = []
                for nm, key in (("R", "R"), ("KAP", "KAP"), ("V", "V"), ("LW", "LW%d" % d), ("B", "B%d" % d), ("KD", "KD%d" % d)):
                    t_ = ld[nm].next()
                    fw.dma("sp", t_, t_[:], S[key], S[key].ap()[rows, :])
                    T_.append(t_)
                R_, KAP_, V_, LW_, B_, KD_ = T_
                bk, off = hbr.next()
                for h in range(16):
                    fw.op("pe", lambda e: e.matmul(bk[0:64, off + h:off + h + 1], lhsT=LW_[:, h * 64:(h + 1) * 64], rhs=ones[:, 0:1], start=True, stop=True), [LW_, ones], [bk])
                fw.op("act", lambda e: e.activation(out=gC[:], in_=bk[0:64, off:off + 16], func=AF.Exp), [bk], [gC])
                for half in range(2):
                    cs = slice(half * 512, (half + 1) * 512)
                    pc = pbanks.next(); ptot = pbanks.next()
                    fw.op("pe", lambda e: e.matmul(pc[:], lhsT=tm[:, MC[d], :], rhs=LW_[:, cs], start=True, stop=True), [tm, LW_], [pc])
                    fw.op("pe", lambda e: e.matmul(ptot[:], lhsT=ones[:], rhs=LW_[:, cs], start=True, stop=True), [ones, LW_], [ptot])
                    fw.op("act", lambda e: e.activation(out=cumS[:], in_=pc[:], func=AF.Copy), [pc], [cumS])
                    e1 = et.next()
                    fw.op("act", lambda e: e.activation(out=e1[:], in_=pc[:], func=AF.Exp), [pc], [e1])
                    fw.op("pool", lambda e: e.tensor_tensor(out=prep["rt"][:, cs], in0=R_[:, cs], in1=e1[:], op=ALU.mult), [R_, e1], [prep["rt"]])
                    e2 = et.next()
                    fw.op("act", lambda e: e.activation(out=e2[:], in_=pc[:], func=AF.Exp, scale=-1.0), [pc], [e2])
                    fw.op("pool", lambda e: e.tensor_tensor(out=prep["bt"][:, cs], in0=B_[:, cs], in1=e2[:], op=ALU.mult), [B_, e2], [prep["bt"]])
                    fw.op("pool", lambda e: e.tensor_tensor(out=prep["kdt"][:, cs], in0=KD_[:, cs], in1=e2[:], op=ALU.mult), [KD_, e2], [prep["kdt"]])
                    e3 = et.next()
                    fw.op("dve", lambda e: e.tensor_tensor(out=e3[:], in0=cumS[:], in1=LW_[:, cs], op=ALU.subtract), [cumS, LW_], [e3])
                    fw.op("act", lambda e: e.activation(out=e3[:], in_=e3[:], func=AF.Exp), [e3], [e3])
                    fw.op("pool", lambda e: e.tensor_tensor(out=prep["kt"][:, cs], in0=KAP_[:, cs], in1=e3[:], op=ALU.mult), [KAP_, e3], [prep["kt"]])
                    e4 = et.next()
                    fw.op("dve", lambda e: e.tensor_tensor(out=e4[:], in0=ptot[:], in1=cumS[:], op=ALU.subtract), [ptot, cumS], [e4])
                    fw.op("act", lambda e: e.activation(out=e4[:], in_=e4[:], func=AF.Exp), [e4], [e4])
                    fw.op("dve", lambda e: e.tensor_tensor(out=prep["bh"][:, cs], in0=B_[:, cs], in1=e4[:], op=ALU.mult), [B_, e4], [prep["bh"]])
                    fw.op("dve", lambda e: e.tensor_tensor(out=prep["kh"][:, cs], in0=KD_[:, cs], in1=e4[:], op=ALU.mult), [KD_, e4], [prep["kh"]])
                for h0 in range(0, 16, NH):
                    gens = [head_gen(h, d, T_) for h in range(h0, h0 + NH)]
                    alive = True
                    while alive:
                        alive = False
                        for gi in gens:
                            try:
                                next(gi)
                                alive = True
                            except StopIteration:
                                pass
                ys = ysb.next()
                for half in range(2):
                    fw.op("act", lambda e: e.activation(out=ys[:, half * 512:(half + 1) * 512], in_=ybanks[half][:], func=AF.Copy), [ybanks[half]], [ys])
                fw.dma("act", self.Y[d], self.Y[d].ap()[rows, :], ys, ys[:])
        fw.pop()

    def l1_phaseC(self, hsrc, hdst):
        fw = self.fw
        I = self.I
        S = self.S1
        fw.push()
        self.bank = Rot(self.banks)
        gate = self.gate_bcast(1, 0)
        w_o = fw.sb("w_o", [128, 8, D], BF16)
        for k in range(8):
            fw.dma("pool", w_o, w_o[:, k, :], I["od_w_o"], I["od_w_o"].ap()[k * 128:(k + 1) * 128, :])
        gnw = fw.sb("gnw", [128, D]); gnb = fw.sb("gnb", [128, D])
        fw.dma("sp", gnw, gnw[:], I["od_gn_w"], I["od_gn_w"].ap().partition_broadcast(128))
        fw.dma("sp", gnb, gnb[:], I["od_gn_b"], I["od_gn_b"].ap().partition_broadcast(128))
        L = {nm: Rot([fw.sb("c" + nm, [128, D]) for _ in range(2)]) for nm in ("y0", "y1", "v", "g", "h")}
        bons = Rot([fw.sb("cbon", [128, 16]) for _ in range(2)])
        st = Rot([fw.sb("cst", [128, 16]) for _ in range(6)])
        wk = Rot([fw.sb("cwk", [128, D]) for _ in range(4)])
        obf = Rot([fw.sb("cob", [128, D], BF16) for _ in range(2)])
        oTs = Rot([fw.sb("coT", [128, 8, 128], BF16) for _ in range(2)])
        v3 = lambda t_: t_[:].rearrange("p (h j) -> p h j", j=64)
        b3 = lambda t_: t_[:].unsqueeze(2).to_broadcast([128, 16, 64])
        for tile in range(2, NT):
            rows = slice(tile * 128, (tile + 1) * 128)
            y0 = L["y0"].next(); y1 = L["y1"].next(); v = L["v"].next(); g = L["g"].next(); ht = L["h"].next(); bon = bons.next()
            fw.dma("sp", y0, y0[:], self.Y[0], self.Y[0].ap()[rows, :])
            fw.dma("sp", y1, y1[:], self.Y[1], self.Y[1].ap()[rows, :])
            fw.dma("sp", v, v[:], S["V"], S["V"].ap()[rows, :])
            fw.dma("sp", g, g[:], S["G"], S["G"].ap()[rows, :])
            fw.dma("sp", bon, bon[:], S["BON"], S["BON"].ap()[rows, :])
            src, sap = self.src_of(tile, hsrc)
            fw.dma("sp", ht, ht[:], src, sap)
            ysum = wk.next()
            fw.op("pool", lambda e: e.tensor_tensor(out=ysum[:], in0=y0[:], in1=y1[:], op=ALU.add), [y0, y1], [ysum])
            mean = st.next(); var = st.next()
            fw.op("dve", lambda e: e.reduce_sum(out=mean[:], in_=v3(ysum), axis=AX.X), [ysum], [mean])
            fw.op("dve", lambda e: e.tensor_scalar(out=mean[:], in0=mean[:], scalar1=1.0 / 64, scalar2=None, op0=ALU.mult), [mean], [mean])
            yc = wk.next()
            fw.op("dve", lambda e: e.tensor_tensor(out=v3(yc), in0=v3(ysum), in1=b3(mean), op=ALU.subtract), [ysum, mean], [yc])
            sq = wk.next()
            fw.op("pool", lambda e: e.tensor_tensor(out=sq[:], in0=yc[:], in1=yc[:], op=ALU.mult), [yc], [sq])
            fw.op("dve", lambda e: e.reduce_sum(out=var[:], in_=v3(sq), axis=AX.X), [sq], [var])
            fw.op("act", lambda e: e.activation(out=var[:], in_=var[:], func=AF.Sqrt, bias=64e-5, scale=1.0 / 64), [var], [var])
            fw.op("dve", lambda e: e.reciprocal(out=var[:], in_=var[:]), [var], [var])
            fw.op("dve", lambda e: e.tensor_tensor(out=v3(yc), in0=v3(yc), in1=b3(var), op=ALU.mult), [yc, var], [yc])
            fw.op("pool", lambda e: e.tensor_tensor(out=yc[:], in0=yc[:], in1=gnw[:], op=ALU.mult), [yc, gnw], [yc])
            fw.op("pool", lambda e: e.tensor_tensor(out=yc[:], in0=yc[:], in1=gnb[:], op=ALU.add), [yc, gnb], [yc])
            bv = wk.next()
            fw.op("dve", lambda e: e.tensor_tensor(out=v3(bv), in0=v3(v), in1=b3(bon), op=ALU.mult), [v, bon], [bv])
            fw.op("pool", lambda e: e.tensor_tensor(out=yc[:], in0=yc[:], in1=bv[:], op=ALU.add), [yc, bv], [yc])
            ob = obf.next()
            fw.op("dve", lambda e: e.tensor_tensor(out=ob[:], in0=yc[:], in1=g[:], op=ALU.mult), [yc, g], [ob])
            pb = self.bank.next(); pv = self.bview(pb)
            for k in range(8):
                fw.op("pe", lambda e: e.transpose(out=pv[:, k * 128:(k + 1) * 128], in_=ob[:, k * 128:(k + 1) * 128], identity=self.identb[:]), [ob, self.identb], [pb])
            oT = oTs.next()
            fw.op("act", lambda e: e.activation(out=oT[:].rearrange("p k n -> p (k n)"), in_=pv[:, 0:1024], func=AF.Copy), [pb], [oT])
            tmp = wk.next()
            for half in range(2):
                po = self.bank.next()
                for k in range(8):
                    fw.op("pe", lambda e: e.matmul(po[:], lhsT=oT[:, k, :], rhs=w_o[:, k, half * 512:(half + 1) * 512], start=(k == 0), stop=(k == 7)), [oT, w_o], [po])
                fw.op("dve", lambda e: e.tensor_tensor(out=tmp[:, half * 512:(half + 1) * 512], in0=po[:], in1=gate[0][:, half * 512:(half + 1) * 512], op=ALU.mult), [po, gate[0]], [tmp])
            fw.op("pool", lambda e: e.tensor_tensor(out=tmp[:], in0=tmp[:], in1=ht[:], op=ALU.add), [tmp, ht], [tmp])
            fw.dma("act", hdst, hdst.ap()[rows, :], tmp, tmp[:])
        fw.pop()

    def finish(self):
        fw = self.fw
        fw.barrier()
        fw.finish("sp", [self.out])
        print("ninst", fw.ninst, "nsem", fw.nsem)
        self.es.close()
        return self.nc


def host_consts():
    c = {}
    c["ident"] = np.eye(128, dtype=np.float32)
    t = np.arange(NLAT)
    row = (t // 64).astype(np.float32); col = (t % 64).astype(np.float32)
    inv = (10000.0 ** (-np.arange(0, 32, 2, dtype=np.float32) / 32)).astype(np.float32)
    ang = np.concatenate([row[:, None] * inv, col[:, None] * inv], axis=-1)
    cosl = np.cos(ang).astype(np.float32); sinl = np.sin(ang).astype(np.float32)
    cos = np.concatenate([np.ones((NCTX, 32), np.float32), cosl], 0)
    sin = np.concatenate([np.zeros((NCTX, 32), np.float32), sinl], 0)
    d = np.arange(128) % 64
    c["ropeC"] = np.ascontiguousarray(cos[:, d // 2].T)
    c["ropeS"] = np.ascontiguousarray(sin[:, d // 2].T)
    P = np.zeros((128, 128), np.float32)
    for i in range(64):
        P[2 * i + 1, 2 * i] = -1.0
        P[2 * i, 2 * i + 1] = 1.0
    c["ropeP"] = P
    b = np.zeros((128, 128), np.float32)
    b[:64, :64] = 1; b[64:, 64:] = 1
    c["blk64"] = b
    s_ = np.arange(128)[:, None]; t_ = np.arange(128)[None, :]
    tm = np.zeros((128, 4, 128), np.float32)
    tm[:, 0] = (s_ < t_); tm[:, 1] = (s_ <= t_); tm[:, 2] = (s_ > t_); tm[:, 3] = (s_ >= t_)
    c["trimask"] = tm
    return c


def shard_inputs(inputs):
    f = lambda a: np.ascontiguousarray(np.asarray(a, dtype=np.float32))
    w_in = f(inputs["ev_w_in"])[0]
    w_in_r = np.concatenate([w_in[:, 0:2048], w_in[:, 2048:2112], w_in[:, 2048:2112], w_in[:, 2112:2176], w_in[:, 2112:2176], w_in[:, 2176:2304]], axis=1)
    shared = dict(
        ada_w=f(inputs["ada_w"]), ada_b=f(inputs["ada_b"]), norm1_g=f(inputs["norm1_g"]), norm2_g=f(inputs["norm2_g"]),
        w_in=f(w_in_r), conv_w=f(inputs["ev_conv_w"])[0], qg2=f(np.tile(f(inputs["ev_q_gain"])[0], 2)), kg2=f(np.tile(f(inputs["ev_k_gain"])[0], 2)),
        w_out=f(inputs["ev_w_out"])[0], od_mu=f(inputs["od_mu"])[0], od_w_r=f(inputs["od_w_r"])[0], od_w_k=f(inputs["od_w_k"])[0],
        od_w_v=f(inputs["od_w_v"])[0], od_w_o=f(inputs["od_w_o"])[0], od_g1=f(inputs["od_g1"])[0], od_g2=f(inputs["od_g2"])[0],
        od_k_k=f(inputs["od_k_k"])[0], od_k_a=f(inputs["od_k_a"])[0], od_r_k=f(inputs["od_r_k"])[0].reshape(-1),
        od_w0=f(inputs["od_w0"])[0], od_w1=f(inputs["od_w1"])[0], od_w2=f(inputs["od_w2"])[0], od_a0=f(inputs["od_a0"])[0],
        od_a1=f(inputs["od_a1"])[0], od_a2=f(inputs["od_a2"])[0], od_gn_w=f(inputs["od_gn_w"])[0], od_gn_b=f(inputs["od_gn_b"])[0],
        peer_wq=f(inputs["peer_wq"]), peer_keys=f(inputs["peer_keys"]).reshape(2, 16, 128, 128), peer_u=f(inputs["peer_u"]), peer_v=f(inputs["peer_v"]),
    )
    shared.update(host_consts())
    x = f(inputs["x"]); ctx = f(inputs["ctx"]); c = f(inputs["c"]); cc = f(inputs["c_ctx"])
    maps = []
    for b in range(8):
        m = dict(shared)
        m["x"] = x[b]; m["ctx"] = ctx[b]; m["cvec"] = np.ascontiguousarray(np.stack([c[b], cc], 0))
        maps.append(m)
    return maps


def build(dbg=False, stop=None):
    p = Prog(dbg=dbg, stop=stop)
    p.phase0()
    if stop == "p0":
        return p
    if stop and stop.startswith("L1"):
        p.l1_phaseA(p.h2)
        if stop == "L1a":
            return p
        p.l1_phaseB()
        if stop == "L1b":
            return p
        p.l1_phaseC(p.h2, p.h3)
        return p
    p.l0_phaseA()
    if stop == "l0a":
        p.fw.pop()
        return p
    p.l0_phaseB()
    if stop == "l0":
        return p
    p.peer_prep(0)
    p.peer(0, p.h1, p.h2)
    if stop == "peer0":
        return p
    p.l1_phaseA(p.h2)
    if stop == "l1a":
        return p
    p.l1_phaseB()
    if stop == "l1b":
        return p
    p.l1_phaseC(p.h2, p.h3)
    if stop == "l1c":
        return p
    p.peer_prep(1)
    p.peer(1, p.h3, None, final=True)
    return p


def kernel(**inputs):
    p = build()
    nc = p.finish()
    maps = shard_inputs(inputs)
    maps = [{k: v for k, v in m.items() if k in p.I} for m in maps]
    res = run_bass_kernel_spmd(nc, maps, core_ids=list(range(8)))
    return np.stack([r["out"] for r in res.results], 0)
```

```python
import numpy as np
from contextlib import ExitStack
import concourse.bass as bass
import concourse.mybir as mybir
from concourse.bass_utils import run_bass_kernel_spmd

F32 = mybir.dt.float32
BF16 = mybir.dt.bfloat16
ALU = mybir.AluOpType
AF = mybir.ActivationFunctionType
AX = mybir.AxisListType

D = 1024
NCTX = 256
NLAT = 4096
TOK = NCTX + NLAT
NT = TOK // 128
TOKP = TOK + 4
EPS = 1e-6
INPUT_SHAPES = {
    "x": [NLAT, D],
    "ctx": [NCTX, D],
    "cvec": [2, D],
    "ada_w": [2, D, 6 * D],
    "ada_b": [2, 6 * D],
    "norm1_g": [2, D],
    "norm2_g": [2, D],
    "w_in": [D, 2432],
    "conv_w": [3, 512],
    "qg2": [128],
    "kg2": [128],
    "w_out": [D, D],
    "od_mu": [6, D],
    "od_w_r": [D, D],
    "od_w_k": [D, D],
    "od_w_v": [D, D],
    "od_w_o": [D, D],
    "od_g1": [D, 128],
    "od_g2": [128, D],
    "od_k_k": [D],
    "od_k_a": [D],
    "od_r_k": [D],
    "od_w0": [2, D],
    "od_w1": [2, D, 64],
    "od_w2": [2, 64, D],
    "od_a0": [2, D],
    "od_a1": [2, D, 64],
    "od_a2": [2, 64, D],
    "od_gn_w": [D],
    "od_gn_b": [D],
    "peer_wq": [2, D, 2048],
    "peer_keys": [2, 16, 128, 128],
    "peer_u": [2, 16384, D],
    "peer_v": [2, 16384, D],
    "ident": [128, 128],
    "ropeC": [128, TOK],
    "ropeS": [128, TOK],
    "ropeP": [128, 128],
    "blk64": [128, 128],
    "trimask": [128, 4, 128],
}


class T:
    __slots__ = ("t", "w", "r", "sem", "dcount", "name")

    def __init__(self, t, name):
        self.t = t
        self.name = name
        self.w = {}
        self.r = {}
        self.sem = None
        self.dcount = 0

    def __getitem__(self, k):
        return self.t[k]

    def ap(self):
        return self.t.ap()


class FW:
    def __init__(self, nc, es):
        self.nc = nc
        self.es = es
        self.root_es = es
        self.stack = []
        self.live = []
        self.drams = []
        self.sempool = []
        self.uid = 0
        self.eng = {"pe": nc.tensor, "act": nc.scalar, "dve": nc.vector, "pool": nc.gpsimd, "sp": nc.sync}
        self.esem = {}
        self.ecount = {}
        self.known = {}
        self.nsem = 0
        for k in self.eng:
            self.esem[k] = self.newsem("e_" + k)
            self.ecount[k] = 0
            self.known[k] = {}
        self.selfwait = {"pe": False, "act": True, "dve": True, "pool": True, "sp": False}
        self.ninst = 0

    def newsem(self, name):
        self.nsem += 1
        assert self.nsem < 200, "too many semaphores"
        return self.root_es.enter_context(self.nc.semaphore(name))

    def push(self):
        self.stack.append((self.es, self.live))
        self.es = ExitStack()
        self.live = []

    def pop(self):
        self.barrier()
        for t in self.live:
            if t.sem is not None:
                self.sempool.append((t.sem, t.dcount))
                t.sem = None
        self.es.close()
        self.es, self.live = self.stack.pop()

    def barrier(self):
        toks = {}
        for k in self.eng:
            if self.ecount[k] > 0:
                toks[self.esem[k]] = self.ecount[k]
        for _, live in self.stack + [(None, self.live)]:
            for t in live:
                if t.sem is not None and t.dcount > 0:
                    toks[t.sem] = 16 * t.dcount
        for t in self.drams:
            if t.sem is not None and t.dcount > 0:
                toks[t.sem] = 16 * t.dcount
        for e in self.eng:
            kn = self.known[e]
            for s_, v in toks.items():
                if s_ is self.esem[e]:
                    continue
                if kn.get(s_, 0) < v:
                    self.eng[e].wait_ge(s_, v)
                    kn[s_] = v
                    self.ninst += 1

    def sb(self, name, shape, dt=F32):
        self.uid += 1
        t = T(self.es.enter_context(self.nc.sbuf_tensor("%s_%d" % (name, self.uid), list(shape), dt)), name)
        self.live.append(t)
        return t

    def ps(self, name, shape, dt=F32):
        t = T(self.es.enter_context(self.nc.psum_tensor(name, list(shape), dt)), name)
        self.live.append(t)
        return t

    def dram(self, name, shape, dt=F32, kind="Internal"):
        t = T(self.nc.dram_tensor(name, list(shape), dt, kind=kind), name)
        self.drams.append(t)
        return t

    def _waits(self, e, reads, writes):
        toks = {}

        def add(tok):
            if tok is None:
                return
            s, v = tok
            if toks.get(s, 0) < v:
                toks[s] = v
        for r in reads:
            for s, v in r.w.items():
                add((s, v))
        for w in writes:
            for s, v in w.w.items():
                add((s, v))
            for s, v in w.r.items():
                add((s, v))
        eng = self.eng[e]
        kn = self.known[e]
        for s, v in toks.items():
            if s is self.esem[e] and not self.selfwait[e]:
                continue
            if kn.get(s, 0) < v:
                eng.wait_ge(s, v)
                kn[s] = v
                self.ninst += 1

    def op(self, e, fn, reads=(), writes=()):
        self._waits(e, reads, writes)
        inst = fn(self.eng[e])
        self.ecount[e] += 1
        self.ninst += 1
        s = self.esem[e]
        inst.then_inc(s, 1)
        tok = (s, self.ecount[e])
        for w in writes:
            w.w = {s: tok[1]}
            w.r = {}
        for r in reads:
            if r.r.get(s, 0) < tok[1]:
                r.r[s] = tok[1]
        return inst

    def dma(self, e, outT, out_ap, inT, in_ap, **kw):
        self._waits(e, [inT], [outT])
        own = inT if (outT in self.drams and inT not in self.drams) else outT
        if own.sem is None:
            if self.sempool:
                own.sem, own.dcount = self.sempool.pop()
            else:
                own.sem = self.newsem("d%d" % self.nsem)
        inst = self.eng[e].dma_start(out=out_ap, in_=in_ap, **kw)
        inst.then_inc(own.sem, 16)
        self.ninst += 1
        own.dcount += 1
        tok = (own.sem, 16 * own.dcount)
        if own is outT:
            outT.w = {tok[0]: tok[1]}
        else:
            outT.w[tok[0]] = tok[1]
        outT.r = {}
        if inT.r.get(tok[0], 0) < tok[1]:
            inT.r[tok[0]] = tok[1]
        return inst

    def finish(self, e, tiles):
        self._waits(e, tiles, [])


class Rot:
    def __init__(self, items):
        self.items = items
        self.i = -1

    def next(self):
        self.i = (self.i + 1) % len(self.items)
        return self.items[self.i]


def tokcol(tile):
    return 1 + tile * 128 if tile < 2 else 259 + (tile - 2) * 128


class Prog:
    def __init__(self, dbg=False, stop=None):
        self.dbg = dbg
        self.stop = stop
        self.nc = bass.Bass("TRN2", target_bir_lowering=False)
        self.es = ExitStack()
        self.fw = FW(self.nc, self.es)
        fw = self.fw

        prog = self

        class Lazy(dict):
            def __missing__(d, name):
                t = fw.dram(name, INPUT_SHAPES[name], F32, kind="ExternalInput")
                d[name] = t
                return t
        self.I = Lazy()
        self.out = fw.dram("out", [NLAT, D], F32, kind="ExternalOutput")
        k = "ExternalOutput" if dbg else "Internal"
        self.modrow = fw.dram("modrow", [2, 2, 6 * D], F32, kind=k)
        self.h1 = fw.dram("h1", [TOK, D], F32, kind=k)
        self.h2 = fw.dram("h2", [TOK, D], F32, kind=("ExternalInput" if stop and stop.startswith("L1") else k))
        self.h3 = fw.dram("h3", [TOK, D], F32, kind=k)
        self.uT = fw.dram("uT", [512, TOKP], BF16)
        if dbg:
            self.dza = fw.dram("dza", [TOK, 512], BF16, kind="ExternalOutput")
            self.dzc = fw.dram("dzc", [512, TOK], BF16, kind="ExternalOutput")
        self.gbT = fw.dram("gbT", [512, TOKP], BF16)
        self.UT = fw.dram("UT", [128, 128, 8, 128], BF16)
        self.Vb = fw.dram("Vb", [128, 128, D], BF16)
        self.banks = [fw.ps("bank%d" % i, [128, 512]) for i in range(8)]
        self.bank = Rot(self.banks)
        self.identf = fw.sb("identf", [128, 128])
        self.identb = fw.sb("identb", [128, 128], BF16)
        fw.dma("sp", self.identf, self.identf[:], self.I["ident"], self.I["ident"].ap())
        fw.dma("pool", self.identb, self.identb[:], self.I["ident"], self.I["ident"].ap())

    def bview(self, bank):
        return bank.ap().bitcast(BF16)

    def load_colT(self, dst, dst_ap, src, src_ap1d):
        self.fw.dma("sp", dst, dst_ap, src, src_ap1d.rearrange("(k p) -> p k", p=128), allow_slow_non_contiguous=True)

    def modvecs(self, layer, which):
        fw = self.fw
        I = self.I
        gname = "norm1_g" if which == 0 else "norm2_g"
        g = fw.sb("g", [128, 8])
        self.load_colT(g, g[:], I[gname], I[gname].ap()[layer])
        GT, shT = [], []
        for s in range(2):
            sc = fw.sb("scl", [128, 8]); sh = fw.sb("shf", [128, 8]); G = fw.sb("G", [128, 8])
            base = which * 3 * D
            self.load_colT(sh, sh[:], self.modrow, self.modrow.ap()[layer, s, base:base + D])
            self.load_colT(sc, sc[:], self.modrow, self.modrow.ap()[layer, s, base + D:base + 2 * D])
            fw.op("dve", lambda e: e.scalar_tensor_tensor(out=G[:], in0=sc[:], scalar=1.0, in1=g[:], op0=ALU.add, op1=ALU.mult), [sc, g], [G])
            GT.append(G); shT.append(sh)
        return GT, shT

    def gate_bcast(self, layer, which):
        fw = self.fw
        res = []
        for s in range(2):
            gt = fw.sb("gate", [128, D])
            off = which * 3 * D + 2 * D
            fw.dma("sp", gt, gt[:], self.modrow, self.modrow.ap()[layer, s, off:off + D].partition_broadcast(128))
            res.append(gt)
        return res

    def make_normer(self):
        fw = self.fw
        self.nb_h = Rot([fw.sb("nh", [128, D]) for _ in range(2)])
        self.nb_sq = fw.sb("nsq", [128, D], BF16)
        self.nb_ss = Rot([fw.sb("nss", [128, 1]) for _ in range(2)])
        self.nb_rs = Rot([fw.sb("nrs", [128, 1]) for _ in range(2)])
        self.nb_hn = Rot([fw.sb("nhn", [128, D], BF16) for _ in range(2)])

    def norm_tile(self, src, src_ap, G, sh, xnT, xnT_cols):
        fw = self.fw
        h = self.nb_h.next(); ss = self.nb_ss.next(); rs = self.nb_rs.next(); hn = self.nb_hn.next(); sq = self.nb_sq
        fw.dma("sp", h, h[:], src, src_ap)
        fw.op("act", lambda e: e.activation(out=sq[:], in_=h[:], func=AF.Square, scale=1.0 / 32, accum_out=ss[:]), [h], [sq, ss])
        fw.op("act", lambda e: e.activation(out=rs[:], in_=ss[:], func=AF.Sqrt, bias=EPS, scale=1.0), [ss], [rs])
        fw.op("dve", lambda e: e.reciprocal(out=rs[:], in_=rs[:]), [rs], [rs])
        fw.op("act", lambda e: e.activation(out=hn[:], in_=h[:], func=AF.Copy, scale=rs[:]), [h, rs], [hn])
        pb = self.bank.next()
        pv = self.bview(pb)
        for k in range(8):
            fw.op("pe", lambda e: e.transpose(out=pv[:, k * 128:(k + 1) * 128], in_=hn[:, k * 128:(k + 1) * 128], identity=self.identb[:]), [hn, self.identb], [pb])
        for k in range(8):
            fw.op("act", lambda e: e.activation(out=xnT[:, k, xnT_cols], in_=pv[:, k * 128:(k + 1) * 128], func=AF.Identity,
                                                scale=G[:, k:k + 1], bias=sh[:, k:k + 1]), [pb, G, sh], [xnT])
        return h

    def src_of(self, tile, hsrc):
        if hsrc is None:
            if tile < 2:
                return self.I["ctx"], self.I["ctx"].ap()[tile * 128:(tile + 1) * 128, :]
            return self.I["x"], self.I["x"].ap()[(tile - 2) * 128:(tile - 1) * 128, :]
        return hsrc, hsrc.ap()[tile * 128:(tile + 1) * 128, :]

    def phase0(self):
        fw = self.fw
        I = self.I
        fw.push()
        sc = fw.sb("sc", [128, 2, 8])
        for s_ in range(2):
            self.load_colT(sc, sc[:, s_, :], I["cvec"], I["cvec"].ap()[s_])
        fw.op("act", lambda e: e.activation(out=sc[:], in_=sc[:], func=AF.Silu), [sc], [sc])
        wb = Rot([fw.sb("adaw", [128, 8, 512]) for _ in range(2)])
        for i in range(2):
            bias = fw.sb("adab", [2, 6 * D])
            modr = fw.sb("modr", [2, 6 * D])
            fw.dma("sp", bias, bias[:], I["ada_b"], I["ada_b"].ap()[i].partition_broadcast(2))
            for cb in range(12):
                wt = wb.next()
                fw.dma("sp" if cb % 2 == 0 else "act", wt, wt[:], I["ada_w"], I["ada_w"].ap()[i, :, cb * 512:(cb + 1) * 512].rearrange("(k p) n -> p k n", p=128))
                pb = self.bank.next()
                for k in range(8):
                    fw.op("pe", lambda e: e.matmul(pb[0:2, :], lhsT=sc[:, :, k], rhs=wt[:, k, :], start=(k == 0), stop=(k == 7)), [sc, wt], [pb])
                fw.op("dve", lambda e: e.tensor_tensor(out=modr[:, cb * 512:(cb + 1) * 512], in0=pb[0:2, :], in1=bias[:, cb * 512:(cb + 1) * 512], op=ALU.add), [pb, bias], [modr])
            fw.dma("sp", self.modrow, self.modrow.ap()[i], modr, modr[:])
        fw.pop()

    def l0_phaseA(self):
        fw = self.fw
        I = self.I
        fw.push()
        self.qT = fw.sb("qT", [128, 4, TOK], BF16)
        self.kT2 = fw.sb("kT2", [128, 2, TOK], BF16)
        self.vext = fw.sb("vext", [128, NT, 2, 66], BF16)
        fw.op("pool", lambda e: e.memset(self.vext[:], 1.0), [], [self.vext])
        fw.push()
        GT, shT = self.modvecs(0, 0)
        self.make_normer()
        w_in = fw.sb("w_in", [128, 8, 2432], BF16)
        for k in range(8):
            fw.dma("pool", w_in, w_in[:, k, :], I["w_in"], I["w_in"].ap()[k * 128:(k + 1) * 128, :])
        rC = fw.sb("rC", [128, TOK], BF16); rS = fw.sb("rS", [128, TOK], BF16)
        fw.dma("pool", rC, rC[:], I["ropeC"], I["ropeC"].ap())
        fw.dma("pool", rS, rS[:], I["ropeS"], I["ropeS"].ap())
        ropeP = fw.sb("ropeP", [128, 128], BF16); blk64 = fw.sb("blk64", [128, 128], BF16)
        fw.dma("pool", ropeP, ropeP[:], I["ropeP"], I["ropeP"].ap())
        fw.dma("pool", blk64, blk64[:], I["blk64"], I["blk64"].ap())
        gains = fw.sb("gains", [128, 2])
        fw.dma("sp", gains, gains[:, 0:1], I["qg2"], I["qg2"].ap().rearrange("(p o) -> p o", o=1))
        fw.dma("sp", gains, gains[:, 1:2], I["kg2"], I["kg2"].ap().rearrange("(p o) -> p o", o=1))
        zt = fw.sb("zt", [128, 4, 2], BF16)
        fw.op("pool", lambda e: e.memset(zt[:], 0.0), [], [zt])
        for dst in (self.uT,):
            v = dst.ap().rearrange("(b p) c -> p b c", p=128)
            fw.dma("pool", dst, v[:, :, 0:1], zt, zt[:, :, 0:1], allow_slow_non_contiguous=True)
            fw.dma("pool", dst, v[:, :, 257:259], zt, zt[:, :, 0:2], allow_slow_non_contiguous=True)
            fw.dma("pool", dst, v[:, :, TOKP - 1:TOKP], zt, zt[:, :, 0:1], allow_slow_non_contiguous=True)
        xnTs = Rot([fw.sb("xnT", [128, 8, 256], BF16) for _ in range(2)])
        hTt = fw.sb("hTt", [128, 4, 256])
        gbst = Rot([fw.sb("gbst", [128, 4, 256], BF16) for _ in range(2)])
        ust = Rot([fw.sb("ust", [128, 4, 256], BF16) for _ in range(2)])
        qraw = Rot([fw.sb("qraw", [128, 256]) for _ in range(2)])
        sqb = Rot([fw.sb("sqb", [128, 256], BF16) for _ in range(2)])
        rsb = Rot([fw.sb("rsb", [128, 256]) for _ in range(2)])
        qnb = Rot([fw.sb("qnb", [128, 256], BF16) for _ in range(2)])
        t1b = Rot([fw.sb("t1b", [128, 256]) for _ in range(2)])
        t2b = Rot([fw.sb("t2b", [128, 256]) for _ in range(2)])
        uTv = self.uT.ap().rearrange("(b p) c -> p b c", p=128)
        gbTv = self.gbT.ap().rearrange("(b p) c -> p b c", p=128)
        import os
        skip = os.environ.get('L0A_SKIP', '').split(',')
        for g in range(int(os.environ.get('L0A_NG', NT // 2))):
            s = 1 if g == 0 else 0
            xnT = xnTs.next()
            for j in range(2):
                tile = 2 * g + j
                src, sap = self.src_of(tile, None)
                self.norm_tile(src, sap, GT[s], shT[s], xnT, slice(j * 128, (j + 1) * 128))
            tok0 = g * 256
            gb_s = gbst.next(); u_s = ust.next()
            for fb in range(18):
                if fb >= 12 and 'qk' in skip:
                    continue
                if fb < 12 and 'conv' in skip:
                    continue
                pb = self.bank.next()
                for k in range(8):
                    fw.op("pe", lambda e: e.matmul(pb[:, 0:256], lhsT=w_in[:, k, fb * 128:(fb + 1) * 128], rhs=xnT[:, k, :], start=(k == 0), stop=(k == 7)), [w_in, xnT], [pb])
                if fb < 4:
                    fw.op("act", lambda e: e.activation(out=hTt[:, fb, :], in_=pb[:, 0:256], func=AF.Copy), [pb], [hTt])
                elif fb < 8:
                    fw.op("act", lambda e: e.activation(out=gb_s[:, fb - 4, :], in_=pb[:, 0:256], func=AF.Copy), [pb], [gb_s])
                elif fb < 12:
                    fw.op("dve", lambda e: e.tensor_tensor(out=u_s[:, fb - 8, :], in0=pb[:, 0:256], in1=hTt[:, fb - 8, :], op=ALU.mult), [pb, hTt], [u_s])
                else:
                    isq = fb < 16
                    qr = qraw.next(); sq = sqb.next(); rs = rsb.next(); qn = qnb.next(); t1 = t1b.next(); t2 = t2b.next()
                    gcol = gains[:, 0:1] if isq else gains[:, 1:2]
                    fw.op("act", lambda e: e.activation(out=qr[:], in_=pb[:, 0:256], func=AF.Copy), [pb], [qr])
                    fw.op("act", lambda e: e.activation(out=sq[:], in_=pb[:, 0:256], func=AF.Square), [pb], [sq])
                    p2 = self.bank.next()
                    fw.op("pe", lambda e: e.matmul(p2[:, 0:256], lhsT=blk64[:], rhs=sq[:], start=True, stop=True), [blk64, sq], [p2])
                    fw.op("act", lambda e: e.activation(out=rs[:], in_=p2[:, 0:256], func=AF.Sqrt, bias=EPS, scale=1.0 / 64), [p2], [rs])
                    fw.op("dve", lambda e: e.reciprocal(out=rs[:], in_=rs[:]), [rs], [rs])
                    fw.op("dve", lambda e: e.scalar_tensor_tensor(out=qn[:], in0=qr[:], scalar=gcol, in1=rs[:], op0=ALU.mult, op1=ALU.mult), [qr, gains, rs], [qn])
                    p3 = self.bank.next()
                    fw.op("pe", lambda e: e.matmul(p3[:, 0:256], lhsT=ropeP[:], rhs=qn[:], start=True, stop=True), [ropeP, qn], [p3])
                    fw.op("dve", lambda e: e.tensor_tensor(out=t2[:], in0=p3[:, 0:256], in1=rS[:, tok0:tok0 + 256], op=ALU.mult), [p3, rS], [t2])
                    fw.op("pool", lambda e: e.tensor_tensor(out=t1[:], in0=qn[:], in1=rC[:, tok0:tok0 + 256], op=ALU.mult), [qn, rC], [t1])
                    dst = self.qT if isq else self.kT2
                    bi = fb - 12 if isq else fb - 16
                    fw.op("dve", lambda e: e.tensor_tensor(out=dst[:, bi, tok0:tok0 + 256], in0=t1[:], in1=t2[:], op=ALU.add), [t1, t2], [dst])
            for j in range(2 if 'v' not in skip else 0):
                tile = 2 * g + j
                pb = self.bank.next()
                for k in range(8):
                    fw.op("pe", lambda e: e.matmul(pb[:, 0:128], lhsT=xnT[:, k, j * 128:(j + 1) * 128], rhs=w_in[:, k, 2304:2432], start=(k == 0), stop=(k == 7)), [xnT, w_in], [pb])
                fw.op("act", lambda e: e.activation(out=self.vext[:, tile, :, 0:64], in_=pb[:, 0:128].rearrange("p (a d) -> p a d", a=2), func=AF.Copy), [pb], [self.vext])
            for j in range(2 if 'store' not in skip else 0):
                c0 = tokcol(2 * g + j)
                fw.dma("sp", self.uT, uTv[:, :, c0:c0 + 128], u_s, u_s[:, :, j * 128:(j + 1) * 128])
                fw.dma("sp", self.gbT, gbTv[:, :, c0:c0 + 128], gb_s, gb_s[:, :, j * 128:(j + 1) * 128])
        fw.pop()

    def l0_phaseB(self):
        fw = self.fw
        I = self.I
        fw.push()
        gate = self.gate_bcast(0, 0)
        w_out = fw.sb("w_out", [128, 8, D], BF16)
        for k in range(8):
            fw.dma("pool", w_out, w_out[:, k, :], I["w_out"], I["w_out"].ap()[k * 128:(k + 1) * 128, :])
        cw = fw.sb("cw", [128, 3, 4])
        for j_ in range(3):
            self.load_colT(cw, cw[:, j_, :], I["conv_w"], I["conv_w"].ap()[j_])
        PTs = Rot([fw.sb("PT", [128, NT, 512], BF16) for _ in range(2)])
        zatt = Rot([fw.sb("zatt", [128, 512], BF16) for _ in range(2)])
        rcb = Rot([fw.sb("rcb", [128, 4]) for _ in range(2)])
        zaTs = Rot([fw.sb("zaT", [128, 4, 128], BF16) for _ in range(2)])
        zcTs = Rot([fw.sb("zcT", [128, 4, 128], BF16) for _ in range(2)])
        uhs = Rot([fw.sb("uh", [128, 4, 130], BF16) for _ in range(2)])
        gbts = Rot([fw.sb("gbt", [128, 4, 128], BF16) for _ in range(2)])
        accs = Rot([fw.sb("acc", [128, 128]) for _ in range(3)])
        hts = Rot([fw.sb("ht", [128, D]) for _ in range(2)])
        tmps = Rot([fw.sb("tmp", [128, D]) for _ in range(2)])
        uTv = self.uT.ap().rearrange("(b p) c -> p b c", p=128)
        gbTv = self.gbT.ap().rearrange("(b p) c -> p b c", p=128)
        obanks = Rot(self.banks[0:2]); sbanks = Rot(self.banks[2:6]); self.bank = Rot(self.banks[6:8])
        import os
        skip = os.environ.get('L0B_SKIP', '').split(',')
        for qt in range(int(os.environ.get('L0B_NT', NT))):
            s = 1 if qt < 2 else 0
            nch = 2 if qt < 2 else NT
            ht = hts.next()
            src, sap = self.src_of(qt, None)
            fw.dma("sp", ht, ht[:], src, sap)
            uh = uhs.next(); gbt = gbts.next()
            c0 = tokcol(qt)
            fw.dma("sp", uh, uh[:], self.uT, uTv[:, :, c0 - 1:c0 + 129])
            fw.dma("sp", gbt, gbt[:], self.gbT, gbTv[:, :, c0:c0 + 128])
            za = zatt.next()
            for kv in range(2 if 'attn' not in skip else 0):
                PTall = PTs.next()
                for sc in range(nch):
                    sbs = [sbanks.next(), sbanks.next()]
                    for b in range(2):
                        for hf in range(2):
                            fw.op("pe", lambda e: e.matmul(sbs[hf][:, b * 128:(b + 1) * 128], lhsT=self.kT2[hf * 64:(hf + 1) * 64, kv, sc * 128:(sc + 1) * 128],
                                                            rhs=self.qT[hf * 64:(hf + 1) * 64, 2 * kv + b, qt * 128:(qt + 1) * 128], start=True, stop=True), [self.kT2, self.qT], [sbs[hf]])
                    PT4 = PTall[:, sc, :].rearrange("p (b h n) -> p b h n", b=2, h=2)
                    for hf in range(2):
                        fw.op("act", lambda e: e.activation(out=PT4[:, :, hf, :], in_=sbs[hf][:, 0:256].rearrange("p (b n) -> p b n", b=2), func=AF.Exp, scale=0.125), [sbs[hf]], [PTall])
                ob = obanks.next()
                O = ob[:, 0:260].rearrange("p (c e) -> p c e", e=65)
                for c in range(4):
                    for sc in range(nch):
                        fw.op("pe", lambda e: e.matmul(O[:, c, :], lhsT=PTall[:, sc, c * 128:(c + 1) * 128], rhs=self.vext[:, sc, kv, 0:65], start=(sc == 0), stop=(sc == nch - 1)), [PTall, self.vext], [ob])
                rc = rcb.next()
                fw.op("dve", lambda e: e.reciprocal(out=rc[:], in_=O[:, :, 64]), [ob], [rc])
                fw.op("dve", lambda e: e.tensor_tensor(out=za[:, kv * 256:(kv + 1) * 256].rearrange("p (c d) -> p c d", c=4), in0=O[:, :, 0:64],
                                                       in1=rc[:].unsqueeze(2).to_broadcast([128, 4, 64]), op=ALU.mult), [ob, rc], [za])
            pb = self.bank.next(); pv = self.bview(pb)
            for k in range(4):
                fw.op("pe", lambda e: e.transpose(out=pv[:, k * 128:(k + 1) * 128], in_=za[:, k * 128:(k + 1) * 128], identity=self.identb[:]), [za, self.identb], [pb])
            zaT = zaTs.next()
            fw.op("act", lambda e: e.activation(out=zaT[:].rearrange("p k n -> p (k n)"), in_=pv[:, 0:512], func=AF.Copy), [pb], [zaT])
            zcT = zcTs.next()
            for b in range(4 if 'conv' not in skip else 0):
                acc = accs.next()
                fw.op("pool", lambda e: e.tensor_scalar(out=acc[:], in0=uh[:, b, 0:128], scalar1=cw[:, 0, b:b + 1], scalar2=None, op0=ALU.mult), [uh, cw], [acc])
                for j_ in (1, 2):
                    t_ = accs.next()
                    fw.op("pool", lambda e: e.tensor_scalar(out=t_[:], in0=uh[:, b, j_:j_ + 128], scalar1=cw[:, j_, b:b + 1], scalar2=None, op0=ALU.mult), [uh, cw], [t_])
                    fw.op("pool", lambda e: e.tensor_tensor(out=acc[:], in0=acc[:], in1=t_[:], op=ALU.add), [acc, t_], [acc])
                fw.op("pool", lambda e: e.tensor_tensor(out=zcT[:, b, :], in0=acc[:], in1=gbt[:, b, :], op=ALU.mult), [acc, gbt], [zcT])
            if self.dbg:
                fw.dma("sp", self.dza, self.dza.ap()[qt * 128:(qt + 1) * 128, :], za, za[:])
                fw.dma("sp", self.dzc, self.dzc.ap().rearrange("(b p) c -> p b c", p=128)[:, :, qt * 128:(qt + 1) * 128], zcT, zcT[:])
            tmp = tmps.next()
            for half in range(2):
                pb = self.bank.next()
                for k in range(8):
                    lt = zcT[:, k, :] if k < 4 else zaT[:, k - 4, :]
                    fw.op("pe", lambda e: e.matmul(pb[:], lhsT=lt, rhs=w_out[:, k, half * 512:(half + 1) * 512], start=(k == 0), stop=(k == 7)), [zcT, zaT, w_out], [pb])
                fw.op("dve", lambda e: e.tensor_tensor(out=tmp[:, half * 512:(half + 1) * 512], in0=pb[:], in1=gate[s][:, half * 512:(half + 1) * 512], op=ALU.mult), [pb, gate[s]], [tmp])
            fw.op("pool", lambda e: e.tensor_tensor(out=tmp[:], in0=tmp[:], in1=ht[:], op=ALU.add), [tmp, ht], [tmp])
            fw.dma("act", self.h1, self.h1.ap()[qt * 128:(qt + 1) * 128, :], tmp, tmp[:])
        self.bank = Rot(self.banks)
        fw.pop()
        fw.pop()

    def peer_prep(self, layer):
        fw = self.fw
        I = self.I
        fw.push()
        banks = Rot(self.banks)
        Uv = I["peer_u"].ap()[layer].rearrange("(i j) d -> j i d", j=128)
        Vv = I["peer_v"].ap()[layer].rearrange("(i j) d -> j i d", j=128)
        UTv = self.UT.ap().rearrange("i p k j -> p i (k j)")
        Vbv = self.Vb.ap().rearrange("i j d -> j i d")
        ubs = Rot([fw.sb("ub", [128, 4, D], BF16) for _ in range(2)])
        vbs = Rot([fw.sb("vb", [128, 4, D], BF16) for _ in range(2)])
        uts = Rot([fw.sb("ut", [128, 4, 1024], BF16) for _ in range(2)])
        for c in range(32):
            ub = ubs.next(); vb = vbs.next(); ut = uts.next()
            fw.dma("pool", ub, ub[:], I["peer_u"], Uv[:, c * 4:(c + 1) * 4, :])
            fw.dma("pool", vb, vb[:], I["peer_v"], Vv[:, c * 4:(c + 1) * 4, :])
            fw.dma("sp", self.Vb, Vbv[:, c * 4:(c + 1) * 4, :], vb, vb[:])
            for ii in range(4):
                pb = banks.next(); pv = self.bview(pb)
                for k in range(8):
                    fw.op("pe", lambda e: e.transpose(out=pv[:, k * 128:(k + 1) * 128], in_=ub[:, ii, k * 128:(k + 1) * 128], identity=self.identb[:]), [ub, self.identb], [pb])
                if ii % 2 == 0:
                    fw.op("act", lambda e: e.activation(out=ut[:, ii, :], in_=pv[:, 0:1024], func=AF.Copy), [pb], [ut])
                else:
                    fw.op("dve", lambda e: e.tensor_copy(out=ut[:, ii, :], in_=pv[:, 0:1024]), [pb], [ut])
            fw.dma("sp", self.UT, UTv[:, c * 4:(c + 1) * 4, :], ut, ut[:])
        fw.pop()

    def peer(self, layer, hsrc, hdst, final=False):
        fw = self.fw
        I = self.I
        fw.push()
        GT, shT = self.modvecs(layer, 1)
        gate = self.gate_bcast(layer, 1)
        self.make_normer()
        obanks = self.banks[0:4]
        self.bank = Rot(self.banks[4:8])
        w_q = fw.sb("w_q", [128, 8, 2048], BF16)
        for k in range(8):
            fw.dma("pool", w_q, w_q[:, k, :], I["peer_wq"], I["peer_wq"].ap()[layer, k * 128:(k + 1) * 128, :])
        keyn = fw.sb("keyn", [128, 16, 128], BF16)
        fw.dma("pool", keyn, keyn[:], I["peer_keys"], I["peer_keys"].ap()[layer].rearrange("h k d -> k h d"))
        keysT = fw.sb("keysT", [128, 16, 128], BF16)
        for c in range(2):
            pb = self.bank.next(); pv = self.bview(pb)
            for q in range(8):
                fw.op("pe", lambda e: e.transpose(out=pv[:, q * 128:(q + 1) * 128], in_=keyn[:, c * 8 + q, :], identity=self.identb[:]), [keyn, self.identb], [pb])
            fw.op("act", lambda e: e.activation(out=keysT[:, c * 8:(c + 1) * 8, :].rearrange("p a b -> p (a b)"), in_=pv[:, 0:1024], func=AF.Copy), [pb], [keysT])
        xn2T = fw.sb("xn2T", [128, 8, 256], BF16)
        qT = fw.sb("pqT", [128, 16, 256], BF16)
        s_all = fw.sb("s_all", [128, 2, 16, 128])
        tau = fw.sb("tau", [128, 2, 8]); bias = fw.sb("pbias", [128, 2, 8])
        m0 = fw.sb("m0", [128, 16]); m1 = fw.sb("m1", [128, 16]); scr = fw.sb("scr", [128, 128])
        cand = fw.sb("cand", [128, 16, 16]); c24 = fw.sb("c24", [128, 24]); cscr = fw.sb("cscr", [128, 256])
        st1 = fw.sb("st1", [128, 4]); e16 = fw.sb("e16", [128, 16])
        IB = 16
        Ws = Rot([fw.sb("W", [128, 2, IB, 128], BF16) for _ in range(2)])
        sums = Rot([fw.sb("sum", [128, IB, 128]) for _ in range(3)])
        Ps = Rot([fw.sb("P", [128, IB, 128], BF16) for _ in range(3)])
        Whs = Rot([fw.sb("Wh", [128, IB, 128], BF16) for _ in range(2)])
        Mks = Rot([fw.sb("Mk", [128, IB, 128], BF16) for _ in range(2)])
        UTs = Rot([fw.sb("UTs", [128, 2, 1024], BF16) for _ in range(3)])
        Vs = Rot([fw.sb("Vs", [128, 2, D], BF16) for _ in range(3)])
        Gs = Rot([fw.sb("Gs", [128, 256], BF16) for _ in range(4)])
        WTs = Rot([fw.sb("WTs", [128, 256], BF16) for _ in range(3)])
        WAs = Rot([fw.sb("WAs", [128, 256], BF16) for _ in range(4)])
        tmps = Rot([fw.sb("ptmp", [128, D]) for _ in range(2)])
        hts = Rot([fw.sb("pht", [128, D]) for _ in range(2)])
        UTv = self.UT.ap().rearrange("i p k j -> p i (k j)")
        Vbv = self.Vb.ap().rearrange("i j d -> j i d")
        import os
        ng = int(os.environ.get("PEER_NG", NT // 2))
        for g in range(ng):
            if final and g == 0:
                continue
            s = 1 if g == 0 else 0
            for j in range(2):
                src, sap = self.src_of(2 * g + j, hsrc)
                self.norm_tile(src, sap, GT[s], shT[s], xn2T, slice(j * 128, (j + 1) * 128))
            for hp in range(16):
                pb = self.bank.next()
                for k in range(8):
                    fw.op("pe", lambda e: e.matmul(pb[:, 0:256], lhsT=w_q[:, k, hp * 128:(hp + 1) * 128], rhs=xn2T[:, k, :], start=(k == 0), stop=(k == 7)), [w_q, xn2T], [pb])
                fw.op("act", lambda e: e.activation(out=qT[:, hp, :], in_=pb[:, 0:256], func=AF.Copy), [pb], [qT])
            for t in range(2):
                for c in range(4):
                    pb = self.bank.next()
                    for q in range(4):
                        hp = c * 4 + q
                        fw.op("pe", lambda e: e.matmul(pb[:, q * 128:(q + 1) * 128], lhsT=qT[:, hp, t * 128:(t + 1) * 128], rhs=keysT[:, hp, :], start=True, stop=True), [qT, keysT], [pb])
                    fw.op("act", lambda e: e.activation(out=s_all[:, t, c * 4:(c + 1) * 4, :].rearrange("p a b -> p (a b)"), in_=pb[:], func=AF.Copy), [pb], [s_all])
            for t in range(2):
                for h in range(8):
                    for (mm, src) in ((m0, s_all[:, t, 2 * h, :]), (m1, s_all[:, t, 2 * h + 1, :])):
                        fw.op("dve", lambda e: e.max(out=mm[:, 0:8], in_=src), [s_all], [mm])
                        fw.op("dve", lambda e: e.match_replace(out=scr[:], in_to_replace=mm[:, 0:8], in_values=src, imm_value=-1e30), [s_all, mm], [scr])
                        fw.op("dve", lambda e: e.max(out=mm[:, 8:16], in_=scr[:]), [scr], [mm])
                    fw.op("dve", lambda e: e.tensor_tensor(out=cand[:], in0=m0[:].unsqueeze(2).to_broadcast([128, 16, 16]), in1=m1[:].unsqueeze(1).to_broadcast([128, 16, 16]), op=ALU.add), [m0, m1], [cand])
                    cf = cand[:].rearrange("p a b -> p (a b)")
                    fw.op("dve", lambda e: e.max(out=c24[:, 0:8], in_=cf), [cand], [c24])
                    fw.op("dve", lambda e: e.match_replace(out=cscr[:], in_to_replace=c24[:, 0:8], in_values=cf, imm_value=-1e30), [cand, c24], [cscr])
                    fw.op("dve", lambda e: e.max(out=c24[:, 8:16], in_=cscr[:]), [cscr], [c24])
                    fw.op("dve", lambda e: e.match_replace(out=cscr[:], in_to_replace=c24[:, 8:16], in_values=cscr[:], imm_value=-1e30), [cscr, c24], [cscr])
                    fw.op("dve", lambda e: e.max(out=c24[:, 16:24], in_=cscr[:]), [cscr], [c24])
                    fw.op("dve", lambda e: e.tensor_scalar(out=st1[:, 0:1], in0=c24[:, 16:17], scalar1=0.5, scalar2=None, op0=ALU.mult), [c24], [st1])
                    fw.op("dve", lambda e: e.scalar_tensor_tensor(out=tau[:, t, h:h + 1], in0=c24[:, 15:16], scalar=0.5, in1=st1[:, 0:1], op0=ALU.mult, op1=ALU.add), [c24, st1], [tau])
                    fw.op("dve", lambda e: e.tensor_scalar(out=st1[:, 1:2], in0=c24[:, 0:1], scalar1=-1.0, scalar2=None, op0=ALU.mult), [c24], [st1])
                    fw.op("act", lambda e: e.activation(out=e16[:], in_=c24[:, 0:16], func=AF.Exp, bias=st1[:, 1:2], scale=1.0, accum_out=st1[:, 2:3]), [c24, st1], [e16, st1])
                    fw.op("act", lambda e: e.activation(out=st1[:, 3:4], in_=st1[:, 2:3], func=AF.Ln), [st1], [st1])
                    fw.op("dve", lambda e: e.tensor_tensor(out=bias[:, t, h:h + 1], in0=st1[:, 1:2], in1=st1[:, 3:4], op=ALU.subtract), [st1], [bias])
            NB = 128 // IB
            Wl = [None] * NB

            def build_unit(ib, u):
                i0 = ib * IB
                if u == 0:
                    Wl[ib] = Ws.next()
                W = Wl[ib]
                t, h = u // 8, u % 8
                sm = sums.next(); P = Ps.next()
                fw.op("dve", lambda e: e.tensor_tensor(out=sm[:], in0=s_all[:, t, 2 * h, i0:i0 + IB].unsqueeze(2).to_broadcast([128, IB, 128]),
                                                       in1=s_all[:, t, 2 * h + 1, :].unsqueeze(1).to_broadcast([128, IB, 128]), op=ALU.add), [s_all], [sm])
                def emitB():
                    fw.op("act", lambda e: e.activation(out=P[:], in_=sm[:], func=AF.Exp, bias=bias[:, t, h:h + 1], scale=1.0), [sm, bias], [P])

                def fin():
                    mk = Mks.next()
                    fw.op("dve", lambda e: e.tensor_scalar(out=mk[:], in0=sm[:], scalar1=tau[:, t, h:h + 1], scalar2=None, op0=ALU.is_ge), [sm, tau], [mk])
                    if h == 0:
                        fw.op("dve", lambda e: e.tensor_tensor(out=W[:, t, :, :], in0=mk[:], in1=P[:], op=ALU.mult), [mk, P], [W])
                    else:
                        Wh = Whs.next()
                        fw.op("dve", lambda e: e.tensor_tensor(out=Wh[:], in0=mk[:], in1=P[:], op=ALU.mult), [mk, P], [Wh])
                        eng = "pool" if h % 2 == 1 else "dve"
                        fw.op(eng, lambda e: e.tensor_tensor(out=W[:, t, :, :], in0=W[:, t, :, :], in1=Wh[:], op=ALU.add), [W, Wh], [W])
                return emitB, fin

            cur_ld = {}
            carry = {}

            def front(i):
                ib, iw = i // IB, i % IB
                W = Wl[ib]
                if i % 2 == 0:
                    UTt = UTs.next(); Vt = Vs.next()
                    fw.dma("sp", UTt, UTt[:], self.UT, UTv[:, i:i + 2, :])
                    fw.dma("sp", Vt, Vt[:], self.Vb, Vbv[:, i:i + 2, :])
                    cur_ld["u"] = UTt; cur_ld["v"] = Vt
                UTt = cur_ld["u"]; Vt = cur_ld["v"]
                i2 = i % 2
                pa = self.bank.next()
                for k in range(8):
                    fw.op("pe", lambda e: e.matmul(pa[:, 0:256], lhsT=UTt[:, i2, k * 128:(k + 1) * 128], rhs=xn2T[:, k, :], start=(k == 0), stop=(k == 7)), [UTt, xn2T], [pa])
                G = Gs.next()
                fw.op("act", lambda e: e.activation(out=G[:], in_=pa[:, 0:256], func=AF.Gelu), [pa], [G])
                pw = self.bank.next(); pwv = self.bview(pw)
                for t in range(2):
                    fw.op("pe", lambda e: e.transpose(out=pwv[:, t * 128:(t + 1) * 128], in_=W[:, t, iw, :], identity=self.identb[:]), [W, self.identb], [pw])
                WA = WAs.next()
                fw.op("dve", lambda e: e.tensor_tensor(out=WA[:], in0=G[:], in1=pwv[:, 0:256], op=ALU.mult), [G, pw], [WA])

                def back():
                    for t in range(2):
                        for half in range(2):
                            ob = obanks[t * 2 + half]
                            fw.op("pe", lambda e: e.matmul(ob[:], lhsT=WA[:, t * 128:(t + 1) * 128], rhs=Vt[:, i2, half * 512:(half + 1) * 512], start=(i == 0), stop=(i == 127)), [WA, Vt], [ob])
                return back

            pend = None
            for u in range(16):
                eb, f = build_unit(0, u)
                eb()
                if pend is not None:
                    pend()
                pend = f
            pend()
            backs = {0: front(0), 1: front(1)}
            sched = {0: [0, 1], 1: [2, 3]}
            for j_ in range(2, 14):
                sched[j_] = [j_ + 2]
            for i in range(128):
                ib, j = i // IB, i % IB
                units = sched.get(j, []) if ib + 1 < NB else []
                ebs = []; fins = []
                for u in units:
                    eb, f = build_unit(ib + 1, u)
                    ebs.append(eb); fins.append(f)
                if i + 2 < 128:
                    backs[i + 2] = front(i + 2)
                for eb in ebs:
                    eb()
                cp = carry.pop("f", None)
                if cp is not None:
                    cp()
                for f in fins[:-1]:
                    f()
                if fins:
                    if j == 13:
                        fins[-1]()
                    else:
                        carry["f"] = fins[-1]
                backs.pop(i)()
            for t in range(2):
                tile = 2 * g + t
                tmp = tmps.next(); ht = hts.next()
                src, sap = self.src_of(tile, hsrc)
                fw.dma("sp", ht, ht[:], src, sap)
                for half in range(2):
                    ob = obanks[t * 2 + half]
                    fw.op("dve", lambda e: e.tensor_tensor(out=tmp[:, half * 512:(half + 1) * 512], in0=ob[:], in1=gate[s][:, half * 512:(half + 1) * 512], op=ALU.mult), [ob, gate[s]], [tmp])
                fw.op("pool", lambda e: e.tensor_tensor(out=tmp[:], in0=tmp[:], in1=ht[:], op=ALU.add), [tmp, ht], [tmp])
                if final:
                    fw.dma("act", self.out, self.out.ap()[(tile - 2) * 128:(tile - 1) * 128, :], tmp, tmp[:])
                else:
                    fw.dma("act", hdst, hdst.ap()[tile * 128:(tile + 1) * 128, :], tmp, tmp[:])
        self.bank = Rot(self.banks)
        fw.pop()

    def l1_phaseA(self, hsrc):
        fw = self.fw
        I = self.I
        S = self.S1 = {}
        for nm in ("R", "KAP", "V", "G", "LW0", "LW1", "B0", "B1", "KD0", "KD1"):
            S[nm] = fw.dram("s1_" + nm, [TOK, D], F32, kind=("ExternalOutput" if self.dbg else "Internal"))
        S["BON"] = fw.dram("s1_BON", [TOK, 16], F32, kind=("ExternalOutput" if self.dbg else "Internal"))
        fw.push()
        GT, shT = self.modvecs(1, 0)
        self.make_normer()
        self.bank = Rot(self.banks)
        W = {}
        for nm in ("od_w_r", "od_w_k", "od_w_v"):
            W[nm] = fw.sb(nm, [128, 8, D], BF16)
            for k in range(8):
                fw.dma("pool", W[nm], W[nm][:, k, :], I[nm], I[nm].ap()[k * 128:(k + 1) * 128, :])
        g1 = fw.sb("g1", [128, 8, 128], BF16); g2 = fw.sb("g2", [128, D], BF16)
        fw.dma("pool", g1, g1[:], I["od_g1"], I["od_g1"].ap().rearrange("(k p) n -> p k n", p=128))
        fw.dma("pool", g2, g2[:], I["od_g2"], I["od_g2"].ap())
        w1c = fw.sb("w1c", [128, 8, 128], BF16); a1c = fw.sb("a1c", [128, 8, 128], BF16)
        for d in range(2):
            fw.dma("pool", w1c, w1c[:, :, d * 64:(d + 1) * 64], I["od_w1"], I["od_w1"].ap()[d].rearrange("(k p) n -> p k n", p=128))
            fw.dma("pool", a1c, a1c[:, :, d * 64:(d + 1) * 64], I["od_a1"], I["od_a1"].ap()[d].rearrange("(k p) n -> p k n", p=128))
        w2c = fw.sb("w2c", [128, D], BF16); a2c = fw.sb("a2c", [128, D], BF16)
        fw.dma("pool", w2c, w2c[:], I["od_w2"], I["od_w2"].ap().rearrange("d l n -> (d l) n"))
        fw.dma("pool", a2c, a2c[:], I["od_a2"], I["od_a2"].ap().rearrange("d l n -> (d l) n"))
        brow = fw.sb("brow", [1, 4, D], BF16)
        fw.dma("pool", brow, brow[:, 0:2, :], I["od_w0"], I["od_w0"].ap().rearrange("(o d) n -> o d n", o=1))
        fw.dma("pool", brow, brow[:, 2:4, :], I["od_a0"], I["od_a0"].ap().rearrange("(o d) n -> o d n", o=1))
        ones1 = fw.sb("ones1", [1, 128], BF16)
        fw.op("dve", lambda e: e.memset(ones1[:], 1.0), [], [ones1])
        kkb = fw.sb("kkb", [128, D]); kab = fw.sb("kab", [128, D]); rkb = fw.sb("rkb", [128, D])
        for t_, nm in ((kkb, "od_k_k"), (kab, "od_k_a"), (rkb, "od_r_k")):
            fw.dma("sp", t_, t_[:], I[nm], I[nm].ap().partition_broadcast(128))
        muT = fw.sb("muT", [128, 6, 8])
        for m in range(6):
            self.load_colT(muT, muT[:, m, :], I["od_mu"], I["od_mu"].ap()[m])
        NG = NT // 2
        win = [fw.sb("xnw", [128, 8, 256], BF16) for _ in range(4)]
        xxT = fw.sb("xxT", [128, 8, 256], BF16)
        xms = Rot([fw.sb("xm", [128, 8, 256], BF16) for _ in range(2)])
        tmpp = Rot([fw.sb("l1t", [128, D]) for _ in range(8)])
        ksb0 = fw.sb("ksb0", [128, D]); ksb1 = fw.sb("ksb1", [128, D])
        kap = fw.sb("kap", [128, D]); kka = fw.sb("kka", [128, D]); kmk = fw.sb("kmk", [128, D]); r_s = fw.sb("r_s", [128, D]); kd0 = fw.sb("kd0", [128, D])
        ss16 = Rot([fw.sb("ss16", [128, 16]) for _ in range(2)])
        lo = Rot([fw.sb("lo", [128, 128], BF16) for _ in range(3)])
        loT = Rot([fw.sb("loT", [128, 128], BF16) for _ in range(3)])

        def norm_group(g):
            s = 1 if g == 0 else 0
            for j in range(2):
                src, sap = self.src_of(2 * g + j, hsrc)
                self.norm_tile(src, sap, GT[s], shT[s], win[g % 4], slice(j * 128, (j + 1) * 128))

        def sub(out, a, b):
            fw.op("pool", lambda e: e.tensor_tensor(out=out, in0=a, in1=b, op=ALU.subtract), [win[0], win[1], win[2], win[3]], [xxT])

        def neg(out, a):
            fw.op("pool", lambda e: e.tensor_scalar(out=out, in0=a, scalar1=-1.0, scalar2=None, op0=ALU.mult), [win[0], win[1], win[2], win[3]], [xxT])

        def proj_tok(xm, t, Wt, nb=2):
            outs = []
            for half in range(nb):
                pb = self.bank.next()
                for k in range(8):
                    fw.op("pe", lambda e: e.matmul(pb[:], lhsT=xm[:, k, t * 128:(t + 1) * 128], rhs=Wt[:, k, half * 512:(half + 1) * 512], start=(k == 0), stop=(k == 7)), [xm, Wt], [pb])
                outs.append(pb)
            return outs

        def evac(dst, pbs):
            for half, pb in enumerate(pbs):
                fw.op("act", lambda e: e.activation(out=dst[:, half * 512:(half + 1) * 512], in_=pb[:], func=AF.Copy), [pb], [dst])

        def store(nm, tile, src, ap=None):
            fw.dma("sp", S[nm], S[nm].ap()[tile * 128:(tile + 1) * 128, :], src, src[:] if ap is None else ap)

        def lerp(m, cur):
            xm = xms.next()
            for k in range(8):
                fw.op("dve", lambda e: e.scalar_tensor_tensor(out=xm[:, k, :], in0=xxT[:, k, :], scalar=muT[:, m, k:k + 1], in1=cur[:, k, :], op0=ALU.mult, op1=ALU.add), [xxT, muT, cur], [xm])
            return xm

        def lora(xm, t, w1, w2, brow_i, func):
            pb = self.bank.next()
            for k in range(8):
                fw.op("pe", lambda e: e.matmul(pb[:, 0:128], lhsT=xm[:, k, t * 128:(t + 1) * 128], rhs=w1[:, k, :], start=(k == 0), stop=(k == 7)), [xm, w1], [pb])
            l_ = lo.next()
            fw.op("act", lambda e: e.activation(out=l_[:], in_=pb[:, 0:128], func=func), [pb], [l_])
            pt = self.bank.next(); pv = self.bview(pt)
            fw.op("pe", lambda e: e.transpose(out=pv[:, 0:128], in_=l_[:], identity=self.identb[:]), [l_, self.identb], [pt])
            lT = loT.next()
            fw.op("act", lambda e: e.activation(out=lT[:], in_=pv[:, 0:128], func=AF.Copy), [pt], [lT])
            res = []
            for d in range(2):
                outs = []
                for half in range(2):
                    po = self.bank.next()
                    fw.op("pe", lambda e: e.matmul(po[:], lhsT=ones1[:], rhs=brow[:, brow_i + d, half * 512:(half + 1) * 512], start=True, stop=False), [ones1, brow], [po])
                    fw.op("pe", lambda e: e.matmul(po[:], lhsT=lT[d * 64:(d + 1) * 64, :], rhs=w2[d * 64:(d + 1) * 64, half * 512:(half + 1) * 512], start=False, stop=True), [lT, w2], [po])
                    outs.append(po)
                res.append(outs)
            return res

        norm_group(0)
        import os
        for g in range(int(os.environ.get("L1A_NG", NG))):
            if g + 1 < NG:
                norm_group(g + 1)
            cur = win[g % 4]; prv = win[(g - 1) % 4]; nxt = win[(g + 1) % 4]
            for k in range(8):
                c = cur[:, k, :]; o = xxT[:, k, :]
                if g == 0:
                    if k < 4:
                        sub(o[:, 1:256], c[:, 0:255], c[:, 1:256]); neg(o[:, 0:1], c[:, 0:1])
                    else:
                        sub(o[:, 0:255], c[:, 1:256], c[:, 0:255]); neg(o[:, 255:256], c[:, 255:256])
                else:
                    c3 = c.rearrange("p (r c) -> p r c", c=64); o3 = o.rearrange("p (r c) -> p r c", c=64)
                    if k < 2:
                        sub(o3[:, :, 1:64], c3[:, :, 0:63], c3[:, :, 1:64]); neg(o3[:, :, 0:1], c3[:, :, 0:1])
                    elif k < 4:
                        sub(o3[:, :, 0:63], c3[:, :, 1:64], c3[:, :, 0:63]); neg(o3[:, :, 63:64], c3[:, :, 63:64])
                    elif k < 6:
                        sub(o[:, 64:256], c[:, 0:192], c[:, 64:256])
                        if g == 1:
                            neg(o[:, 0:64], c[:, 0:64])
                        else:
                            sub(o[:, 0:64], prv[:, k, 192:256], c[:, 0:64])
                    else:
                        sub(o[:, 0:192], c[:, 64:256], c[:, 0:192])
                        if g == NG - 1:
                            neg(o[:, 192:256], c[:, 192:256])
                        else:
                            sub(o[:, 192:256], nxt[:, k, 0:64], c[:, 192:256])
            xr = lerp(0, cur)
            for t in range(2):
                tile = 2 * g + t
                rr_ = tmpp.next()
                evac(rr_, proj_tok(xr, t, W["od_w_r"]))
                store("R", tile, rr_)
            xk = lerp(2, cur)
            ktiles = [ksb0, ksb1]
            for t in range(2):
                evac(ktiles[t], proj_tok(xk, t, W["od_w_k"]))
            xv = lerp(3, cur)
            for t in range(2):
                v_ = tmpp.next()
                evac(v_, proj_tok(xv, t, W["od_w_v"]))
                store("V", 2 * g + t, v_)
            xg = lerp(5, cur)
            for t in range(2):
                pb = self.bank.next()
                for k in range(8):
                    fw.op("pe", lambda e: e.matmul(pb[:, 0:128], lhsT=xg[:, k, t * 128:(t + 1) * 128], rhs=g1[:, k, :], start=(k == 0), stop=(k == 7)), [xg, g1], [pb])
                l_ = lo.next()
                fw.op("act", lambda e: e.activation(out=l_[:], in_=pb[:, 0:128], func=AF.Sigmoid), [pb], [l_])
                pt = self.bank.next(); pv = self.bview(pt)
                fw.op("pe", lambda e: e.transpose(out=pv[:, 0:128], in_=l_[:], identity=self.identb[:]), [l_, self.identb], [pt])
                lT = loT.next()
                fw.op("act", lambda e: e.activation(out=lT[:], in_=pv[:, 0:128], func=AF.Copy), [pt], [lT])
                g_ = tmpp.next()
                outs = []
                for half in range(2):
                    po = self.bank.next()
                    fw.op("pe", lambda e: e.matmul(po[:], lhsT=lT[:], rhs=g2[:, half * 512:(half + 1) * 512], start=True, stop=True), [lT, g2], [po])
                    outs.append(po)
                evac(g_, outs)
                store("G", 2 * g + t, g_)
            xw = lerp(1, cur)
            xa = lerp(4, cur)
            for t in range(2):
                tile = 2 * g + t
                ksb = ktiles[t]
                kkx = tmpp.next(); sq = tmpp.next()
                fw.op("pool", lambda e: e.tensor_tensor(out=kkx[:], in0=ksb[:], in1=kkb[:], op=ALU.mult), [ksb, kkb], [kkx])
                fw.op("pool", lambda e: e.tensor_tensor(out=sq[:], in0=kkx[:], in1=kkx[:], op=ALU.mult), [kkx], [sq])
                s16 = ss16.next()
                fw.op("dve", lambda e: e.reduce_sum(out=s16[:], in_=sq[:].rearrange("p (h j) -> p h j", j=64), axis=AX.X), [sq], [s16])
                fw.op("act", lambda e: e.activation(out=s16[:], in_=s16[:], func=AF.Sqrt, bias=1e-12, scale=1.0), [s16], [s16])
                fw.op("dve", lambda e: e.reciprocal(out=s16[:], in_=s16[:]), [s16], [s16])
                fw.op("dve", lambda e: e.tensor_tensor(out=kap[:].rearrange("p (h j) -> p h j", j=64), in0=kkx[:].rearrange("p (h j) -> p h j", j=64),
                                                       in1=s16[:].unsqueeze(2).to_broadcast([128, 16, 64]), op=ALU.mult), [kkx, s16], [kap])
                store("KAP", tile, kap)
                fw.op("pool", lambda e: e.tensor_tensor(out=kka[:], in0=ksb[:], in1=kab[:], op=ALU.mult), [ksb, kab], [kka])
                fw.op("pool", lambda e: e.tensor_tensor(out=kmk[:], in0=ksb[:], in1=kka[:], op=ALU.subtract), [ksb, kka], [kmk])
                wl = lora(xw, t, w1c, w2c, 0, AF.Tanh)
                for d in range(2):
                    lw = tmpp.next()
                    for half in range(2):
                        fw.op("act", lambda e: e.activation(out=lw[:, half * 512:(half + 1) * 512], in_=wl[d][half][:], func=AF.Sigmoid), [wl[d][half]], [lw])
                    fw.op("pool", lambda e: e.tensor_scalar(out=lw[:], in0=lw[:], scalar1=-0.6065306597126334, scalar2=None, op0=ALU.mult), [lw], [lw])
                    store("LW%d" % d, tile, lw)
                al = lora(xa, t, a1c, a2c, 2, AF.Copy)
                avs = []
                for d in range(2):
                    a_ = tmpp.next()
                    for half in range(2):
                        fw.op("act", lambda e: e.activation(out=a_[:, half * 512:(half + 1) * 512], in_=al[d][half][:], func=AF.Sigmoid), [al[d][half]], [a_])
                    avs.append(a_)
                kds = []
                for d in range(2):
                    a_ = avs[d]
                    b_ = tmpp.next()
                    fw.op("pool", lambda e: e.tensor_tensor(out=b_[:], in0=kap[:], in1=a_[:], op=ALU.mult), [kap, a_], [b_])
                    store("B%d" % d, tile, b_)
                    kd = kd0 if d == 0 else tmpp.next()
                    fw.op("pool", lambda e: e.tensor_tensor(out=kd[:], in0=kka[:], in1=a_[:], op=ALU.mult), [kka, a_], [kd])
                    fw.op("pool", lambda e: e.tensor_tensor(out=kd[:], in0=kd[:], in1=kmk[:], op=ALU.add), [kd, kmk], [kd])
                    store("KD%d" % d, tile, kd)
                    kds.append(kd)
                fw.dma("sp", r_s, r_s[:], S["R"], S["R"].ap()[tile * 128:(tile + 1) * 128, :])
                ks = tmpp.next()
                fw.op("pool", lambda e: e.tensor_tensor(out=ks[:], in0=kds[0][:], in1=kds[1][:], op=ALU.add), [kds[0], kds[1]], [ks])
                fw.op("pool", lambda e: e.tensor_tensor(out=ks[:], in0=ks[:], in1=rkb[:], op=ALU.mult), [ks, rkb], [ks])
                fw.op("pool", lambda e: e.tensor_tensor(out=ks[:], in0=ks[:], in1=r_s[:], op=ALU.mult), [ks, r_s], [ks])
                bon = ss16.next()
                fw.op("dve", lambda e: e.reduce_sum(out=bon[:], in_=ks[:].rearrange("p (h j) -> p h j", j=64), axis=AX.X), [ks], [bon])
                fw.dma("sp", S["BON"], S["BON"].ap()[tile * 128:(tile + 1) * 128, :], bon, bon[:])
        fw.pop()

    def l1_phaseB(self):
        fw = self.fw
        I = self.I
        S = self.S1
        k_ = "ExternalOutput" if self.dbg else "Internal"
        self.Y = [fw.dram("s1_Y%d" % d, [TOK, D], F32, kind=k_) for d in range(2)]
        fw.push()
        ybanks = self.banks[0:2]
        pbanks = Rot(self.banks[2:4])
        hb = []
        hbr = Rot([(self.banks[4 + i // 2], (i % 2) * 256) for i in range(8)])
        tm = fw.sb("tm", [128, 4, 128])
        fw.dma("sp", tm, tm[:], I["trimask"], I["trimask"].ap())
        ones = fw.sb("ones", [128, 128])
        fw.op("dve", lambda e: e.memset(ones[:], 1.0), [], [ones])
        MA = []; MB = []; MN = []; MC = []
        for d in range(2):
            mS, mI, mSn, mC = (0, 1, 2, 1) if d == 0 else (2, 3, 0, 3)
            a = fw.sb("MA", [128, 2, 128]); b = fw.sb("MB", [128, 2, 128]); n = fw.sb("MN", [128, 128])
            fw.op("dve", lambda e: e.tensor_scalar(out=a[:, 0, :], in0=tm[:, mS, :], scalar1=-1.0, scalar2=None, op0=ALU.mult), [tm], [a])
            fw.op("dve", lambda e: e.tensor_copy(out=a[:, 1, :], in_=tm[:, mI, :]), [tm], [a])
            fw.op("dve", lambda e: e.tensor_copy(out=b[:, 0, :], in_=tm[:, mS, :]), [tm], [b])
            fw.op("dve", lambda e: e.tensor_copy(out=b[:, 1, :], in_=tm[:, mI, :]), [tm], [b])
            fw.op("dve", lambda e: e.tensor_scalar(out=n[:], in0=tm[:, mSn, :], scalar1=-1.0, scalar2=None, op0=ALU.mult), [tm], [n])
            MA.append(a); MB.append(b); MN.append(n); MC.append(mC)
        H = fw.sb("H", [64, 16, 64])
        ld = {nm: Rot([fw.sb("ld" + nm, [128, D]) for _ in range(2)]) for nm in ("R", "KAP", "V", "LW", "B", "KD")}
        prep = {nm: fw.sb("pp" + nm, [128, D], BF16 if nm in ("rt", "kt", "bt", "kdt") else F32) for nm in ("rt", "kt", "bt", "kdt", "bh", "kh")}
        Vb = fw.sb("Vbf", [128, D], BF16); Hb = fw.sb("Hb", [64, 16, 64], BF16)
        cumS = fw.sb("cumS", [128, 512]); et = Rot([fw.sb("et", [128, 512]) for _ in range(3)])
        gC = fw.sb("gC", [64, 16])
        NH = 8
        TT = Rot([fw.sb("TT", [64, 4, 128], BF16) for _ in range(NH)])
        SC = Rot([fw.sb("SC", [128, 4, 128], BF16) for _ in range(NH)])
        NM = Rot([fw.sb("NM", [128, 2, 128], BF16) for _ in range(NH * 3)])
        XS = Rot([fw.sb("XS", [128, 64], BF16) for _ in range(NH * 3)])
        UF = Rot([fw.sb("UF", [128, 64]) for _ in range(NH)])
        ysb = Rot([fw.sb("ysb", [128, D]) for _ in range(2)])
        import os
        nchunks = int(os.environ.get("L1B_NC", NT))

        def head_gen(h, d, T_):
            R_, KAP_, V_, LW_, B_, KD_ = T_
            hs = slice(h * 64, (h + 1) * 64)
            tt = TT.next(); sc = SC.next()
            for pair, (x0, x1) in enumerate(((prep["kt"], prep["rt"]), (prep["bt"], prep["kdt"]))):
                bk, off = hbr.next()
                bv = self.bview(bk)
                fw.op("pe", lambda e: e.transpose(out=bv[0:64, 2 * off:2 * off + 128], in_=x0[:, hs], identity=self.identb[:]), [x0, self.identb], [bk])
                fw.op("pe", lambda e: e.transpose(out=bv[0:64, 2 * off + 128:2 * off + 256], in_=x1[:, hs], identity=self.identb[:]), [x1, self.identb], [bk])
                fw.op("act", lambda e: e.activation(out=tt[:, 2 * pair:2 * pair + 2, :].rearrange("p a b -> p (a b)"), in_=bv[0:64, 2 * off:2 * off + 256], func=AF.Copy), [bk], [tt])
            yield
            kT = tt[:, 0, :]; rT = tt[:, 1, :]; bT = tt[:, 2, :]; kdT = tt[:, 3, :]
            kr = tt[:, 0:2, :].rearrange("p a b -> p (a b)")
            bk, off = hbr.next()
            fw.op("pe", lambda e: e.matmul(bk[:, off:off + 256], lhsT=bT, rhs=kr, start=True, stop=True), [tt], [bk])
            fw.op("dve", lambda e: e.tensor_tensor(out=sc[:, 0:2, :].rearrange("p a b -> p (a b)"), in0=bk[:, off:off + 256], in1=MA[d][:].rearrange("p a b -> p (a b)"), op=ALU.mult), [bk, MA[d]], [sc])
            bk, off = hbr.next()
            fw.op("pe", lambda e: e.matmul(bk[:, off:off + 256], lhsT=kdT, rhs=kr, start=True, stop=True), [tt], [bk])
            fw.op("dve", lambda e: e.tensor_tensor(out=sc[:, 2:4, :].rearrange("p a b -> p (a b)"), in0=bk[:, off:off + 256], in1=MB[d][:].rearrange("p a b -> p (a b)"), op=ALU.mult), [bk, MB[d]], [sc])
            nm = NM.next()
            bk, off = hbr.next()
            fw.op("pe", lambda e: e.matmul(bk[:, off:off + 128], lhsT=kT, rhs=bT, start=True, stop=True), [tt], [bk])
            fw.op("dve", lambda e: e.tensor_tensor(out=nm[:, 0, :], in0=bk[:, off:off + 128], in1=MN[d][:], op=ALU.mult), [bk, MN[d]], [nm])
            fw.op("act", lambda e: e.activation(out=nm[:, 1, :], in_=sc[:, 0, :], func=AF.Copy), [sc], [nm])
            yield
            NT_ = sc[:, 0, :]; BrbT = sc[:, 1, :]; AkT = sc[:, 2, :]; BrkT = sc[:, 3, :]
            bk, off = hbr.next()
            fw.op("pe", lambda e: e.matmul(bk[:, off:off + 64], lhsT=kT, rhs=Hb[:, h, :], start=True, stop=False), [tt, Hb], [bk])
            fw.op("pe", lambda e: e.matmul(bk[:, off:off + 64], lhsT=AkT, rhs=Vb[:, hs], start=False, stop=True), [sc, Vb], [bk])
            X = XS.next()
            fw.op("act", lambda e: e.activation(out=X[:], in_=bk[:, off:off + 64], func=AF.Copy, scale=-1.0), [bk], [X])
            yield
            cur = nm
            for lvl in range(7):
                bk, off = hbr.next()
                fw.op("pe", lambda e: e.matmul(bk[:, off:off + 64], lhsT=cur[:, 1, :], rhs=X[:], start=True, stop=True), [cur, X], [bk])
                X2 = XS.next() if lvl < 6 else UF.next()
                fw.op("dve", lambda e: e.tensor_tensor(out=X2[:], in0=bk[:, off:off + 64], in1=X[:], op=ALU.add), [bk, X], [X2])
                X = X2
                if lvl < 6:
                    nxt = NM.next()
                    bk, off = hbr.next()
                    if lvl < 5:
                        fw.op("pe", lambda e: e.matmul(bk[:, off:off + 128], lhsT=cur[:, 1, :], rhs=cur[:, 0, :], start=True, stop=True), [cur], [bk])
                    fw.op("pe", lambda e: e.matmul(bk[:, off + 128:off + 256], lhsT=cur[:, 0, :], rhs=cur[:, 1, :], start=True, stop=True), [cur], [bk])
                    if lvl < 5:
                        fw.op("act", lambda e: e.activation(out=nxt[:].rearrange("p a b -> p (a b)"), in_=bk[:, off:off + 256], func=AF.Copy), [bk], [nxt])
                    else:
                        fw.op("act", lambda e: e.activation(out=nxt[:, 1, :], in_=bk[:, off + 128:off + 256], func=AF.Copy), [bk], [nxt])
                    cur = nxt
                yield
            U = X
            Ub = XS.next()
            fw.op("act", lambda e: e.activation(out=Ub[:], in_=U[:], func=AF.Copy), [U], [Ub])
            yb = ybanks[h // 8]; yo = (h % 8) * 64
            fw.op("pe", lambda e: e.matmul(yb[:, yo:yo + 64], lhsT=rT, rhs=Hb[:, h, :], start=True, stop=False), [tt, Hb], [yb])
            fw.op("pe", lambda e: e.matmul(yb[:, yo:yo + 64], lhsT=BrbT, rhs=Ub[:], start=False, stop=False), [sc, Ub], [yb])
            fw.op("pe", lambda e: e.matmul(yb[:, yo:yo + 64], lhsT=BrkT, rhs=Vb[:, hs], start=False, stop=True), [sc, Vb], [yb])
            bk, off = hbr.next()
            fw.op("pe", lambda e: e.matmul(bk[0:64, off:off + 64], lhsT=prep["bh"][:, hs], rhs=U[:], start=True, stop=False), [prep["bh"], U], [bk])
            fw.op("pe", lambda e: e.matmul(bk[0:64, off:off + 64], lhsT=prep["kh"][:, hs], rhs=V_[:, hs], start=False, stop=True), [prep["kh"], V_], [bk])
            fw.op("dve", lambda e: e.scalar_tensor_tensor(out=H[:, h, :], in0=H[:, h, :], scalar=gC[:, h:h + 1], in1=bk[0:64, off:off + 64], op0=ALU.mult, op1=ALU.add), [H, gC, bk], [H])
            yield

        for d in range(2):
            fw.op("dve", lambda e: e.memset(H[:], 0.0), [], [H])
            fw.op("dve", lambda e: e.memset(Hb[:], 0.0), [], [Hb])
            order = list(range(NT)) if d == 0 else [1, 0] + list(range(NT - 1, 1, -1))
            for c in order[:nchunks]:
                rows = slice(c * 128, (c + 1) * 128)
                T_ = []
                for nm, key in (("R", "R"), ("KAP", "KAP"), ("V", "V"), ("LW", "LW%d" % d), ("B", "B%d" % d), ("KD", "KD%d" % d)):
                    t_ = ld[nm].next()
                    fw.dma("sp", t_, t_[:], S[key], S[key].ap()[rows, :])
                    T_.append(t_)
                R_, KAP_, V_, LW_, B_, KD_ = T_
                fw.op("pool", lambda e: e.tensor_copy(out=Vb[:], in_=V_[:]), [V_], [Vb])
                bk, off = hbr.next()
                for h in range(16):
                    fw.op("pe", lambda e: e.matmul(bk[0:64, off + h:off + h + 1], lhsT=LW_[:, h * 64:(h + 1) * 64], rhs=ones[:, 0:1], start=True, stop=True), [LW_, ones], [bk])
                fw.op("act", lambda e: e.activation(out=gC[:], in_=bk[0:64, off:off + 16], func=AF.Exp), [bk], [gC])
                for half in range(2):
                    cs = slice(half * 512, (half + 1) * 512)
                    pc = pbanks.next(); ptot = pbanks.next()
                    fw.op("pe", lambda e: e.matmul(pc[:], lhsT=tm[:, MC[d], :], rhs=LW_[:, cs], start=True, stop=True), [tm, LW_], [pc])
                    fw.op("pe", lambda e: e.matmul(ptot[:], lhsT=ones[:], rhs=LW_[:, cs], start=True, stop=True), [ones, LW_], [ptot])
                    fw.op("act", lambda e: e.activation(out=cumS[:], in_=pc[:], func=AF.Copy), [pc], [cumS])
                    e1 = et.next()
                    fw.op("act", lambda e: e.activation(out=e1[:], in_=pc[:], func=AF.Exp), [pc], [e1])
                    fw.op("pool", lambda e: e.tensor_tensor(out=prep["rt"][:, cs], in0=R_[:, cs], in1=e1[:], op=ALU.mult), [R_, e1], [prep["rt"]])
                    e2 = et.next()
                    fw.op("act", lambda e: e.activation(out=e2[:], in_=pc[:], func=AF.Exp, scale=-1.0), [pc], [e2])
                    fw.op("pool", lambda e: e.tensor_tensor(out=prep["bt"][:, cs], in0=B_[:, cs], in1=e2[:], op=ALU.mult), [B_, e2], [prep["bt"]])
                    fw.op("pool", lambda e: e.tensor_tensor(out=prep["kdt"][:, cs], in0=KD_[:, cs], in1=e2[:], op=ALU.mult), [KD_, e2], [prep["kdt"]])
                    e3 = et.next()
                    fw.op("dve", lambda e: e.tensor_tensor(out=e3[:], in0=cumS[:], in1=LW_[:, cs], op=ALU.subtract), [cumS, LW_], [e3])
                    fw.op("act", lambda e: e.activation(out=e3[:], in_=e3[:], func=AF.Exp), [e3], [e3])
                    fw.op("pool", lambda e: e.tensor_tensor(out=prep["kt"][:, cs], in0=KAP_[:, cs], in1=e3[:], op=ALU.mult), [KAP_, e3], [prep["kt"]])
                    e4 = et.next()
                    fw.op("dve", lambda e: e.tensor_tensor(out=e4[:], in0=ptot[:], in1=cumS[:], op=ALU.subtract), [ptot, cumS], [e4])
                    fw.op("act", lambda e: e.activation(out=e4[:], in_=e4[:], func=AF.Exp), [e4], [e4])
                    fw.op("dve", lambda e: e.tensor_tensor(out=prep["bh"][:, cs], in0=B_[:, cs], in1=e4[:], op=ALU.mult), [B_, e4], [prep["bh"]])
                    fw.op("dve", lambda e: e.tensor_tensor(out=prep["kh"][:, cs], in0=KD_[:, cs], in1=e4[:], op=ALU.mult), [KD_, e4], [prep["kh"]])
                for h0 in range(0, 16, NH):
                    gens = [head_gen(h, d, T_) for h in range(h0, h0 + NH)]
                    alive = True
                    while alive:
                        alive = False
                        for gi in gens:
                            try:
                                next(gi)
                                alive = True
                            except StopIteration:
                                pass
                fw.op("act", lambda e: e.activation(out=Hb[:].rearrange("p a b -> p (a b)"), in_=H[:].rearrange("p a b -> p (a b)"), func=AF.Copy), [H], [Hb])
                ys = ysb.next()
                for half in range(2):
                    fw.op("act", lambda e: e.activation(out=ys[:, half * 512:(half + 1) * 512], in_=ybanks[half][:], func=AF.Copy), [ybanks[half]], [ys])
                fw.dma("act", self.Y[d], self.Y[d].ap()[rows, :], ys, ys[:])
        fw.pop()

    def l1_phaseC(self, hsrc, hdst):
        fw = self.fw
        I = self.I
        S = self.S1
        fw.push()
        self.bank = Rot(self.banks)
        gate = self.gate_bcast(1, 0)
        w_o = fw.sb("w_o", [128, 8, D], BF16)
        for k in range(8):
            fw.dma("pool", w_o, w_o[:, k, :], I["od_w_o"], I["od_w_o"].ap()[k * 128:(k + 1) * 128, :])
        gnw = fw.sb("gnw", [128, D]); gnb = fw.sb("gnb", [128, D])
        fw.dma("sp", gnw, gnw[:], I["od_gn_w"], I["od_gn_w"].ap().partition_broadcast(128))
        fw.dma("sp", gnb, gnb[:], I["od_gn_b"], I["od_gn_b"].ap().partition_broadcast(128))
        L = {nm: Rot([fw.sb("c" + nm, [128, D]) for _ in range(2)]) for nm in ("y0", "y1", "v", "g", "h")}
        bons = Rot([fw.sb("cbon", [128, 16]) for _ in range(2)])
        st = Rot([fw.sb("cst", [128, 16]) for _ in range(6)])
        wk = Rot([fw.sb("cwk", [128, D]) for _ in range(4)])
        obf = Rot([fw.sb("cob", [128, D], BF16) for _ in range(2)])
        oTs = Rot([fw.sb("coT", [128, 8, 128], BF16) for _ in range(2)])
        v3 = lambda t_: t_[:].rearrange("p (h j) -> p h j", j=64)
        b3 = lambda t_: t_[:].unsqueeze(2).to_broadcast([128, 16, 64])
        for tile in range(2, NT):
            rows = slice(tile * 128, (tile + 1) * 128)
            y0 = L["y0"].next(); y1 = L["y1"].next(); v = L["v"].next(); g = L["g"].next(); ht = L["h"].next(); bon = bons.next()
            fw.dma("sp", y0, y0[:], self.Y[0], self.Y[0].ap()[rows, :])
            fw.dma("sp", y1, y1[:], self.Y[1], self.Y[1].ap()[rows, :])
            fw.dma("sp", v, v[:], S["V"], S["V"].ap()[rows, :])
            fw.dma("sp", g, g[:], S["G"], S["G"].ap()[rows, :])
            fw.dma("sp", bon, bon[:], S["BON"], S["BON"].ap()[rows, :])
            src, sap = self.src_of(tile, hsrc)
            fw.dma("sp", ht, ht[:], src, sap)
            ysum = wk.next()
            fw.op("pool", lambda e: e.tensor_tensor(out=ysum[:], in0=y0[:], in1=y1[:], op=ALU.add), [y0, y1], [ysum])
            mean = st.next(); var = st.next()
            fw.op("dve", lambda e: e.reduce_sum(out=mean[:], in_=v3(ysum), axis=AX.X), [ysum], [mean])
            fw.op("dve", lambda e: e.tensor_scalar(out=mean[:], in0=mean[:], scalar1=1.0 / 64, scalar2=None, op0=ALU.mult), [mean], [mean])
            yc = wk.next()
            fw.op("dve", lambda e: e.tensor_tensor(out=v3(yc), in0=v3(ysum), in1=b3(mean), op=ALU.subtract), [ysum, mean], [yc])
            sq = wk.next()
            fw.op("pool", lambda e: e.tensor_tensor(out=sq[:], in0=yc[:], in1=yc[:], op=ALU.mult), [yc], [sq])
            fw.op("dve", lambda e: e.reduce_sum(out=var[:], in_=v3(sq), axis=AX.X), [sq], [var])
            fw.op("act", lambda e: e.activation(out=var[:], in_=var[:], func=AF.Sqrt, bias=64e-5, scale=1.0 / 64), [var], [var])
            fw.op("dve", lambda e: e.reciprocal(out=var[:], in_=var[:]), [var], [var])
            fw.op("dve", lambda e: e.tensor_tensor(out=v3(yc), in0=v3(yc), in1=b3(var), op=ALU.mult), [yc, var], [yc])
            fw.op("pool", lambda e: e.tensor_tensor(out=yc[:], in0=yc[:], in1=gnw[:], op=ALU.mult), [yc, gnw], [yc])
            fw.op("pool", lambda e: e.tensor_tensor(out=yc[:], in0=yc[:], in1=gnb[:], op=ALU.add), [yc, gnb], [yc])
            bv = wk.next()
            fw.op("dve", lambda e: e.tensor_tensor(out=v3(bv), in0=v3(v), in1=b3(bon), op=ALU.mult), [v, bon], [bv])
            fw.op("pool", lambda e: e.tensor_tensor(out=yc[:], in0=yc[:], in1=bv[:], op=ALU.add), [yc, bv], [yc])
            ob = obf.next()
            fw.op("dve", lambda e: e.tensor_tensor(out=ob[:], in0=yc[:], in1=g[:], op=ALU.mult), [yc, g], [ob])
            pb = self.bank.next(); pv = self.bview(pb)
            for k in range(8):
                fw.op("pe", lambda e: e.transpose(out=pv[:, k * 128:(k + 1) * 128], in_=ob[:, k * 128:(k + 1) * 128], identity=self.identb[:]), [ob, self.identb], [pb])
            oT = oTs.next()
            fw.op("act", lambda e: e.activation(out=oT[:].rearrange("p k n -> p (k n)"), in_=pv[:, 0:1024], func=AF.Copy), [pb], [oT])
            tmp = wk.next()
            for half in range(2):
                po = self.bank.next()
                for k in range(8):
                    fw.op("pe", lambda e: e.matmul(po[:], lhsT=oT[:, k, :], rhs=w_o[:, k, half * 512:(half + 1) * 512], start=(k == 0), stop=(k == 7)), [oT, w_o], [po])
                fw.op("dve", lambda e: e.tensor_tensor(out=tmp[:, half * 512:(half + 1) * 512], in0=po[:], in1=gate[0][:, half * 512:(half + 1) * 512], op=ALU.mult), [po, gate[0]], [tmp])
            fw.op("pool", lambda e: e.tensor_tensor(out=tmp[:], in0=tmp[:], in1=ht[:], op=ALU.add), [tmp, ht], [tmp])
            fw.dma("act", hdst, hdst.ap()[rows, :], tmp, tmp[:])
        fw.pop()

    def finish(self):
        fw = self.fw
        fw.barrier()
        fw.finish("sp", [self.out])
        print("ninst", fw.ninst, "nsem", fw.nsem)
        self.es.close()
        return self.nc


def host_consts():
    c = {}
    c["ident"] = np.eye(128, dtype=np.float32)
    t = np.arange(NLAT)
    row = (t // 64).astype(np.float32); col = (t % 64).astype(np.float32)
    inv = (10000.0 ** (-np.arange(0, 32, 2, dtype=np.float32) / 32)).astype(np.float32)
    ang = np.concatenate([row[:, None] * inv, col[:, None] * inv], axis=-1)
    cosl = np.cos(ang).astype(np.float32); sinl = np.sin(ang).astype(np.float32)
    cos = np.concatenate([np.ones((NCTX, 32), np.float32), cosl], 0)
    sin = np.concatenate([np.zeros((NCTX, 32), np.float32), sinl], 0)
    d = np.arange(128) % 64
    c["ropeC"] = np.ascontiguousarray(cos[:, d // 2].T)
    c["ropeS"] = np.ascontiguousarray(sin[:, d // 2].T)
    P = np.zeros((128, 128), np.float32)
    for i in range(64):
        P[2 * i + 1, 2 * i] = -1.0
        P[2 * i, 2 * i + 1] = 1.0
    c["ropeP"] = P
    b = np.zeros((128, 128), np.float32)
    b[:64, :64] = 1; b[64:, 64:] = 1
    c["blk64"] = b
    s_ = np.arange(128)[:, None]; t_ = np.arange(128)[None, :]
    tm = np.zeros((128, 4, 128), np.float32)
    tm[:, 0] = (s_ < t_); tm[:, 1] = (s_ <= t_); tm[:, 2] = (s_ > t_); tm[:, 3] = (s_ >= t_)
    c["trimask"] = tm
    return c


def shard_inputs(inputs):
    f = lambda a: np.ascontiguousarray(np.asarray(a, dtype=np.float32))
    w_in = f(inputs["ev_w_in"])[0]
    w_in_r = np.concatenate([w_in[:, 0:2048], w_in[:, 2048:2112], w_in[:, 2048:2112], w_in[:, 2112:2176], w_in[:, 2112:2176], w_in[:, 2176:2304]], axis=1)
    shared = dict(
        ada_w=f(inputs["ada_w"]), ada_b=f(inputs["ada_b"]), norm1_g=f(inputs["norm1_g"]), norm2_g=f(inputs["norm2_g"]),
        w_in=f(w_in_r), conv_w=f(inputs["ev_conv_w"])[0], qg2=f(np.tile(f(inputs["ev_q_gain"])[0], 2)), kg2=f(np.tile(f(inputs["ev_k_gain"])[0], 2)),
        w_out=f(inputs["ev_w_out"])[0], od_mu=f(inputs["od_mu"])[0], od_w_r=f(inputs["od_w_r"])[0], od_w_k=f(inputs["od_w_k"])[0],
        od_w_v=f(inputs["od_w_v"])[0], od_w_o=f(inputs["od_w_o"])[0], od_g1=f(inputs["od_g1"])[0], od_g2=f(inputs["od_g2"])[0],
        od_k_k=f(inputs["od_k_k"])[0], od_k_a=f(inputs["od_k_a"])[0], od_r_k=f(inputs["od_r_k"])[0].reshape(-1),
        od_w0=f(inputs["od_w0"])[0], od_w1=f(inputs["od_w1"])[0], od_w2=f(inputs["od_w2"])[0], od_a0=f(inputs["od_a0"])[0],
        od_a1=f(inputs["od_a1"])[0], od_a2=f(inputs["od_a2"])[0], od_gn_w=f(inputs["od_gn_w"])[0], od_gn_b=f(inputs["od_gn_b"])[0],
        peer_wq=f(inputs["peer_wq"]), peer_keys=f(inputs["peer_keys"]).reshape(2, 16, 128, 128), peer_u=f(inputs["peer_u"]), peer_v=f(inputs["peer_v"]),
    )
    shared.update(host_consts())
    x = f(inputs["x"]); ctx = f(inputs["ctx"]); c = f(inputs["c"]); cc = f(inputs["c_ctx"])
    maps = []
    for b in range(8):
        m = dict(shared)
        m["x"] = x[b]; m["ctx"] = ctx[b]; m["cvec"] = np.ascontiguousarray(np.stack([c[b], cc], 0))
        maps.append(m)
    return maps


def build(dbg=False, stop=None):
    p = Prog(dbg=dbg, stop=stop)
    p.phase0()
    if stop == "p0":
        return p
    if stop and stop.startswith("L1"):
        p.l1_phaseA(p.h2)
        if stop == "L1a":
            return p
        p.l1_phaseB()
        if stop == "L1b":
            return p
        p.l1_phaseC(p.h2, p.h3)
        return p
    p.l0_phaseA()
    if stop == "l0a":
        p.fw.pop()
        return p
    p.l0_phaseB()
    if stop == "l0":
        return p
    p.peer_prep(0)
    p.peer(0, p.h1, p.h2)
    if stop == "peer0":
        return p
    p.l1_phaseA(p.h2)
    if stop == "l1a":
        return p
    p.l1_phaseB()
    if stop == "l1b":
        return p
    p.l1_phaseC(p.h2, p.h3)
    if stop == "l1c":
        return p
    p.peer_prep(1)
    p.peer(1, p.h3, None, final=True)
    return p


def kernel(**inputs):
    p = build()
    nc = p.finish()
    maps = shard_inputs(inputs)
    maps = [{k: v for k, v in m.items() if k in p.I} for m in maps]
    res = run_bass_kernel_spmd(nc, maps, core_ids=list(range(8)))
    return np.stack([r["out"] for r in res.results], 0)
```

```python
import numpy as np
from contextlib import ExitStack
import concourse.bass as bass
import concourse.mybir as mybir
from concourse.bass_utils import run_bass_kernel_spmd

F32 = mybir.dt.float32
BF16 = mybir.dt.bfloat16
ALU = mybir.AluOpType
AF = mybir.ActivationFunctionType
AX = mybir.AxisListType

D = 1024
NCTX = 256
NLAT = 4096
TOK = NCTX + NLAT
NT = TOK // 128
TOKP = TOK + 4
EPS = 1e-6
INPUT_SHAPES = {
    "x": [NLAT, D],
    "ctx": [NCTX, D],
    "cvec": [2, D],
    "ada_w": [2, D, 6 * D],
    "ada_b": [2, 6 * D],
    "norm1_g": [2, D],
    "norm2_g": [2, D],
    "w_in": [D, 2432],
    "conv_w": [3, 512],
    "qg2": [128],
    "kg2": [128],
    "w_out": [D, D],
    "od_mu": [6, D],
    "od_w_r": [D, D],
    "od_w_k": [D, D],
    "od_w_v": [D, D],
    "od_w_o": [D, D],
    "od_g1": [D, 128],
    "od_g2": [128, D],
    "od_k_k": [D],
    "od_k_a": [D],
    "od_r_k": [D],
    "od_w0": [2, D],
    "od_w1": [2, D, 64],
    "od_w2": [2, 64, D],
    "od_a0": [2, D],
    "od_a1": [2, D, 64],
    "od_a2": [2, 64, D],
    "od_gn_w": [D],
    "od_gn_b": [D],
    "peer_wq": [2, D, 2048],
    "peer_keys": [2, 16, 128, 128],
    "peer_u": [2, 16384, D],
    "peer_v": [2, 16384, D],
    "ident": [128, 128],
    "ropeC": [128, TOK],
    "ropeS": [128, TOK],
    "ropeP": [128, 128],
    "blk64": [128, 128],
    "trimask": [128, 4, 128],
}


class T:
    __slots__ = ("t", "w", "r", "sem", "dcount", "name")

    def __init__(self, t, name):
        self.t = t
        self.name = name
        self.w = {}
        self.r = {}
        self.sem = None
        self.dcount = 0

    def __getitem__(self, k):
        return self.t[k]

    def ap(self):
        return self.t.ap()


class FW:
    def __init__(self, nc, es):
        self.nc = nc
        self.es = es
        self.root_es = es
        self.stack = []
        self.live = []
        self.drams = []
        self.sempool = []
        self.uid = 0
        self.eng = {"pe": nc.tensor, "act": nc.scalar, "dve": nc.vector, "pool": nc.gpsimd, "sp": nc.sync}
        self.esem = {}
        self.ecount = {}
        self.known = {}
        self.nsem = 0
        for k in self.eng:
            self.esem[k] = self.newsem("e_" + k)
            self.ecount[k] = 0
            self.known[k] = {}
        self.selfwait = {"pe": False, "act": True, "dve": True, "pool": True, "sp": False}
        self.ninst = 0

    def newsem(self, name):
        self.nsem += 1
        assert self.nsem < 200, "too many semaphores"
        return self.root_es.enter_context(self.nc.semaphore(name))

    def push(self):
        self.stack.append((self.es, self.live))
        self.es = ExitStack()
        self.live = []

    def pop(self):
        self.barrier()
        for t in self.live:
            if t.sem is not None:
                self.sempool.append((t.sem, t.dcount))
                t.sem = None
        self.es.close()
        self.es, self.live = self.stack.pop()

    def barrier(self):
        toks = {}
        for k in self.eng:
            if self.ecount[k] > 0:
                toks[self.esem[k]] = self.ecount[k]
        for _, live in self.stack + [(None, self.live)]:
            for t in live:
                if t.sem is not None and t.dcount > 0:
                    toks[t.sem] = 16 * t.dcount
        for t in self.drams:
            if t.sem is not None and t.dcount > 0:
                toks[t.sem] = 16 * t.dcount
        for e in self.eng:
            kn = self.known[e]
            for s_, v in toks.items():
                if s_ is self.esem[e]:
                    continue
                if kn.get(s_, 0) < v:
                    self.eng[e].wait_ge(s_, v)
                    kn[s_] = v
                    self.ninst += 1

    def sb(self, name, shape, dt=F32):
        self.uid += 1
        t = T(self.es.enter_context(self.nc.sbuf_tensor("%s_%d" % (name, self.uid), list(shape), dt)), name)
        self.live.append(t)
        return t

    def ps(self, name, shape, dt=F32):
        t = T(self.es.enter_context(self.nc.psum_tensor(name, list(shape), dt)), name)
        self.live.append(t)
        return t

    def dram(self, name, shape, dt=F32, kind="Internal"):
        t = T(self.nc.dram_tensor(name, list(shape), dt, kind=kind), name)
        self.drams.append(t)
        return t

    def _waits(self, e, reads, writes):
        toks = {}

        def add(tok):
            if tok is None:
                return
            s, v = tok
            if toks.get(s, 0) < v:
                toks[s] = v
        for r in reads:
            for s, v in r.w.items():
                add((s, v))
        for w in writes:
            for s, v in w.w.items():
                add((s, v))
            for s, v in w.r.items():
                add((s, v))
        eng = self.eng[e]
        kn = self.known[e]
        for s, v in toks.items():
            if s is self.esem[e] and not self.selfwait[e]:
                continue
            if kn.get(s, 0) < v:
                eng.wait_ge(s, v)
                kn[s] = v
                self.ninst += 1

    def op(self, e, fn, reads=(), writes=()):
        self._waits(e, reads, writes)
        inst = fn(self.eng[e])
        self.ecount[e] += 1
        self.ninst += 1
        s = self.esem[e]
        inst.then_inc(s, 1)
        tok = (s, self.ecount[e])
        for w in writes:
            w.w = {s: tok[1]}
            w.r = {}
        for r in reads:
            if r.r.get(s, 0) < tok[1]:
                r.r[s] = tok[1]
        return inst

    def dma(self, e, outT, out_ap, inT, in_ap, **kw):
        self._waits(e, [inT], [outT])
        own = inT if (outT in self.drams and inT not in self.drams) else outT
        if own.sem is None:
            if self.sempool:
                own.sem, own.dcount = self.sempool.pop()
            else:
                own.sem = self.newsem("d%d" % self.nsem)
        inst = self.eng[e].dma_start(out=out_ap, in_=in_ap, **kw)
        inst.then_inc(own.sem, 16)
        self.ninst += 1
        own.dcount += 1
        tok = (own.sem, 16 * own.dcount)
        if own is outT:
            outT.w = {tok[0]: tok[1]}
        else:
            outT.w[tok[0]] = tok[1]
        outT.r = {}
        if inT.r.get(tok[0], 0) < tok[1]:
            inT.r[tok[0]] = tok[1]
        return inst

    def finish(self, e, tiles):
        self._waits(e, tiles, [])


class Rot:
    def __init__(self, items):
        self.items = items
        self.i = -1

    def next(self):
        self.i = (self.i + 1) % len(self.items)
        return self.items[self.i]


def tokcol(tile):
    return 1 + tile * 128 if tile < 2 else 259 + (tile - 2) * 128


class Prog:
    def __init__(self, dbg=False, stop=None):
        self.dbg = dbg
        self.stop = stop
        self.nc = bass.Bass("TRN2", target_bir_lowering=False)
        self.es = ExitStack()
        self.fw = FW(self.nc, self.es)
        fw = self.fw

        prog = self

        class Lazy(dict):
            def __missing__(d, name):
                t = fw.dram(name, INPUT_SHAPES[name], F32, kind="ExternalInput")
                d[name] = t
                return t
        self.I = Lazy()
        self.out = fw.dram("out", [NLAT, D], F32, kind="ExternalOutput")
        k = "ExternalOutput" if dbg else "Internal"
        self.modrow = fw.dram("modrow", [2, 2, 6 * D], F32, kind=k)
        self.h1 = fw.dram("h1", [TOK, D], F32, kind=k)
        self.h2 = fw.dram("h2", [TOK, D], F32, kind=("ExternalInput" if stop and stop.startswith("L1") else k))
        self.h3 = fw.dram("h3", [TOK, D], F32, kind=k)
        self.uT = fw.dram("uT", [512, TOKP], BF16)
        if dbg:
            self.dza = fw.dram("dza", [TOK, 512], BF16, kind="ExternalOutput")
            self.dzc = fw.dram("dzc", [512, TOK], BF16, kind="ExternalOutput")
        self.gbT = fw.dram("gbT", [512, TOKP], BF16)
        self.UT = fw.dram("UT", [128, 128, 8, 128], BF16)
        self.Vb = fw.dram("Vb", [128, 128, D], BF16)
        self.banks = [fw.ps("bank%d" % i, [128, 512]) for i in range(8)]
        self.bank = Rot(self.banks)
        self.identf = fw.sb("identf", [128, 128])
        self.identb = fw.sb("identb", [128, 128], BF16)
        fw.dma("sp", self.identf, self.identf[:], self.I["ident"], self.I["ident"].ap())
        fw.dma("pool", self.identb, self.identb[:], self.I["ident"], self.I["ident"].ap())

    def bview(self, bank):
        return bank.ap().bitcast(BF16)

    def load_colT(self, dst, dst_ap, src, src_ap1d):
        self.fw.dma("sp", dst, dst_ap, src, src_ap1d.rearrange("(k p) -> p k", p=128), allow_slow_non_contiguous=True)

    def modvecs(self, layer, which):
        fw = self.fw
        I = self.I
        gname = "norm1_g" if which == 0 else "norm2_g"
        g = fw.sb("g", [128, 8])
        self.load_colT(g, g[:], I[gname], I[gname].ap()[layer])
        GT, shT = [], []
        for s in range(2):
            sc = fw.sb("scl", [128, 8]); sh = fw.sb("shf", [128, 8]); G = fw.sb("G", [128, 8])
            base = which * 3 * D
            self.load_colT(sh, sh[:], self.modrow, self.modrow.ap()[layer, s, base:base + D])
            self.load_colT(sc, sc[:], self.modrow, self.modrow.ap()[layer, s, base + D:base + 2 * D])
            fw.op("dve", lambda e: e.scalar_tensor_tensor(out=G[:], in0=sc[:], scalar=1.0, in1=g[:], op0=ALU.add, op1=ALU.mult), [sc, g], [G])
            GT.append(G); shT.append(sh)
        return GT, shT

    def gate_bcast(self, layer, which):
        fw = self.fw
        res = []
        for s in range(2):
            gt = fw.sb("gate", [128, D])
            off = which * 3 * D + 2 * D
            fw.dma("sp", gt, gt[:], self.modrow, self.modrow.ap()[layer, s, off:off + D].partition_broadcast(128))
            res.append(gt)
        return res

    def make_normer(self):
        fw = self.fw
        self.nb_h = Rot([fw.sb("nh", [128, D]) for _ in range(2)])
        self.nb_sq = fw.sb("nsq", [128, D], BF16)
        self.nb_ss = Rot([fw.sb("nss", [128, 1]) for _ in range(2)])
        self.nb_rs = Rot([fw.sb("nrs", [128, 1]) for _ in range(2)])
        self.nb_hn = Rot([fw.sb("nhn", [128, D], BF16) for _ in range(2)])

    def norm_tile(self, src, src_ap, G, sh, xnT, xnT_cols):
        fw = self.fw
        h = self.nb_h.next(); ss = self.nb_ss.next(); rs = self.nb_rs.next(); hn = self.nb_hn.next(); sq = self.nb_sq
        fw.dma("sp", h, h[:], src, src_ap)
        fw.op("act", lambda e: e.activation(out=sq[:], in_=h[:], func=AF.Square, scale=1.0 / 32, accum_out=ss[:]), [h], [sq, ss])
        fw.op("act", lambda e: e.activation(out=rs[:], in_=ss[:], func=AF.Sqrt, bias=EPS, scale=1.0), [ss], [rs])
        fw.op("dve", lambda e: e.reciprocal(out=rs[:], in_=rs[:]), [rs], [rs])
        fw.op("act", lambda e: e.activation(out=hn[:], in_=h[:], func=AF.Copy, scale=rs[:]), [h, rs], [hn])
        pb = self.bank.next()
        pv = self.bview(pb)
        for k in range(8):
            fw.op("pe", lambda e: e.transpose(out=pv[:, k * 128:(k + 1) * 128], in_=hn[:, k * 128:(k + 1) * 128], identity=self.identb[:]), [hn, self.identb], [pb])
        for k in range(8):
            fw.op("act", lambda e: e.activation(out=xnT[:, k, xnT_cols], in_=pv[:, k * 128:(k + 1) * 128], func=AF.Identity,
                                                scale=G[:, k:k + 1], bias=sh[:, k:k + 1]), [pb, G, sh], [xnT])
        return h

    def src_of(self, tile, hsrc):
        if hsrc is None:
            if tile < 2:
                return self.I["ctx"], self.I["ctx"].ap()[tile * 128:(tile + 1) * 128, :]
            return self.I["x"], self.I["x"].ap()[(tile - 2) * 128:(tile - 1) * 128, :]
        return hsrc, hsrc.ap()[tile * 128:(tile + 1) * 128, :]

    def phase0(self):
        fw = self.fw
        I = self.I
        fw.push()
        sc = fw.sb("sc", [128, 2, 8])
        for s_ in range(2):
            self.load_colT(sc, sc[:, s_, :], I["cvec"], I["cvec"].ap()[s_])
        fw.op("act", lambda e: e.activation(out=sc[:], in_=sc[:], func=AF.Silu), [sc], [sc])
        wb = Rot([fw.sb("adaw", [128, 8, 512]) for _ in range(2)])
        for i in range(2):
            bias = fw.sb("adab", [2, 6 * D])
            modr = fw.sb("modr", [2, 6 * D])
            fw.dma("sp", bias, bias[:], I["ada_b"], I["ada_b"].ap()[i].partition_broadcast(2))
            for cb in range(12):
                wt = wb.next()
                fw.dma("sp" if cb % 2 == 0 else "act", wt, wt[:], I["ada_w"], I["ada_w"].ap()[i, :, cb * 512:(cb + 1) * 512].rearrange("(k p) n -> p k n", p=128))
                pb = self.bank.next()
                for k in range(8):
                    fw.op("pe", lambda e: e.matmul(pb[0:2, :], lhsT=sc[:, :, k], rhs=wt[:, k, :], start=(k == 0), stop=(k == 7)), [sc, wt], [pb])
                fw.op("dve", lambda e: e.tensor_tensor(out=modr[:, cb * 512:(cb + 1) * 512], in0=pb[0:2, :], in1=bias[:, cb * 512:(cb + 1) * 512], op=ALU.add), [pb, bias], [modr])
            fw.dma("sp", self.modrow, self.modrow.ap()[i], modr, modr[:])
        fw.pop()

    def l0_phaseA(self):
        fw = self.fw
        I = self.I
        fw.push()
        self.qT = fw.sb("qT", [128, 4, TOK], BF16)
        self.kT2 = fw.sb("kT2", [128, 2, TOK], BF16)
        self.vext = fw.sb("vext", [128, NT, 2, 66], BF16)
        fw.op("pool", lambda e: e.memset(self.vext[:], 1.0), [], [self.vext])
        fw.push()
        GT, shT = self.modvecs(0, 0)
        self.make_normer()
        w_in = fw.sb("w_in", [128, 8, 2432], BF16)
        for k in range(8):
            fw.dma("pool", w_in, w_in[:, k, :], I["w_in"], I["w_in"].ap()[k * 128:(k + 1) * 128, :])
        rC = fw.sb("rC", [128, TOK], BF16); rS = fw.sb("rS", [128, TOK], BF16)
        fw.dma("pool", rC, rC[:], I["ropeC"], I["ropeC"].ap())
        fw.dma("pool", rS, rS[:], I["ropeS"], I["ropeS"].ap())
        ropeP = fw.sb("ropeP", [128, 128], BF16); blk64 = fw.sb("blk64", [128, 128], BF16)
        fw.dma("pool", ropeP, ropeP[:], I["ropeP"], I["ropeP"].ap())
        fw.dma("pool", blk64, blk64[:], I["blk64"], I["blk64"].ap())
        gains = fw.sb("gains", [128, 2])
        fw.dma("sp", gains, gains[:, 0:1], I["qg2"], I["qg2"].ap().rearrange("(p o) -> p o", o=1))
        fw.dma("sp", gains, gains[:, 1:2], I["kg2"], I["kg2"].ap().rearrange("(p o) -> p o", o=1))
        zt = fw.sb("zt", [128, 4, 2], BF16)
        fw.op("pool", lambda e: e.memset(zt[:], 0.0), [], [zt])
        for dst in (self.uT,):
            v = dst.ap().rearrange("(b p) c -> p b c", p=128)
            fw.dma("pool", dst, v[:, :, 0:1], zt, zt[:, :, 0:1], allow_slow_non_contiguous=True)
            fw.dma("pool", dst, v[:, :, 257:259], zt, zt[:, :, 0:2], allow_slow_non_contiguous=True)
            fw.dma("pool", dst, v[:, :, TOKP - 1:TOKP], zt, zt[:, :, 0:1], allow_slow_non_contiguous=True)
        xnTs = Rot([fw.sb("xnT", [128, 8, 256], BF16) for _ in range(2)])
        hTt = fw.sb("hTt", [128, 4, 256])
        gbst = Rot([fw.sb("gbst", [128, 4, 256], BF16) for _ in range(2)])
        ust = Rot([fw.sb("ust", [128, 4, 256], BF16) for _ in range(2)])
        qraw = Rot([fw.sb("qraw", [128, 256]) for _ in range(2)])
        sqb = Rot([fw.sb("sqb", [128, 256], BF16) for _ in range(2)])
        rsb = Rot([fw.sb("rsb", [128, 256]) for _ in range(2)])
        qnb = Rot([fw.sb("qnb", [128, 256], BF16) for _ in range(2)])
        t1b = Rot([fw.sb("t1b", [128, 256]) for _ in range(2)])
        t2b = Rot([fw.sb("t2b", [128, 256]) for _ in range(2)])
        uTv = self.uT.ap().rearrange("(b p) c -> p b c", p=128)
        gbTv = self.gbT.ap().rearrange("(b p) c -> p b c", p=128)
        import os
        skip = os.environ.get('L0A_SKIP', '').split(',')
        for g in range(int(os.environ.get('L0A_NG', NT // 2))):
            s = 1 if g == 0 else 0
            xnT = xnTs.next()
            for j in range(2):
                tile = 2 * g + j
                src, sap = self.src_of(tile, None)
                self.norm_tile(src, sap, GT[s], shT[s], xnT, slice(j * 128, (j + 1) * 128))
            tok0 = g * 256
            gb_s = gbst.next(); u_s = ust.next()
            for fb in range(18):
                if fb >= 12 and 'qk' in skip:
                    continue
                if fb < 12 and 'conv' in skip:
                    continue
                pb = self.bank.next()
                for k in range(8):
                    fw.op("pe", lambda e: e.matmul(pb[:, 0:256], lhsT=w_in[:, k, fb * 128:(fb + 1) * 128], rhs=xnT[:, k, :], start=(k == 0), stop=(k == 7)), [w_in, xnT], [pb])
                if fb < 4:
                    fw.op("act", lambda e: e.activation(out=hTt[:, fb, :], in_=pb[:, 0:256], func=AF.Copy), [pb], [hTt])
                elif fb < 8:
                    fw.op("act", lambda e: e.activation(out=gb_s[:, fb - 4, :], in_=pb[:, 0:256], func=AF.Copy), [pb], [gb_s])
                elif fb < 12:
                    fw.op("dve", lambda e: e.tensor_tensor(out=u_s[:, fb - 8, :], in0=pb[:, 0:256], in1=hTt[:, fb - 8, :], op=ALU.mult), [pb, hTt], [u_s])
                else:
                    isq = fb < 16
                    qr = qraw.next(); sq = sqb.next(); rs = rsb.next(); qn = qnb.next(); t1 = t1b.next(); t2 = t2b.next()
                    gcol = gains[:, 0:1] if isq else gains[:, 1:2]
                    fw.op("act", lambda e: e.activation(out=qr[:], in_=pb[:, 0:256], func=AF.Copy), [pb], [qr])
                    fw.op("act", lambda e: e.activation(out=sq[:], in_=pb[:, 0:256], func=AF.Square), [pb], [sq])
                    p2 = self.bank.next()
                    fw.op("pe", lambda e: e.matmul(p2[:, 0:256], lhsT=blk64[:], rhs=sq[:], start=True, stop=True), [blk64, sq], [p2])
                    fw.op("act", lambda e: e.activation(out=rs[:], in_=p2[:, 0:256], func=AF.Sqrt, bias=EPS, scale=1.0 / 64), [p2], [rs])
                    fw.op("dve", lambda e: e.reciprocal(out=rs[:], in_=rs[:]), [rs], [rs])
                    fw.op("dve", lambda e: e.scalar_tensor_tensor(out=qn[:], in0=qr[:], scalar=gcol, in1=rs[:], op0=ALU.mult, op1=ALU.mult), [qr, gains, rs], [qn])
                    p3 = self.bank.next()
                    fw.op("pe", lambda e: e.matmul(p3[:, 0:256], lhsT=ropeP[:], rhs=qn[:], start=True, stop=True), [ropeP, qn], [p3])
                    fw.op("dve", lambda e: e.tensor_tensor(out=t2[:], in0=p3[:, 0:256], in1=rS[:, tok0:tok0 + 256], op=ALU.mult), [p3, rS], [t2])
                    fw.op("pool", lambda e: e.tensor_tensor(out=t1[:], in0=qn[:], in1=rC[:, tok0:tok0 + 256], op=ALU.mult), [qn, rC], [t1])
                    dst = self.qT if isq else self.kT2
                    bi = fb - 12 if isq else fb - 16
                    fw.op("dve", lambda e: e.tensor_tensor(out=dst[:, bi, tok0:tok0 + 256], in0=t1[:], in1=t2[:], op=ALU.add), [t1, t2], [dst])
            for j in range(2 if 'v' not in skip else 0):
                tile = 2 * g + j
                pb = self.bank.next()
                for k in range(8):
                    fw.op("pe", lambda e: e.matmul(pb[:, 0:128], lhsT=xnT[:, k, j * 128:(j + 1) * 128], rhs=w_in[:, k, 2304:2432], start=(k == 0), stop=(k == 7)), [xnT, w_in], [pb])
                fw.op("act", lambda e: e.activation(out=self.vext[:, tile, :, 0:64], in_=pb[:, 0:128].rearrange("p (a d) -> p a d", a=2), func=AF.Copy), [pb], [self.vext])
            for j in range(2 if 'store' not in skip else 0):
                c0 = tokcol(2 * g + j)
                fw.dma("sp", self.uT, uTv[:, :, c0:c0 + 128], u_s, u_s[:, :, j * 128:(j + 1) * 128])
                fw.dma("sp", self.gbT, gbTv[:, :, c0:c0 + 128], gb_s, gb_s[:, :, j * 128:(j + 1) * 128])
        fw.pop()

    def l0_phaseB(self):
        fw = self.fw
        I = self.I
        fw.push()
        gate = self.gate_bcast(0, 0)
        w_out = fw.sb("w_out", [128, 8, D], BF16)
        for k in range(8):
            fw.dma("pool", w_out, w_out[:, k, :], I["w_out"], I["w_out"].ap()[k * 128:(k + 1) * 128, :])
        cw = fw.sb("cw", [128, 3, 4])
        for j_ in range(3):
            self.load_colT(cw, cw[:, j_, :], I["conv_w"], I["conv_w"].ap()[j_])
        PTs = Rot([fw.sb("PT", [128, NT, 512], BF16) for _ in range(2)])
        zatt = Rot([fw.sb("zatt", [128, 512], BF16) for _ in range(2)])
        rcb = Rot([fw.sb("rcb", [128, 4]) for _ in range(2)])
        zaTs = Rot([fw.sb("zaT", [128, 4, 128], BF16) for _ in range(2)])
        zcTs = Rot([fw.sb("zcT", [128, 4, 128], BF16) for _ in range(2)])
        uhs = Rot([fw.sb("uh", [128, 4, 130], BF16) for _ in range(2)])
        gbts = Rot([fw.sb("gbt", [128, 4, 128], BF16) for _ in range(2)])
        accs = Rot([fw.sb("acc", [128, 128]) for _ in range(3)])
        hts = Rot([fw.sb("ht", [128, D]) for _ in range(2)])
        tmps = Rot([fw.sb("tmp", [128, D]) for _ in range(2)])
        uTv = self.uT.ap().rearrange("(b p) c -> p b c", p=128)
        gbTv = self.gbT.ap().rearrange("(b p) c -> p b c", p=128)
        obanks = Rot(self.banks[0:2]); sbanks = Rot(self.banks[2:6]); self.bank = Rot(self.banks[6:8])
        import os
        skip = os.environ.get('L0B_SKIP', '').split(',')
        for qt in range(int(os.environ.get('L0B_NT', NT))):
            s = 1 if qt < 2 else 0
            nch = 2 if qt < 2 else NT
            ht = hts.next()
            src, sap = self.src_of(qt, None)
            fw.dma("sp", ht, ht[:], src, sap)
            uh = uhs.next(); gbt = gbts.next()
            c0 = tokcol(qt)
            fw.dma("sp", uh, uh[:], self.uT, uTv[:, :, c0 - 1:c0 + 129])
            fw.dma("sp", gbt, gbt[:], self.gbT, gbTv[:, :, c0:c0 + 128])
            za = zatt.next()
            for kv in range(2 if 'attn' not in skip else 0):
                PTall = PTs.next()
                for sc in range(nch):
                    sbs = [sbanks.next(), sbanks.next()]
                    for b in range(2):
                        for hf in range(2):
                            fw.op("pe", lambda e: e.matmul(sbs[hf][:, b * 128:(b + 1) * 128], lhsT=self.kT2[hf * 64:(hf + 1) * 64, kv, sc * 128:(sc + 1) * 128],
                                                            rhs=self.qT[hf * 64:(hf + 1) * 64, 2 * kv + b, qt * 128:(qt + 1) * 128], start=True, stop=True), [self.kT2, self.qT], [sbs[hf]])
                    PT4 = PTall[:, sc, :].rearrange("p (b h n) -> p b h n", b=2, h=2)
                    for hf in range(2):
                        fw.op("act", lambda e: e.activation(out=PT4[:, :, hf, :], in_=sbs[hf][:, 0:256].rearrange("p (b n) -> p b n", b=2), func=AF.Exp, scale=0.125), [sbs[hf]], [PTall])
                ob = obanks.next()
                O = ob[:, 0:260].rearrange("p (c e) -> p c e", e=65)
                for c in range(4):
                    for sc in range(nch):
                        fw.op("pe", lambda e: e.matmul(O[:, c, :], lhsT=PTall[:, sc, c * 128:(c + 1) * 128], rhs=self.vext[:, sc, kv, 0:65], start=(sc == 0), stop=(sc == nch - 1)), [PTall, self.vext], [ob])
                rc = rcb.next()
                fw.op("dve", lambda e: e.reciprocal(out=rc[:], in_=O[:, :, 64]), [ob], [rc])
                fw.op("dve", lambda e: e.tensor_tensor(out=za[:, kv * 256:(kv + 1) * 256].rearrange("p (c d) -> p c d", c=4), in0=O[:, :, 0:64],
                                                       in1=rc[:].unsqueeze(2).to_broadcast([128, 4, 64]), op=ALU.mult), [ob, rc], [za])
            pb = self.bank.next(); pv = self.bview(pb)
            for k in range(4):
                fw.op("pe", lambda e: e.transpose(out=pv[:, k * 128:(k + 1) * 128], in_=za[:, k * 128:(k + 1) * 128], identity=self.identb[:]), [za, self.identb], [pb])
            zaT = zaTs.next()
            fw.op("act", lambda e: e.activation(out=zaT[:].rearrange("p k n -> p (k n)"), in_=pv[:, 0:512], func=AF.Copy), [pb], [zaT])
            zcT = zcTs.next()
            for b in range(4 if 'conv' not in skip else 0):
                acc = accs.next()
                fw.op("pool", lambda e: e.tensor_scalar(out=acc[:], in0=uh[:, b, 0:128], scalar1=cw[:, 0, b:b + 1], scalar2=None, op0=ALU.mult), [uh, cw], [acc])
                for j_ in (1, 2):
                    t_ = accs.next()
                    fw.op("pool", lambda e: e.tensor_scalar(out=t_[:], in0=uh[:, b, j_:j_ + 128], scalar1=cw[:, j_, b:b + 1], scalar2=None, op0=ALU.mult), [uh, cw], [t_])
                    fw.op("pool", lambda e: e.tensor_tensor(out=acc[:], in0=acc[:], in1=t_[:], op=ALU.add), [acc, t_], [acc])
                fw.op("pool", lambda e: e.tensor_tensor(out=zcT[:, b, :], in0=acc[:], in1=gbt[:, b, :], op=ALU.mult), [acc, gbt], [zcT])
            if self.dbg:
                fw.dma("sp", self.dza, self.dza.ap()[qt * 128:(qt + 1) * 128, :], za, za[:])
                fw.dma("sp", self.dzc, self.dzc.ap().rearrange("(b p) c -> p b c", p=128)[:, :, qt * 128:(qt + 1) * 128], zcT, zcT[:])
            tmp = tmps.next()
            for half in range(2):
                pb = self.bank.next()
                for k in range(8):
                    lt = zcT[:, k, :] if k < 4 else zaT[:, k - 4, :]
                    fw.op("pe", lambda e: e.matmul(pb[:], lhsT=lt, rhs=w_out[:, k, half * 512:(half + 1) * 512], start=(k == 0), stop=(k == 7)), [zcT, zaT, w_out], [pb])
                fw.op("dve", lambda e: e.tensor_tensor(out=tmp[:, half * 512:(half + 1) * 512], in0=pb[:], in1=gate[s][:, half * 512:(half + 1) * 512], op=ALU.mult), [pb, gate[s]], [tmp])
            fw.op("pool", lambda e: e.tensor_tensor(out=tmp[:], in0=tmp[:], in1=ht[:], op=ALU.add), [tmp, ht], [tmp])
            fw.dma("act", self.h1, self.h1.ap()[qt * 128:(qt + 1) * 128, :], tmp, tmp[:])
        self.bank = Rot(self.banks)
        fw.pop()
        fw.pop()

    def peer_prep(self, layer):
        fw = self.fw
        I = self.I
        fw.push()
        banks = Rot(self.banks)
        Uv = I["peer_u"].ap()[layer].rearrange("(i j) d -> j i d", j=128)
        Vv = I["peer_v"].ap()[layer].rearrange("(i j) d -> j i d", j=128)
        UTv = self.UT.ap().rearrange("i p k j -> p i (k j)")
        Vbv = self.Vb.ap().rearrange("i j d -> j i d")
        ubs = Rot([fw.sb("ub", [128, 4, D], BF16) for _ in range(2)])
        vbs = Rot([fw.sb("vb", [128, 4, D], BF16) for _ in range(2)])
        uts = Rot([fw.sb("ut", [128, 4, 1024], BF16) for _ in range(2)])
        for c in range(32):
            ub = ubs.next(); vb = vbs.next(); ut = uts.next()
            fw.dma("pool", ub, ub[:], I["peer_u"], Uv[:, c * 4:(c + 1) * 4, :])
            fw.dma("pool", vb, vb[:], I["peer_v"], Vv[:, c * 4:(c + 1) * 4, :])
            fw.dma("sp", self.Vb, Vbv[:, c * 4:(c + 1) * 4, :], vb, vb[:])
            for ii in range(4):
                pb = banks.next(); pv = self.bview(pb)
                for k in range(8):
                    fw.op("pe", lambda e: e.transpose(out=pv[:, k * 128:(k + 1) * 128], in_=ub[:, ii, k * 128:(k + 1) * 128], identity=self.identb[:]), [ub, self.identb], [pb])
                if ii % 2 == 0:
                    fw.op("act", lambda e: e.activation(out=ut[:, ii, :], in_=pv[:, 0:1024], func=AF.Copy), [pb], [ut])
                else:
                    fw.op("dve", lambda e: e.tensor_copy(out=ut[:, ii, :], in_=pv[:, 0:1024]), [pb], [ut])
            fw.dma("sp", self.UT, UTv[:, c * 4:(c + 1) * 4, :], ut, ut[:])
        fw.pop()

    def peer(self, layer, hsrc, hdst, final=False):
        fw = self.fw
        I = self.I
        fw.push()
        GT, shT = self.modvecs(layer, 1)
        gate = self.gate_bcast(layer, 1)
        self.make_normer()
        obanks = self.banks[0:4]
        self.bank = Rot(self.banks[4:8])
        w_q = fw.sb("w_q", [128, 8, 2048], BF16)
        for k in range(8):
            fw.dma("pool", w_q, w_q[:, k, :], I["peer_wq"], I["peer_wq"].ap()[layer, k * 128:(k + 1) * 128, :])
        keyn = fw.sb("keyn", [128, 16, 128], BF16)
        fw.dma("pool", keyn, keyn[:], I["peer_keys"], I["peer_keys"].ap()[layer].rearrange("h k d -> k h d"))
        keysT = fw.sb("keysT", [128, 16, 128], BF16)
        for c in range(2):
            pb = self.bank.next(); pv = self.bview(pb)
            for q in range(8):
                fw.op("pe", lambda e: e.transpose(out=pv[:, q * 128:(q + 1) * 128], in_=keyn[:, c * 8 + q, :], identity=self.identb[:]), [keyn, self.identb], [pb])
            fw.op("act", lambda e: e.activation(out=keysT[:, c * 8:(c + 1) * 8, :].rearrange("p a b -> p (a b)"), in_=pv[:, 0:1024], func=AF.Copy), [pb], [keysT])
        xn2T = fw.sb("xn2T", [128, 8, 256], BF16)
        qT = fw.sb("pqT", [128, 16, 256], BF16)
        s_all = fw.sb("s_all", [128, 2, 16, 128])
        tau = fw.sb("tau", [128, 2, 8]); bias = fw.sb("pbias", [128, 2, 8])
        SB = [(fw.sb("m0", [128, 16]), fw.sb("m1", [128, 16]), fw.sb("scr", [128, 128]), fw.sb("cand", [128, 16, 16]), fw.sb("c24", [128, 24]),
               fw.sb("cscr", [128, 256]), fw.sb("st1", [128, 4]), fw.sb("e16", [128, 16])) for _ in range(4)]
        IB = 16
        Ws = Rot([fw.sb("W", [128, 2, IB, 128], BF16) for _ in range(2)])
        sums = Rot([fw.sb("sum", [128, IB, 128]) for _ in range(3)])
        Ps = Rot([fw.sb("P", [128, IB, 128], BF16) for _ in range(3)])
        Whs = Rot([fw.sb("Wh", [128, IB, 128], BF16) for _ in range(2)])
        Mks = Rot([fw.sb("Mk", [128, IB, 128], BF16) for _ in range(2)])
        UTs = Rot([fw.sb("UTs", [128, 2, 1024], BF16) for _ in range(2)])
        Vs = Rot([fw.sb("Vs", [128, 2, D], BF16) for _ in range(2)])
        Gs = Rot([fw.sb("Gs", [128, 256], BF16) for _ in range(4)])
        WAs = Rot([fw.sb("WAs", [128, 256], BF16) for _ in range(4)])
        tmps = Rot([fw.sb("ptmp", [128, D]) for _ in range(1)])
        hts = Rot([fw.sb("pht", [128, D]) for _ in range(2)])
        UTv = self.UT.ap().rearrange("i p k j -> p i (k j)")
        Vbv = self.Vb.ap().rearrange("i j d -> j i d")
        import os
        ng = int(os.environ.get("PEER_NG", NT // 2))
        for g in range(ng):
            if final and g == 0:
                continue
            s = 1 if g == 0 else 0
            for j in range(2):
                src, sap = self.src_of(2 * g + j, hsrc)
                self.norm_tile(src, sap, GT[s], shT[s], xn2T, slice(j * 128, (j + 1) * 128))
            for hp in range(16):
                pb = self.bank.next()
                for k in range(8):
                    fw.op("pe", lambda e: e.matmul(pb[:, 0:256], lhsT=w_q[:, k, hp * 128:(hp + 1) * 128], rhs=xn2T[:, k, :], start=(k == 0), stop=(k == 7)), [w_q, xn2T], [pb])
                fw.op("act", lambda e: e.activation(out=qT[:, hp, :], in_=pb[:, 0:256], func=AF.Copy), [pb], [qT])
            for t in range(2):
                for c in range(4):
                    pb = self.bank.next()
                    for q in range(4):
                        hp = c * 4 + q
                        fw.op("pe", lambda e: e.matmul(pb[:, q * 128:(q + 1) * 128], lhsT=qT[:, hp, t * 128:(t + 1) * 128], rhs=keysT[:, hp, :], start=True, stop=True), [qT, keysT], [pb])
                    fw.op("act", lambda e: e.activation(out=s_all[:, t, c * 4:(c + 1) * 4, :].rearrange("p a b -> p (a b)"), in_=pb[:], func=AF.Copy), [pb], [s_all])
            def stats_gen(t, h, B):
                m0, m1, scr, cand, c24, cscr, st1, e16 = B
                for (mm, src) in ((m0, s_all[:, t, 2 * h, :]), (m1, s_all[:, t, 2 * h + 1, :])):
                    fw.op("dve", lambda e: e.max(out=mm[:, 0:8], in_=src), [s_all], [mm]); yield
                    fw.op("dve", lambda e: e.match_replace(out=scr[:], in_to_replace=mm[:, 0:8], in_values=src, imm_value=-1e30), [s_all, mm], [scr]); yield
                    fw.op("dve", lambda e: e.max(out=mm[:, 8:16], in_=scr[:]), [scr], [mm]); yield
                fw.op("dve", lambda e: e.tensor_tensor(out=cand[:], in0=m0[:].unsqueeze(2).to_broadcast([128, 16, 16]), in1=m1[:].unsqueeze(1).to_broadcast([128, 16, 16]), op=ALU.add), [m0, m1], [cand]); yield
                cf = cand[:].rearrange("p a b -> p (a b)")
                fw.op("dve", lambda e: e.max(out=c24[:, 0:8], in_=cf), [cand], [c24]); yield
                fw.op("dve", lambda e: e.match_replace(out=cscr[:], in_to_replace=c24[:, 0:8], in_values=cf, imm_value=-1e30), [cand, c24], [cscr]); yield
                fw.op("dve", lambda e: e.max(out=c24[:, 8:16], in_=cscr[:]), [cscr], [c24]); yield
                fw.op("dve", lambda e: e.match_replace(out=cscr[:], in_to_replace=c24[:, 8:16], in_values=cscr[:], imm_value=-1e30), [cscr, c24], [cscr]); yield
                fw.op("dve", lambda e: e.max(out=c24[:, 16:24], in_=cscr[:]), [cscr], [c24]); yield
                fw.op("dve", lambda e: e.tensor_scalar(out=st1[:, 0:1], in0=c24[:, 16:17], scalar1=0.5, scalar2=None, op0=ALU.mult), [c24], [st1]); yield
                fw.op("dve", lambda e: e.scalar_tensor_tensor(out=tau[:, t, h:h + 1], in0=c24[:, 15:16], scalar=0.5, in1=st1[:, 0:1], op0=ALU.mult, op1=ALU.add), [c24, st1], [tau]); yield
                fw.op("dve", lambda e: e.tensor_scalar(out=st1[:, 1:2], in0=c24[:, 0:1], scalar1=-1.0, scalar2=None, op0=ALU.mult), [c24], [st1]); yield
                fw.op("act", lambda e: e.activation(out=e16[:], in_=c24[:, 0:16], func=AF.Exp, bias=st1[:, 1:2], scale=1.0, accum_out=st1[:, 2:3]), [c24, st1], [e16, st1]); yield
                fw.op("act", lambda e: e.activation(out=st1[:, 3:4], in_=st1[:, 2:3], func=AF.Ln), [st1], [st1]); yield
                fw.op("dve", lambda e: e.tensor_tensor(out=bias[:, t, h:h + 1], in0=st1[:, 1:2], in1=st1[:, 3:4], op=ALU.subtract), [st1], [bias]); yield

            for h0 in range(0, 8, 2):
                gens = [stats_gen(t, h0 + dh, SB[t * 2 + dh]) for t in range(2) for dh in range(2)]
                alive = True
                while alive:
                    alive = False
                    for gi in gens:
                        try:
                            next(gi); alive = True
                        except StopIteration:
                            pass
            NB = 128 // IB
            Wl = [None] * NB

            def build_unit(ib, u):
                i0 = ib * IB
                if u == 0:
                    Wl[ib] = Ws.next()
                W = Wl[ib]
                t, h = u // 8, u % 8
                sm = sums.next(); P = Ps.next()
                fw.op("dve", lambda e: e.tensor_tensor(out=sm[:], in0=s_all[:, t, 2 * h, i0:i0 + IB].unsqueeze(2).to_broadcast([128, IB, 128]),
                                                       in1=s_all[:, t, 2 * h + 1, :].unsqueeze(1).to_broadcast([128, IB, 128]), op=ALU.add), [s_all], [sm])
                def emitB():
                    fw.op("act", lambda e: e.activation(out=P[:], in_=sm[:], func=AF.Exp, bias=bias[:, t, h:h + 1], scale=1.0), [sm, bias], [P])

                def fin():
                    mk = Mks.next()
                    fw.op("dve", lambda e: e.tensor_scalar(out=mk[:], in0=sm[:], scalar1=tau[:, t, h:h + 1], scalar2=None, op0=ALU.is_ge), [sm, tau], [mk])
                    if h == 0:
                        fw.op("dve", lambda e: e.tensor_tensor(out=W[:, t, :, :], in0=mk[:], in1=P[:], op=ALU.mult), [mk, P], [W])
                    else:
                        Wh = Whs.next()
                        fw.op("dve", lambda e: e.tensor_tensor(out=Wh[:], in0=mk[:], in1=P[:], op=ALU.mult), [mk, P], [Wh])
                        eng = "pool" if h % 2 == 1 else "dve"
                        fw.op(eng, lambda e: e.tensor_tensor(out=W[:, t, :, :], in0=W[:, t, :, :], in1=Wh[:], op=ALU.add), [W, Wh], [W])
                return emitB, fin

            cur_ld = {}
            carry = {}

            def front(i):
                ib, iw = i // IB, i % IB
                W = Wl[ib]
                if i % 2 == 0:
                    UTt = UTs.next(); Vt = Vs.next()
                    fw.dma("sp", UTt, UTt[:], self.UT, UTv[:, i:i + 2, :])
                    fw.dma("sp", Vt, Vt[:], self.Vb, Vbv[:, i:i + 2, :])
                    cur_ld["u"] = UTt; cur_ld["v"] = Vt
                UTt = cur_ld["u"]; Vt = cur_ld["v"]
                i2 = i % 2
                pa = self.bank.next()
                for k in range(8):
                    fw.op("pe", lambda e: e.matmul(pa[:, 0:256], lhsT=UTt[:, i2, k * 128:(k + 1) * 128], rhs=xn2T[:, k, :], start=(k == 0), stop=(k == 7)), [UTt, xn2T], [pa])
                G = Gs.next()
                fw.op("act", lambda e: e.activation(out=G[:], in_=pa[:, 0:256], func=AF.Gelu), [pa], [G])
                pw = self.bank.next(); pwv = self.bview(pw)
                for t in range(2):
                    fw.op("pe", lambda e: e.transpose(out=pwv[:, t * 128:(t + 1) * 128], in_=W[:, t, iw, :], identity=self.identb[:]), [W, self.identb], [pw])
                WA = WAs.next()
                fw.op("dve", lambda e: e.tensor_tensor(out=WA[:], in0=G[:], in1=pwv[:, 0:256], op=ALU.mult), [G, pw], [WA])

                def back():
                    for t in range(2):
                        for half in range(2):
                            ob = obanks[t * 2 + half]
                            fw.op("pe", lambda e: e.matmul(ob[:], lhsT=WA[:, t * 128:(t + 1) * 128], rhs=Vt[:, i2, half * 512:(half + 1) * 512], start=(i == 0), stop=(i == 127)), [WA, Vt], [ob])
                return back

            pend = None
            for u in range(16):
                eb, f = build_unit(0, u)
                eb()
                if pend is not None:
                    pend()
                pend = f
            pend()
            backs = {0: front(0), 1: front(1)}
            sched = {0: [0, 1], 1: [2, 3]}
            for j_ in range(2, 14):
                sched[j_] = [j_ + 2]
            for i in range(128):
                ib, j = i // IB, i % IB
                units = sched.get(j, []) if ib + 1 < NB else []
                ebs = []; fins = []
                for u in units:
                    eb, f = build_unit(ib + 1, u)
                    ebs.append(eb); fins.append(f)
                if i + 2 < 128:
                    backs[i + 2] = front(i + 2)
                for eb in ebs:
                    eb()
                cp = carry.pop("f", None)
                if cp is not None:
                    cp()
                for f in fins[:-1]:
                    f()
                if fins:
                    if j == 13:
                        fins[-1]()
                    else:
                        carry["f"] = fins[-1]
                backs.pop(i)()
            for t in range(2):
                tile = 2 * g + t
                tmp = tmps.next(); ht = hts.next()
                src, sap = self.src_of(tile, hsrc)
                fw.dma("sp", ht, ht[:], src, sap)
                for half in range(2):
                    ob = obanks[t * 2 + half]
                    fw.op("dve", lambda e: e.tensor_tensor(out=tmp[:, half * 512:(half + 1) * 512], in0=ob[:], in1=gate[s][:, half * 512:(half + 1) * 512], op=ALU.mult), [ob, gate[s]], [tmp])
                fw.op("pool", lambda e: e.tensor_tensor(out=tmp[:], in0=tmp[:], in1=ht[:], op=ALU.add), [tmp, ht], [tmp])
                if final:
                    fw.dma("act", self.out, self.out.ap()[(tile - 2) * 128:(tile - 1) * 128, :], tmp, tmp[:])
                else:
                    fw.dma("act", hdst, hdst.ap()[tile * 128:(tile + 1) * 128, :], tmp, tmp[:])
        self.bank = Rot(self.banks)
        fw.pop()

    def l1_phaseA(self, hsrc):
        fw = self.fw
        I = self.I
        S = self.S1 = {}
        for nm in ("R", "KAP", "V", "G", "LW0", "LW1", "B0", "B1", "KD0", "KD1"):
            S[nm] = fw.dram("s1_" + nm, [TOK, D], F32, kind=("ExternalOutput" if self.dbg else "Internal"))
        S["BON"] = fw.dram("s1_BON", [TOK, 16], F32, kind=("ExternalOutput" if self.dbg else "Internal"))
        fw.push()
        GT, shT = self.modvecs(1, 0)
        self.make_normer()
        self.bank = Rot(self.banks)
        W = {}
        for nm in ("od_w_r", "od_w_k", "od_w_v"):
            W[nm] = fw.sb(nm, [128, 8, D], BF16)
            for k in range(8):
                fw.dma("pool", W[nm], W[nm][:, k, :], I[nm], I[nm].ap()[k * 128:(k + 1) * 128, :])
        g1 = fw.sb("g1", [128, 8, 128], BF16); g2 = fw.sb("g2", [128, D], BF16)
        fw.dma("pool", g1, g1[:], I["od_g1"], I["od_g1"].ap().rearrange("(k p) n -> p k n", p=128))
        fw.dma("pool", g2, g2[:], I["od_g2"], I["od_g2"].ap())
        w1c = fw.sb("w1c", [128, 8, 128], BF16); a1c = fw.sb("a1c", [128, 8, 128], BF16)
        for d in range(2):
            fw.dma("pool", w1c, w1c[:, :, d * 64:(d + 1) * 64], I["od_w1"], I["od_w1"].ap()[d].rearrange("(k p) n -> p k n", p=128))
            fw.dma("pool", a1c, a1c[:, :, d * 64:(d + 1) * 64], I["od_a1"], I["od_a1"].ap()[d].rearrange("(k p) n -> p k n", p=128))
        w2c = fw.sb("w2c", [128, D], BF16); a2c = fw.sb("a2c", [128, D], BF16)
        fw.dma("pool", w2c, w2c[:], I["od_w2"], I["od_w2"].ap().rearrange("d l n -> (d l) n"))
        fw.dma("pool", a2c, a2c[:], I["od_a2"], I["od_a2"].ap().rearrange("d l n -> (d l) n"))
        brow = fw.sb("brow", [1, 4, D], BF16)
        fw.dma("pool", brow, brow[:, 0:2, :], I["od_w0"], I["od_w0"].ap().rearrange("(o d) n -> o d n", o=1))
        fw.dma("pool", brow, brow[:, 2:4, :], I["od_a0"], I["od_a0"].ap().rearrange("(o d) n -> o d n", o=1))
        ones1 = fw.sb("ones1", [1, 128], BF16)
        fw.op("dve", lambda e: e.memset(ones1[:], 1.0), [], [ones1])
        kkb = fw.sb("kkb", [128, D]); kab = fw.sb("kab", [128, D]); rkb = fw.sb("rkb", [128, D])
        for t_, nm in ((kkb, "od_k_k"), (kab, "od_k_a"), (rkb, "od_r_k")):
            fw.dma("sp", t_, t_[:], I[nm], I[nm].ap().partition_broadcast(128))
        muT = fw.sb("muT", [128, 6, 8])
        for m in range(6):
            self.load_colT(muT, muT[:, m, :], I["od_mu"], I["od_mu"].ap()[m])
        NG = NT // 2
        win = [fw.sb("xnw", [128, 8, 256], BF16) for _ in range(4)]
        xxT = fw.sb("xxT", [128, 8, 256], BF16)
        xms = Rot([fw.sb("xm", [128, 8, 256], BF16) for _ in range(2)])
        tmpp = Rot([fw.sb("l1t", [128, D]) for _ in range(8)])
        ksb0 = fw.sb("ksb0", [128, D]); ksb1 = fw.sb("ksb1", [128, D])
        kap = fw.sb("kap", [128, D]); kka = fw.sb("kka", [128, D]); kmk = fw.sb("kmk", [128, D]); r_s = fw.sb("r_s", [128, D]); kd0 = fw.sb("kd0", [128, D])
        ss16 = Rot([fw.sb("ss16", [128, 16]) for _ in range(2)])
        lo = Rot([fw.sb("lo", [128, 128], BF16) for _ in range(3)])
        loT = Rot([fw.sb("loT", [128, 128], BF16) for _ in range(3)])

        def norm_group(g):
            s = 1 if g == 0 else 0
            for j in range(2):
                src, sap = self.src_of(2 * g + j, hsrc)
                self.norm_tile(src, sap, GT[s], shT[s], win[g % 4], slice(j * 128, (j + 1) * 128))

        def sub(out, a, b):
            fw.op("pool", lambda e: e.tensor_tensor(out=out, in0=a, in1=b, op=ALU.subtract), [win[0], win[1], win[2], win[3]], [xxT])

        def neg(out, a):
            fw.op("pool", lambda e: e.tensor_scalar(out=out, in0=a, scalar1=-1.0, scalar2=None, op0=ALU.mult), [win[0], win[1], win[2], win[3]], [xxT])

        def proj_tok(xm, t, Wt, nb=2):
            outs = []
            for half in range(nb):
                pb = self.bank.next()
                for k in range(8):
                    fw.op("pe", lambda e: e.matmul(pb[:], lhsT=xm[:, k, t * 128:(t + 1) * 128], rhs=Wt[:, k, half * 512:(half + 1) * 512], start=(k == 0), stop=(k == 7)), [xm, Wt], [pb])
                outs.append(pb)
            return outs

        def evac(dst, pbs):
            for half, pb in enumerate(pbs):
                fw.op("act", lambda e: e.activation(out=dst[:, half * 512:(half + 1) * 512], in_=pb[:], func=AF.Copy), [pb], [dst])

        def store(nm, tile, src, ap=None):
            fw.dma("sp", S[nm], S[nm].ap()[tile * 128:(tile + 1) * 128, :], src, src[:] if ap is None else ap)

        def lerp(m, cur):
            xm = xms.next()
            for k in range(8):
                fw.op("dve", lambda e: e.scalar_tensor_tensor(out=xm[:, k, :], in0=xxT[:, k, :], scalar=muT[:, m, k:k + 1], in1=cur[:, k, :], op0=ALU.mult, op1=ALU.add), [xxT, muT, cur], [xm])
            return xm

        def lora(xm, t, w1, w2, brow_i, func):
            pb = self.bank.next()
            for k in range(8):
                fw.op("pe", lambda e: e.matmul(pb[:, 0:128], lhsT=xm[:, k, t * 128:(t + 1) * 128], rhs=w1[:, k, :], start=(k == 0), stop=(k == 7)), [xm, w1], [pb])
            l_ = lo.next()
            fw.op("act", lambda e: e.activation(out=l_[:], in_=pb[:, 0:128], func=func), [pb], [l_])
            pt = self.bank.next(); pv = self.bview(pt)
            fw.op("pe", lambda e: e.transpose(out=pv[:, 0:128], in_=l_[:], identity=self.identb[:]), [l_, self.identb], [pt])
            lT = loT.next()
            fw.op("act", lambda e: e.activation(out=lT[:], in_=pv[:, 0:128], func=AF.Copy), [pt], [lT])
            res = []
            for d in range(2):
                outs = []
                for half in range(2):
                    po = self.bank.next()
                    fw.op("pe", lambda e: e.matmul(po[:], lhsT=ones1[:], rhs=brow[:, brow_i + d, half * 512:(half + 1) * 512], start=True, stop=False), [ones1, brow], [po])
                    fw.op("pe", lambda e: e.matmul(po[:], lhsT=lT[d * 64:(d + 1) * 64, :], rhs=w2[d * 64:(d + 1) * 64, half * 512:(half + 1) * 512], start=False, stop=True), [lT, w2], [po])
                    outs.append(po)
                res.append(outs)
            return res

        norm_group(0)
        import os
        for g in range(int(os.environ.get("L1A_NG", NG))):
            if g + 1 < NG:
                norm_group(g + 1)
            cur = win[g % 4]; prv = win[(g - 1) % 4]; nxt = win[(g + 1) % 4]
            for k in range(8):
                c = cur[:, k, :]; o = xxT[:, k, :]
                if g == 0:
                    if k < 4:
                        sub(o[:, 1:256], c[:, 0:255], c[:, 1:256]); neg(o[:, 0:1], c[:, 0:1])
                    else:
                        sub(o[:, 0:255], c[:, 1:256], c[:, 0:255]); neg(o[:, 255:256], c[:, 255:256])
                else:
                    c3 = c.rearrange("p (r c) -> p r c", c=64); o3 = o.rearrange("p (r c) -> p r c", c=64)
                    if k < 2:
                        sub(o3[:, :, 1:64], c3[:, :, 0:63], c3[:, :, 1:64]); neg(o3[:, :, 0:1], c3[:, :, 0:1])
                    elif k < 4:
                        sub(o3[:, :, 0:63], c3[:, :, 1:64], c3[:, :, 0:63]); neg(o3[:, :, 63:64], c3[:, :, 63:64])
                    elif k < 6:
                        sub(o[:, 64:256], c[:, 0:192], c[:, 64:256])
                        if g == 1:
                            neg(o[:, 0:64], c[:, 0:64])
                        else:
                            sub(o[:, 0:64], prv[:, k, 192:256], c[:, 0:64])
                    else:
                        sub(o[:, 0:192], c[:, 64:256], c[:, 0:192])
                        if g == NG - 1:
                            neg(o[:, 192:256], c[:, 192:256])
                        else:
                            sub(o[:, 192:256], nxt[:, k, 0:64], c[:, 192:256])
            xr = lerp(0, cur)
            for t in range(2):
                tile = 2 * g + t
                rr_ = tmpp.next()
                evac(rr_, proj_tok(xr, t, W["od_w_r"]))
                store("R", tile, rr_)
            xk = lerp(2, cur)
            ktiles = [ksb0, ksb1]
            for t in range(2):
                evac(ktiles[t], proj_tok(xk, t, W["od_w_k"]))
            xv = lerp(3, cur)
            for t in range(2):
                v_ = tmpp.next()
                evac(v_, proj_tok(xv, t, W["od_w_v"]))
                store("V", 2 * g + t, v_)
            xg = lerp(5, cur)
            for t in range(2):
                pb = self.bank.next()
                for k in range(8):
                    fw.op("pe", lambda e: e.matmul(pb[:, 0:128], lhsT=xg[:, k, t * 128:(t + 1) * 128], rhs=g1[:, k, :], start=(k == 0), stop=(k == 7)), [xg, g1], [pb])
                l_ = lo.next()
                fw.op("act", lambda e: e.activation(out=l_[:], in_=pb[:, 0:128], func=AF.Sigmoid), [pb], [l_])
                pt = self.bank.next(); pv = self.bview(pt)
                fw.op("pe", lambda e: e.transpose(out=pv[:, 0:128], in_=l_[:], identity=self.identb[:]), [l_, self.identb], [pt])
                lT = loT.next()
                fw.op("act", lambda e: e.activation(out=lT[:], in_=pv[:, 0:128], func=AF.Copy), [pt], [lT])
                g_ = tmpp.next()
                outs = []
                for half in range(2):
                    po = self.bank.next()
                    fw.op("pe", lambda e: e.matmul(po[:], lhsT=lT[:], rhs=g2[:, half * 512:(half + 1) * 512], start=True, stop=True), [lT, g2], [po])
                    outs.append(po)
                evac(g_, outs)
                store("G", 2 * g + t, g_)
            xw = lerp(1, cur)
            xa = lerp(4, cur)
            for t in range(2):
                tile = 2 * g + t
                ksb = ktiles[t]
                kkx = tmpp.next(); sq = tmpp.next()
                fw.op("pool", lambda e: e.tensor_tensor(out=kkx[:], in0=ksb[:], in1=kkb[:], op=ALU.mult), [ksb, kkb], [kkx])
                fw.op("pool", lambda e: e.tensor_tensor(out=sq[:], in0=kkx[:], in1=kkx[:], op=ALU.mult), [kkx], [sq])
                s16 = ss16.next()
                fw.op("dve", lambda e: e.reduce_sum(out=s16[:], in_=sq[:].rearrange("p (h j) -> p h j", j=64), axis=AX.X), [sq], [s16])
                fw.op("act", lambda e: e.activation(out=s16[:], in_=s16[:], func=AF.Sqrt, bias=1e-12, scale=1.0), [s16], [s16])
                fw.op("dve", lambda e: e.reciprocal(out=s16[:], in_=s16[:]), [s16], [s16])
                fw.op("dve", lambda e: e.tensor_tensor(out=kap[:].rearrange("p (h j) -> p h j", j=64), in0=kkx[:].rearrange("p (h j) -> p h j", j=64),
                                                       in1=s16[:].unsqueeze(2).to_broadcast([128, 16, 64]), op=ALU.mult), [kkx, s16], [kap])
                store("KAP", tile, kap)
                fw.op("pool", lambda e: e.tensor_tensor(out=kka[:], in0=ksb[:], in1=kab[:], op=ALU.mult), [ksb, kab], [kka])
                fw.op("pool", lambda e: e.tensor_tensor(out=kmk[:], in0=ksb[:], in1=kka[:], op=ALU.subtract), [ksb, kka], [kmk])
                wl = lora(xw, t, w1c, w2c, 0, AF.Tanh)
                for d in range(2):
                    lw = tmpp.next()
                    for half in range(2):
                        fw.op("act", lambda e: e.activation(out=lw[:, half * 512:(half + 1) * 512], in_=wl[d][half][:], func=AF.Sigmoid), [wl[d][half]], [lw])
                    fw.op("pool", lambda e: e.tensor_scalar(out=lw[:], in0=lw[:], scalar1=-0.6065306597126334, scalar2=None, op0=ALU.mult), [lw], [lw])
                    store("LW%d" % d, tile, lw)
                al = lora(xa, t, a1c, a2c, 2, AF.Copy)
                avs = []
                for d in range(2):
                    a_ = tmpp.next()
                    for half in range(2):
                        fw.op("act", lambda e: e.activation(out=a_[:, half * 512:(half + 1) * 512], in_=al[d][half][:], func=AF.Sigmoid), [al[d][half]], [a_])
                    avs.append(a_)
                kds = []
                for d in range(2):
                    a_ = avs[d]
                    b_ = tmpp.next()
                    fw.op("pool", lambda e: e.tensor_tensor(out=b_[:], in0=kap[:], in1=a_[:], op=ALU.mult), [kap, a_], [b_])
                    store("B%d" % d, tile, b_)
                    kd = kd0 if d == 0 else tmpp.next()
                    fw.op("pool", lambda e: e.tensor_tensor(out=kd[:], in0=kka[:], in1=a_[:], op=ALU.mult), [kka, a_], [kd])
                    fw.op("pool", lambda e: e.tensor_tensor(out=kd[:], in0=kd[:], in1=kmk[:], op=ALU.add), [kd, kmk], [kd])
                    store("KD%d" % d, tile, kd)
                    kds.append(kd)
                fw.dma("sp", r_s, r_s[:], S["R"], S["R"].ap()[tile * 128:(tile + 1) * 128, :])
                ks = tmpp.next()
                fw.op("pool", lambda e: e.tensor_tensor(out=ks[:], in0=kds[0][:], in1=kds[1][:], op=ALU.add), [kds[0], kds[1]], [ks])
                fw.op("pool", lambda e: e.tensor_tensor(out=ks[:], in0=ks[:], in1=rkb[:], op=ALU.mult), [ks, rkb], [ks])
                fw.op("pool", lambda e: e.tensor_tensor(out=ks[:], in0=ks[:], in1=r_s[:], op=ALU.mult), [ks, r_s], [ks])
                bon = ss16.next()
                fw.op("dve", lambda e: e.reduce_sum(out=bon[:], in_=ks[:].rearrange("p (h j) -> p h j", j=64), axis=AX.X), [ks], [bon])
                fw.dma("sp", S["BON"], S["BON"].ap()[tile * 128:(tile + 1) * 128, :], bon, bon[:])
        fw.pop()

    def l1_phaseB(self):
        fw = self.fw
        I = self.I
        S = self.S1
        k_ = "ExternalOutput" if self.dbg else "Internal"
        self.Y = [fw.dram("s1_Y%d" % d, [TOK, D], F32, kind=k_) for d in range(2)]
        fw.push()
        ybanks = self.banks[0:2]
        pbanks = Rot(self.banks[2:8])
        hb = []
        hbr = Rot([(self.banks[2 + (i % 6)], (i // 6) * 256) for i in range(12)])
        sml = hbr
        tm = fw.sb("tm", [128, 4, 128])
        fw.dma("sp", tm, tm[:], I["trimask"], I["trimask"].ap())
        ones = fw.sb("ones", [128, 128])
        fw.op("dve", lambda e: e.memset(ones[:], 1.0), [], [ones])
        MA = []; MB = []; MN = []; MC = []
        for d in range(2):
            mS, mI, mSn, mC = (0, 1, 2, 1) if d == 0 else (2, 3, 0, 3)
            a = fw.sb("MA", [128, 2, 128]); b = fw.sb("MB", [128, 2, 128]); n = fw.sb("MN", [128, 128])
            fw.op("dve", lambda e: e.tensor_scalar(out=a[:, 0, :], in0=tm[:, mS, :], scalar1=-1.0, scalar2=None, op0=ALU.mult), [tm], [a])
            fw.op("dve", lambda e: e.tensor_copy(out=a[:, 1, :], in_=tm[:, mI, :]), [tm], [a])
            fw.op("dve", lambda e: e.tensor_copy(out=b[:, 0, :], in_=tm[:, mS, :]), [tm], [b])
            fw.op("dve", lambda e: e.tensor_copy(out=b[:, 1, :], in_=tm[:, mI, :]), [tm], [b])
            fw.op("dve", lambda e: e.tensor_scalar(out=n[:], in0=tm[:, mSn, :], scalar1=-1.0, scalar2=None, op0=ALU.mult), [tm], [n])
            MA.append(a); MB.append(b); MN.append(n); MC.append(mC)
        H = fw.sb("H", [64, 16, 64])
        ld = {nm: Rot([fw.sb("ld" + nm, [128, D]) for _ in range(2)]) for nm in ("R", "KAP", "V", "LW", "B", "KD")}
        prep = {nm: fw.sb("pp" + nm, [128, D], BF16 if nm in ("rt", "kt", "bt", "kdt") else F32) for nm in ("rt", "kt", "bt", "kdt", "bh", "kh")}
        Vb = fw.sb("Vbf", [128, D], BF16); Hb = fw.sb("Hb", [64, 16, 64], BF16)
        cumS = fw.sb("cumS", [128, 512]); et = Rot([fw.sb("et", [128, 512]) for _ in range(3)])
        gC = fw.sb("gC", [64, 16])
        NH = 16
        TT = Rot([fw.sb("TT", [64, 4, 128], BF16) for _ in range(NH)])
        SC = Rot([fw.sb("SC", [128, 4, 128], BF16) for _ in range(NH)])
        NM = Rot([fw.sb("NM", [128, 2, 128], BF16) for _ in range(NH * 3)])
        XS = Rot([fw.sb("XS", [128, 64], BF16) for _ in range(NH * 3)])
        UF = Rot([fw.sb("UF", [128, 64]) for _ in range(NH)])
        ysb = Rot([fw.sb("ysb", [128, D]) for _ in range(2)])
        import os
        nchunks = int(os.environ.get("L1B_NC", NT))

        def head_gen(h, d, T_):
            R_, KAP_, V_, LW_, B_, KD_ = T_
            hs = slice(h * 64, (h + 1) * 64)
            tt = TT.next(); sc = SC.next()
            for pair, (x0, x1) in enumerate(((prep["kt"], prep["rt"]), (prep["bt"], prep["kdt"]))):
                bk, off = hbr.next()
                bv = self.bview(bk)
                fw.op("pe", lambda e: e.transpose(out=bv[0:64, 2 * off:2 * off + 128], in_=x0[:, hs], identity=self.identb[:]), [x0, self.identb], [bk])
                fw.op("pe", lambda e: e.transpose(out=bv[0:64, 2 * off + 128:2 * off + 256], in_=x1[:, hs], identity=self.identb[:]), [x1, self.identb], [bk])
                fw.op("act", lambda e: e.activation(out=tt[:, 2 * pair:2 * pair + 2, :].rearrange("p a b -> p (a b)"), in_=bv[0:64, 2 * off:2 * off + 256], func=AF.Copy), [bk], [tt])
            yield
            kT = tt[:, 0, :]; rT = tt[:, 1, :]; bT = tt[:, 2, :]; kdT = tt[:, 3, :]
            kr = tt[:, 0:2, :].rearrange("p a b -> p (a b)")
            bk, off = hbr.next()
            fw.op("pe", lambda e: e.matmul(bk[:, off:off + 256], lhsT=bT, rhs=kr, start=True, stop=True), [tt], [bk])
            fw.op("dve", lambda e: e.tensor_tensor(out=sc[:, 0:2, :].rearrange("p a b -> p (a b)"), in0=bk[:, off:off + 256], in1=MA[d][:].rearrange("p a b -> p (a b)"), op=ALU.mult), [bk, MA[d]], [sc])
            bk, off = hbr.next()
            fw.op("pe", lambda e: e.matmul(bk[:, off:off + 256], lhsT=kdT, rhs=kr, start=True, stop=True), [tt], [bk])
            fw.op("dve", lambda e: e.tensor_tensor(out=sc[:, 2:4, :].rearrange("p a b -> p (a b)"), in0=bk[:, off:off + 256], in1=MB[d][:].rearrange("p a b -> p (a b)"), op=ALU.mult), [bk, MB[d]], [sc])
            nm = NM.next()
            bk, off = hbr.next()
            fw.op("pe", lambda e: e.matmul(bk[:, off:off + 128], lhsT=kT, rhs=bT, start=True, stop=True), [tt], [bk])
            fw.op("dve", lambda e: e.tensor_tensor(out=nm[:, 0, :], in0=bk[:, off:off + 128], in1=MN[d][:], op=ALU.mult), [bk, MN[d]], [nm])
            fw.op("act", lambda e: e.activation(out=nm[:, 1, :], in_=sc[:, 0, :], func=AF.Copy), [sc], [nm])
            yield
            NT_ = sc[:, 0, :]; BrbT = sc[:, 1, :]; AkT = sc[:, 2, :]; BrkT = sc[:, 3, :]
            bk, off = sml.next()
            fw.op("pe", lambda e: e.matmul(bk[:, off:off + 64], lhsT=kT, rhs=Hb[:, h, :], start=True, stop=False), [tt, Hb], [bk])
            fw.op("pe", lambda e: e.matmul(bk[:, off:off + 64], lhsT=AkT, rhs=Vb[:, hs], start=False, stop=True), [sc, Vb], [bk])
            X = XS.next()
            fw.op("act", lambda e: e.activation(out=X[:], in_=bk[:, off:off + 64], func=AF.Copy, scale=-1.0), [bk], [X])
            yield
            cur = nm
            for lvl in range(7):
                bk, off = sml.next()
                fw.op("pe", lambda e: e.matmul(bk[:, off:off + 64], lhsT=cur[:, 1, :], rhs=X[:], start=True, stop=True), [cur, X], [bk])
                X2 = XS.next() if lvl < 6 else UF.next()
                fw.op("dve", lambda e: e.tensor_tensor(out=X2[:], in0=bk[:, off:off + 64], in1=X[:], op=ALU.add), [bk, X], [X2])
                X = X2
                if lvl < 6:
                    nxt = NM.next()
                    bk, off = hbr.next()
                    if lvl < 5:
                        fw.op("pe", lambda e: e.matmul(bk[:, off:off + 128], lhsT=cur[:, 1, :], rhs=cur[:, 0, :], start=True, stop=True), [cur], [bk])
                    fw.op("pe", lambda e: e.matmul(bk[:, off + 128:off + 256], lhsT=cur[:, 0, :], rhs=cur[:, 1, :], start=True, stop=True), [cur], [bk])
                    if lvl < 5:
                        fw.op("act", lambda e: e.activation(out=nxt[:].rearrange("p a b -> p (a b)"), in_=bk[:, off:off + 256], func=AF.Copy), [bk], [nxt])
                    else:
                        fw.op("act", lambda e: e.activation(out=nxt[:, 1, :], in_=bk[:, off + 128:off + 256], func=AF.Copy), [bk], [nxt])
                    cur = nxt
                yield
            U = X
            Ub = XS.next()
            fw.op("act", lambda e: e.activation(out=Ub[:], in_=U[:], func=AF.Copy), [U], [Ub])
            yb = ybanks[h // 8]; yo = (h % 8) * 64
            fw.op("pe", lambda e: e.matmul(yb[:, yo:yo + 64], lhsT=rT, rhs=Hb[:, h, :], start=True, stop=False), [tt, Hb], [yb])
            fw.op("pe", lambda e: e.matmul(yb[:, yo:yo + 64], lhsT=BrbT, rhs=Ub[:], start=False, stop=False), [sc, Ub], [yb])
            fw.op("pe", lambda e: e.matmul(yb[:, yo:yo + 64], lhsT=BrkT, rhs=Vb[:, hs], start=False, stop=True), [sc, Vb], [yb])
            bk, off = sml.next()
            fw.op("pe", lambda e: e.matmul(bk[0:64, off:off + 64], lhsT=prep["bh"][:, hs], rhs=U[:], start=True, stop=False), [prep["bh"], U], [bk])
            fw.op("pe", lambda e: e.matmul(bk[0:64, off:off + 64], lhsT=prep["kh"][:, hs], rhs=V_[:, hs], start=False, stop=True), [prep["kh"], V_], [bk])
            fw.op("dve", lambda e: e.scalar_tensor_tensor(out=H[:, h, :], in0=H[:, h, :], scalar=gC[:, h:h + 1], in1=bk[0:64, off:off + 64], op0=ALU.mult, op1=ALU.add), [H, gC, bk], [H])
            yield

        for d in range(2):
            fw.op("dve", lambda e: e.memset(H[:], 0.0), [], [H])
            fw.op("dve", lambda e: e.memset(Hb[:], 0.0), [], [Hb])
            order = list(range(NT)) if d == 0 else [1, 0] + list(range(NT - 1, 1, -1))
            for c in order[:nchunks]:
                rows = slice(c * 128, (c + 1) * 128)
                T_ = []
                for nm, key in (("R", "R"), ("KAP", "KAP"), ("V", "V"), ("LW", "LW%d" % d), ("B", "B%d" % d), ("KD", "KD%d" % d)):
                    t_ = ld[nm].next()
                    fw.dma("sp", t_, t_[:], S[key], S[key].ap()[rows, :])
                    T_.append(t_)
                R_, KAP_, V_, LW_, B_, KD_ = T_
                fw.op("pool", lambda e: e.tensor_copy(out=Vb[:], in_=V_[:]), [V_], [Vb])
                bk, off = sml.next()
                for h in range(16):
                    fw.op("pe", lambda e: e.matmul(bk[0:64, off + h:off + h + 1], lhsT=LW_[:, h * 64:(h + 1) * 64], rhs=ones[:, 0:1], start=True, stop=True), [LW_, ones], [bk])
                fw.op("act", lambda e: e.activation(out=gC[:], in_=bk[0:64, off:off + 16], func=AF.Exp), [bk], [gC])
                for half in range(2):
                    cs = slice(half * 512, (half + 1) * 512)
                    pc = pbanks.next(); ptot = pbanks.next()
                    fw.op("pe", lambda e: e.matmul(pc[:], lhsT=tm[:, MC[d], :], rhs=LW_[:, cs], start=True, stop=True), [tm, LW_], [pc])
                    fw.op("pe", lambda e: e.matmul(ptot[:], lhsT=ones[:], rhs=LW_[:, cs], start=True, stop=True), [ones, LW_], [ptot])
                    fw.op("act", lambda e: e.activation(out=cumS[:], in_=pc[:], func=AF.Copy), [pc], [cumS])
                    e1 = et.next()
                    fw.op("act", lambda e: e.activation(out=e1[:], in_=pc[:], func=AF.Exp), [pc], [e1])
                    fw.op("pool", lambda e: e.tensor_tensor(out=prep["rt"][:, cs], in0=R_[:, cs], in1=e1[:], op=ALU.mult), [R_, e1], [prep["rt"]])
                    e2 = et.next()
                    fw.op("act", lambda e: e.activation(out=e2[:], in_=pc[:], func=AF.Exp, scale=-1.0), [pc], [e2])
                    fw.op("pool", lambda e: e.tensor_tensor(out=prep["bt"][:, cs], in0=B_[:, cs], in1=e2[:], op=ALU.mult), [B_, e2], [prep["bt"]])
                    fw.op("pool", lambda e: e.tensor_tensor(out=prep["kdt"][:, cs], in0=KD_[:, cs], in1=e2[:], op=ALU.mult), [KD_, e2], [prep["kdt"]])
                    e3 = et.next()
                    fw.op("dve", lambda e: e.tensor_tensor(out=e3[:], in0=cumS[:], in1=LW_[:, cs], op=ALU.subtract), [cumS, LW_], [e3])
                    fw.op("act", lambda e: e.activation(out=e3[:], in_=e3[:], func=AF.Exp), [e3], [e3])
                    fw.op("pool", lambda e: e.tensor_tensor(out=prep["kt"][:, cs], in0=KAP_[:, cs], in1=e3[:], op=ALU.mult), [KAP_, e3], [prep["kt"]])
                    e4 = et.next()
                    fw.op("dve", lambda e: e.tensor_tensor(out=e4[:], in0=ptot[:], in1=cumS[:], op=ALU.subtract), [ptot, cumS], [e4])
                    fw.op("act", lambda e: e.activation(out=e4[:], in_=e4[:], func=AF.Exp), [e4], [e4])
                    fw.op("dve", lambda e: e.tensor_tensor(out=prep["bh"][:, cs], in0=B_[:, cs], in1=e4[:], op=ALU.mult), [B_, e4], [prep["bh"]])
                    fw.op("dve", lambda e: e.tensor_tensor(out=prep["kh"][:, cs], in0=KD_[:, cs], in1=e4[:], op=ALU.mult), [KD_, e4], [prep["kh"]])
                for h0 in range(0, 16, NH):
                    gens = [head_gen(h, d, T_) for h in range(h0, h0 + NH)]
                    alive = True
                    while alive:
                        alive = False
                        for gi in gens:
                            try:
                                next(gi)
                                alive = True
                            except StopIteration:
                                pass
                fw.op("act", lambda e: e.activation(out=Hb[:].rearrange("p a b -> p (a b)"), in_=H[:].rearrange("p a b -> p (a b)"), func=AF.Copy), [H], [Hb])
                ys = ysb.next()
                for half in range(2):
                    fw.op("act", lambda e: e.activation(out=ys[:, half * 512:(half + 1) * 512], in_=ybanks[half][:], func=AF.Copy), [ybanks[half]], [ys])
                fw.dma("act", self.Y[d], self.Y[d].ap()[rows, :], ys, ys[:])
        fw.pop()

    def l1_phaseC(self, hsrc, hdst):
        fw = self.fw
        I = self.I
        S = self.S1
        fw.push()
        self.bank = Rot(self.banks)
        gate = self.gate_bcast(1, 0)
        w_o = fw.sb("w_o", [128, 8, D], BF16)
        for k in range(8):
            fw.dma("pool", w_o, w_o[:, k, :], I["od_w_o"], I["od_w_o"].ap()[k * 128:(k + 1) * 128, :])
        gnw = fw.sb("gnw", [128, D]); gnb = fw.sb("gnb", [128, D])
        fw.dma("sp", gnw, gnw[:], I["od_gn_w"], I["od_gn_w"].ap().partition_broadcast(128))
        fw.dma("sp", gnb, gnb[:], I["od_gn_b"], I["od_gn_b"].ap().partition_broadcast(128))
        L = {nm: Rot([fw.sb("c" + nm, [128, D]) for _ in range(2)]) for nm in ("y0", "y1", "v", "g", "h")}
        bons = Rot([fw.sb("cbon", [128, 16]) for _ in range(2)])
        st = Rot([fw.sb("cst", [128, 16]) for _ in range(6)])
        wk = Rot([fw.sb("cwk", [128, D]) for _ in range(4)])
        obf = Rot([fw.sb("cob", [128, D], BF16) for _ in range(2)])
        oTs = Rot([fw.sb("coT", [128, 8, 128], BF16) for _ in range(2)])
        v3 = lambda t_: t_[:].rearrange("p (h j) -> p h j", j=64)
        b3 = lambda t_: t_[:].unsqueeze(2).to_broadcast([128, 16, 64])
        for tile in range(2, NT):
            rows = slice(tile * 128, (tile + 1) * 128)
            y0 = L["y0"].next(); y1 = L["y1"].next(); v = L["v"].next(); g = L["g"].next(); ht = L["h"].next(); bon = bons.next()
            fw.dma("sp", y0, y0[:], self.Y[0], self.Y[0].ap()[rows, :])
            fw.dma("sp", y1, y1[:], self.Y[1], self.Y[1].ap()[rows, :])
            fw.dma("sp", v, v[:], S["V"], S["V"].ap()[rows, :])
            fw.dma("sp", g, g[:], S["G"], S["G"].ap()[rows, :])
            fw.dma("sp", bon, bon[:], S["BON"], S["BON"].ap()[rows, :])
            src, sap = self.src_of(tile, hsrc)
            fw.dma("sp", ht, ht[:], src, sap)
            ysum = wk.next()
            fw.op("pool", lambda e: e.tensor_tensor(out=ysum[:], in0=y0[:], in1=y1[:], op=ALU.add), [y0, y1], [ysum])
            mean = st.next(); var = st.next()
            fw.op("dve", lambda e: e.reduce_sum(out=mean[:], in_=v3(ysum), axis=AX.X), [ysum], [mean])
            fw.op("dve", lambda e: e.tensor_scalar(out=mean[:], in0=mean[:], scalar1=1.0 / 64, scalar2=None, op0=ALU.mult), [mean], [mean])
            yc = wk.next()
            fw.op("dve", lambda e: e.tensor_tensor(out=v3(yc), in0=v3(ysum), in1=b3(mean), op=ALU.subtract), [ysum, mean], [yc])
            sq = wk.next()
            fw.op("pool", lambda e: e.tensor_tensor(out=sq[:], in0=yc[:], in1=yc[:], op=ALU.mult), [yc], [sq])
            fw.op("dve", lambda e: e.reduce_sum(out=var[:], in_=v3(sq), axis=AX.X), [sq], [var])
            fw.op("act", lambda e: e.activation(out=var[:], in_=var[:], func=AF.Sqrt, bias=64e-5, scale=1.0 / 64), [var], [var])
            fw.op("dve", lambda e: e.reciprocal(out=var[:], in_=var[:]), [var], [var])
            fw.op("dve", lambda e: e.tensor_tensor(out=v3(yc), in0=v3(yc), in1=b3(var), op=ALU.mult), [yc, var], [yc])
            fw.op("pool", lambda e: e.tensor_tensor(out=yc[:], in0=yc[:], in1=gnw[:], op=ALU.mult), [yc, gnw], [yc])
            fw.op("pool", lambda e: e.tensor_tensor(out=yc[:], in0=yc[:], in1=gnb[:], op=ALU.add), [yc, gnb], [yc])
            bv = wk.next()
            fw.op("dve", lambda e: e.tensor_tensor(out=v3(bv), in0=v3(v), in1=b3(bon), op=ALU.mult), [v, bon], [bv])
            fw.op("pool", lambda e: e.tensor_tensor(out=yc[:], in0=yc[:], in1=bv[:], op=ALU.add), [yc, bv], [yc])
            ob = obf.next()
            fw.op("dve", lambda e: e.tensor_tensor(out=ob[:], in0=yc[:], in1=g[:], op=ALU.mult), [yc, g], [ob])
            pb = self.bank.next(); pv = self.bview(pb)
            for k in range(8):
                fw.op("pe", lambda e: e.transpose(out=pv[:, k * 128:(k + 1) * 128], in_=ob[:, k * 128:(k + 1) * 128], identity=self.identb[:]), [ob, self.identb], [pb])
            oT = oTs.next()
            fw.op("act", lambda e: e.activation(out=oT[:].rearrange("p k n -> p (k n)"), in_=pv[:, 0:1024], func=AF.Copy), [pb], [oT])
            tmp = wk.next()
            for half in range(2):
                po = self.bank.next()
                for k in range(8):
                    fw.op("pe", lambda e: e.matmul(po[:], lhsT=oT[:, k, :], rhs=w_o[:, k, half * 512:(half + 1) * 512], start=(k == 0), stop=(k == 7)), [oT, w_o], [po])
                fw.op("dve", lambda e: e.tensor_tensor(out=tmp[:, half * 512:(half + 1) * 512], in0=po[:], in1=gate[0][:, half * 512:(half + 1) * 512], op=ALU.mult), [po, gate[0]], [tmp])
            fw.op("pool", lambda e: e.tensor_tensor(out=tmp[:], in0=tmp[:], in1=ht[:], op=ALU.add), [tmp, ht], [tmp])
            fw.dma("act", hdst, hdst.ap()[rows, :], tmp, tmp[:])
        fw.pop()

    def finish(self):
        fw = self.fw
        fw.barrier()
        fw.finish("sp", [self.out])
        print("ninst", fw.ninst, "nsem", fw.nsem)
        self.es.close()
        return self.nc


def host_consts():
    c = {}
    c["ident"] = np.eye(128, dtype=np.float32)
    t = np.arange(NLAT)
    row = (t // 64).astype(np.float32); col = (t % 64).astype(np.float32)
    inv = (10000.0 ** (-np.arange(0, 32, 2, dtype=np.float32) / 32)).astype(np.float32)
    ang = np.concatenate([row[:, None] * inv, col[:, None] * inv], axis=-1)
    cosl = np.cos(ang).astype(np.float32); sinl = np.sin(ang).astype(np.float32)
    cos = np.concatenate([np.ones((NCTX, 32), np.float32), cosl], 0)
    sin = np.concatenate([np.zeros((NCTX, 32), np.float32), sinl], 0)
    d = np.arange(128) % 64
    c["ropeC"] = np.ascontiguousarray(cos[:, d // 2].T)
    c["ropeS"] = np.ascontiguousarray(sin[:, d // 2].T)
    P = np.zeros((128, 128), np.float32)
    for i in range(64):
        P[2 * i + 1, 2 * i] = -1.0
        P[2 * i, 2 * i + 1] = 1.0
    c["ropeP"] = P
    b = np.zeros((128, 128), np.float32)
    b[:64, :64] = 1; b[64:, 64:] = 1
    c["blk64"] = b
    s_ = np.arange(128)[:, None]; t_ = np.arange(128)[None, :]
    tm = np.zeros((128, 4, 128), np.float32)
    tm[:, 0] = (s_ < t_); tm[:, 1] = (s_ <= t_); tm[:, 2] = (s_ > t_); tm[:, 3] = (s_ >= t_)
    c["trimask"] = tm
    return c


def shard_inputs(inputs):
    f = lambda a: np.ascontiguousarray(np.asarray(a, dtype=np.float32))
    w_in = f(inputs["ev_w_in"])[0]
    w_in_r = np.concatenate([w_in[:, 0:2048], w_in[:, 2048:2112], w_in[:, 2048:2112], w_in[:, 2112:2176], w_in[:, 2112:2176], w_in[:, 2176:2304]], axis=1)
    shared = dict(
        ada_w=f(inputs["ada_w"]), ada_b=f(inputs["ada_b"]), norm1_g=f(inputs["norm1_g"]), norm2_g=f(inputs["norm2_g"]),
        w_in=f(w_in_r), conv_w=f(inputs["ev_conv_w"])[0], qg2=f(np.tile(f(inputs["ev_q_gain"])[0], 2)), kg2=f(np.tile(f(inputs["ev_k_gain"])[0], 2)),
        w_out=f(inputs["ev_w_out"])[0], od_mu=f(inputs["od_mu"])[0], od_w_r=f(inputs["od_w_r"])[0], od_w_k=f(inputs["od_w_k"])[0],
        od_w_v=f(inputs["od_w_v"])[0], od_w_o=f(inputs["od_w_o"])[0], od_g1=f(inputs["od_g1"])[0], od_g2=f(inputs["od_g2"])[0],
        od_k_k=f(inputs["od_k_k"])[0], od_k_a=f(inputs["od_k_a"])[0], od_r_k=f(inputs["od_r_k"])[0].reshape(-1),
        od_w0=f(inputs["od_w0"])[0], od_w1=f(inputs["od_w1"])[0], od_w2=f(inputs["od_w2"])[0], od_a0=f(inputs["od_a0"])[0],
        od_a1=f(inputs["od_a1"])[0], od_a2=f(inputs["od_a2"])[0], od_gn_w=f(inputs["od_gn_w"])[0], od_gn_b=f(inputs["od_gn_b"])[0],
        peer_wq=f(inputs["peer_wq"]), peer_keys=f(inputs["peer_keys"]).reshape(2, 16, 128, 128), peer_u=f(inputs["peer_u"]), peer_v=f(inputs["peer_v"]),
    )
    shared.update(host_consts())
    x = f(inputs["x"]); ctx = f(inputs["ctx"]); c = f(inputs["c"]); cc = f(inputs["c_ctx"])
    maps = []
    for b in range(8):
        m = dict(shared)
        m["x"] = x[b]; m["ctx"] = ctx[b]; m["cvec"] = np.ascontiguousarray(np.stack([c[b], cc], 0))
        maps.append(m)
    return maps


def build(dbg=False, stop=None):
    p = Prog(dbg=dbg, stop=stop)
    p.phase0()
    if stop == "p0":
        return p
    if stop and stop.startswith("L1"):
        p.l1_phaseA(p.h2)
        if stop == "L1a":
            return p
        p.l1_phaseB()
        if stop == "L1b":
            return p
        p.l1_phaseC(p.h2, p.h3)
        return p
    p.l0_phaseA()
    if stop == "l0a":
        p.fw.pop()
        return p
    p.l0_phaseB()
    if stop == "l0":
        return p
    p.peer_prep(0)
    p.peer(0, p.h1, p.h2)
    if stop == "peer0":
        return p
    p.l1_phaseA(p.h2)
    if stop == "l1a":
        return p
    p.l1_phaseB()
    if stop == "l1b":
        return p
    p.l1_phaseC(p.h2, p.h3)
    if stop == "l1c":
        return p
    p.peer_prep(1)
    p.peer(1, p.h3, None, final=True)
    return p


def kernel(**inputs):
    p = build()
    nc = p.finish()
    maps = shard_inputs(inputs)
    maps = [{k: v for k, v in m.items() if k in p.I} for m in maps]
    res = run_bass_kernel_spmd(nc, maps, core_ids=list(range(8)))
    return np.stack([r["out"] for r in res.results], 0)
```

```python
import os
import numpy as np
from contextlib import ExitStack
import concourse.bass as bass
import concourse.mybir as mybir
from concourse.bass_utils import run_bass_kernel_spmd

F32 = mybir.dt.float32
BF16 = mybir.dt.bfloat16
U32 = mybir.dt.uint32
ALU = mybir.AluOpType
AF = mybir.ActivationFunctionType
AX = mybir.AxisListType

D = 1024
NCTX = 256
NLAT = 4096
TOK = NCTX + NLAT
NT = TOK // 128
TOKP = TOK + 4
EPS = 1e-6
INPUT_SHAPES = {
    "x": [NLAT, D],
    "ctx": [NCTX, D],
    "cvec": [2, D],
    "ada_w": [2, D, 6 * D],
    "ada_b": [2, 6 * D],
    "norm1_g": [2, D],
    "norm2_g": [2, D],
    "w_in": [D, 2432],
    "conv_w": [3, 512],
    "qg2": [128],
    "kg2": [128],
    "w_out": [D, D],
    "od_mu": [6, D],
    "od_w_r": [D, D],
    "od_w_k": [D, D],
    "od_w_v": [D, D],
    "od_w_o": [D, D],
    "od_g1": [D, 128],
    "od_g2": [128, D],
    "od_k_k": [D],
    "od_k_a": [D],
    "od_r_k": [D],
    "od_w0": [2, D],
    "od_w1": [2, D, 64],
    "od_w2": [2, 64, D],
    "od_a0": [2, D],
    "od_a1": [2, D, 64],
    "od_a2": [2, 64, D],
    "od_gn_w": [D],
    "od_gn_b": [D],
    "peer_wq": [2, D, 2048],
    "peer_keys": [2, 16, 128, 128],
    "peer_u": [2, 16384, D],
    "peer_v": [2, 16384, D],
    "ident": [128, 128],
    "ropeC": [128, TOK],
    "ropeS": [128, TOK],
    "ropeP": [128, 128],
    "blk64": [128, 128],
    "trimask": [128, 4, 128],
    "iota128": [128, 128],
}


class T:
    __slots__ = ("t", "w", "r", "sem", "dcount", "name")

    def __init__(self, t, name):
        self.t = t
        self.name = name
        self.w = {}
        self.r = {}
        self.sem = None
        self.dcount = 0

    def __getitem__(self, k):
        return self.t[k]

    def ap(self):
        return self.t.ap()


class FW:
    def __init__(self, nc, es):
        self.nc = nc
        self.es = es
        self.root_es = es
        self.stack = []
        self.live = []
        self.drams = []
        self.sempool = []
        self.uid = 0
        self.eng = {"pe": nc.tensor, "act": nc.scalar, "dve": nc.vector, "pool": nc.gpsimd, "sp": nc.sync}
        self.esem = {}
        self.ecount = {}
        self.known = {}
        self.nsem = 0
        for k in self.eng:
            self.esem[k] = self.newsem("e_" + k)
            self.ecount[k] = 0
            self.known[k] = {}
        self.selfwait = {"pe": False, "act": True, "dve": True, "pool": True, "sp": False}
        self.ninst = 0

    def newsem(self, name):
        self.nsem += 1
        assert self.nsem < 200, "too many semaphores"
        return self.root_es.enter_context(self.nc.semaphore(name))

    def push(self):
        self.stack.append((self.es, self.live))
        self.es = ExitStack()
        self.live = []

    def pop(self):
        self.barrier()
        for t in self.live:
            if t.sem is not None:
                self.sempool.append((t.sem, t.dcount))
                t.sem = None
        self.es.close()
        self.es, self.live = self.stack.pop()

    def barrier(self):
        toks = {}
        for k in self.eng:
            if self.ecount[k] > 0:
                toks[self.esem[k]] = self.ecount[k]
        for _, live in self.stack + [(None, self.live)]:
            for t in live:
                if t.sem is not None and t.dcount > 0:
                    toks[t.sem] = 16 * t.dcount
        for t in self.drams:
            if t.sem is not None and t.dcount > 0:
                toks[t.sem] = 16 * t.dcount
        for e in self.eng:
            kn = self.known[e]
            for s_, v in toks.items():
                if s_ is self.esem[e]:
                    continue
                if kn.get(s_, 0) < v:
                    self.eng[e].wait_ge(s_, v)
                    kn[s_] = v
                    self.ninst += 1

    def sb(self, name, shape, dt=F32):
        self.uid += 1
        t = T(self.es.enter_context(self.nc.sbuf_tensor("%s_%d" % (name, self.uid), list(shape), dt)), name)
        self.live.append(t)
        return t

    def ps(self, name, shape, dt=F32):
        t = T(self.es.enter_context(self.nc.psum_tensor(name, list(shape), dt)), name)
        self.live.append(t)
        return t

    def dram(self, name, shape, dt=F32, kind="Internal"):
        t = T(self.nc.dram_tensor(name, list(shape), dt, kind=kind), name)
        self.drams.append(t)
        return t

    def _waits(self, e, reads, writes):
        toks = {}

        def add(tok):
            if tok is None:
                return
            s, v = tok
            if toks.get(s, 0) < v:
                toks[s] = v
        for r in reads:
            for s, v in r.w.items():
                add((s, v))
        for w in writes:
            for s, v in w.w.items():
                add((s, v))
            for s, v in w.r.items():
                add((s, v))
        eng = self.eng[e]
        kn = self.known[e]
        for s, v in toks.items():
            if s is self.esem[e] and not self.selfwait[e]:
                continue
            if kn.get(s, 0) < v:
                eng.wait_ge(s, v)
                kn[s] = v
                self.ninst += 1

    def op(self, e, fn, reads=(), writes=()):
        self._waits(e, reads, writes)
        inst = fn(self.eng[e])
        self.ecount[e] += 1
        self.ninst += 1
        s = self.esem[e]
        inst.then_inc(s, 1)
        tok = (s, self.ecount[e])
        for w in writes:
            w.w = {s: tok[1]}
            w.r = {}
        for r in reads:
            if r.r.get(s, 0) < tok[1]:
                r.r[s] = tok[1]
        return inst

    def dma(self, e, outT, out_ap, inT, in_ap, **kw):
        self._waits(e, [inT], [outT])
        own = inT if (outT in self.drams and inT not in self.drams) else outT
        if own.sem is None:
            if self.sempool:
                own.sem, own.dcount = self.sempool.pop()
            else:
                own.sem = self.newsem("d%d" % self.nsem)
        inst = self.eng[e].dma_start(out=out_ap, in_=in_ap, **kw)
        inst.then_inc(own.sem, 16)
        self.ninst += 1
        own.dcount += 1
        tok = (own.sem, 16 * own.dcount)
        if own is outT:
            outT.w = {tok[0]: tok[1]}
        else:
            outT.w[tok[0]] = tok[1]
        outT.r = {}
        if inT.r.get(tok[0], 0) < tok[1]:
            inT.r[tok[0]] = tok[1]
        return inst

    def finish(self, e, tiles):
        self._waits(e, tiles, [])


class Rot:
    def __init__(self, items):
        self.items = items
        self.i = -1

    def next(self):
        self.i = (self.i + 1) % len(self.items)
        return self.items[self.i]


def tokcol(tile):
    return 1 + tile * 128 if tile < 2 else 259 + (tile - 2) * 128


class Prog:
    def __init__(self, dbg=False, stop=None):
        self.dbg = dbg
        self.stop = stop
        self.nc = bass.Bass("TRN2", target_bir_lowering=False)
        self.es = ExitStack()
        self.fw = FW(self.nc, self.es)
        fw = self.fw

        prog = self

        class Lazy(dict):
            def __missing__(d, name):
                t = fw.dram(name, INPUT_SHAPES[name], F32, kind="ExternalInput")
                d[name] = t
                return t
        self.I = Lazy()
        self.out = fw.dram("out", [NLAT, D], F32, kind="ExternalOutput")
        k = "ExternalOutput" if dbg else "Internal"
        self.modrow = fw.dram("modrow", [2, 2, 6 * D], F32, kind=k)
        self.h1 = fw.dram("h1", [TOK, D], F32, kind=k)
        self.h2 = fw.dram("h2", [TOK, D], F32, kind=("ExternalInput" if stop and stop.startswith("L1") else k))
        self.h3 = fw.dram("h3", [TOK, D], F32, kind=k)
        self.uT = fw.dram("uT", [512, TOKP], BF16)
        if dbg:
            self.dza = fw.dram("dza", [TOK, 512], BF16, kind="ExternalOutput")
            self.dzc = fw.dram("dzc", [512, TOK], BF16, kind="ExternalOutput")
        self.gbT = fw.dram("gbT", [512, TOKP], BF16)
        self.UT = fw.dram("UT", [128, 128, 8, 128], BF16)
        self.Vb = fw.dram("Vb", [128, 128, D], BF16)
        self.banks = [fw.ps("bank%d" % i, [128, 512]) for i in range(8)]
        self.bank = Rot(self.banks)
        self.identf = fw.sb("identf", [128, 128])
        self.identb = fw.sb("identb", [128, 128], BF16)
        fw.dma("sp", self.identf, self.identf[:], self.I["ident"], self.I["ident"].ap())
        fw.dma("pool", self.identb, self.identb[:], self.I["ident"], self.I["ident"].ap())

    def bview(self, bank):
        return bank.ap().bitcast(BF16)

    def load_colT(self, dst, dst_ap, src, src_ap1d):
        self.fw.dma("sp", dst, dst_ap, src, src_ap1d.rearrange("(k p) -> p k", p=128), allow_slow_non_contiguous=True)

    def modvecs(self, layer, which):
        fw = self.fw
        I = self.I
        gname = "norm1_g" if which == 0 else "norm2_g"
        g = fw.sb("g", [128, 8])
        self.load_colT(g, g[:], I[gname], I[gname].ap()[layer])
        GT, shT = [], []
        for s in range(2):
            sc = fw.sb("scl", [128, 8]); sh = fw.sb("shf", [128, 8]); G = fw.sb("G", [128, 8])
            base = which * 3 * D
            self.load_colT(sh, sh[:], self.modrow, self.modrow.ap()[layer, s, base:base + D])
            self.load_colT(sc, sc[:], self.modrow, self.modrow.ap()[layer, s, base + D:base + 2 * D])
            fw.op("dve", lambda e: e.scalar_tensor_tensor(out=G[:], in0=sc[:], scalar=1.0, in1=g[:], op0=ALU.add, op1=ALU.mult), [sc, g], [G])
            GT.append(G); shT.append(sh)
        return GT, shT

    def gate_bcast(self, layer, which):
        fw = self.fw
        res = []
        for s in range(2):
            gt = fw.sb("gate", [128, D])
            off = which * 3 * D + 2 * D
            fw.dma("sp", gt, gt[:], self.modrow, self.modrow.ap()[layer, s, off:off + D].partition_broadcast(128))
            res.append(gt)
        return res

    def make_normer(self):
        fw = self.fw
        self.nb_h = Rot([fw.sb("nh", [128, D]) for _ in range(2)])
        self.nb_sq = fw.sb("nsq", [128, D], BF16)
        self.nb_ss = Rot([fw.sb("nss", [128, 1]) for _ in range(2)])
        self.nb_rs = Rot([fw.sb("nrs", [128, 1]) for _ in range(2)])
        self.nb_hn = Rot([fw.sb("nhn", [128, D], BF16) for _ in range(2)])

    def norm_tile(self, src, src_ap, G, sh, xnT, xnT_cols):
        fw = self.fw
        h = self.nb_h.next(); ss = self.nb_ss.next(); rs = self.nb_rs.next(); hn = self.nb_hn.next(); sq = self.nb_sq
        fw.dma("sp", h, h[:], src, src_ap)
        fw.op("act", lambda e: e.activation(out=sq[:], in_=h[:], func=AF.Square, scale=1.0 / 32, accum_out=ss[:]), [h], [sq, ss])
        fw.op("act", lambda e: e.activation(out=rs[:], in_=ss[:], func=AF.Sqrt, bias=EPS, scale=1.0), [ss], [rs])
        fw.op("dve", lambda e: e.reciprocal(out=rs[:], in_=rs[:]), [rs], [rs])
        fw.op("act", lambda e: e.activation(out=hn[:], in_=h[:], func=AF.Copy, scale=rs[:]), [h, rs], [hn])
        pb = self.bank.next()
        pv = self.bview(pb)
        for k in range(8):
            fw.op("pe", lambda e: e.transpose(out=pv[:, k * 128:(k + 1) * 128], in_=hn[:, k * 128:(k + 1) * 128], identity=self.identb[:]), [hn, self.identb], [pb])
        for k in range(8):
            fw.op("act", lambda e: e.activation(out=xnT[:, k, xnT_cols], in_=pv[:, k * 128:(k + 1) * 128], func=AF.Identity,
                                                scale=G[:, k:k + 1], bias=sh[:, k:k + 1]), [pb, G, sh], [xnT])
        return h

    def src_of(self, tile, hsrc):
        if hsrc is None:
            if tile < 2:
                return self.I["ctx"], self.I["ctx"].ap()[tile * 128:(tile + 1) * 128, :]
            return self.I["x"], self.I["x"].ap()[(tile - 2) * 128:(tile - 1) * 128, :]
        return hsrc, hsrc.ap()[tile * 128:(tile + 1) * 128, :]

    def phase0(self):
        fw = self.fw
        I = self.I
        fw.push()
        sc = fw.sb("sc", [128, 2, 8])
        for s_ in range(2):
            self.load_colT(sc, sc[:, s_, :], I["cvec"], I["cvec"].ap()[s_])
        fw.op("act", lambda e: e.activation(out=sc[:], in_=sc[:], func=AF.Silu), [sc], [sc])
        wb = Rot([fw.sb("adaw", [128, 8, 512]) for _ in range(2)])
        for i in range(2):
            bias = fw.sb("adab", [2, 6 * D])
            modr = fw.sb("modr", [2, 6 * D])
            fw.dma("sp", bias, bias[:], I["ada_b"], I["ada_b"].ap()[i].partition_broadcast(2))
            for cb in range(12):
                wt = wb.next()
                fw.dma("sp" if cb % 2 == 0 else "act", wt, wt[:], I["ada_w"], I["ada_w"].ap()[i, :, cb * 512:(cb + 1) * 512].rearrange("(k p) n -> p k n", p=128))
                pb = self.bank.next()
                for k in range(8):
                    fw.op("pe", lambda e: e.matmul(pb[0:2, :], lhsT=sc[:, :, k], rhs=wt[:, k, :], start=(k == 0), stop=(k == 7)), [sc, wt], [pb])
                fw.op("dve", lambda e: e.tensor_tensor(out=modr[:, cb * 512:(cb + 1) * 512], in0=pb[0:2, :], in1=bias[:, cb * 512:(cb + 1) * 512], op=ALU.add), [pb, bias], [modr])
            fw.dma("sp", self.modrow, self.modrow.ap()[i], modr, modr[:])
        fw.pop()

    def l0_phaseA(self):
        fw = self.fw
        I = self.I
        fw.push()
        self.qT = fw.sb("qT", [128, 4, TOK], BF16)
        self.kT2 = fw.sb("kT2", [128, 2, TOK], BF16)
        self.vext = fw.sb("vext", [128, NT, 2, 66], BF16)
        fw.op("pool", lambda e: e.memset(self.vext[:], 1.0), [], [self.vext])
        fw.push()
        GT, shT = self.modvecs(0, 0)
        self.make_normer()
        w_in = fw.sb("w_in", [128, 8, 2432], BF16)
        for k in range(8):
            fw.dma("pool", w_in, w_in[:, k, :], I["w_in"], I["w_in"].ap()[k * 128:(k + 1) * 128, :])
        rC = fw.sb("rC", [128, TOK], BF16); rS = fw.sb("rS", [128, TOK], BF16)
        fw.dma("pool", rC, rC[:], I["ropeC"], I["ropeC"].ap())
        fw.dma("pool", rS, rS[:], I["ropeS"], I["ropeS"].ap())
        ropeP = fw.sb("ropeP", [128, 128], BF16); blk64 = fw.sb("blk64", [128, 128], BF16)
        fw.dma("pool", ropeP, ropeP[:], I["ropeP"], I["ropeP"].ap())
        fw.dma("pool", blk64, blk64[:], I["blk64"], I["blk64"].ap())
        gains = fw.sb("gains", [128, 2])
        fw.dma("sp", gains, gains[:, 0:1], I["qg2"], I["qg2"].ap().rearrange("(p o) -> p o", o=1))
        fw.dma("sp", gains, gains[:, 1:2], I["kg2"], I["kg2"].ap().rearrange("(p o) -> p o", o=1))
        zt = fw.sb("zt", [128, 4, 2], BF16)
        fw.op("pool", lambda e: e.memset(zt[:], 0.0), [], [zt])
        for dst in (self.uT,):
            v = dst.ap().rearrange("(b p) c -> p b c", p=128)
            fw.dma("pool", dst, v[:, :, 0:1], zt, zt[:, :, 0:1], allow_slow_non_contiguous=True)
            fw.dma("pool", dst, v[:, :, 257:259], zt, zt[:, :, 0:2], allow_slow_non_contiguous=True)
            fw.dma("pool", dst, v[:, :, TOKP - 1:TOKP], zt, zt[:, :, 0:1], allow_slow_non_contiguous=True)
        xnTs = Rot([fw.sb("xnT", [128, 8, 256], BF16) for _ in range(2)])
        hTt = fw.sb("hTt", [128, 4, 256])
        gbst = Rot([fw.sb("gbst", [128, 4, 256], BF16) for _ in range(2)])
        ust = Rot([fw.sb("ust", [128, 4, 256], BF16) for _ in range(2)])
        qraw = Rot([fw.sb("qraw", [128, 256]) for _ in range(2)])
        sqb = Rot([fw.sb("sqb", [128, 256], BF16) for _ in range(2)])
        rsb = Rot([fw.sb("rsb", [128, 256]) for _ in range(2)])
        qnb = Rot([fw.sb("qnb", [128, 256], BF16) for _ in range(2)])
        t1b = Rot([fw.sb("t1b", [128, 256]) for _ in range(2)])
        t2b = Rot([fw.sb("t2b", [128, 256]) for _ in range(2)])
        uTv = self.uT.ap().rearrange("(b p) c -> p b c", p=128)
        gbTv = self.gbT.ap().rearrange("(b p) c -> p b c", p=128)
        import os
        skip = os.environ.get('L0A_SKIP', '').split(',')
        for g in range(int(os.environ.get('L0A_NG', NT // 2))):
            s = 1 if g == 0 else 0
            xnT = xnTs.next()
            for j in range(2):
                tile = 2 * g + j
                src, sap = self.src_of(tile, None)
                self.norm_tile(src, sap, GT[s], shT[s], xnT, slice(j * 128, (j + 1) * 128))
            tok0 = g * 256
            gb_s = gbst.next(); u_s = ust.next()
            for fb in range(18):
                if fb >= 12 and 'qk' in skip:
                    continue
                if fb < 12 and 'conv' in skip:
                    continue
                pb = self.bank.next()
                for k in range(8):
                    fw.op("pe", lambda e: e.matmul(pb[:, 0:256], lhsT=w_in[:, k, fb * 128:(fb + 1) * 128], rhs=xnT[:, k, :], start=(k == 0), stop=(k == 7)), [w_in, xnT], [pb])
                if fb < 4:
                    fw.op("act", lambda e: e.activation(out=hTt[:, fb, :], in_=pb[:, 0:256], func=AF.Copy), [pb], [hTt])
                elif fb < 8:
                    fw.op("act", lambda e: e.activation(out=gb_s[:, fb - 4, :], in_=pb[:, 0:256], func=AF.Copy), [pb], [gb_s])
                elif fb < 12:
                    fw.op("dve", lambda e: e.tensor_tensor(out=u_s[:, fb - 8, :], in0=pb[:, 0:256], in1=hTt[:, fb - 8, :], op=ALU.mult), [pb, hTt], [u_s])
                else:
                    isq = fb < 16
                    qr = qraw.next(); sq = sqb.next(); rs = rsb.next(); qn = qnb.next(); t1 = t1b.next(); t2 = t2b.next()
                    gcol = gains[:, 0:1] if isq else gains[:, 1:2]
                    fw.op("act", lambda e: e.activation(out=qr[:], in_=pb[:, 0:256], func=AF.Copy), [pb], [qr])
                    fw.op("act", lambda e: e.activation(out=sq[:], in_=pb[:, 0:256], func=AF.Square), [pb], [sq])
                    p2 = self.bank.next()
                    fw.op("pe", lambda e: e.matmul(p2[:, 0:256], lhsT=blk64[:], rhs=sq[:], start=True, stop=True), [blk64, sq], [p2])
                    fw.op("act", lambda e: e.activation(out=rs[:], in_=p2[:, 0:256], func=AF.Sqrt, bias=EPS, scale=1.0 / 64), [p2], [rs])
                    fw.op("dve", lambda e: e.reciprocal(out=rs[:], in_=rs[:]), [rs], [rs])
                    fw.op("dve", lambda e: e.scalar_tensor_tensor(out=qn[:], in0=qr[:], scalar=gcol, in1=rs[:], op0=ALU.mult, op1=ALU.mult), [qr, gains, rs], [qn])
                    p3 = self.bank.next()
                    fw.op("pe", lambda e: e.matmul(p3[:, 0:256], lhsT=ropeP[:], rhs=qn[:], start=True, stop=True), [ropeP, qn], [p3])
                    fw.op("dve", lambda e: e.tensor_tensor(out=t2[:], in0=p3[:, 0:256], in1=rS[:, tok0:tok0 + 256], op=ALU.mult), [p3, rS], [t2])
                    fw.op("pool", lambda e: e.tensor_tensor(out=t1[:], in0=qn[:], in1=rC[:, tok0:tok0 + 256], op=ALU.mult), [qn, rC], [t1])
                    dst = self.qT if isq else self.kT2
                    bi = fb - 12 if isq else fb - 16
                    fw.op("dve", lambda e: e.tensor_tensor(out=dst[:, bi, tok0:tok0 + 256], in0=t1[:], in1=t2[:], op=ALU.add), [t1, t2], [dst])
            for j in range(2 if 'v' not in skip else 0):
                tile = 2 * g + j
                pb = self.bank.next()
                for k in range(8):
                    fw.op("pe", lambda e: e.matmul(pb[:, 0:128], lhsT=xnT[:, k, j * 128:(j + 1) * 128], rhs=w_in[:, k, 2304:2432], start=(k == 0), stop=(k == 7)), [xnT, w_in], [pb])
                fw.op("act", lambda e: e.activation(out=self.vext[:, tile, :, 0:64], in_=pb[:, 0:128].rearrange("p (a d) -> p a d", a=2), func=AF.Copy), [pb], [self.vext])
            for j in range(2 if 'store' not in skip else 0):
                c0 = tokcol(2 * g + j)
                fw.dma("sp", self.uT, uTv[:, :, c0:c0 + 128], u_s, u_s[:, :, j * 128:(j + 1) * 128])
                fw.dma("sp", self.gbT, gbTv[:, :, c0:c0 + 128], gb_s, gb_s[:, :, j * 128:(j + 1) * 128])
        fw.pop()

    def l0_phaseB(self):
        fw = self.fw
        I = self.I
        fw.push()
        gate = self.gate_bcast(0, 0)
        w_out = fw.sb("w_out", [128, 8, D], BF16)
        for k in range(8):
            fw.dma("pool", w_out, w_out[:, k, :], I["w_out"], I["w_out"].ap()[k * 128:(k + 1) * 128, :])
        cw = fw.sb("cw", [128, 3, 4])
        for j_ in range(3):
            self.load_colT(cw, cw[:, j_, :], I["conv_w"], I["conv_w"].ap()[j_])
        PTs = Rot([fw.sb("PT", [128, NT, 512], BF16) for _ in range(2)])
        zatt = Rot([fw.sb("zatt", [128, 512], BF16) for _ in range(2)])
        rcb = Rot([fw.sb("rcb", [128, 4]) for _ in range(2)])
        zaTs = Rot([fw.sb("zaT", [128, 4, 128], BF16) for _ in range(2)])
        zcTs = Rot([fw.sb("zcT", [128, 4, 128], BF16) for _ in range(2)])
        uhs = Rot([fw.sb("uh", [128, 4, 130], BF16) for _ in range(2)])
        gbts = Rot([fw.sb("gbt", [128, 4, 128], BF16) for _ in range(2)])
        accs = Rot([fw.sb("acc", [128, 128]) for _ in range(3)])
        hts = Rot([fw.sb("ht", [128, D]) for _ in range(2)])
        tmps = Rot([fw.sb("tmp", [128, D]) for _ in range(2)])
        uTv = self.uT.ap().rearrange("(b p) c -> p b c", p=128)
        gbTv = self.gbT.ap().rearrange("(b p) c -> p b c", p=128)
        obanks = Rot(self.banks[0:2]); sbanks = Rot(self.banks[2:6]); self.bank = Rot(self.banks[6:8])
        import os
        skip = os.environ.get('L0B_SKIP', '').split(',')
        for qt in range(int(os.environ.get('L0B_NT', NT))):
            s = 1 if qt < 2 else 0
            nch = 2 if qt < 2 else NT
            ht = hts.next()
            src, sap = self.src_of(qt, None)
            fw.dma("sp", ht, ht[:], src, sap)
            uh = uhs.next(); gbt = gbts.next()
            c0 = tokcol(qt)
            fw.dma("sp", uh, uh[:], self.uT, uTv[:, :, c0 - 1:c0 + 129])
            fw.dma("sp", gbt, gbt[:], self.gbT, gbTv[:, :, c0:c0 + 128])
            za = zatt.next()
            for kv in range(2 if 'attn' not in skip else 0):
                PTall = PTs.next()
                for sc in range(nch):
                    sbs = [sbanks.next(), sbanks.next()]
                    for b in range(2):
                        for hf in range(2):
                            fw.op("pe", lambda e: e.matmul(sbs[hf][:, b * 128:(b + 1) * 128], lhsT=self.kT2[hf * 64:(hf + 1) * 64, kv, sc * 128:(sc + 1) * 128],
                                                            rhs=self.qT[hf * 64:(hf + 1) * 64, 2 * kv + b, qt * 128:(qt + 1) * 128], start=True, stop=True), [self.kT2, self.qT], [sbs[hf]])
                    PT4 = PTall[:, sc, :].rearrange("p (b h n) -> p b h n", b=2, h=2)
                    for hf in range(2):
                        fw.op("act", lambda e: e.activation(out=PT4[:, :, hf, :], in_=sbs[hf][:, 0:256].rearrange("p (b n) -> p b n", b=2), func=AF.Exp, scale=0.125), [sbs[hf]], [PTall])
                ob = obanks.next()
                O = ob[:, 0:260].rearrange("p (c e) -> p c e", e=65)
                for c in range(4):
                    for sc in range(nch):
                        fw.op("pe", lambda e: e.matmul(O[:, c, :], lhsT=PTall[:, sc, c * 128:(c + 1) * 128], rhs=self.vext[:, sc, kv, 0:65], start=(sc == 0), stop=(sc == nch - 1)), [PTall, self.vext], [ob])
                rc = rcb.next()
                fw.op("dve", lambda e: e.reciprocal(out=rc[:], in_=O[:, :, 64]), [ob], [rc])
                fw.op("dve", lambda e: e.tensor_tensor(out=za[:, kv * 256:(kv + 1) * 256].rearrange("p (c d) -> p c d", c=4), in0=O[:, :, 0:64],
                                                       in1=rc[:].unsqueeze(2).to_broadcast([128, 4, 64]), op=ALU.mult), [ob, rc], [za])
            pb = self.bank.next(); pv = self.bview(pb)
            for k in range(4):
                fw.op("pe", lambda e: e.transpose(out=pv[:, k * 128:(k + 1) * 128], in_=za[:, k * 128:(k + 1) * 128], identity=self.identb[:]), [za, self.identb], [pb])
            zaT = zaTs.next()
            fw.op("act", lambda e: e.activation(out=zaT[:].rearrange("p k n -> p (k n)"), in_=pv[:, 0:512], func=AF.Copy), [pb], [zaT])
            zcT = zcTs.next()
            for b in range(4 if 'conv' not in skip else 0):
                acc = accs.next()
                fw.op("pool", lambda e: e.tensor_scalar(out=acc[:], in0=uh[:, b, 0:128], scalar1=cw[:, 0, b:b + 1], scalar2=None, op0=ALU.mult), [uh, cw], [acc])
                for j_ in (1, 2):
                    t_ = accs.next()
                    fw.op("pool", lambda e: e.tensor_scalar(out=t_[:], in0=uh[:, b, j_:j_ + 128], scalar1=cw[:, j_, b:b + 1], scalar2=None, op0=ALU.mult), [uh, cw], [t_])
                    fw.op("pool", lambda e: e.tensor_tensor(out=acc[:], in0=acc[:], in1=t_[:], op=ALU.add), [acc, t_], [acc])
                fw.op("pool", lambda e: e.tensor_tensor(out=zcT[:, b, :], in0=acc[:], in1=gbt[:, b, :], op=ALU.mult), [acc, gbt], [zcT])
            if self.dbg:
                fw.dma("sp", self.dza, self.dza.ap()[qt * 128:(qt + 1) * 128, :], za, za[:])
                fw.dma("sp", self.dzc, self.dzc.ap().rearrange("(b p) c -> p b c", p=128)[:, :, qt * 128:(qt + 1) * 128], zcT, zcT[:])
            tmp = tmps.next()
            for half in range(2):
                pb = self.bank.next()
                for k in range(8):
                    lt = zcT[:, k, :] if k < 4 else zaT[:, k - 4, :]
                    fw.op("pe", lambda e: e.matmul(pb[:], lhsT=lt, rhs=w_out[:, k, half * 512:(half + 1) * 512], start=(k == 0), stop=(k == 7)), [zcT, zaT, w_out], [pb])
                fw.op("dve", lambda e: e.tensor_tensor(out=tmp[:, half * 512:(half + 1) * 512], in0=pb[:], in1=gate[s][:, half * 512:(half + 1) * 512], op=ALU.mult), [pb, gate[s]], [tmp])
            fw.op("pool", lambda e: e.tensor_tensor(out=tmp[:], in0=tmp[:], in1=ht[:], op=ALU.add), [tmp, ht], [tmp])
            fw.dma("act", self.h1, self.h1.ap()[qt * 128:(qt + 1) * 128, :], tmp, tmp[:])
        self.bank = Rot(self.banks)
        fw.pop()
        fw.pop()

    def peer_prep(self, layer):
        fw = self.fw
        I = self.I
        fw.push()
        banks = Rot(self.banks)
        Uv = I["peer_u"].ap()[layer].rearrange("(i j) d -> j i d", j=128)
        Vv = I["peer_v"].ap()[layer].rearrange("(i j) d -> j i d", j=128)
        UTv = self.UT.ap().rearrange("i p k j -> p i (k j)")
        Vbv = self.Vb.ap().rearrange("i j d -> j i d")
        ubs = Rot([fw.sb("ub", [128, 4, D], BF16) for _ in range(2)])
        vbs = Rot([fw.sb("vb", [128, 4, D], BF16) for _ in range(2)])
        uts = Rot([fw.sb("ut", [128, 4, 1024], BF16) for _ in range(2)])
        for c in range(32):
            ub = ubs.next(); vb = vbs.next(); ut = uts.next()
            fw.dma("pool", ub, ub[:], I["peer_u"], Uv[:, c * 4:(c + 1) * 4, :])
            fw.dma("pool", vb, vb[:], I["peer_v"], Vv[:, c * 4:(c + 1) * 4, :])
            fw.dma("sp", self.Vb, Vbv[:, c * 4:(c + 1) * 4, :], vb, vb[:])
            for ii in range(4):
                pb = banks.next(); pv = self.bview(pb)
                for k in range(8):
                    fw.op("pe", lambda e: e.transpose(out=pv[:, k * 128:(k + 1) * 128], in_=ub[:, ii, k * 128:(k + 1) * 128], identity=self.identb[:]), [ub, self.identb], [pb])
                if ii % 2 == 0:
                    fw.op("act", lambda e: e.activation(out=ut[:, ii, :], in_=pv[:, 0:1024], func=AF.Copy), [pb], [ut])
                else:
                    fw.op("dve", lambda e: e.tensor_copy(out=ut[:, ii, :], in_=pv[:, 0:1024]), [pb], [ut])
            fw.dma("sp", self.UT, UTv[:, c * 4:(c + 1) * 4, :], ut, ut[:])
        fw.pop()

    def peer(self, layer, hsrc, hdst, final=False):
        fw = self.fw
        I = self.I
        fw.push()
        GT, shT = self.modvecs(layer, 1)
        gate = self.gate_bcast(layer, 1)
        self.make_normer()
        obanks = self.banks[0:4]
        self.bank = Rot(self.banks[4:8])
        w_q = fw.sb("w_q", [128, 8, 2048], BF16)
        for k in range(8):
            fw.dma("pool", w_q, w_q[:, k, :], I["peer_wq"], I["peer_wq"].ap()[layer, k * 128:(k + 1) * 128, :])
        keyn = fw.sb("keyn", [128, 16, 128], BF16)
        fw.dma("pool", keyn, keyn[:], I["peer_keys"], I["peer_keys"].ap()[layer].rearrange("h k d -> k h d"))
        keysT = fw.sb("keysT", [128, 16, 128], BF16)
        for c in range(2):
            pb = self.bank.next(); pv = self.bview(pb)
            for q in range(8):
                fw.op("pe", lambda e: e.transpose(out=pv[:, q * 128:(q + 1) * 128], in_=keyn[:, c * 8 + q, :], identity=self.identb[:]), [keyn, self.identb], [pb])
            fw.op("act", lambda e: e.activation(out=keysT[:, c * 8:(c + 1) * 8, :].rearrange("p a b -> p (a b)"), in_=pv[:, 0:1024], func=AF.Copy), [pb], [keysT])
        xn2T = fw.sb("xn2T", [128, 8, 256], BF16)
        qT = fw.sb("pqT", [128, 16, 256], BF16)
        s_all = fw.sb("s_all", [128, 2, 16, 128])
        tau = fw.sb("tau", [128, 2, 8]); bias = fw.sb("pbias", [128, 2, 8])
        SB = [(fw.sb("m0", [128, 16]), fw.sb("m1", [128, 16]), fw.sb("scr", [128, 128]), fw.sb("cand", [128, 16, 16]), fw.sb("c24", [128, 24]),
               fw.sb("cscr", [128, 256]), fw.sb("st1", [128, 4]), fw.sb("e16", [128, 16])) for _ in range(4)]
        IB = 16
        Ws = Rot([fw.sb("W", [128, 2, IB, 128], BF16) for _ in range(2)])
        sums = Rot([fw.sb("sum", [128, IB, 128]) for _ in range(3)])
        Ps = Rot([fw.sb("P", [128, IB, 128], BF16) for _ in range(3)])
        Whs = Rot([fw.sb("Wh", [128, IB, 128], BF16) for _ in range(2)])
        Mks = Rot([fw.sb("Mk", [128, IB, 128], BF16) for _ in range(2)])
        UTs = Rot([fw.sb("UTs", [128, 2, 1024], BF16) for _ in range(2)])
        Vs = Rot([fw.sb("Vs", [128, 2, D], BF16) for _ in range(2)])
        Gs = Rot([fw.sb("Gs", [128, 256], BF16) for _ in range(4)])
        WAs = Rot([fw.sb("WAs", [128, 256], BF16) for _ in range(4)])
        tmps = Rot([fw.sb("ptmp", [128, D]) for _ in range(1)])
        hts = Rot([fw.sb("pht", [128, D]) for _ in range(2)])
        UTv = self.UT.ap().rearrange("i p k j -> p i (k j)")
        Vbv = self.Vb.ap().rearrange("i j d -> j i d")
        import os
        ng = int(os.environ.get("PEER_NG", NT // 2))
        for g in range(ng):
            if final and g == 0:
                continue
            s = 1 if g == 0 else 0
            for j in range(2):
                src, sap = self.src_of(2 * g + j, hsrc)
                self.norm_tile(src, sap, GT[s], shT[s], xn2T, slice(j * 128, (j + 1) * 128))
            for hp in range(16):
                pb = self.bank.next()
                for k in range(8):
                    fw.op("pe", lambda e: e.matmul(pb[:, 0:256], lhsT=w_q[:, k, hp * 128:(hp + 1) * 128], rhs=xn2T[:, k, :], start=(k == 0), stop=(k == 7)), [w_q, xn2T], [pb])
                fw.op("act", lambda e: e.activation(out=qT[:, hp, :], in_=pb[:, 0:256], func=AF.Copy), [pb], [qT])
            for t in range(2):
                for c in range(4):
                    pb = self.bank.next()
                    for q in range(4):
                        hp = c * 4 + q
                        fw.op("pe", lambda e: e.matmul(pb[:, q * 128:(q + 1) * 128], lhsT=qT[:, hp, t * 128:(t + 1) * 128], rhs=keysT[:, hp, :], start=True, stop=True), [qT, keysT], [pb])
                    fw.op("act", lambda e: e.activation(out=s_all[:, t, c * 4:(c + 1) * 4, :].rearrange("p a b -> p (a b)"), in_=pb[:], func=AF.Copy), [pb], [s_all])
            def stats_gen(t, h, B):
                m0, m1, scr, cand, c24, cscr, st1, e16 = B
                for (mm, src) in ((m0, s_all[:, t, 2 * h, :]), (m1, s_all[:, t, 2 * h + 1, :])):
                    fw.op("dve", lambda e: e.max(out=mm[:, 0:8], in_=src), [s_all], [mm]); yield
                    fw.op("dve", lambda e: e.match_replace(out=scr[:], in_to_replace=mm[:, 0:8], in_values=src, imm_value=-1e30), [s_all, mm], [scr]); yield
                    fw.op("dve", lambda e: e.max(out=mm[:, 8:16], in_=scr[:]), [scr], [mm]); yield
                fw.op("dve", lambda e: e.tensor_tensor(out=cand[:], in0=m0[:].unsqueeze(2).to_broadcast([128, 16, 16]), in1=m1[:].unsqueeze(1).to_broadcast([128, 16, 16]), op=ALU.add), [m0, m1], [cand]); yield
                cf = cand[:].rearrange("p a b -> p (a b)")
                fw.op("dve", lambda e: e.max(out=c24[:, 0:8], in_=cf), [cand], [c24]); yield
                fw.op("dve", lambda e: e.match_replace(out=cscr[:], in_to_replace=c24[:, 0:8], in_values=cf, imm_value=-1e30), [cand, c24], [cscr]); yield
                fw.op("dve", lambda e: e.max(out=c24[:, 8:16], in_=cscr[:]), [cscr], [c24]); yield
                fw.op("dve", lambda e: e.match_replace(out=cscr[:], in_to_replace=c24[:, 8:16], in_values=cscr[:], imm_value=-1e30), [cscr, c24], [cscr]); yield
                fw.op("dve", lambda e: e.max(out=c24[:, 16:24], in_=cscr[:]), [cscr], [c24]); yield
                fw.op("dve", lambda e: e.tensor_scalar(out=st1[:, 0:1], in0=c24[:, 16:17], scalar1=0.5, scalar2=None, op0=ALU.mult), [c24], [st1]); yield
                fw.op("dve", lambda e: e.scalar_tensor_tensor(out=tau[:, t, h:h + 1], in0=c24[:, 15:16], scalar=0.5, in1=st1[:, 0:1], op0=ALU.mult, op1=ALU.add), [c24, st1], [tau]); yield
                fw.op("dve", lambda e: e.tensor_scalar(out=st1[:, 1:2], in0=c24[:, 0:1], scalar1=-1.0, scalar2=None, op0=ALU.mult), [c24], [st1]); yield
                fw.op("act", lambda e: e.activation(out=e16[:], in_=c24[:, 0:16], func=AF.Exp, bias=st1[:, 1:2], scale=1.0, accum_out=st1[:, 2:3]), [c24, st1], [e16, st1]); yield
                fw.op("act", lambda e: e.activation(out=st1[:, 3:4], in_=st1[:, 2:3], func=AF.Ln), [st1], [st1]); yield
                fw.op("dve", lambda e: e.tensor_tensor(out=bias[:, t, h:h + 1], in0=st1[:, 1:2], in1=st1[:, 3:4], op=ALU.subtract), [st1], [bias]); yield

            for h0 in range(0, 8, 2):
                gens = [stats_gen(t, h0 + dh, SB[t * 2 + dh]) for t in range(2) for dh in range(2)]
                alive = True
                while alive:
                    alive = False
                    for gi in gens:
                        try:
                            next(gi); alive = True
                        except StopIteration:
                            pass
            NB = 128 // IB
            Wl = [None] * NB

            def build_unit(ib, u):
                i0 = ib * IB
                if u == 0:
                    Wl[ib] = Ws.next()
                W = Wl[ib]
                t, h = u // 8, u % 8
                sm = sums.next(); P = Ps.next()
                fw.op("dve", lambda e: e.tensor_tensor(out=sm[:], in0=s_all[:, t, 2 * h, i0:i0 + IB].unsqueeze(2).to_broadcast([128, IB, 128]),
                                                       in1=s_all[:, t, 2 * h + 1, :].unsqueeze(1).to_broadcast([128, IB, 128]), op=ALU.add), [s_all], [sm])
                def emitB():
                    fw.op("act", lambda e: e.activation(out=P[:], in_=sm[:], func=AF.Exp, bias=bias[:, t, h:h + 1], scale=1.0), [sm, bias], [P])

                def fin():
                    mk = Mks.next()
                    fw.op("dve", lambda e: e.tensor_scalar(out=mk[:], in0=sm[:], scalar1=tau[:, t, h:h + 1], scalar2=None, op0=ALU.is_ge), [sm, tau], [mk])
                    if h == 0:
                        fw.op("dve", lambda e: e.tensor_tensor(out=W[:, t, :, :], in0=mk[:], in1=P[:], op=ALU.mult), [mk, P], [W])
                    else:
                        Wh = Whs.next()
                        fw.op("dve", lambda e: e.tensor_tensor(out=Wh[:], in0=mk[:], in1=P[:], op=ALU.mult), [mk, P], [Wh])
                        eng = "pool" if h % 2 == 1 else "dve"
                        fw.op(eng, lambda e: e.tensor_tensor(out=W[:, t, :, :], in0=W[:, t, :, :], in1=Wh[:], op=ALU.add), [W, Wh], [W])
                return emitB, fin

            cur_ld = {}
            carry = {}

            def front(i):
                ib, iw = i // IB, i % IB
                W = Wl[ib]
                if i % 2 == 0:
                    UTt = UTs.next(); Vt = Vs.next()
                    fw.dma("sp", UTt, UTt[:], self.UT, UTv[:, i:i + 2, :])
                    fw.dma("sp", Vt, Vt[:], self.Vb, Vbv[:, i:i + 2, :])
                    cur_ld["u"] = UTt; cur_ld["v"] = Vt
                UTt = cur_ld["u"]; Vt = cur_ld["v"]
                i2 = i % 2
                pa = self.bank.next()
                for k in range(8):
                    fw.op("pe", lambda e: e.matmul(pa[:, 0:256], lhsT=UTt[:, i2, k * 128:(k + 1) * 128], rhs=xn2T[:, k, :], start=(k == 0), stop=(k == 7)), [UTt, xn2T], [pa])
                G = Gs.next()
                fw.op("act", lambda e: e.activation(out=G[:], in_=pa[:, 0:256], func=AF.Gelu), [pa], [G])
                pw = self.bank.next(); pwv = self.bview(pw)
                for t in range(2):
                    fw.op("pe", lambda e: e.transpose(out=pwv[:, t * 128:(t + 1) * 128], in_=W[:, t, iw, :], identity=self.identb[:]), [W, self.identb], [pw])
                WA = WAs.next()
                fw.op("dve", lambda e: e.tensor_tensor(out=WA[:], in0=G[:], in1=pwv[:, 0:256], op=ALU.mult), [G, pw], [WA])

                def back():
                    for t in range(2):
                        for half in range(2):
                            ob = obanks[t * 2 + half]
                            fw.op("pe", lambda e: e.matmul(ob[:], lhsT=WA[:, t * 128:(t + 1) * 128], rhs=Vt[:, i2, half * 512:(half + 1) * 512], start=(i == 0), stop=(i == 127)), [WA, Vt], [ob])
                return back

            pend = None
            for u in range(16):
                eb, f = build_unit(0, u)
                eb()
                if pend is not None:
                    pend()
                pend = f
            pend()
            backs = {0: front(0), 1: front(1)}
            sched = {0: [0, 1], 1: [2, 3]}
            for j_ in range(2, 14):
                sched[j_] = [j_ + 2]
            for i in range(128):
                ib, j = i // IB, i % IB
                units = sched.get(j, []) if ib + 1 < NB else []
                ebs = []; fins = []
                for u in units:
                    eb, f = build_unit(ib + 1, u)
                    ebs.append(eb); fins.append(f)
                if i + 2 < 128:
                    backs[i + 2] = front(i + 2)
                for eb in ebs:
                    eb()
                cp = carry.pop("f", None)
                if cp is not None:
                    cp()
                for f in fins[:-1]:
                    f()
                if fins:
                    if j == 13:
                        fins[-1]()
                    else:
                        carry["f"] = fins[-1]
                backs.pop(i)()
            for t in range(2):
                tile = 2 * g + t
                tmp = tmps.next(); ht = hts.next()
                src, sap = self.src_of(tile, hsrc)
                fw.dma("sp", ht, ht[:], src, sap)
                for half in range(2):
                    ob = obanks[t * 2 + half]
                    fw.op("dve", lambda e: e.tensor_tensor(out=tmp[:, half * 512:(half + 1) * 512], in0=ob[:], in1=gate[s][:, half * 512:(half + 1) * 512], op=ALU.mult), [ob, gate[s]], [tmp])
                fw.op("pool", lambda e: e.tensor_tensor(out=tmp[:], in0=tmp[:], in1=ht[:], op=ALU.add), [tmp, ht], [tmp])
                if final:
                    fw.dma("act", self.out, self.out.ap()[(tile - 2) * 128:(tile - 1) * 128, :], tmp, tmp[:])
                else:
                    fw.dma("act", hdst, hdst.ap()[tile * 128:(tile + 1) * 128, :], tmp, tmp[:])
        self.bank = Rot(self.banks)
        fw.pop()

    def peer_prep2(self, layer):
        fw = self.fw
        I = self.I
        fw.push()
        banks = Rot(self.banks)
        Uv = I["peer_u"].ap()[layer].rearrange("(i j) d -> i j d", j=128)
        Vv = I["peer_v"].ap()[layer].rearrange("(i j) d -> i j d", j=128)
        UTv = self.UT.ap().rearrange("j p k i -> p j (k i)")
        Vbv = self.Vb.ap().rearrange("j i d -> i j d")
        ubs = Rot([fw.sb("ub", [128, 4, D], BF16) for _ in range(2)])
        vbs = Rot([fw.sb("vb", [128, 4, D], BF16) for _ in range(2)])
        uts = Rot([fw.sb("ut", [128, 4, 1024], BF16) for _ in range(2)])
        for c in range(32):
            ub = ubs.next(); vb = vbs.next(); ut = uts.next()
            fw.dma("pool", ub, ub[:], I["peer_u"], Uv[:, c * 4:(c + 1) * 4, :])
            fw.dma("pool", vb, vb[:], I["peer_v"], Vv[:, c * 4:(c + 1) * 4, :])
            fw.dma("sp", self.Vb, Vbv[:, c * 4:(c + 1) * 4, :], vb, vb[:])
            for jj in range(4):
                pb = banks.next(); pv = self.bview(pb)
                for k in range(8):
                    fw.op("pe", lambda e: e.transpose(out=pv[:, k * 128:(k + 1) * 128], in_=ub[:, jj, k * 128:(k + 1) * 128], identity=self.identb[:]), [ub, self.identb], [pb])
                if jj % 2 == 0:
                    fw.op("act", lambda e: e.activation(out=ut[:, jj, :], in_=pv[:, 0:1024], func=AF.Copy), [pb], [ut])
                else:
                    fw.op("dve", lambda e: e.tensor_copy(out=ut[:, jj, :], in_=pv[:, 0:1024]), [pb], [ut])
            fw.dma("sp", self.UT, UTv[:, c * 4:(c + 1) * 4, :], ut, ut[:])
        fw.pop()

    def peer2(self, layer, hsrc, hdst, final=False):
        fw = self.fw
        I = self.I
        fw.push()
        GT, shT = self.modvecs(layer, 1)
        gate = self.gate_bcast(layer, 1)
        self.make_normer()
        obanks = self.banks[0:4]
        self.bank = Rot(self.banks[4:8])
        w_q = fw.sb("w_q", [128, 8, 2048], BF16)
        for k in range(8):
            fw.dma("pool", w_q, w_q[:, k, :], I["peer_wq"], I["peer_wq"].ap()[layer, k * 128:(k + 1) * 128, :])
        keyn = fw.sb("keyn", [128, 16, 128], BF16)
        fw.dma("pool", keyn, keyn[:], I["peer_keys"], I["peer_keys"].ap()[layer].rearrange("h k d -> k h d"))
        keysT = fw.sb("keysT", [128, 16, 128], BF16)
        for c in range(2):
            pb = self.bank.next(); pv = self.bview(pb)
            for q in range(8):
                fw.op("pe", lambda e: e.transpose(out=pv[:, q * 128:(q + 1) * 128], in_=keyn[:, c * 8 + q, :], identity=self.identb[:]), [keyn, self.identb], [pb])
            fw.op("act", lambda e: e.activation(out=keysT[:, c * 8:(c + 1) * 8, :].rearrange("p a b -> p (a b)"), in_=pv[:, 0:1024], func=AF.Copy), [pb], [keysT])
        iota = fw.sb("iota", [128, 128], BF16)
        fw.dma("pool", iota, iota[:], I["iota128"], I["iota128"].ap())
        xn2T = fw.sb("xn2T", [128, 8, 256], BF16)
        qT = fw.sb("pqT", [128, 16, 256], BF16)
        s_all = fw.sb("s_all", [128, 2, 16, 128])
        tau = fw.sb("tau", [128, 2, 8]); bias = fw.sb("pbias", [128, 2, 8])
        m0b = fw.sb("m0b", [128, 2, 8, 16])
        idxa = fw.sb("idxa", [128, 2, 128])
        idxT = fw.sb("idxT", [128, 256], BF16)
        SB = [(fw.sb("m0", [128, 16]), fw.sb("m1", [128, 16]), fw.sb("scr", [128, 128]), fw.sb("cand", [128, 16, 16]), fw.sb("c24", [128, 24]),
               fw.sb("cscr", [128, 256]), fw.sb("st1", [128, 4]), fw.sb("e16", [128, 16]), fw.sb("ix", [128, 16], U32)) for _ in range(4)]
        JB = 16
        sms = Rot([fw.sb("sm", [128, 8, 16, JB]) for _ in range(1)])
        Pss = Rot([fw.sb("P", [128, 8, 16, JB], BF16) for _ in range(1)])
        mks = Rot([fw.sb("mk", [128, 8, 16, JB], BF16) for _ in range(1)])
        Wp = fw.sb("Wp", [128, 2, 128, JB], BF16)
        WT = fw.sb("WT", [128, JB, 256], BF16)
        Walls = Rot([fw.sb("Wall", [128, 256, JB], BF16) for _ in range(2)])
        ohs = Rot([fw.sb("oh", [128, 32, 128], BF16) for _ in range(2)])
        UTs = Rot([fw.sb("UTs", [128, 2, 1024], BF16) for _ in range(2)])
        Vs = Rot([fw.sb("Vs", [128, 2, D], BF16) for _ in range(2)])
        Gs = Rot([fw.sb("Gs", [128, 256], BF16) for _ in range(4)])
        WAs = Rot([fw.sb("WAs", [128, 256], BF16) for _ in range(4)])
        tmps = Rot([fw.sb("ptmp", [128, D]) for _ in range(1)])
        hts = Rot([fw.sb("pht", [128, D]) for _ in range(2)])
        UTv = self.UT.ap().rearrange("j p k i -> p j (k i)")
        Vbv = self.Vb.ap().rearrange("j i d -> i j d")
        import os
        ng = int(os.environ.get("PEER_NG", NT // 2))
        NJB = 128 // JB
        for g in range(ng):
            if final and g == 0:
                continue
            s = 1 if g == 0 else 0
            for j in range(2):
                src, sap = self.src_of(2 * g + j, hsrc)
                self.norm_tile(src, sap, GT[s], shT[s], xn2T, slice(j * 128, (j + 1) * 128))
            for hp in range(16):
                pb = self.bank.next()
                for k in range(8):
                    fw.op("pe", lambda e: e.matmul(pb[:, 0:256], lhsT=w_q[:, k, hp * 128:(hp + 1) * 128], rhs=xn2T[:, k, :], start=(k == 0), stop=(k == 7)), [w_q, xn2T], [pb])
                fw.op("act", lambda e: e.activation(out=qT[:, hp, :], in_=pb[:, 0:256], func=AF.Copy), [pb], [qT])
            for t in range(2):
                for c in range(4):
                    pb = self.bank.next()
                    for q in range(4):
                        hp = c * 4 + q
                        fw.op("pe", lambda e: e.matmul(pb[:, q * 128:(q + 1) * 128], lhsT=qT[:, hp, t * 128:(t + 1) * 128], rhs=keysT[:, hp, :], start=True, stop=True), [qT, keysT], [pb])
                    fw.op("act", lambda e: e.activation(out=s_all[:, t, c * 4:(c + 1) * 4, :].rearrange("p a b -> p (a b)"), in_=pb[:], func=AF.Copy), [pb], [s_all])

            def stats_gen(t, h, B):
                m0, m1, scr, cand, c24, cscr, st1, e16, ix = B
                s0 = s_all[:, t, 2 * h, :]; s1 = s_all[:, t, 2 * h + 1, :]
                fw.op("dve", lambda e: e.max(out=m0[:, 0:8], in_=s0), [s_all], [m0]); yield
                fw.op("dve", lambda e: e.max_index(out=ix[:, 0:8], in_max=m0[:, 0:8], in_values=s0), [s_all, m0], [ix]); yield
                fw.op("dve", lambda e: e.match_replace(out=scr[:], in_to_replace=m0[:, 0:8], in_values=s0, imm_value=-1e30), [s_all, m0], [scr]); yield
                fw.op("dve", lambda e: e.max(out=m0[:, 8:16], in_=scr[:]), [scr], [m0]); yield
                fw.op("dve", lambda e: e.max_index(out=ix[:, 8:16], in_max=m0[:, 8:16], in_values=scr[:]), [scr, m0], [ix]); yield
                fw.op("dve", lambda e: e.tensor_copy(out=idxa[:, t, h * 16:(h + 1) * 16], in_=ix[:]), [ix], [idxa]); yield
                fw.op("dve", lambda e: e.max(out=m1[:, 0:8], in_=s1), [s_all], [m1]); yield
                fw.op("dve", lambda e: e.match_replace(out=scr[:], in_to_replace=m1[:, 0:8], in_values=s1, imm_value=-1e30), [s_all, m1], [scr]); yield
                fw.op("dve", lambda e: e.max(out=m1[:, 8:16], in_=scr[:]), [scr], [m1]); yield
                fw.op("dve", lambda e: e.tensor_tensor(out=cand[:], in0=m0[:].unsqueeze(2).to_broadcast([128, 16, 16]), in1=m1[:].unsqueeze(1).to_broadcast([128, 16, 16]), op=ALU.add), [m0, m1], [cand]); yield
                cf = cand[:].rearrange("p a b -> p (a b)")
                fw.op("dve", lambda e: e.max(out=c24[:, 0:8], in_=cf), [cand], [c24]); yield
                fw.op("dve", lambda e: e.match_replace(out=cscr[:], in_to_replace=c24[:, 0:8], in_values=cf, imm_value=-1e30), [cand, c24], [cscr]); yield
                fw.op("dve", lambda e: e.max(out=c24[:, 8:16], in_=cscr[:]), [cscr], [c24]); yield
                fw.op("dve", lambda e: e.match_replace(out=cscr[:], in_to_replace=c24[:, 8:16], in_values=cscr[:], imm_value=-1e30), [cscr, c24], [cscr]); yield
                fw.op("dve", lambda e: e.max(out=c24[:, 16:24], in_=cscr[:]), [cscr], [c24]); yield
                fw.op("dve", lambda e: e.tensor_scalar(out=st1[:, 0:1], in0=c24[:, 16:17], scalar1=0.5, scalar2=None, op0=ALU.mult), [c24], [st1]); yield
                fw.op("dve", lambda e: e.scalar_tensor_tensor(out=st1[:, 0:1], in0=c24[:, 15:16], scalar=0.5, in1=st1[:, 0:1], op0=ALU.mult, op1=ALU.add), [c24, st1], [st1]); yield
                fw.op("dve", lambda e: e.tensor_scalar(out=st1[:, 1:2], in0=c24[:, 0:1], scalar1=-1.0, scalar2=None, op0=ALU.mult), [c24], [st1]); yield
                fw.op("act", lambda e: e.activation(out=e16[:], in_=c24[:, 0:16], func=AF.Exp, bias=st1[:, 1:2], scale=1.0, accum_out=st1[:, 2:3]), [c24, st1], [e16, st1]); yield
                fw.op("act", lambda e: e.activation(out=st1[:, 3:4], in_=st1[:, 2:3], func=AF.Ln), [st1], [st1]); yield
                fw.op("dve", lambda e: e.tensor_tensor(out=bias[:, t, h:h + 1], in0=st1[:, 1:2], in1=st1[:, 3:4], op=ALU.subtract), [st1], [bias]); yield
                fw.op("dve", lambda e: e.tensor_scalar(out=m0b[:, t, h, :], in0=m0[:], scalar1=bias[:, t, h:h + 1], scalar2=None, op0=ALU.add), [m0, bias], [m0b]); yield
                fw.op("dve", lambda e: e.tensor_tensor(out=tau[:, t, h:h + 1], in0=st1[:, 0:1], in1=bias[:, t, h:h + 1], op=ALU.add), [st1, bias], [tau]); yield

            for h0 in range(0, 8, 2):
                gens = [stats_gen(t, h0 + dh, SB[t * 2 + dh]) for t in range(2) for dh in range(2)]
                alive = True
                while alive:
                    alive = False
                    for gi in gens:
                        try:
                            next(gi); alive = True
                        except StopIteration:
                            pass
            for t in range(2):
                pb = self.bank.next()
                fw.op("pe", lambda e: e.transpose(out=pb[:, 0:128], in_=idxa[:, t, :], identity=self.identf[:]), [idxa, self.identf], [pb])
                fw.op("act", lambda e: e.activation(out=idxT[:, t * 128:(t + 1) * 128], in_=pb[:, 0:128], func=AF.Copy), [pb], [idxT])

            Wl = [None] * NJB

            def build_items(jb):
                j0 = jb * JB
                items = []

                def rank(t):
                    def f():
                        sm = sms.next(); P = Pss.next(); mk = mks.next()
                        fw.op("dve", lambda e: e.tensor_tensor(out=sm[:].rearrange("p h a j -> p h (a j)").rearrange("p h (a j) -> p h a j", a=16),
                                                               in0=m0b[:, t, :, :].unsqueeze(3).to_broadcast([128, 8, 16, JB]),
                                                               in1=s_all[:, t, :, j0:j0 + JB].rearrange("p (h two) j -> p h two j", two=2)[:, :, 1, :].unsqueeze(2).to_broadcast([128, 8, 16, JB]),
                                                               op=ALU.add), [m0b, s_all], [sm])
                        fw.op("act", lambda e: e.activation(out=P[:].rearrange("p h a j -> p (h a j)"), in_=sm[:].rearrange("p h a j -> p (h a j)"), func=AF.Exp), [sm], [P])
                        fw.op("dve", lambda e: e.tensor_tensor(out=mk[:].rearrange("p h a j -> p h (a j)"), in0=sm[:].rearrange("p h a j -> p h (a j)"),
                                                               in1=tau[:, t, :].unsqueeze(2).to_broadcast([128, 8, 16 * JB]), op=ALU.is_ge), [sm, tau], [mk])
                        fw.op("dve", lambda e: e.tensor_tensor(out=Wp[:, t, :, :].rearrange("p (h a) j -> p h a j", a=16), in0=mk[:], in1=P[:], op=ALU.mult), [mk, P], [Wp])
                    return f
                items.append(rank(0)); items.append(rank(1))

                def transp(q):
                    def f():
                        pb = self.bank.next(); pv = self.bview(pb)
                        for jj in range(4):
                            for t in range(2):
                                c = jj * 2 + t
                                fw.op("pe", lambda e: e.transpose(out=pv[:, c * 128:(c + 1) * 128], in_=Wp[:, t, :, q * 4 + jj], identity=self.identb[:]), [Wp, self.identb], [pb])
                        fw.op("act", lambda e: e.activation(out=WT[:, q * 4:(q + 1) * 4, :].rearrange("p a b -> p (a b)"), in_=pv[:, 0:1024], func=AF.Copy), [pb], [WT])
                    return f
                for q in range(JB // 4):
                    items.append(transp(q))

                def scatter(nb):
                    def f():
                        if nb == 0:
                            Wl[jb] = Walls.next()
                        Wall = Wl[jb]
                        oh = ohs.next()
                        n0 = nb * 32
                        fw.op("dve", lambda e: e.tensor_tensor(out=oh[:], in0=idxT[:, n0:n0 + 32].unsqueeze(2).to_broadcast([128, 32, 128]),
                                                               in1=iota[:].unsqueeze(1).to_broadcast([128, 32, 128]), op=ALU.is_equal), [idxT, iota], [oh])
                        for half in range(2):
                            pb = self.bank.next()
                            for q in range(16):
                                n = n0 + half * 16 + q
                                fw.op("pe", lambda e: e.matmul(pb[:, q * JB:(q + 1) * JB], lhsT=oh[:, half * 16 + q, :], rhs=WT[:, :, n], start=True, stop=True), [oh, WT], [pb])
                            eng = "act" if half == 0 else "dve"
                            if eng == "act":
                                fw.op("act", lambda e: e.activation(out=Wall[:, n0 + half * 16:n0 + half * 16 + 16, :].rearrange("p a b -> p (a b)"), in_=pb[:, 0:16 * JB], func=AF.Copy), [pb], [Wall])
                            else:
                                fw.op("dve", lambda e: e.tensor_copy(out=Wall[:, n0 + half * 16:n0 + half * 16 + 16, :].rearrange("p a b -> p (a b)"), in_=pb[:, 0:16 * JB]), [pb], [Wall])
                    return f
                for nb in range(8):
                    items.append(scatter(nb))
                return items

            cur_ld = {}

            def front(jx):
                jb, jj = jx // JB, jx % JB
                Wall = Wl[jb]
                if jx % 2 == 0:
                    UTt = UTs.next(); Vt = Vs.next()
                    fw.dma("sp", UTt, UTt[:], self.UT, UTv[:, jx:jx + 2, :])
                    fw.dma("sp", Vt, Vt[:], self.Vb, Vbv[:, jx:jx + 2, :])
                    cur_ld["u"] = UTt; cur_ld["v"] = Vt
                UTt = cur_ld["u"]; Vt = cur_ld["v"]
                i2 = jx % 2
                pa = self.bank.next()
                for k in range(8):
                    fw.op("pe", lambda e: e.matmul(pa[:, 0:256], lhsT=UTt[:, i2, k * 128:(k + 1) * 128], rhs=xn2T[:, k, :], start=(k == 0), stop=(k == 7)), [UTt, xn2T], [pa])
                G = Gs.next()
                fw.op("act", lambda e: e.activation(out=G[:], in_=pa[:, 0:256], func=AF.Gelu), [pa], [G])
                WA = WAs.next()
                fw.op("dve", lambda e: e.tensor_tensor(out=WA[:], in0=G[:], in1=Wall[:, :, jj], op=ALU.mult), [G, Wall], [WA])

                def back():
                    for t in range(2):
                        for half in range(2):
                            ob = obanks[t * 2 + half]
                            fw.op("pe", lambda e: e.matmul(ob[:], lhsT=WA[:, t * 128:(t + 1) * 128], rhs=Vt[:, i2, half * 512:(half + 1) * 512], start=(jx == 0), stop=(jx == 127)), [WA, Vt], [ob])
                return back

            for it in build_items(0):
                it()
            backs = {0: front(0), 1: front(1)}
            pending = []
            for jx in range(128):
                jb, jj = jx // JB, jx % JB
                if jj == 0:
                    pending = build_items(jb + 1) if jb + 1 < NJB else []
                if pending and jj < JB - 2:
                    per = -(-len(pending) // max(1, (JB - 2 - jj)))
                    for _ in range(per):
                        if pending:
                            pending.pop(0)()
                if jx + 2 < 128:
                    backs[jx + 2] = front(jx + 2)
                backs.pop(jx)()
            for t in range(2):
                tile = 2 * g + t
                tmp = tmps.next(); ht = hts.next()
                src, sap = self.src_of(tile, hsrc)
                fw.dma("sp", ht, ht[:], src, sap)
                for half in range(2):
                    ob = obanks[t * 2 + half]
                    fw.op("dve", lambda e: e.tensor_tensor(out=tmp[:, half * 512:(half + 1) * 512], in0=ob[:], in1=gate[s][:, half * 512:(half + 1) * 512], op=ALU.mult), [ob, gate[s]], [tmp])
                fw.op("pool", lambda e: e.tensor_tensor(out=tmp[:], in0=tmp[:], in1=ht[:], op=ALU.add), [tmp, ht], [tmp])
                if final:
                    fw.dma("act", self.out, self.out.ap()[(tile - 2) * 128:(tile - 1) * 128, :], tmp, tmp[:])
                else:
                    fw.dma("act", hdst, hdst.ap()[tile * 128:(tile + 1) * 128, :], tmp, tmp[:])
        self.bank = Rot(self.banks)
        fw.pop()

    def l1_phaseA(self, hsrc):
        fw = self.fw
        I = self.I
        S = self.S1 = {}
        for nm in ("R", "KAP", "V", "G", "LW0", "LW1", "B0", "B1", "KD0", "KD1"):
            S[nm] = fw.dram("s1_" + nm, [TOK, D], F32, kind=("ExternalOutput" if self.dbg else "Internal"))
        S["BON"] = fw.dram("s1_BON", [TOK, 16], F32, kind=("ExternalOutput" if self.dbg else "Internal"))
        fw.push()
        GT, shT = self.modvecs(1, 0)
        self.make_normer()
        self.bank = Rot(self.banks)
        W = {}
        for nm in ("od_w_r", "od_w_k", "od_w_v"):
            W[nm] = fw.sb(nm, [128, 8, D], BF16)
            for k in range(8):
                fw.dma("pool", W[nm], W[nm][:, k, :], I[nm], I[nm].ap()[k * 128:(k + 1) * 128, :])
        g1 = fw.sb("g1", [128, 8, 128], BF16); g2 = fw.sb("g2", [128, D], BF16)
        fw.dma("pool", g1, g1[:], I["od_g1"], I["od_g1"].ap().rearrange("(k p) n -> p k n", p=128))
        fw.dma("pool", g2, g2[:], I["od_g2"], I["od_g2"].ap())
        w1c = fw.sb("w1c", [128, 8, 128], BF16); a1c = fw.sb("a1c", [128, 8, 128], BF16)
        for d in range(2):
            fw.dma("pool", w1c, w1c[:, :, d * 64:(d + 1) * 64], I["od_w1"], I["od_w1"].ap()[d].rearrange("(k p) n -> p k n", p=128))
            fw.dma("pool", a1c, a1c[:, :, d * 64:(d + 1) * 64], I["od_a1"], I["od_a1"].ap()[d].rearrange("(k p) n -> p k n", p=128))
        w2c = fw.sb("w2c", [128, D], BF16); a2c = fw.sb("a2c", [128, D], BF16)
        fw.dma("pool", w2c, w2c[:], I["od_w2"], I["od_w2"].ap().rearrange("d l n -> (d l) n"))
        fw.dma("pool", a2c, a2c[:], I["od_a2"], I["od_a2"].ap().rearrange("d l n -> (d l) n"))
        brow = fw.sb("brow", [1, 4, D], BF16)
        fw.dma("pool", brow, brow[:, 0:2, :], I["od_w0"], I["od_w0"].ap().rearrange("(o d) n -> o d n", o=1))
        fw.dma("pool", brow, brow[:, 2:4, :], I["od_a0"], I["od_a0"].ap().rearrange("(o d) n -> o d n", o=1))
        ones1 = fw.sb("ones1", [1, 128], BF16)
        fw.op("dve", lambda e: e.memset(ones1[:], 1.0), [], [ones1])
        kkb = fw.sb("kkb", [128, D]); kab = fw.sb("kab", [128, D]); rkb = fw.sb("rkb", [128, D])
        for t_, nm in ((kkb, "od_k_k"), (kab, "od_k_a"), (rkb, "od_r_k")):
            fw.dma("sp", t_, t_[:], I[nm], I[nm].ap().partition_broadcast(128))
        muT = fw.sb("muT", [128, 6, 8])
        for m in range(6):
            self.load_colT(muT, muT[:, m, :], I["od_mu"], I["od_mu"].ap()[m])
        NG = NT // 2
        win = [fw.sb("xnw", [128, 8, 256], BF16) for _ in range(4)]
        xxT = fw.sb("xxT", [128, 8, 256], BF16)
        xms = Rot([fw.sb("xm", [128, 8, 256], BF16) for _ in range(2)])
        tmpp = Rot([fw.sb("l1t", [128, D]) for _ in range(8)])
        ksb0 = fw.sb("ksb0", [128, D]); ksb1 = fw.sb("ksb1", [128, D])
        kap = fw.sb("kap", [128, D]); kka = fw.sb("kka", [128, D]); kmk = fw.sb("kmk", [128, D]); r_s = fw.sb("r_s", [128, D]); kd0 = fw.sb("kd0", [128, D])
        ss16 = Rot([fw.sb("ss16", [128, 16]) for _ in range(2)])
        lo = Rot([fw.sb("lo", [128, 128], BF16) for _ in range(3)])
        loT = Rot([fw.sb("loT", [128, 128], BF16) for _ in range(3)])

        def norm_group(g):
            s = 1 if g == 0 else 0
            for j in range(2):
                src, sap = self.src_of(2 * g + j, hsrc)
                self.norm_tile(src, sap, GT[s], shT[s], win[g % 4], slice(j * 128, (j + 1) * 128))

        def sub(out, a, b):
            fw.op("pool", lambda e: e.tensor_tensor(out=out, in0=a, in1=b, op=ALU.subtract), [win[0], win[1], win[2], win[3]], [xxT])

        def neg(out, a):
            fw.op("pool", lambda e: e.tensor_scalar(out=out, in0=a, scalar1=-1.0, scalar2=None, op0=ALU.mult), [win[0], win[1], win[2], win[3]], [xxT])

        def proj_tok(xm, t, Wt, nb=2):
            outs = []
            for half in range(nb):
                pb = self.bank.next()
                for k in range(8):
                    fw.op("pe", lambda e: e.matmul(pb[:], lhsT=xm[:, k, t * 128:(t + 1) * 128], rhs=Wt[:, k, half * 512:(half + 1) * 512], start=(k == 0), stop=(k == 7)), [xm, Wt], [pb])
                outs.append(pb)
            return outs

        def evac(dst, pbs):
            for half, pb in enumerate(pbs):
                fw.op("act", lambda e: e.activation(out=dst[:, half * 512:(half + 1) * 512], in_=pb[:], func=AF.Copy), [pb], [dst])

        def store(nm, tile, src, ap=None):
            fw.dma("sp", S[nm], S[nm].ap()[tile * 128:(tile + 1) * 128, :], src, src[:] if ap is None else ap)

        def lerp(m, cur):
            xm = xms.next()
            for k in range(8):
                fw.op("dve", lambda e: e.scalar_tensor_tensor(out=xm[:, k, :], in0=xxT[:, k, :], scalar=muT[:, m, k:k + 1], in1=cur[:, k, :], op0=ALU.mult, op1=ALU.add), [xxT, muT, cur], [xm])
            return xm

        def lora(xm, t, w1, w2, brow_i, func):
            pb = self.bank.next()
            for k in range(8):
                fw.op("pe", lambda e: e.matmul(pb[:, 0:128], lhsT=xm[:, k, t * 128:(t + 1) * 128], rhs=w1[:, k, :], start=(k == 0), stop=(k == 7)), [xm, w1], [pb])
            l_ = lo.next()
            fw.op("act", lambda e: e.activation(out=l_[:], in_=pb[:, 0:128], func=func), [pb], [l_])
            pt = self.bank.next(); pv = self.bview(pt)
            fw.op("pe", lambda e: e.transpose(out=pv[:, 0:128], in_=l_[:], identity=self.identb[:]), [l_, self.identb], [pt])
            lT = loT.next()
            fw.op("act", lambda e: e.activation(out=lT[:], in_=pv[:, 0:128], func=AF.Copy), [pt], [lT])
            res = []
            for d in range(2):
                outs = []
                for half in range(2):
                    po = self.bank.next()
                    fw.op("pe", lambda e: e.matmul(po[:], lhsT=ones1[:], rhs=brow[:, brow_i + d, half * 512:(half + 1) * 512], start=True, stop=False), [ones1, brow], [po])
                    fw.op("pe", lambda e: e.matmul(po[:], lhsT=lT[d * 64:(d + 1) * 64, :], rhs=w2[d * 64:(d + 1) * 64, half * 512:(half + 1) * 512], start=False, stop=True), [lT, w2], [po])
                    outs.append(po)
                res.append(outs)
            return res

        norm_group(0)
        import os
        for g in range(int(os.environ.get("L1A_NG", NG))):
            if g + 1 < NG:
                norm_group(g + 1)
            cur = win[g % 4]; prv = win[(g - 1) % 4]; nxt = win[(g + 1) % 4]
            for k in range(8):
                c = cur[:, k, :]; o = xxT[:, k, :]
                if g == 0:
                    if k < 4:
                        sub(o[:, 1:256], c[:, 0:255], c[:, 1:256]); neg(o[:, 0:1], c[:, 0:1])
                    else:
                        sub(o[:, 0:255], c[:, 1:256], c[:, 0:255]); neg(o[:, 255:256], c[:, 255:256])
                else:
                    c3 = c.rearrange("p (r c) -> p r c", c=64); o3 = o.rearrange("p (r c) -> p r c", c=64)
                    if k < 2:
                        sub(o3[:, :, 1:64], c3[:, :, 0:63], c3[:, :, 1:64]); neg(o3[:, :, 0:1], c3[:, :, 0:1])
                    elif k < 4:
                        sub(o3[:, :, 0:63], c3[:, :, 1:64], c3[:, :, 0:63]); neg(o3[:, :, 63:64], c3[:, :, 63:64])
                    elif k < 6:
                        sub(o[:, 64:256], c[:, 0:192], c[:, 64:256])
                        if g == 1:
                            neg(o[:, 0:64], c[:, 0:64])
                        else:
                            sub(o[:, 0:64], prv[:, k, 192:256], c[:, 0:64])
                    else:
                        sub(o[:, 0:192], c[:, 64:256], c[:, 0:192])
                        if g == NG - 1:
                            neg(o[:, 192:256], c[:, 192:256])
                        else:
                            sub(o[:, 192:256], nxt[:, k, 0:64], c[:, 192:256])
            xr = lerp(0, cur)
            for t in range(2):
                tile = 2 * g + t
                rr_ = tmpp.next()
                evac(rr_, proj_tok(xr, t, W["od_w_r"]))
                store("R", tile, rr_)
            xk = lerp(2, cur)
            ktiles = [ksb0, ksb1]
            for t in range(2):
                evac(ktiles[t], proj_tok(xk, t, W["od_w_k"]))
            xv = lerp(3, cur)
            for t in range(2):
                v_ = tmpp.next()
                evac(v_, proj_tok(xv, t, W["od_w_v"]))
                store("V", 2 * g + t, v_)
            xg = lerp(5, cur)
            for t in range(2):
                pb = self.bank.next()
                for k in range(8):
                    fw.op("pe", lambda e: e.matmul(pb[:, 0:128], lhsT=xg[:, k, t * 128:(t + 1) * 128], rhs=g1[:, k, :], start=(k == 0), stop=(k == 7)), [xg, g1], [pb])
                l_ = lo.next()
                fw.op("act", lambda e: e.activation(out=l_[:], in_=pb[:, 0:128], func=AF.Sigmoid), [pb], [l_])
                pt = self.bank.next(); pv = self.bview(pt)
                fw.op("pe", lambda e: e.transpose(out=pv[:, 0:128], in_=l_[:], identity=self.identb[:]), [l_, self.identb], [pt])
                lT = loT.next()
                fw.op("act", lambda e: e.activation(out=lT[:], in_=pv[:, 0:128], func=AF.Copy), [pt], [lT])
                g_ = tmpp.next()
                outs = []
                for half in range(2):
                    po = self.bank.next()
                    fw.op("pe", lambda e: e.matmul(po[:], lhsT=lT[:], rhs=g2[:, half * 512:(half + 1) * 512], start=True, stop=True), [lT, g2], [po])
                    outs.append(po)
                evac(g_, outs)
                store("G", 2 * g + t, g_)
            xw = lerp(1, cur)
            xa = lerp(4, cur)
            for t in range(2):
                tile = 2 * g + t
                ksb = ktiles[t]
                kkx = tmpp.next(); sq = tmpp.next()
                fw.op("pool", lambda e: e.tensor_tensor(out=kkx[:], in0=ksb[:], in1=kkb[:], op=ALU.mult), [ksb, kkb], [kkx])
                fw.op("pool", lambda e: e.tensor_tensor(out=sq[:], in0=kkx[:], in1=kkx[:], op=ALU.mult), [kkx], [sq])
                s16 = ss16.next()
                fw.op("dve", lambda e: e.reduce_sum(out=s16[:], in_=sq[:].rearrange("p (h j) -> p h j", j=64), axis=AX.X), [sq], [s16])
                fw.op("act", lambda e: e.activation(out=s16[:], in_=s16[:], func=AF.Sqrt, bias=1e-12, scale=1.0), [s16], [s16])
                fw.op("dve", lambda e: e.reciprocal(out=s16[:], in_=s16[:]), [s16], [s16])
                fw.op("dve", lambda e: e.tensor_tensor(out=kap[:].rearrange("p (h j) -> p h j", j=64), in0=kkx[:].rearrange("p (h j) -> p h j", j=64),
                                                       in1=s16[:].unsqueeze(2).to_broadcast([128, 16, 64]), op=ALU.mult), [kkx, s16], [kap])
                store("KAP", tile, kap)
                fw.op("pool", lambda e: e.tensor_tensor(out=kka[:], in0=ksb[:], in1=kab[:], op=ALU.mult), [ksb, kab], [kka])
                fw.op("pool", lambda e: e.tensor_tensor(out=kmk[:], in0=ksb[:], in1=kka[:], op=ALU.subtract), [ksb, kka], [kmk])
                wl = lora(xw, t, w1c, w2c, 0, AF.Tanh)
                for d in range(2):
                    lw = tmpp.next()
                    for half in range(2):
                        fw.op("act", lambda e: e.activation(out=lw[:, half * 512:(half + 1) * 512], in_=wl[d][half][:], func=AF.Sigmoid), [wl[d][half]], [lw])
                    fw.op("pool", lambda e: e.tensor_scalar(out=lw[:], in0=lw[:], scalar1=-0.6065306597126334, scalar2=None, op0=ALU.mult), [lw], [lw])
                    store("LW%d" % d, tile, lw)
                al = lora(xa, t, a1c, a2c, 2, AF.Copy)
                avs = []
                for d in range(2):
                    a_ = tmpp.next()
                    for half in range(2):
                        fw.op("act", lambda e: e.activation(out=a_[:, half * 512:(half + 1) * 512], in_=al[d][half][:], func=AF.Sigmoid), [al[d][half]], [a_])
                    avs.append(a_)
                kds = []
                for d in range(2):
                    a_ = avs[d]
                    b_ = tmpp.next()
                    fw.op("pool", lambda e: e.tensor_tensor(out=b_[:], in0=kap[:], in1=a_[:], op=ALU.mult), [kap, a_], [b_])
                    store("B%d" % d, tile, b_)
                    kd = kd0 if d == 0 else tmpp.next()
                    fw.op("pool", lambda e: e.tensor_tensor(out=kd[:], in0=kka[:], in1=a_[:], op=ALU.mult), [kka, a_], [kd])
                    fw.op("pool", lambda e: e.tensor_tensor(out=kd[:], in0=kd[:], in1=kmk[:], op=ALU.add), [kd, kmk], [kd])
                    store("KD%d" % d, tile, kd)
                    kds.append(kd)
                fw.dma("sp", r_s, r_s[:], S["R"], S["R"].ap()[tile * 128:(tile + 1) * 128, :])
                ks = tmpp.next()
                fw.op("pool", lambda e: e.tensor_tensor(out=ks[:], in0=kds[0][:], in1=kds[1][:], op=ALU.add), [kds[0], kds[1]], [ks])
                fw.op("pool", lambda e: e.tensor_tensor(out=ks[:], in0=ks[:], in1=rkb[:], op=ALU.mult), [ks, rkb], [ks])
                fw.op("pool", lambda e: e.tensor_tensor(out=ks[:], in0=ks[:], in1=r_s[:], op=ALU.mult), [ks, r_s], [ks])
                bon = ss16.next()
                fw.op("dve", lambda e: e.reduce_sum(out=bon[:], in_=ks[:].rearrange("p (h j) -> p h j", j=64), axis=AX.X), [ks], [bon])
                fw.dma("sp", S["BON"], S["BON"].ap()[tile * 128:(tile + 1) * 128, :], bon, bon[:])
        fw.pop()

    def l1_phaseB(self):
        fw = self.fw
        I = self.I
        S = self.S1
        k_ = "ExternalOutput" if self.dbg else "Internal"
        self.Y = [fw.dram("s1_Y%d" % d, [TOK, D], F32, kind=k_) for d in range(2)]
        fw.push()
        ybanks = self.banks[0:2]
        pbanks = Rot(self.banks[2:8])
        hb = []
        hbr = Rot([(self.banks[2 + (i % 6)], (i // 6) * 256) for i in range(12)])
        sml = hbr
        tm = fw.sb("tm", [128, 4, 128])
        fw.dma("sp", tm, tm[:], I["trimask"], I["trimask"].ap())
        ones = fw.sb("ones", [128, 128])
        fw.op("dve", lambda e: e.memset(ones[:], 1.0), [], [ones])
        MA = []; MB = []; MN = []; MC = []
        for d in range(2):
            mS, mI, mSn, mC = (0, 1, 2, 1) if d == 0 else (2, 3, 0, 3)
            a = fw.sb("MA", [128, 2, 128]); b = fw.sb("MB", [128, 2, 128]); n = fw.sb("MN", [128, 128])
            fw.op("dve", lambda e: e.tensor_scalar(out=a[:, 0, :], in0=tm[:, mS, :], scalar1=-1.0, scalar2=None, op0=ALU.mult), [tm], [a])
            fw.op("dve", lambda e: e.tensor_copy(out=a[:, 1, :], in_=tm[:, mI, :]), [tm], [a])
            fw.op("dve", lambda e: e.tensor_copy(out=b[:, 0, :], in_=tm[:, mS, :]), [tm], [b])
            fw.op("dve", lambda e: e.tensor_copy(out=b[:, 1, :], in_=tm[:, mI, :]), [tm], [b])
            fw.op("dve", lambda e: e.tensor_scalar(out=n[:], in0=tm[:, mSn, :], scalar1=-1.0, scalar2=None, op0=ALU.mult), [tm], [n])
            MA.append(a); MB.append(b); MN.append(n); MC.append(mC)
        H = fw.sb("H", [64, 16, 64])
        ld = {nm: Rot([fw.sb("ld" + nm, [128, D]) for _ in range(2)]) for nm in ("R", "KAP", "V", "LW", "B", "KD")}
        prep = {nm: fw.sb("pp" + nm, [128, D], BF16 if nm in ("rt", "kt", "bt", "kdt") else F32) for nm in ("rt", "kt", "bt", "kdt", "bh", "kh")}
        Vb = fw.sb("Vbf", [128, D], BF16); Hb = fw.sb("Hb", [64, 16, 64], BF16)
        cumS = fw.sb("cumS", [128, 512]); et = Rot([fw.sb("et", [128, 512]) for _ in range(3)])
        gC = fw.sb("gC", [64, 16])
        NH = 16
        TT = Rot([fw.sb("TT", [64, 4, 128], BF16) for _ in range(NH)])
        SC = Rot([fw.sb("SC", [128, 4, 128], BF16) for _ in range(NH)])
        NM = Rot([fw.sb("NM", [128, 2, 128], BF16) for _ in range(NH * 3)])
        XS = Rot([fw.sb("XS", [128, 64], BF16) for _ in range(NH * 3)])
        UF = Rot([fw.sb("UF", [128, 64]) for _ in range(NH)])
        ysb = Rot([fw.sb("ysb", [128, D]) for _ in range(2)])
        import os
        nchunks = int(os.environ.get("L1B_NC", NT))

        def head_gen(h, d, T_):
            R_, KAP_, V_, LW_, B_, KD_ = T_
            hs = slice(h * 64, (h + 1) * 64)
            tt = TT.next(); sc = SC.next()
            for pair, (x0, x1) in enumerate(((prep["kt"], prep["rt"]), (prep["bt"], prep["kdt"]))):
                bk, off = hbr.next()
                bv = self.bview(bk)
                fw.op("pe", lambda e: e.transpose(out=bv[0:64, 2 * off:2 * off + 128], in_=x0[:, hs], identity=self.identb[:]), [x0, self.identb], [bk])
                fw.op("pe", lambda e: e.transpose(out=bv[0:64, 2 * off + 128:2 * off + 256], in_=x1[:, hs], identity=self.identb[:]), [x1, self.identb], [bk])
                fw.op("act", lambda e: e.activation(out=tt[:, 2 * pair:2 * pair + 2, :].rearrange("p a b -> p (a b)"), in_=bv[0:64, 2 * off:2 * off + 256], func=AF.Copy), [bk], [tt])
            yield
            kT = tt[:, 0, :]; rT = tt[:, 1, :]; bT = tt[:, 2, :]; kdT = tt[:, 3, :]
            kr = tt[:, 0:2, :].rearrange("p a b -> p (a b)")
            bk, off = hbr.next()
            fw.op("pe", lambda e: e.matmul(bk[:, off:off + 256], lhsT=bT, rhs=kr, start=True, stop=True), [tt], [bk])
            fw.op("dve", lambda e: e.tensor_tensor(out=sc[:, 0:2, :].rearrange("p a b -> p (a b)"), in0=bk[:, off:off + 256], in1=MA[d][:].rearrange("p a b -> p (a b)"), op=ALU.mult), [bk, MA[d]], [sc])
            bk, off = hbr.next()
            fw.op("pe", lambda e: e.matmul(bk[:, off:off + 256], lhsT=kdT, rhs=kr, start=True, stop=True), [tt], [bk])
            fw.op("dve", lambda e: e.tensor_tensor(out=sc[:, 2:4, :].rearrange("p a b -> p (a b)"), in0=bk[:, off:off + 256], in1=MB[d][:].rearrange("p a b -> p (a b)"), op=ALU.mult), [bk, MB[d]], [sc])
            nm = NM.next()
            bk, off = hbr.next()
            fw.op("pe", lambda e: e.matmul(bk[:, off:off + 128], lhsT=kT, rhs=bT, start=True, stop=True), [tt], [bk])
            fw.op("dve", lambda e: e.tensor_tensor(out=nm[:, 0, :], in0=bk[:, off:off + 128], in1=MN[d][:], op=ALU.mult), [bk, MN[d]], [nm])
            fw.op("act", lambda e: e.activation(out=nm[:, 1, :], in_=sc[:, 0, :], func=AF.Copy), [sc], [nm])
            yield
            NT_ = sc[:, 0, :]; BrbT = sc[:, 1, :]; AkT = sc[:, 2, :]; BrkT = sc[:, 3, :]
            bk, off = sml.next()
            fw.op("pe", lambda e: e.matmul(bk[:, off:off + 64], lhsT=kT, rhs=Hb[:, h, :], start=True, stop=False), [tt, Hb], [bk])
            fw.op("pe", lambda e: e.matmul(bk[:, off:off + 64], lhsT=AkT, rhs=Vb[:, hs], start=False, stop=True), [sc, Vb], [bk])
            X = XS.next()
            fw.op("act", lambda e: e.activation(out=X[:], in_=bk[:, off:off + 64], func=AF.Copy, scale=-1.0), [bk], [X])
            yield
            cur = nm
            for lvl in range(7):
                bk, off = sml.next()
                fw.op("pe", lambda e: e.matmul(bk[:, off:off + 64], lhsT=cur[:, 1, :], rhs=X[:], start=True, stop=True), [cur, X], [bk])
                X2 = XS.next() if lvl < 6 else UF.next()
                fw.op("dve", lambda e: e.tensor_tensor(out=X2[:], in0=bk[:, off:off + 64], in1=X[:], op=ALU.add), [bk, X], [X2])
                X = X2
                if lvl < 6:
                    nxt = NM.next()
                    bk, off = hbr.next()
                    if lvl < 5:
                        fw.op("pe", lambda e: e.matmul(bk[:, off:off + 128], lhsT=cur[:, 1, :], rhs=cur[:, 0, :], start=True, stop=True), [cur], [bk])
                    fw.op("pe", lambda e: e.matmul(bk[:, off + 128:off + 256], lhsT=cur[:, 0, :], rhs=cur[:, 1, :], start=True, stop=True), [cur], [bk])
                    if lvl < 5:
                        fw.op("act", lambda e: e.activation(out=nxt[:].rearrange("p a b -> p (a b)"), in_=bk[:, off:off + 256], func=AF.Copy), [bk], [nxt])
                    else:
                        fw.op("act", lambda e: e.activation(out=nxt[:, 1, :], in_=bk[:, off + 128:off + 256], func=AF.Copy), [bk], [nxt])
                    cur = nxt
                yield
            U = X
            Ub = XS.next()
            fw.op("act", lambda e: e.activation(out=Ub[:], in_=U[:], func=AF.Copy), [U], [Ub])
            yb = ybanks[h // 8]; yo = (h % 8) * 64
            fw.op("pe", lambda e: e.matmul(yb[:, yo:yo + 64], lhsT=rT, rhs=Hb[:, h, :], start=True, stop=False), [tt, Hb], [yb])
            fw.op("pe", lambda e: e.matmul(yb[:, yo:yo + 64], lhsT=BrbT, rhs=Ub[:], start=False, stop=False), [sc, Ub], [yb])
            fw.op("pe", lambda e: e.matmul(yb[:, yo:yo + 64], lhsT=BrkT, rhs=Vb[:, hs], start=False, stop=True), [sc, Vb], [yb])
            bk, off = sml.next()
            fw.op("pe", lambda e: e.matmul(bk[0:64, off:off + 64], lhsT=prep["bh"][:, hs], rhs=U[:], start=True, stop=False), [prep["bh"], U], [bk])
            fw.op("pe", lambda e: e.matmul(bk[0:64, off:off + 64], lhsT=prep["kh"][:, hs], rhs=V_[:, hs], start=False, stop=True), [prep["kh"], V_], [bk])
            fw.op("dve", lambda e: e.scalar_tensor_tensor(out=H[:, h, :], in0=H[:, h, :], scalar=gC[:, h:h + 1], in1=bk[0:64, off:off + 64], op0=ALU.mult, op1=ALU.add), [H, gC, bk], [H])
            yield

        for d in range(2):
            fw.op("dve", lambda e: e.memset(H[:], 0.0), [], [H])
            fw.op("dve", lambda e: e.memset(Hb[:], 0.0), [], [Hb])
            order = list(range(NT)) if d == 0 else [1, 0] + list(range(NT - 1, 1, -1))
            for c in order[:nchunks]:
                rows = slice(c * 128, (c + 1) * 128)
                T_ = []
                for nm, key in (("R", "R"), ("KAP", "KAP"), ("V", "V"), ("LW", "LW%d" % d), ("B", "B%d" % d), ("KD", "KD%d" % d)):
                    t_ = ld[nm].next()
                    fw.dma("sp", t_, t_[:], S[key], S[key].ap()[rows, :])
                    T_.append(t_)
                R_, KAP_, V_, LW_, B_, KD_ = T_
                fw.op("pool", lambda e: e.tensor_copy(out=Vb[:], in_=V_[:]), [V_], [Vb])
                bk, off = sml.next()
                for h in range(16):
                    fw.op("pe", lambda e: e.matmul(bk[0:64, off + h:off + h + 1], lhsT=LW_[:, h * 64:(h + 1) * 64], rhs=ones[:, 0:1], start=True, stop=True), [LW_, ones], [bk])
                fw.op("act", lambda e: e.activation(out=gC[:], in_=bk[0:64, off:off + 16], func=AF.Exp), [bk], [gC])
                for half in range(2):
                    cs = slice(half * 512, (half + 1) * 512)
                    pc = pbanks.next(); ptot = pbanks.next()
                    fw.op("pe", lambda e: e.matmul(pc[:], lhsT=tm[:, MC[d], :], rhs=LW_[:, cs], start=True, stop=True), [tm, LW_], [pc])
                    fw.op("pe", lambda e: e.matmul(ptot[:], lhsT=ones[:], rhs=LW_[:, cs], start=True, stop=True), [ones, LW_], [ptot])
                    fw.op("act", lambda e: e.activation(out=cumS[:], in_=pc[:], func=AF.Copy), [pc], [cumS])
                    e1 = et.next()
                    fw.op("act", lambda e: e.activation(out=e1[:], in_=pc[:], func=AF.Exp), [pc], [e1])
                    fw.op("pool", lambda e: e.tensor_tensor(out=prep["rt"][:, cs], in0=R_[:, cs], in1=e1[:], op=ALU.mult), [R_, e1], [prep["rt"]])
                    e2 = et.next()
                    fw.op("act", lambda e: e.activation(out=e2[:], in_=pc[:], func=AF.Exp, scale=-1.0), [pc], [e2])
                    fw.op("pool", lambda e: e.tensor_tensor(out=prep["bt"][:, cs], in0=B_[:, cs], in1=e2[:], op=ALU.mult), [B_, e2], [prep["bt"]])
                    fw.op("pool", lambda e: e.tensor_tensor(out=prep["kdt"][:, cs], in0=KD_[:, cs], in1=e2[:], op=ALU.mult), [KD_, e2], [prep["kdt"]])
                    e3 = et.next()
                    fw.op("dve", lambda e: e.tensor_tensor(out=e3[:], in0=cumS[:], in1=LW_[:, cs], op=ALU.subtract), [cumS, LW_], [e3])
                    fw.op("act", lambda e: e.activation(out=e3[:], in_=e3[:], func=AF.Exp), [e3], [e3])
                    fw.op("pool", lambda e: e.tensor_tensor(out=prep["kt"][:, cs], in0=KAP_[:, cs], in1=e3[:], op=ALU.mult), [KAP_, e3], [prep["kt"]])
                    e4 = et.next()
                    fw.op("dve", lambda e: e.tensor_tensor(out=e4[:], in0=ptot[:], in1=cumS[:], op=ALU.subtract), [ptot, cumS], [e4])
                    fw.op("act", lambda e: e.activation(out=e4[:], in_=e4[:], func=AF.Exp), [e4], [e4])
                    fw.op("dve", lambda e: e.tensor_tensor(out=prep["bh"][:, cs], in0=B_[:, cs], in1=e4[:], op=ALU.mult), [B_, e4], [prep["bh"]])
                    fw.op("dve", lambda e: e.tensor_tensor(out=prep["kh"][:, cs], in0=KD_[:, cs], in1=e4[:], op=ALU.mult), [KD_, e4], [prep["kh"]])
                for h0 in range(0, 16, NH):
                    gens = [head_gen(h, d, T_) for h in range(h0, h0 + NH)]
                    alive = True
                    while alive:
                        alive = False
                        for gi in gens:
                            try:
                                next(gi)
                                alive = True
                            except StopIteration:
                                pass
                fw.op("act", lambda e: e.activation(out=Hb[:].rearrange("p a b -> p (a b)"), in_=H[:].rearrange("p a b -> p (a b)"), func=AF.Copy), [H], [Hb])
                ys = ysb.next()
                for half in range(2):
                    fw.op("act", lambda e: e.activation(out=ys[:, half * 512:(half + 1) * 512], in_=ybanks[half][:], func=AF.Copy), [ybanks[half]], [ys])
                fw.dma("act", self.Y[d], self.Y[d].ap()[rows, :], ys, ys[:])
        fw.pop()

    def l1_phaseC(self, hsrc, hdst):
        fw = self.fw
        I = self.I
        S = self.S1
        fw.push()
        self.bank = Rot(self.banks)
        gate = self.gate_bcast(1, 0)
        w_o = fw.sb("w_o", [128, 8, D], BF16)
        for k in range(8):
            fw.dma("pool", w_o, w_o[:, k, :], I["od_w_o"], I["od_w_o"].ap()[k * 128:(k + 1) * 128, :])
        gnw = fw.sb("gnw", [128, D]); gnb = fw.sb("gnb", [128, D])
        fw.dma("sp", gnw, gnw[:], I["od_gn_w"], I["od_gn_w"].ap().partition_broadcast(128))
        fw.dma("sp", gnb, gnb[:], I["od_gn_b"], I["od_gn_b"].ap().partition_broadcast(128))
        L = {nm: Rot([fw.sb("c" + nm, [128, D]) for _ in range(2)]) for nm in ("y0", "y1", "v", "g", "h")}
        bons = Rot([fw.sb("cbon", [128, 16]) for _ in range(2)])
        st = Rot([fw.sb("cst", [128, 16]) for _ in range(6)])
        wk = Rot([fw.sb("cwk", [128, D]) for _ in range(4)])
        obf = Rot([fw.sb("cob", [128, D], BF16) for _ in range(2)])
        oTs = Rot([fw.sb("coT", [128, 8, 128], BF16) for _ in range(2)])
        v3 = lambda t_: t_[:].rearrange("p (h j) -> p h j", j=64)
        b3 = lambda t_: t_[:].unsqueeze(2).to_broadcast([128, 16, 64])
        for tile in range(2, NT):
            rows = slice(tile * 128, (tile + 1) * 128)
            y0 = L["y0"].next(); y1 = L["y1"].next(); v = L["v"].next(); g = L["g"].next(); ht = L["h"].next(); bon = bons.next()
            fw.dma("sp", y0, y0[:], self.Y[0], self.Y[0].ap()[rows, :])
            fw.dma("sp", y1, y1[:], self.Y[1], self.Y[1].ap()[rows, :])
            fw.dma("sp", v, v[:], S["V"], S["V"].ap()[rows, :])
            fw.dma("sp", g, g[:], S["G"], S["G"].ap()[rows, :])
            fw.dma("sp", bon, bon[:], S["BON"], S["BON"].ap()[rows, :])
            src, sap = self.src_of(tile, hsrc)
            fw.dma("sp", ht, ht[:], src, sap)
            ysum = wk.next()
            fw.op("pool", lambda e: e.tensor_tensor(out=ysum[:], in0=y0[:], in1=y1[:], op=ALU.add), [y0, y1], [ysum])
            mean = st.next(); var = st.next()
            fw.op("dve", lambda e: e.reduce_sum(out=mean[:], in_=v3(ysum), axis=AX.X), [ysum], [mean])
            fw.op("dve", lambda e: e.tensor_scalar(out=mean[:], in0=mean[:], scalar1=1.0 / 64, scalar2=None, op0=ALU.mult), [mean], [mean])
            yc = wk.next()
            fw.op("dve", lambda e: e.tensor_tensor(out=v3(yc), in0=v3(ysum), in1=b3(mean), op=ALU.subtract), [ysum, mean], [yc])
            sq = wk.next()
            fw.op("pool", lambda e: e.tensor_tensor(out=sq[:], in0=yc[:], in1=yc[:], op=ALU.mult), [yc], [sq])
            fw.op("dve", lambda e: e.reduce_sum(out=var[:], in_=v3(sq), axis=AX.X), [sq], [var])
            fw.op("act", lambda e: e.activation(out=var[:], in_=var[:], func=AF.Sqrt, bias=64e-5, scale=1.0 / 64), [var], [var])
            fw.op("dve", lambda e: e.reciprocal(out=var[:], in_=var[:]), [var], [var])
            fw.op("dve", lambda e: e.tensor_tensor(out=v3(yc), in0=v3(yc), in1=b3(var), op=ALU.mult), [yc, var], [yc])
            fw.op("pool", lambda e: e.tensor_tensor(out=yc[:], in0=yc[:], in1=gnw[:], op=ALU.mult), [yc, gnw], [yc])
            fw.op("pool", lambda e: e.tensor_tensor(out=yc[:], in0=yc[:], in1=gnb[:], op=ALU.add), [yc, gnb], [yc])
            bv = wk.next()
            fw.op("dve", lambda e: e.tensor_tensor(out=v3(bv), in0=v3(v), in1=b3(bon), op=ALU.mult), [v, bon], [bv])
            fw.op("pool", lambda e: e.tensor_tensor(out=yc[:], in0=yc[:], in1=bv[:], op=ALU.add), [yc, bv], [yc])
            ob = obf.next()
            fw.op("dve", lambda e: e.tensor_tensor(out=ob[:], in0=yc[:], in1=g[:], op=ALU.mult), [yc, g], [ob])
            pb = self.bank.next(); pv = self.bview(pb)
            for k in range(8):
                fw.op("pe", lambda e: e.transpose(out=pv[:, k * 128:(k + 1) * 128], in_=ob[:, k * 128:(k + 1) * 128], identity=self.identb[:]), [ob, self.identb], [pb])
            oT = oTs.next()
            fw.op("act", lambda e: e.activation(out=oT[:].rearrange("p k n -> p (k n)"), in_=pv[:, 0:1024], func=AF.Copy), [pb], [oT])
            tmp = wk.next()
            for half in range(2):
                po = self.bank.next()
                for k in range(8):
                    fw.op("pe", lambda e: e.matmul(po[:], lhsT=oT[:, k, :], rhs=w_o[:, k, half * 512:(half + 1) * 512], start=(k == 0), stop=(k == 7)), [oT, w_o], [po])
                fw.op("dve", lambda e: e.tensor_tensor(out=tmp[:, half * 512:(half + 1) * 512], in0=po[:], in1=gate[0][:, half * 512:(half + 1) * 512], op=ALU.mult), [po, gate[0]], [tmp])
            fw.op("pool", lambda e: e.tensor_tensor(out=tmp[:], in0=tmp[:], in1=ht[:], op=ALU.add), [tmp, ht], [tmp])
            fw.dma("act", hdst, hdst.ap()[rows, :], tmp, tmp[:])
        fw.pop()

    def finish(self):
        fw = self.fw
        fw.barrier()
        fw.finish("sp", [self.out])
        print("ninst", fw.ninst, "nsem", fw.nsem)
        self.es.close()
        return self.nc


def host_consts():
    c = {}
    c["ident"] = np.eye(128, dtype=np.float32)
    t = np.arange(NLAT)
    row = (t // 64).astype(np.float32); col = (t % 64).astype(np.float32)
    inv = (10000.0 ** (-np.arange(0, 32, 2, dtype=np.float32) / 32)).astype(np.float32)
    ang = np.concatenate([row[:, None] * inv, col[:, None] * inv], axis=-1)
    cosl = np.cos(ang).astype(np.float32); sinl = np.sin(ang).astype(np.float32)
    cos = np.concatenate([np.ones((NCTX, 32), np.float32), cosl], 0)
    sin = np.concatenate([np.zeros((NCTX, 32), np.float32), sinl], 0)
    d = np.arange(128) % 64
    c["ropeC"] = np.ascontiguousarray(cos[:, d // 2].T)
    c["ropeS"] = np.ascontiguousarray(sin[:, d // 2].T)
    P = np.zeros((128, 128), np.float32)
    for i in range(64):
        P[2 * i + 1, 2 * i] = -1.0
        P[2 * i, 2 * i + 1] = 1.0
    c["ropeP"] = P
    b = np.zeros((128, 128), np.float32)
    b[:64, :64] = 1; b[64:, 64:] = 1
    c["blk64"] = b
    s_ = np.arange(128)[:, None]; t_ = np.arange(128)[None, :]
    tm = np.zeros((128, 4, 128), np.float32)
    tm[:, 0] = (s_ < t_); tm[:, 1] = (s_ <= t_); tm[:, 2] = (s_ > t_); tm[:, 3] = (s_ >= t_)
    c["trimask"] = tm
    c["iota128"] = np.tile(np.arange(128, dtype=np.float32)[None, :], (128, 1))
    return c


def shard_inputs(inputs):
    f = lambda a: np.ascontiguousarray(np.asarray(a, dtype=np.float32))
    w_in = f(inputs["ev_w_in"])[0]
    w_in_r = np.concatenate([w_in[:, 0:2048], w_in[:, 2048:2112], w_in[:, 2048:2112], w_in[:, 2112:2176], w_in[:, 2112:2176], w_in[:, 2176:2304]], axis=1)
    shared = dict(
        ada_w=f(inputs["ada_w"]), ada_b=f(inputs["ada_b"]), norm1_g=f(inputs["norm1_g"]), norm2_g=f(inputs["norm2_g"]),
        w_in=f(w_in_r), conv_w=f(inputs["ev_conv_w"])[0], qg2=f(np.tile(f(inputs["ev_q_gain"])[0], 2)), kg2=f(np.tile(f(inputs["ev_k_gain"])[0], 2)),
        w_out=f(inputs["ev_w_out"])[0], od_mu=f(inputs["od_mu"])[0], od_w_r=f(inputs["od_w_r"])[0], od_w_k=f(inputs["od_w_k"])[0],
        od_w_v=f(inputs["od_w_v"])[0], od_w_o=f(inputs["od_w_o"])[0], od_g1=f(inputs["od_g1"])[0], od_g2=f(inputs["od_g2"])[0],
        od_k_k=f(inputs["od_k_k"])[0], od_k_a=f(inputs["od_k_a"])[0], od_r_k=f(inputs["od_r_k"])[0].reshape(-1),
        od_w0=f(inputs["od_w0"])[0], od_w1=f(inputs["od_w1"])[0], od_w2=f(inputs["od_w2"])[0], od_a0=f(inputs["od_a0"])[0],
        od_a1=f(inputs["od_a1"])[0], od_a2=f(inputs["od_a2"])[0], od_gn_w=f(inputs["od_gn_w"])[0], od_gn_b=f(inputs["od_gn_b"])[0],
        peer_wq=f(inputs["peer_wq"]), peer_keys=f(inputs["peer_keys"]).reshape(2, 16, 128, 128), peer_u=f(inputs["peer_u"]), peer_v=f(inputs["peer_v"]),
    )
    shared.update(host_consts())
    x = f(inputs["x"]); ctx = f(inputs["ctx"]); c = f(inputs["c"]); cc = f(inputs["c_ctx"])
    maps = []
    for b in range(8):
        m = dict(shared)
        m["x"] = x[b]; m["ctx"] = ctx[b]; m["cvec"] = np.ascontiguousarray(np.stack([c[b], cc], 0))
        maps.append(m)
    return maps


def build(dbg=False, stop=None):
    p = Prog(dbg=dbg, stop=stop)
    p.phase0()
    if stop == "p0":
        return p
    if stop and stop.startswith("L1"):
        p.l1_phaseA(p.h2)
        if stop == "L1a":
            return p
        p.l1_phaseB()
        if stop == "L1b":
            return p
        p.l1_phaseC(p.h2, p.h3)
        return p
    p.l0_phaseA()
    if stop == "l0a":
        p.fw.pop()
        return p
    p.l0_phaseB()
    if stop == "l0":
        return p
    if os.environ.get("PEER_V", "2") == "2":
        p.peer_prep2(0)
        p.peer2(0, p.h1, p.h2)
    else:
        p.peer_prep(0)
        p.peer(0, p.h1, p.h2)
    if stop == "peer0":
        return p
    p.l1_phaseA(p.h2)
    if stop == "l1a":
        return p
    p.l1_phaseB()
    if stop == "l1b":
        return p
    p.l1_phaseC(p.h2, p.h3)
    if stop == "l1c":
        return p
    if os.environ.get("PEER_V", "2") == "2":
        p.peer_prep2(1)
        p.peer2(1, p.h3, None, final=True)
    else:
        p.peer_prep(1)
        p.peer(1, p.h3, None, final=True)
    return p


def kernel(**inputs):
    p = build()
    nc = p.finish()
    maps = shard_inputs(inputs)
    maps = [{k: v for k, v in m.items() if k in p.I} for m in maps]
    res = run_bass_kernel_spmd(nc, maps, core_ids=list(range(8)))
    return np.stack([r["out"] for r in res.results], 0)
```

```python
import os
import numpy as np
from contextlib import ExitStack
import concourse.bass as bass
import concourse.mybir as mybir
from concourse.bass_utils import run_bass_kernel_spmd

F32 = mybir.dt.float32
BF16 = mybir.dt.bfloat16
U32 = mybir.dt.uint32
ALU = mybir.AluOpType
AF = mybir.ActivationFunctionType
AX = mybir.AxisListType

D = 1024
NCTX = 256
NLAT = 4096
TOK = NCTX + NLAT
NT = TOK // 128
TOKP = TOK + 4
EPS = 1e-6
INPUT_SHAPES = {
    "x": [NLAT, D],
    "ctx": [NCTX, D],
    "cvec": [2, D],
    "ada_w": [2, D, 6 * D],
    "ada_b": [2, 6 * D],
    "norm1_g": [2, D],
    "norm2_g": [2, D],
    "w_in": [D, 2432],
    "conv_w": [3, 512],
    "qg2": [128],
    "kg2": [128],
    "w_out": [D, D],
    "od_mu": [6, D],
    "od_w_r": [D, D],
    "od_w_k": [D, D],
    "od_w_v": [D, D],
    "od_w_o": [D, D],
    "od_g1": [D, 128],
    "od_g2": [128, D],
    "od_k_k": [D],
    "od_k_a": [D],
    "od_r_k": [D],
    "od_w0": [2, D],
    "od_w1": [2, D, 64],
    "od_w2": [2, 64, D],
    "od_a0": [2, D],
    "od_a1": [2, D, 64],
    "od_a2": [2, 64, D],
    "od_gn_w": [D],
    "od_gn_b": [D],
    "peer_wq": [2, D, 2048],
    "peer_keys": [2, 16, 128, 128],
    "peer_u": [2, 16384, D],
    "peer_v": [2, 16384, D],
    "ident": [128, 128],
    "ropeC": [128, TOK],
    "ropeS": [128, TOK],
    "ropeP": [128, 128],
    "blk64": [128, 128],
    "trimask": [128, 4, 128],
    "iota128": [128, 128],
}


class T:
    __slots__ = ("t", "w", "r", "sem", "dcount", "name")

    def __init__(self, t, name):
        self.t = t
        self.name = name
        self.w = {}
        self.r = {}
        self.sem = None
        self.dcount = 0

    def __getitem__(self, k):
        return self.t[k]

    def ap(self):
        return self.t.ap()


class FW:
    def __init__(self, nc, es):
        self.nc = nc
        self.es = es
        self.root_es = es
        self.stack = []
        self.live = []
        self.drams = []
        self.sempool = []
        self.uid = 0
        self.eng = {"pe": nc.tensor, "act": nc.scalar, "dve": nc.vector, "pool": nc.gpsimd, "sp": nc.sync}
        self.esem = {}
        self.ecount = {}
        self.known = {}
        self.nsem = 0
        for k in self.eng:
            self.esem[k] = self.newsem("e_" + k)
            self.ecount[k] = 0
            self.known[k] = {}
        self.selfwait = {"pe": False, "act": True, "dve": True, "pool": True, "sp": False}
        self.ninst = 0

    def newsem(self, name):
        self.nsem += 1
        assert self.nsem < 200, "too many semaphores"
        return self.root_es.enter_context(self.nc.semaphore(name))

    def push(self):
        self.stack.append((self.es, self.live))
        self.es = ExitStack()
        self.live = []

    def pop(self):
        self.barrier()
        for t in self.live:
            if t.sem is not None:
                self.sempool.append((t.sem, t.dcount))
                t.sem = None
        self.es.close()
        self.es, self.live = self.stack.pop()

    def barrier(self):
        toks = {}
        for k in self.eng:
            if self.ecount[k] > 0:
                toks[self.esem[k]] = self.ecount[k]
        for _, live in self.stack + [(None, self.live)]:
            for t in live:
                if t.sem is not None and t.dcount > 0:
                    toks[t.sem] = 16 * t.dcount
        for t in self.drams:
            if t.sem is not None and t.dcount > 0:
                toks[t.sem] = 16 * t.dcount
        for e in self.eng:
            kn = self.known[e]
            for s_, v in toks.items():
                if s_ is self.esem[e]:
                    continue
                if kn.get(s_, 0) < v:
                    self.eng[e].wait_ge(s_, v)
                    kn[s_] = v
                    self.ninst += 1

    def sb(self, name, shape, dt=F32):
        self.uid += 1
        t = T(self.es.enter_context(self.nc.sbuf_tensor("%s_%d" % (name, self.uid), list(shape), dt)), name)
        self.live.append(t)
        return t

    def ps(self, name, shape, dt=F32):
        t = T(self.es.enter_context(self.nc.psum_tensor(name, list(shape), dt)), name)
        self.live.append(t)
        return t

    def dram(self, name, shape, dt=F32, kind="Internal"):
        t = T(self.nc.dram_tensor(name, list(shape), dt, kind=kind), name)
        self.drams.append(t)
        return t

    def _waits(self, e, reads, writes):
        toks = {}

        def add(tok):
            if tok is None:
                return
            s, v = tok
            if toks.get(s, 0) < v:
                toks[s] = v
        for r in reads:
            for s, v in r.w.items():
                add((s, v))
        for w in writes:
            for s, v in w.w.items():
                add((s, v))
            for s, v in w.r.items():
                add((s, v))
        eng = self.eng[e]
        kn = self.known[e]
        for s, v in toks.items():
            if s is self.esem[e] and not self.selfwait[e]:
                continue
            if kn.get(s, 0) < v:
                eng.wait_ge(s, v)
                kn[s] = v
                self.ninst += 1

    def op(self, e, fn, reads=(), writes=()):
        self._waits(e, reads, writes)
        inst = fn(self.eng[e])
        self.ecount[e] += 1
        self.ninst += 1
        s = self.esem[e]
        inst.then_inc(s, 1)
        tok = (s, self.ecount[e])
        for w in writes:
            w.w = {s: tok[1]}
            w.r = {}
        for r in reads:
            if r.r.get(s, 0) < tok[1]:
                r.r[s] = tok[1]
        return inst

    def dma(self, e, outT, out_ap, inT, in_ap, **kw):
        self._waits(e, [inT], [outT])
        own = inT if (outT in self.drams and inT not in self.drams) else outT
        if own.sem is None:
            if self.sempool:
                own.sem, own.dcount = self.sempool.pop()
            else:
                own.sem = self.newsem("d%d" % self.nsem)
        inst = self.eng[e].dma_start(out=out_ap, in_=in_ap, **kw)
        inst.then_inc(own.sem, 16)
        self.ninst += 1
        own.dcount += 1
        tok = (own.sem, 16 * own.dcount)
        if own is outT:
            outT.w = {tok[0]: tok[1]}
        else:
            outT.w[tok[0]] = tok[1]
        outT.r = {}
        if inT.r.get(tok[0], 0) < tok[1]:
            inT.r[tok[0]] = tok[1]
        return inst

    def finish(self, e, tiles):
        self._waits(e, tiles, [])


class Rot:
    def __init__(self, items):
        self.items = items
        self.i = -1

    def next(self):
        self.i = (self.i + 1) % len(self.items)
        return self.items[self.i]


def tokcol(tile):
    return 1 + tile * 128 if tile < 2 else 259 + (tile - 2) * 128


class Prog:
    def __init__(self, dbg=False, stop=None):
        self.dbg = dbg
        self.stop = stop
        self.nc = bass.Bass("TRN2", target_bir_lowering=False)
        self.es = ExitStack()
        self.fw = FW(self.nc, self.es)
        fw = self.fw

        prog = self

        class Lazy(dict):
            def __missing__(d, name):
                t = fw.dram(name, INPUT_SHAPES[name], F32, kind="ExternalInput")
                d[name] = t
                return t
        self.I = Lazy()
        self.out = fw.dram("out", [NLAT, D], F32, kind="ExternalOutput")
        k = "ExternalOutput" if dbg else "Internal"
        self.modrow = fw.dram("modrow", [2, 2, 6 * D], F32, kind=k)
        self.h1 = fw.dram("h1", [TOK, D], F32, kind=k)
        self.h2 = fw.dram("h2", [TOK, D], F32, kind=("ExternalInput" if stop and stop.startswith("L1") else k))
        self.h3 = fw.dram("h3", [TOK, D], F32, kind=k)
        self.uT = fw.dram("uT", [512, TOKP], BF16)
        if dbg:
            self.dza = fw.dram("dza", [TOK, 512], BF16, kind="ExternalOutput")
            self.dzc = fw.dram("dzc", [512, TOK], BF16, kind="ExternalOutput")
        self.gbT = fw.dram("gbT", [512, TOKP], BF16)
        self.UT = fw.dram("UT", [128, 128, 8, 128], BF16)
        self.Vb = fw.dram("Vb", [128, 128, D], BF16)
        self.banks = [fw.ps("bank%d" % i, [128, 512]) for i in range(8)]
        self.bank = Rot(self.banks)
        self.identf = fw.sb("identf", [128, 128])
        self.identb = fw.sb("identb", [128, 128], BF16)
        fw.dma("sp", self.identf, self.identf[:], self.I["ident"], self.I["ident"].ap())
        fw.dma("pool", self.identb, self.identb[:], self.I["ident"], self.I["ident"].ap())

    def bview(self, bank):
        return bank.ap().bitcast(BF16)

    def load_colT(self, dst, dst_ap, src, src_ap1d):
        self.fw.dma("sp", dst, dst_ap, src, src_ap1d.rearrange("(k p) -> p k", p=128), allow_slow_non_contiguous=True)

    def modvecs(self, layer, which):
        fw = self.fw
        I = self.I
        gname = "norm1_g" if which == 0 else "norm2_g"
        g = fw.sb("g", [128, 8])
        self.load_colT(g, g[:], I[gname], I[gname].ap()[layer])
        GT, shT = [], []
        for s in range(2):
            sc = fw.sb("scl", [128, 8]); sh = fw.sb("shf", [128, 8]); G = fw.sb("G", [128, 8])
            base = which * 3 * D
            self.load_colT(sh, sh[:], self.modrow, self.modrow.ap()[layer, s, base:base + D])
            self.load_colT(sc, sc[:], self.modrow, self.modrow.ap()[layer, s, base + D:base + 2 * D])
            fw.op("dve", lambda e: e.scalar_tensor_tensor(out=G[:], in0=sc[:], scalar=1.0, in1=g[:], op0=ALU.add, op1=ALU.mult), [sc, g], [G])
            GT.append(G); shT.append(sh)
        return GT, shT

    def gate_bcast(self, layer, which):
        fw = self.fw
        res = []
        for s in range(2):
            gt = fw.sb("gate", [128, D])
            off = which * 3 * D + 2 * D
            fw.dma("sp", gt, gt[:], self.modrow, self.modrow.ap()[layer, s, off:off + D].partition_broadcast(128))
            res.append(gt)
        return res

    def make_normer(self, nh=2):
        fw = self.fw
        self.nb_h = Rot([fw.sb("nh", [128, D]) for _ in range(nh)])
        self.nb_sq = fw.sb("nsq", [128, D], BF16)
        self.nb_ss = Rot([fw.sb("nss", [128, 1]) for _ in range(2)])
        self.nb_rs = Rot([fw.sb("nrs", [128, 1]) for _ in range(2)])
        self.nb_hn = Rot([fw.sb("nhn", [128, D], BF16) for _ in range(2)])

    def norm_tile(self, src, src_ap, G, sh, xnT, xnT_cols):
        fw = self.fw
        h = self.nb_h.next(); ss = self.nb_ss.next(); rs = self.nb_rs.next(); hn = self.nb_hn.next(); sq = self.nb_sq
        fw.dma("sp", h, h[:], src, src_ap)
        fw.op("act", lambda e: e.activation(out=sq[:], in_=h[:], func=AF.Square, scale=1.0 / 32, accum_out=ss[:]), [h], [sq, ss])
        fw.op("act", lambda e: e.activation(out=rs[:], in_=ss[:], func=AF.Sqrt, bias=EPS, scale=1.0), [ss], [rs])
        fw.op("dve", lambda e: e.reciprocal(out=rs[:], in_=rs[:]), [rs], [rs])
        fw.op("act", lambda e: e.activation(out=hn[:], in_=h[:], func=AF.Copy, scale=rs[:]), [h, rs], [hn])
        pb = self.bank.next()
        pv = self.bview(pb)
        for k in range(8):
            fw.op("pe", lambda e: e.transpose(out=pv[:, k * 128:(k + 1) * 128], in_=hn[:, k * 128:(k + 1) * 128], identity=self.identb[:]), [hn, self.identb], [pb])
        for k in range(8):
            fw.op("act", lambda e: e.activation(out=xnT[:, k, xnT_cols], in_=pv[:, k * 128:(k + 1) * 128], func=AF.Identity,
                                                scale=G[:, k:k + 1], bias=sh[:, k:k + 1]), [pb, G, sh], [xnT])
        return h

    def src_of(self, tile, hsrc):
        if hsrc is None:
            if tile < 2:
                return self.I["ctx"], self.I["ctx"].ap()[tile * 128:(tile + 1) * 128, :]
            return self.I["x"], self.I["x"].ap()[(tile - 2) * 128:(tile - 1) * 128, :]
        return hsrc, hsrc.ap()[tile * 128:(tile + 1) * 128, :]

    def phase0(self):
        fw = self.fw
        I = self.I
        fw.push()
        sc = fw.sb("sc", [128, 2, 8])
        for s_ in range(2):
            self.load_colT(sc, sc[:, s_, :], I["cvec"], I["cvec"].ap()[s_])
        fw.op("act", lambda e: e.activation(out=sc[:], in_=sc[:], func=AF.Silu), [sc], [sc])
        wb = Rot([fw.sb("adaw", [128, 8, 512]) for _ in range(2)])
        for i in range(2):
            bias = fw.sb("adab", [2, 6 * D])
            modr = fw.sb("modr", [2, 6 * D])
            fw.dma("sp", bias, bias[:], I["ada_b"], I["ada_b"].ap()[i].partition_broadcast(2))
            for cb in range(12):
                wt = wb.next()
                fw.dma("sp" if cb % 2 == 0 else "act", wt, wt[:], I["ada_w"], I["ada_w"].ap()[i, :, cb * 512:(cb + 1) * 512].rearrange("(k p) n -> p k n", p=128))
                pb = self.bank.next()
                for k in range(8):
                    fw.op("pe", lambda e: e.matmul(pb[0:2, :], lhsT=sc[:, :, k], rhs=wt[:, k, :], start=(k == 0), stop=(k == 7)), [sc, wt], [pb])
                fw.op("dve", lambda e: e.tensor_tensor(out=modr[:, cb * 512:(cb + 1) * 512], in0=pb[0:2, :], in1=bias[:, cb * 512:(cb + 1) * 512], op=ALU.add), [pb, bias], [modr])
            fw.dma("sp", self.modrow, self.modrow.ap()[i], modr, modr[:])
        fw.pop()

    def l0_phaseA(self):
        fw = self.fw
        I = self.I
        fw.push()
        self.qT = fw.sb("qT", [128, 4, TOK], BF16)
        self.kT2 = fw.sb("kT2", [128, 2, TOK], BF16)
        self.vext = fw.sb("vext", [128, NT, 2, 66], BF16)
        fw.op("pool", lambda e: e.memset(self.vext[:], 1.0), [], [self.vext])
        fw.push()
        GT, shT = self.modvecs(0, 0)
        self.make_normer()
        w_in = fw.sb("w_in", [128, 8, 2432], BF16)
        for k in range(8):
            fw.dma("pool", w_in, w_in[:, k, :], I["w_in"], I["w_in"].ap()[k * 128:(k + 1) * 128, :])
        rC = fw.sb("rC", [128, TOK], BF16); rS = fw.sb("rS", [128, TOK], BF16)
        fw.dma("pool", rC, rC[:], I["ropeC"], I["ropeC"].ap())
        fw.dma("pool", rS, rS[:], I["ropeS"], I["ropeS"].ap())
        ropeP = fw.sb("ropeP", [128, 128], BF16); blk64 = fw.sb("blk64", [128, 128], BF16)
        fw.dma("pool", ropeP, ropeP[:], I["ropeP"], I["ropeP"].ap())
        fw.dma("pool", blk64, blk64[:], I["blk64"], I["blk64"].ap())
        gains = fw.sb("gains", [128, 2])
        fw.dma("sp", gains, gains[:, 0:1], I["qg2"], I["qg2"].ap().rearrange("(p o) -> p o", o=1))
        fw.dma("sp", gains, gains[:, 1:2], I["kg2"], I["kg2"].ap().rearrange("(p o) -> p o", o=1))
        zt = fw.sb("zt", [128, 4, 2], BF16)
        fw.op("pool", lambda e: e.memset(zt[:], 0.0), [], [zt])
        for dst in (self.uT,):
            v = dst.ap().rearrange("(b p) c -> p b c", p=128)
            fw.dma("pool", dst, v[:, :, 0:1], zt, zt[:, :, 0:1], allow_slow_non_contiguous=True)
            fw.dma("pool", dst, v[:, :, 257:259], zt, zt[:, :, 0:2], allow_slow_non_contiguous=True)
            fw.dma("pool", dst, v[:, :, TOKP - 1:TOKP], zt, zt[:, :, 0:1], allow_slow_non_contiguous=True)
        xnTs = Rot([fw.sb("xnT", [128, 8, 256], BF16) for _ in range(2)])
        hTt = fw.sb("hTt", [128, 4, 256])
        gbst = Rot([fw.sb("gbst", [128, 4, 256], BF16) for _ in range(2)])
        ust = Rot([fw.sb("ust", [128, 4, 256], BF16) for _ in range(2)])
        qraw = Rot([fw.sb("qraw", [128, 256]) for _ in range(2)])
        sqb = Rot([fw.sb("sqb", [128, 256], BF16) for _ in range(2)])
        rsb = Rot([fw.sb("rsb", [128, 256]) for _ in range(2)])
        qnb = Rot([fw.sb("qnb", [128, 256], BF16) for _ in range(2)])
        t1b = Rot([fw.sb("t1b", [128, 256]) for _ in range(2)])
        t2b = Rot([fw.sb("t2b", [128, 256]) for _ in range(2)])
        uTv = self.uT.ap().rearrange("(b p) c -> p b c", p=128)
        gbTv = self.gbT.ap().rearrange("(b p) c -> p b c", p=128)
        import os
        skip = os.environ.get('L0A_SKIP', '').split(',')
        for g in range(int(os.environ.get('L0A_NG', NT // 2))):
            s = 1 if g == 0 else 0
            xnT = xnTs.next()
            for j in range(2):
                tile = 2 * g + j
                src, sap = self.src_of(tile, None)
                self.norm_tile(src, sap, GT[s], shT[s], xnT, slice(j * 128, (j + 1) * 128))
            tok0 = g * 256
            gb_s = gbst.next(); u_s = ust.next()
            for fb in range(18):
                if fb >= 12 and 'qk' in skip:
                    continue
                if fb < 12 and 'conv' in skip:
                    continue
                pb = self.bank.next()
                for k in range(8):
                    fw.op("pe", lambda e: e.matmul(pb[:, 0:256], lhsT=w_in[:, k, fb * 128:(fb + 1) * 128], rhs=xnT[:, k, :], start=(k == 0), stop=(k == 7)), [w_in, xnT], [pb])
                if fb < 4:
                    fw.op("act", lambda e: e.activation(out=hTt[:, fb, :], in_=pb[:, 0:256], func=AF.Copy), [pb], [hTt])
                elif fb < 8:
                    fw.op("act", lambda e: e.activation(out=gb_s[:, fb - 4, :], in_=pb[:, 0:256], func=AF.Copy), [pb], [gb_s])
                elif fb < 12:
                    fw.op("dve", lambda e: e.tensor_tensor(out=u_s[:, fb - 8, :], in0=pb[:, 0:256], in1=hTt[:, fb - 8, :], op=ALU.mult), [pb, hTt], [u_s])
                else:
                    isq = fb < 16
                    qr = qraw.next(); sq = sqb.next(); rs = rsb.next(); qn = qnb.next(); t1 = t1b.next(); t2 = t2b.next()
                    gcol = gains[:, 0:1] if isq else gains[:, 1:2]
                    fw.op("act", lambda e: e.activation(out=qr[:], in_=pb[:, 0:256], func=AF.Copy), [pb], [qr])
                    fw.op("act", lambda e: e.activation(out=sq[:], in_=pb[:, 0:256], func=AF.Square), [pb], [sq])
                    p2 = self.bank.next()
                    fw.op("pe", lambda e: e.matmul(p2[:, 0:256], lhsT=blk64[:], rhs=sq[:], start=True, stop=True), [blk64, sq], [p2])
                    fw.op("act", lambda e: e.activation(out=rs[:], in_=p2[:, 0:256], func=AF.Sqrt, bias=EPS, scale=1.0 / 64), [p2], [rs])
                    fw.op("dve", lambda e: e.reciprocal(out=rs[:], in_=rs[:]), [rs], [rs])
                    fw.op("dve", lambda e: e.scalar_tensor_tensor(out=qn[:], in0=qr[:], scalar=gcol, in1=rs[:], op0=ALU.mult, op1=ALU.mult), [qr, gains, rs], [qn])
                    p3 = self.bank.next()
                    fw.op("pe", lambda e: e.matmul(p3[:, 0:256], lhsT=ropeP[:], rhs=qn[:], start=True, stop=True), [ropeP, qn], [p3])
                    fw.op("dve", lambda e: e.tensor_tensor(out=t2[:], in0=p3[:, 0:256], in1=rS[:, tok0:tok0 + 256], op=ALU.mult), [p3, rS], [t2])
                    fw.op("pool", lambda e: e.tensor_tensor(out=t1[:], in0=qn[:], in1=rC[:, tok0:tok0 + 256], op=ALU.mult), [qn, rC], [t1])
                    dst = self.qT if isq else self.kT2
                    bi = fb - 12 if isq else fb - 16
                    fw.op("dve", lambda e: e.tensor_tensor(out=dst[:, bi, tok0:tok0 + 256], in0=t1[:], in1=t2[:], op=ALU.add), [t1, t2], [dst])
            for j in range(2 if 'v' not in skip else 0):
                tile = 2 * g + j
                pb = self.bank.next()
                for k in range(8):
                    fw.op("pe", lambda e: e.matmul(pb[:, 0:128], lhsT=xnT[:, k, j * 128:(j + 1) * 128], rhs=w_in[:, k, 2304:2432], start=(k == 0), stop=(k == 7)), [xnT, w_in], [pb])
                fw.op("act", lambda e: e.activation(out=self.vext[:, tile, :, 0:64], in_=pb[:, 0:128].rearrange("p (a d) -> p a d", a=2), func=AF.Copy), [pb], [self.vext])
            for j in range(2 if 'store' not in skip else 0):
                c0 = tokcol(2 * g + j)
                fw.dma("sp", self.uT, uTv[:, :, c0:c0 + 128], u_s, u_s[:, :, j * 128:(j + 1) * 128])
                fw.dma("sp", self.gbT, gbTv[:, :, c0:c0 + 128], gb_s, gb_s[:, :, j * 128:(j + 1) * 128])
        fw.pop()

    def l0_phaseB(self):
        fw = self.fw
        I = self.I
        fw.push()
        gate = self.gate_bcast(0, 0)
        w_out = fw.sb("w_out", [128, 8, D], BF16)
        for k in range(8):
            fw.dma("pool", w_out, w_out[:, k, :], I["w_out"], I["w_out"].ap()[k * 128:(k + 1) * 128, :])
        cw = fw.sb("cw", [128, 3, 4])
        for j_ in range(3):
            self.load_colT(cw, cw[:, j_, :], I["conv_w"], I["conv_w"].ap()[j_])
        PTs = Rot([fw.sb("PT", [128, NT, 512], BF16) for _ in range(2)])
        zatt = Rot([fw.sb("zatt", [128, 512], BF16) for _ in range(2)])
        rcb = Rot([fw.sb("rcb", [128, 4]) for _ in range(2)])
        zaTs = Rot([fw.sb("zaT", [128, 4, 128], BF16) for _ in range(2)])
        zcTs = Rot([fw.sb("zcT", [128, 4, 128], BF16) for _ in range(2)])
        uhs = Rot([fw.sb("uh", [128, 4, 130], BF16) for _ in range(2)])
        gbts = Rot([fw.sb("gbt", [128, 4, 128], BF16) for _ in range(2)])
        accs = Rot([fw.sb("acc", [128, 128]) for _ in range(3)])
        hts = Rot([fw.sb("ht", [128, D]) for _ in range(2)])
        tmps = Rot([fw.sb("tmp", [128, D]) for _ in range(2)])
        uTv = self.uT.ap().rearrange("(b p) c -> p b c", p=128)
        gbTv = self.gbT.ap().rearrange("(b p) c -> p b c", p=128)
        obanks = Rot(self.banks[0:2]); sbanks = Rot(self.banks[2:6]); self.bank = Rot(self.banks[6:8])
        import os
        skip = os.environ.get('L0B_SKIP', '').split(',')
        for qt in range(int(os.environ.get('L0B_NT', NT))):
            s = 1 if qt < 2 else 0
            nch = 2 if qt < 2 else NT
            ht = hts.next()
            src, sap = self.src_of(qt, None)
            fw.dma("sp", ht, ht[:], src, sap)
            uh = uhs.next(); gbt = gbts.next()
            c0 = tokcol(qt)
            fw.dma("sp", uh, uh[:], self.uT, uTv[:, :, c0 - 1:c0 + 129])
            fw.dma("sp", gbt, gbt[:], self.gbT, gbTv[:, :, c0:c0 + 128])
            za = zatt.next()
            for kv in range(2 if 'attn' not in skip else 0):
                PTall = PTs.next()
                for sc in range(nch):
                    sbs = [sbanks.next(), sbanks.next()]
                    for b in range(2):
                        for hf in range(2):
                            fw.op("pe", lambda e: e.matmul(sbs[hf][:, b * 128:(b + 1) * 128], lhsT=self.kT2[hf * 64:(hf + 1) * 64, kv, sc * 128:(sc + 1) * 128],
                                                            rhs=self.qT[hf * 64:(hf + 1) * 64, 2 * kv + b, qt * 128:(qt + 1) * 128], start=True, stop=True), [self.kT2, self.qT], [sbs[hf]])
                    PT4 = PTall[:, sc, :].rearrange("p (b h n) -> p b h n", b=2, h=2)
                    for hf in range(2):
                        fw.op("act", lambda e: e.activation(out=PT4[:, :, hf, :], in_=sbs[hf][:, 0:256].rearrange("p (b n) -> p b n", b=2), func=AF.Exp, scale=0.125), [sbs[hf]], [PTall])
                ob = obanks.next()
                O = ob[:, 0:260].rearrange("p (c e) -> p c e", e=65)
                for c in range(4):
                    for sc in range(nch):
                        fw.op("pe", lambda e: e.matmul(O[:, c, :], lhsT=PTall[:, sc, c * 128:(c + 1) * 128], rhs=self.vext[:, sc, kv, 0:65], start=(sc == 0), stop=(sc == nch - 1)), [PTall, self.vext], [ob])
                rc = rcb.next()
                fw.op("dve", lambda e: e.reciprocal(out=rc[:], in_=O[:, :, 64]), [ob], [rc])
                fw.op("dve", lambda e: e.tensor_tensor(out=za[:, kv * 256:(kv + 1) * 256].rearrange("p (c d) -> p c d", c=4), in0=O[:, :, 0:64],
                                                       in1=rc[:].unsqueeze(2).to_broadcast([128, 4, 64]), op=ALU.mult), [ob, rc], [za])
            pb = self.bank.next(); pv = self.bview(pb)
            for k in range(4):
                fw.op("pe", lambda e: e.transpose(out=pv[:, k * 128:(k + 1) * 128], in_=za[:, k * 128:(k + 1) * 128], identity=self.identb[:]), [za, self.identb], [pb])
            zaT = zaTs.next()
            fw.op("act", lambda e: e.activation(out=zaT[:].rearrange("p k n -> p (k n)"), in_=pv[:, 0:512], func=AF.Copy), [pb], [zaT])
            zcT = zcTs.next()
            for b in range(4 if 'conv' not in skip else 0):
                acc = accs.next()
                fw.op("pool", lambda e: e.tensor_scalar(out=acc[:], in0=uh[:, b, 0:128], scalar1=cw[:, 0, b:b + 1], scalar2=None, op0=ALU.mult), [uh, cw], [acc])
                for j_ in (1, 2):
                    t_ = accs.next()
                    fw.op("pool", lambda e: e.tensor_scalar(out=t_[:], in0=uh[:, b, j_:j_ + 128], scalar1=cw[:, j_, b:b + 1], scalar2=None, op0=ALU.mult), [uh, cw], [t_])
                    fw.op("pool", lambda e: e.tensor_tensor(out=acc[:], in0=acc[:], in1=t_[:], op=ALU.add), [acc, t_], [acc])
                fw.op("pool", lambda e: e.tensor_tensor(out=zcT[:, b, :], in0=acc[:], in1=gbt[:, b, :], op=ALU.mult), [acc, gbt], [zcT])
            if self.dbg:
                fw.dma("sp", self.dza, self.dza.ap()[qt * 128:(qt + 1) * 128, :], za, za[:])
                fw.dma("sp", self.dzc, self.dzc.ap().rearrange("(b p) c -> p b c", p=128)[:, :, qt * 128:(qt + 1) * 128], zcT, zcT[:])
            tmp = tmps.next()
            for half in range(2):
                pb = self.bank.next()
                for k in range(8):
                    lt = zcT[:, k, :] if k < 4 else zaT[:, k - 4, :]
                    fw.op("pe", lambda e: e.matmul(pb[:], lhsT=lt, rhs=w_out[:, k, half * 512:(half + 1) * 512], start=(k == 0), stop=(k == 7)), [zcT, zaT, w_out], [pb])
                fw.op("dve", lambda e: e.tensor_tensor(out=tmp[:, half * 512:(half + 1) * 512], in0=pb[:], in1=gate[s][:, half * 512:(half + 1) * 512], op=ALU.mult), [pb, gate[s]], [tmp])
            fw.op("pool", lambda e: e.tensor_tensor(out=tmp[:], in0=tmp[:], in1=ht[:], op=ALU.add), [tmp, ht], [tmp])
            fw.dma("act", self.h1, self.h1.ap()[qt * 128:(qt + 1) * 128, :], tmp, tmp[:])
        self.bank = Rot(self.banks)
        fw.pop()
        fw.pop()

    def peer_prep(self, layer):
        fw = self.fw
        I = self.I
        fw.push()
        banks = Rot(self.banks)
        Uv = I["peer_u"].ap()[layer].rearrange("(i j) d -> j i d", j=128)
        Vv = I["peer_v"].ap()[layer].rearrange("(i j) d -> j i d", j=128)
        UTv = self.UT.ap().rearrange("i p k j -> p i (k j)")
        Vbv = self.Vb.ap().rearrange("i j d -> j i d")
        ubs = Rot([fw.sb("ub", [128, 4, D], BF16) for _ in range(2)])
        vbs = Rot([fw.sb("vb", [128, 4, D], BF16) for _ in range(2)])
        uts = Rot([fw.sb("ut", [128, 4, 1024], BF16) for _ in range(2)])
        for c in range(32):
            ub = ubs.next(); vb = vbs.next(); ut = uts.next()
            fw.dma("pool", ub, ub[:], I["peer_u"], Uv[:, c * 4:(c + 1) * 4, :])
            fw.dma("pool", vb, vb[:], I["peer_v"], Vv[:, c * 4:(c + 1) * 4, :])
            fw.dma("sp", self.Vb, Vbv[:, c * 4:(c + 1) * 4, :], vb, vb[:])
            for ii in range(4):
                pb = banks.next(); pv = self.bview(pb)
                for k in range(8):
                    fw.op("pe", lambda e: e.transpose(out=pv[:, k * 128:(k + 1) * 128], in_=ub[:, ii, k * 128:(k + 1) * 128], identity=self.identb[:]), [ub, self.identb], [pb])
                if ii % 2 == 0:
                    fw.op("act", lambda e: e.activation(out=ut[:, ii, :], in_=pv[:, 0:1024], func=AF.Copy), [pb], [ut])
                else:
                    fw.op("dve", lambda e: e.tensor_copy(out=ut[:, ii, :], in_=pv[:, 0:1024]), [pb], [ut])
            fw.dma("sp", self.UT, UTv[:, c * 4:(c + 1) * 4, :], ut, ut[:])
        fw.pop()

    def peer(self, layer, hsrc, hdst, final=False):
        fw = self.fw
        I = self.I
        fw.push()
        GT, shT = self.modvecs(layer, 1)
        gate = self.gate_bcast(layer, 1)
        self.make_normer()
        obanks = self.banks[0:4]
        self.bank = Rot(self.banks[4:8])
        w_q = fw.sb("w_q", [128, 8, 2048], BF16)
        for k in range(8):
            fw.dma("pool", w_q, w_q[:, k, :], I["peer_wq"], I["peer_wq"].ap()[layer, k * 128:(k + 1) * 128, :])
        keyn = fw.sb("keyn", [128, 16, 128], BF16)
        fw.dma("pool", keyn, keyn[:], I["peer_keys"], I["peer_keys"].ap()[layer].rearrange("h k d -> k h d"))
        keysT = fw.sb("keysT", [128, 16, 128], BF16)
        for c in range(2):
            pb = self.bank.next(); pv = self.bview(pb)
            for q in range(8):
                fw.op("pe", lambda e: e.transpose(out=pv[:, q * 128:(q + 1) * 128], in_=keyn[:, c * 8 + q, :], identity=self.identb[:]), [keyn, self.identb], [pb])
            fw.op("act", lambda e: e.activation(out=keysT[:, c * 8:(c + 1) * 8, :].rearrange("p a b -> p (a b)"), in_=pv[:, 0:1024], func=AF.Copy), [pb], [keysT])
        xn2T = fw.sb("xn2T", [128, 8, 256], BF16)
        qT = fw.sb("pqT", [128, 16, 256], BF16)
        s_all = fw.sb("s_all", [128, 2, 16, 128])
        tau = fw.sb("tau", [128, 2, 8]); bias = fw.sb("pbias", [128, 2, 8])
        SB = [(fw.sb("m0", [128, 16]), fw.sb("m1", [128, 16]), fw.sb("scr", [128, 128]), fw.sb("cand", [128, 16, 16]), fw.sb("c24", [128, 24]),
               fw.sb("cscr", [128, 256]), fw.sb("st1", [128, 4]), fw.sb("e16", [128, 16])) for _ in range(4)]
        IB = 16
        Ws = Rot([fw.sb("W", [128, 2, IB, 128], BF16) for _ in range(2)])
        sums = Rot([fw.sb("sum", [128, IB, 128]) for _ in range(3)])
        Ps = Rot([fw.sb("P", [128, IB, 128], BF16) for _ in range(3)])
        Whs = Rot([fw.sb("Wh", [128, IB, 128], BF16) for _ in range(2)])
        Mks = Rot([fw.sb("Mk", [128, IB, 128], BF16) for _ in range(2)])
        UTs = Rot([fw.sb("UTs", [128, 2, 1024], BF16) for _ in range(2)])
        Vs = Rot([fw.sb("Vs", [128, 2, D], BF16) for _ in range(2)])
        Gs = Rot([fw.sb("Gs", [128, 256], BF16) for _ in range(4)])
        WAs = Rot([fw.sb("WAs", [128, 256], BF16) for _ in range(4)])
        tmps = Rot([fw.sb("ptmp", [128, D]) for _ in range(1)])
        hts = Rot([fw.sb("pht", [128, D]) for _ in range(2)])
        UTv = self.UT.ap().rearrange("i p k j -> p i (k j)")
        Vbv = self.Vb.ap().rearrange("i j d -> j i d")
        import os
        ng = int(os.environ.get("PEER_NG", NT // 2))
        for g in range(ng):
            if final and g == 0:
                continue
            s = 1 if g == 0 else 0
            for j in range(2):
                src, sap = self.src_of(2 * g + j, hsrc)
                self.norm_tile(src, sap, GT[s], shT[s], xn2T, slice(j * 128, (j + 1) * 128))
            for hp in range(16):
                pb = self.bank.next()
                for k in range(8):
                    fw.op("pe", lambda e: e.matmul(pb[:, 0:256], lhsT=w_q[:, k, hp * 128:(hp + 1) * 128], rhs=xn2T[:, k, :], start=(k == 0), stop=(k == 7)), [w_q, xn2T], [pb])
                fw.op("act", lambda e: e.activation(out=qT[:, hp, :], in_=pb[:, 0:256], func=AF.Copy), [pb], [qT])
            for t in range(2):
                for c in range(4):
                    pb = self.bank.next()
                    for q in range(4):
                        hp = c * 4 + q
                        fw.op("pe", lambda e: e.matmul(pb[:, q * 128:(q + 1) * 128], lhsT=qT[:, hp, t * 128:(t + 1) * 128], rhs=keysT[:, hp, :], start=True, stop=True), [qT, keysT], [pb])
                    fw.op("act", lambda e: e.activation(out=s_all[:, t, c * 4:(c + 1) * 4, :].rearrange("p a b -> p (a b)"), in_=pb[:], func=AF.Copy), [pb], [s_all])
            def stats_gen(t, h, B):
                m0, m1, scr, cand, c24, cscr, st1, e16 = B
                for (mm, src) in ((m0, s_all[:, t, 2 * h, :]), (m1, s_all[:, t, 2 * h + 1, :])):
                    fw.op("dve", lambda e: e.max(out=mm[:, 0:8], in_=src), [s_all], [mm]); yield
                    fw.op("dve", lambda e: e.match_replace(out=scr[:], in_to_replace=mm[:, 0:8], in_values=src, imm_value=-1e30), [s_all, mm], [scr]); yield
                    fw.op("dve", lambda e: e.max(out=mm[:, 8:16], in_=scr[:]), [scr], [mm]); yield
                fw.op("dve", lambda e: e.tensor_tensor(out=cand[:], in0=m0[:].unsqueeze(2).to_broadcast([128, 16, 16]), in1=m1[:].unsqueeze(1).to_broadcast([128, 16, 16]), op=ALU.add), [m0, m1], [cand]); yield
                cf = cand[:].rearrange("p a b -> p (a b)")
                fw.op("dve", lambda e: e.max(out=c24[:, 0:8], in_=cf), [cand], [c24]); yield
                fw.op("dve", lambda e: e.match_replace(out=cscr[:], in_to_replace=c24[:, 0:8], in_values=cf, imm_value=-1e30), [cand, c24], [cscr]); yield
                fw.op("dve", lambda e: e.max(out=c24[:, 8:16], in_=cscr[:]), [cscr], [c24]); yield
                fw.op("dve", lambda e: e.match_replace(out=cscr[:], in_to_replace=c24[:, 8:16], in_values=cscr[:], imm_value=-1e30), [cscr, c24], [cscr]); yield
                fw.op("dve", lambda e: e.max(out=c24[:, 16:24], in_=cscr[:]), [cscr], [c24]); yield
                fw.op("dve", lambda e: e.tensor_scalar(out=st1[:, 0:1], in0=c24[:, 16:17], scalar1=0.5, scalar2=None, op0=ALU.mult), [c24], [st1]); yield
                fw.op("dve", lambda e: e.scalar_tensor_tensor(out=tau[:, t, h:h + 1], in0=c24[:, 15:16], scalar=0.5, in1=st1[:, 0:1], op0=ALU.mult, op1=ALU.add), [c24, st1], [tau]); yield
                fw.op("dve", lambda e: e.tensor_scalar(out=st1[:, 1:2], in0=c24[:, 0:1], scalar1=-1.0, scalar2=None, op0=ALU.mult), [c24], [st1]); yield
                fw.op("act", lambda e: e.activation(out=e16[:], in_=c24[:, 0:16], func=AF.Exp, bias=st1[:, 1:2], scale=1.0, accum_out=st1[:, 2:3]), [c24, st1], [e16, st1]); yield
                fw.op("act", lambda e: e.activation(out=st1[:, 3:4], in_=st1[:, 2:3], func=AF.Ln), [st1], [st1]); yield
                fw.op("dve", lambda e: e.tensor_tensor(out=bias[:, t, h:h + 1], in0=st1[:, 1:2], in1=st1[:, 3:4], op=ALU.subtract), [st1], [bias]); yield

            for h0 in range(0, 8, 2):
                gens = [stats_gen(t, h0 + dh, SB[t * 2 + dh]) for t in range(2) for dh in range(2)]
                alive = True
                while alive:
                    alive = False
                    for gi in gens:
                        try:
                            next(gi); alive = True
                        except StopIteration:
                            pass
            NB = 128 // IB
            Wl = [None] * NB

            def build_unit(ib, u):
                i0 = ib * IB
                if u == 0:
                    Wl[ib] = Ws.next()
                W = Wl[ib]
                t, h = u // 8, u % 8
                sm = sums.next(); P = Ps.next()
                fw.op("dve", lambda e: e.tensor_tensor(out=sm[:], in0=s_all[:, t, 2 * h, i0:i0 + IB].unsqueeze(2).to_broadcast([128, IB, 128]),
                                                       in1=s_all[:, t, 2 * h + 1, :].unsqueeze(1).to_broadcast([128, IB, 128]), op=ALU.add), [s_all], [sm])
                def emitB():
                    fw.op("act", lambda e: e.activation(out=P[:], in_=sm[:], func=AF.Exp, bias=bias[:, t, h:h + 1], scale=1.0), [sm, bias], [P])

                def fin():
                    mk = Mks.next()
                    fw.op("dve", lambda e: e.tensor_scalar(out=mk[:], in0=sm[:], scalar1=tau[:, t, h:h + 1], scalar2=None, op0=ALU.is_ge), [sm, tau], [mk])
                    if h == 0:
                        fw.op("dve", lambda e: e.tensor_tensor(out=W[:, t, :, :], in0=mk[:], in1=P[:], op=ALU.mult), [mk, P], [W])
                    else:
                        Wh = Whs.next()
                        fw.op("dve", lambda e: e.tensor_tensor(out=Wh[:], in0=mk[:], in1=P[:], op=ALU.mult), [mk, P], [Wh])
                        eng = "pool" if h % 2 == 1 else "dve"
                        fw.op(eng, lambda e: e.tensor_tensor(out=W[:, t, :, :], in0=W[:, t, :, :], in1=Wh[:], op=ALU.add), [W, Wh], [W])
                return emitB, fin

            cur_ld = {}
            carry = {}

            def front(i):
                ib, iw = i // IB, i % IB
                W = Wl[ib]
                if i % 2 == 0:
                    UTt = UTs.next(); Vt = Vs.next()
                    fw.dma("sp", UTt, UTt[:], self.UT, UTv[:, i:i + 2, :])
                    fw.dma("sp", Vt, Vt[:], self.Vb, Vbv[:, i:i + 2, :])
                    cur_ld["u"] = UTt; cur_ld["v"] = Vt
                UTt = cur_ld["u"]; Vt = cur_ld["v"]
                i2 = i % 2
                pa = self.bank.next()
                for k in range(8):
                    fw.op("pe", lambda e: e.matmul(pa[:, 0:256], lhsT=UTt[:, i2, k * 128:(k + 1) * 128], rhs=xn2T[:, k, :], start=(k == 0), stop=(k == 7)), [UTt, xn2T], [pa])
                G = Gs.next()
                fw.op("act", lambda e: e.activation(out=G[:], in_=pa[:, 0:256], func=AF.Gelu), [pa], [G])
                pw = self.bank.next(); pwv = self.bview(pw)
                for t in range(2):
                    fw.op("pe", lambda e: e.transpose(out=pwv[:, t * 128:(t + 1) * 128], in_=W[:, t, iw, :], identity=self.identb[:]), [W, self.identb], [pw])
                WA = WAs.next()
                fw.op("dve", lambda e: e.tensor_tensor(out=WA[:], in0=G[:], in1=pwv[:, 0:256], op=ALU.mult), [G, pw], [WA])

                def back():
                    for t in range(2):
                        for half in range(2):
                            ob = obanks[t * 2 + half]
                            fw.op("pe", lambda e: e.matmul(ob[:], lhsT=WA[:, t * 128:(t + 1) * 128], rhs=Vt[:, i2, half * 512:(half + 1) * 512], start=(i == 0), stop=(i == 127)), [WA, Vt], [ob])
                return back

            pend = None
            for u in range(16):
                eb, f = build_unit(0, u)
                eb()
                if pend is not None:
                    pend()
                pend = f
            pend()
            backs = {0: front(0), 1: front(1)}
            sched = {0: [0, 1], 1: [2, 3]}
            for j_ in range(2, 14):
                sched[j_] = [j_ + 2]
            for i in range(128):
                ib, j = i // IB, i % IB
                units = sched.get(j, []) if ib + 1 < NB else []
                ebs = []; fins = []
                for u in units:
                    eb, f = build_unit(ib + 1, u)
                    ebs.append(eb); fins.append(f)
                if i + 2 < 128:
                    backs[i + 2] = front(i + 2)
                for eb in ebs:
                    eb()
                cp = carry.pop("f", None)
                if cp is not None:
                    cp()
                for f in fins[:-1]:
                    f()
                if fins:
                    if j == 13:
                        fins[-1]()
                    else:
                        carry["f"] = fins[-1]
                backs.pop(i)()
            for t in range(2):
                tile = 2 * g + t
                tmp = tmps.next(); ht = hts.next()
                src, sap = self.src_of(tile, hsrc)
                fw.dma("sp", ht, ht[:], src, sap)
                for half in range(2):
                    ob = obanks[t * 2 + half]
                    fw.op("dve", lambda e: e.tensor_tensor(out=tmp[:, half * 512:(half + 1) * 512], in0=ob[:], in1=gate[s][:, half * 512:(half + 1) * 512], op=ALU.mult), [ob, gate[s]], [tmp])
                fw.op("pool", lambda e: e.tensor_tensor(out=tmp[:], in0=tmp[:], in1=ht[:], op=ALU.add), [tmp, ht], [tmp])
                if final:
                    fw.dma("act", self.out, self.out.ap()[(tile - 2) * 128:(tile - 1) * 128, :], tmp, tmp[:])
                else:
                    fw.dma("act", hdst, hdst.ap()[tile * 128:(tile + 1) * 128, :], tmp, tmp[:])
        self.bank = Rot(self.banks)
        fw.pop()

    def peer_prep2(self, layer):
        fw = self.fw
        I = self.I
        fw.push()
        banks = Rot(self.banks)
        Uv = I["peer_u"].ap()[layer].rearrange("(i j) d -> i j d", j=128)
        Vv = I["peer_v"].ap()[layer].rearrange("(i j) d -> i j d", j=128)
        UTv = self.UT.ap().rearrange("j p k i -> p j (k i)")
        Vbv = self.Vb.ap().rearrange("j i d -> i j d")
        ubs = Rot([fw.sb("ub", [128, 4, D], BF16) for _ in range(2)])
        vbs = Rot([fw.sb("vb", [128, 4, D], BF16) for _ in range(2)])
        uts = Rot([fw.sb("ut", [128, 4, 1024], BF16) for _ in range(2)])
        for c in range(32):
            ub = ubs.next(); vb = vbs.next(); ut = uts.next()
            fw.dma("pool", ub, ub[:], I["peer_u"], Uv[:, c * 4:(c + 1) * 4, :])
            fw.dma("pool", vb, vb[:], I["peer_v"], Vv[:, c * 4:(c + 1) * 4, :])
            fw.dma("sp", self.Vb, Vbv[:, c * 4:(c + 1) * 4, :], vb, vb[:])
            for jj in range(4):
                pb = banks.next(); pv = self.bview(pb)
                for k in range(8):
                    fw.op("pe", lambda e: e.transpose(out=pv[:, k * 128:(k + 1) * 128], in_=ub[:, jj, k * 128:(k + 1) * 128], identity=self.identb[:]), [ub, self.identb], [pb])
                if jj % 2 == 0:
                    fw.op("act", lambda e: e.activation(out=ut[:, jj, :], in_=pv[:, 0:1024], func=AF.Copy), [pb], [ut])
                else:
                    fw.op("dve", lambda e: e.tensor_copy(out=ut[:, jj, :], in_=pv[:, 0:1024]), [pb], [ut])
            fw.dma("sp", self.UT, UTv[:, c * 4:(c + 1) * 4, :], ut, ut[:])
        fw.pop()

    def peer2(self, layer, hsrc, hdst, final=False):
        fw = self.fw
        I = self.I
        fw.push()
        GT, shT = self.modvecs(layer, 1)
        gate = self.gate_bcast(layer, 1)
        self.make_normer(nh=1)
        obanks = self.banks[0:4]
        self.bank = Rot(self.banks[4:8])
        w_q = fw.sb("w_q", [128, 8, 2048], BF16)
        for k in range(8):
            fw.dma("pool", w_q, w_q[:, k, :], I["peer_wq"], I["peer_wq"].ap()[layer, k * 128:(k + 1) * 128, :])
        keysT = fw.sb("keysT", [128, 16, 128], BF16)
        fw.push()
        keyn = fw.sb("keyn", [128, 16, 128], BF16)
        fw.dma("pool", keyn, keyn[:], I["peer_keys"], I["peer_keys"].ap()[layer].rearrange("h k d -> k h d"))
        for c in range(2):
            pb = self.bank.next(); pv = self.bview(pb)
            for q in range(8):
                fw.op("pe", lambda e: e.transpose(out=pv[:, q * 128:(q + 1) * 128], in_=keyn[:, c * 8 + q, :], identity=self.identb[:]), [keyn, self.identb], [pb])
            fw.op("act", lambda e: e.activation(out=keysT[:, c * 8:(c + 1) * 8, :].rearrange("p a b -> p (a b)"), in_=pv[:, 0:1024], func=AF.Copy), [pb], [keysT])
        fw.pop()
        iota = fw.sb("iota", [128, 128], BF16)
        fw.dma("pool", iota, iota[:], I["iota128"], I["iota128"].ap())
        xn2Ts = [fw.sb("xn2T", [128, 8, 256], BF16) for _ in range(2)]
        qT = fw.sb("pqT", [128, 16, 256], BF16)
        s_all = fw.sb("s_all", [128, 2, 16, 128])
        s1cs = [fw.sb("s1c", [128, 2, 8, 128]) for _ in range(2)]
        taus = [fw.sb("tau", [128, 2, 8]) for _ in range(2)]; bias = fw.sb("pbias", [128, 2, 8])
        m0bs = [fw.sb("m0b", [128, 2, 8, 16]) for _ in range(2)]
        idxa = fw.sb("idxa", [128, 2, 128])
        idxTs = [fw.sb("idxT", [128, 256], BF16) for _ in range(2)]
        SB = [(fw.sb("m0", [128, 16]), fw.sb("m1", [128, 16]), fw.sb("scr", [128, 128]), fw.sb("cand", [128, 16, 16]), fw.sb("c24", [128, 24]),
               fw.sb("cscr", [128, 256]), fw.sb("st1", [128, 4]), fw.sb("e16", [128, 16]), fw.sb("ix", [128, 16], U32)) for _ in range(3)]
        JB = 16
        sms = Rot([fw.sb("sm", [128, 8, 16, JB]) for _ in range(1)])
        Pss = Rot([fw.sb("P", [128, 8, 16, JB], BF16) for _ in range(1)])
        mks = Rot([fw.sb("mk", [128, 8, 16, JB], BF16) for _ in range(1)])
        Wp = fw.sb("Wp", [128, 2, 128, JB], BF16)
        WT = fw.sb("WT", [128, JB, 256], BF16)
        Walls = Rot([fw.sb("Wall", [128, 256, JB], BF16) for _ in range(2)])
        ohs = Rot([fw.sb("oh", [128, 32, 128], BF16) for _ in range(2)])
        UTs = Rot([fw.sb("UTs", [128, 2, 1024], BF16) for _ in range(2)])
        Vs = Rot([fw.sb("Vs", [128, 2, D], BF16) for _ in range(2)])
        Gs = Rot([fw.sb("Gs", [128, 256], BF16) for _ in range(3)])
        WAs = Rot([fw.sb("WAs", [128, 256], BF16) for _ in range(3)])
        tmps = Rot([fw.sb("ptmp", [128, D]) for _ in range(1)])
        hts = Rot([fw.sb("pht", [128, D]) for _ in range(1)])
        UTv = self.UT.ap().rearrange("j p k i -> p j (k i)")
        Vbv = self.Vb.ap().rearrange("j i d -> i j d")
        import os
        ng = int(os.environ.get("PEER_NG", NT // 2))
        NJB = 128 // JB
        glist = [g for g in range(ng) if not (final and g == 0)]
        Wls = {}

        def prelude_items(g):
            par = g % 2
            xn2T = xn2Ts[par]; s1c = s1cs[par]; tau = taus[par]; m0b = m0bs[par]; idxT = idxTs[par]
            s = 1 if g == 0 else 0
            items = []

            def normit(j):
                def f():
                    src, sap = self.src_of(2 * g + j, hsrc)
                    self.norm_tile(src, sap, GT[s], shT[s], xn2T, slice(j * 128, (j + 1) * 128))
                return f
            items += [normit(0), normit(1)]

            def qit(hp):
                def f():
                    pb = self.bank.next()
                    for k in range(8):
                        fw.op("pe", lambda e: e.matmul(pb[:, 0:256], lhsT=w_q[:, k, hp * 128:(hp + 1) * 128], rhs=xn2T[:, k, :], start=(k == 0), stop=(k == 7)), [w_q, xn2T], [pb])
                    fw.op("act", lambda e: e.activation(out=qT[:, hp, :], in_=pb[:, 0:256], func=AF.Copy), [pb], [qT])
                return f
            items += [qit(hp) for hp in range(16)]

            def scit(t, c):
                def f():
                    pb = self.bank.next()
                    for q in range(4):
                        hp = c * 4 + q
                        fw.op("pe", lambda e: e.matmul(pb[:, q * 128:(q + 1) * 128], lhsT=qT[:, hp, t * 128:(t + 1) * 128], rhs=keysT[:, hp, :], start=True, stop=True), [qT, keysT], [pb])
                    fw.op("act", lambda e: e.activation(out=s_all[:, t, c * 4:(c + 1) * 4, :].rearrange("p a b -> p (a b)"), in_=pb[:], func=AF.Copy), [pb], [s_all])
                    if c == 3:
                        fw.op("act", lambda e: e.activation(out=s1c[:, t, :, :], in_=s_all[:, t, :, :].rearrange("p (h two) j -> p h two j", two=2)[:, :, 1, :], func=AF.Copy), [s_all], [s1c])
                return f
            items += [scit(t, c) for t in range(2) for c in range(4)]

            def stats_gen(t, h, B):
                m0, m1, scr, cand, c24, cscr, st1, e16, ix = B
                s0 = s_all[:, t, 2 * h, :]; s1 = s_all[:, t, 2 * h + 1, :]
                fw.op("dve", lambda e: e.max(out=m0[:, 0:8], in_=s0), [s_all], [m0]); yield
                fw.op("dve", lambda e: e.max_index(out=ix[:, 0:8], in_max=m0[:, 0:8], in_values=s0), [s_all, m0], [ix]); yield
                fw.op("dve", lambda e: e.match_replace(out=scr[:], in_to_replace=m0[:, 0:8], in_values=s0, imm_value=-1e30), [s_all, m0], [scr]); yield
                fw.op("dve", lambda e: e.max(out=m0[:, 8:16], in_=scr[:]), [scr], [m0]); yield
                fw.op("dve", lambda e: e.max_index(out=ix[:, 8:16], in_max=m0[:, 8:16], in_values=scr[:]), [scr, m0], [ix]); yield
                fw.op("dve", lambda e: e.tensor_copy(out=idxa[:, t, h * 16:(h + 1) * 16], in_=ix[:]), [ix], [idxa]); yield
                fw.op("dve", lambda e: e.max(out=m1[:, 0:8], in_=s1), [s_all], [m1]); yield
                fw.op("dve", lambda e: e.match_replace(out=scr[:], in_to_replace=m1[:, 0:8], in_values=s1, imm_value=-1e30), [s_all, m1], [scr]); yield
                fw.op("dve", lambda e: e.max(out=m1[:, 8:16], in_=scr[:]), [scr], [m1]); yield
                fw.op("dve", lambda e: e.tensor_tensor(out=cand[:], in0=m0[:].unsqueeze(2).to_broadcast([128, 16, 16]), in1=m1[:].unsqueeze(1).to_broadcast([128, 16, 16]), op=ALU.add), [m0, m1], [cand]); yield
                cf = cand[:].rearrange("p a b -> p (a b)")
                fw.op("dve", lambda e: e.max(out=c24[:, 0:8], in_=cf), [cand], [c24]); yield
                fw.op("dve", lambda e: e.match_replace(out=cscr[:], in_to_replace=c24[:, 0:8], in_values=cf, imm_value=-1e30), [cand, c24], [cscr]); yield
                fw.op("dve", lambda e: e.max(out=c24[:, 8:16], in_=cscr[:]), [cscr], [c24]); yield
                fw.op("dve", lambda e: e.match_replace(out=cscr[:], in_to_replace=c24[:, 8:16], in_values=cscr[:], imm_value=-1e30), [cscr, c24], [cscr]); yield
                fw.op("dve", lambda e: e.max(out=c24[:, 16:24], in_=cscr[:]), [cscr], [c24]); yield
                fw.op("dve", lambda e: e.tensor_scalar(out=st1[:, 0:1], in0=c24[:, 16:17], scalar1=0.5, scalar2=None, op0=ALU.mult), [c24], [st1]); yield
                fw.op("dve", lambda e: e.scalar_tensor_tensor(out=st1[:, 0:1], in0=c24[:, 15:16], scalar=0.5, in1=st1[:, 0:1], op0=ALU.mult, op1=ALU.add), [c24, st1], [st1]); yield
                fw.op("dve", lambda e: e.tensor_scalar(out=st1[:, 1:2], in0=c24[:, 0:1], scalar1=-1.0, scalar2=None, op0=ALU.mult), [c24], [st1]); yield
                fw.op("act", lambda e: e.activation(out=e16[:], in_=c24[:, 0:16], func=AF.Exp, bias=st1[:, 1:2], scale=1.0, accum_out=st1[:, 2:3]), [c24, st1], [e16, st1]); yield
                fw.op("act", lambda e: e.activation(out=st1[:, 3:4], in_=st1[:, 2:3], func=AF.Ln), [st1], [st1]); yield
                fw.op("dve", lambda e: e.tensor_tensor(out=bias[:, t, h:h + 1], in0=st1[:, 1:2], in1=st1[:, 3:4], op=ALU.subtract), [st1], [bias]); yield
                fw.op("dve", lambda e: e.tensor_scalar(out=m0b[:, t, h, :], in0=m0[:], scalar1=bias[:, t, h:h + 1], scalar2=None, op0=ALU.add), [m0, bias], [m0b]); yield
                fw.op("dve", lambda e: e.tensor_tensor(out=tau[:, t, h:h + 1], in0=st1[:, 0:1], in1=bias[:, t, h:h + 1], op=ALU.add), [st1, bias], [tau]); yield

            def statit(ths):
                def f():
                    gens = [stats_gen(t, h, SB[i]) for i, (t, h) in enumerate(ths)]
                    alive = True
                    while alive:
                        alive = False
                        for gi in gens:
                            try:
                                next(gi); alive = True
                            except StopIteration:
                                pass
                return f
            allth = [(t, h) for h in range(8) for t in range(2)]
            for q in range(0, 16, 3):
                items.append(statit(allth[q:q + 3]))

            def idxit(t):
                def f():
                    pb = self.bank.next()
                    fw.op("pe", lambda e: e.transpose(out=pb[:, 0:128], in_=idxa[:, t, :], identity=self.identf[:]), [idxa, self.identf], [pb])
                    fw.op("act", lambda e: e.activation(out=idxT[:, t * 128:(t + 1) * 128], in_=pb[:, 0:128], func=AF.Copy), [pb], [idxT])
                return f
            items += [idxit(0), idxit(1)]
            return items

        for it in prelude_items(glist[0]):
            it()
        first_built = False
        for gi_, g in enumerate(glist):
            par = g % 2
            xn2T = xn2Ts[par]; s1c = s1cs[par]; tau = taus[par]; m0b = m0bs[par]; idxT = idxTs[par]
            s = 1 if g == 0 else 0
            gnext = glist[gi_ + 1] if gi_ + 1 < len(glist) else None
            def build_items(jb, g=g, m0b=m0b, s1c=s1c, tau=tau, idxT=idxT):
                Wl = Wls.setdefault(g, [None] * NJB)
                j0 = jb * JB
                items = []

                def rank(t):
                    def f():
                        sm = sms.next(); P = Pss.next(); mk = mks.next()
                        fw.op("dve", lambda e: e.tensor_tensor(out=sm[:].rearrange("p h a j -> p h (a j)").rearrange("p h (a j) -> p h a j", a=16),
                                                               in0=m0b[:, t, :, :].unsqueeze(3).to_broadcast([128, 8, 16, JB]),
                                                               in1=s1c[:, t, :, j0:j0 + JB].unsqueeze(2).to_broadcast([128, 8, 16, JB]),
                                                               op=ALU.add), [m0b, s1c], [sm])
                        fw.op("act", lambda e: e.activation(out=P[:].rearrange("p h a j -> p (h a j)"), in_=sm[:].rearrange("p h a j -> p (h a j)"), func=AF.Exp), [sm], [P])
                        fw.op("dve", lambda e: e.tensor_tensor(out=mk[:].rearrange("p h a j -> p h (a j)"), in0=sm[:].rearrange("p h a j -> p h (a j)"),
                                                               in1=tau[:, t, :].unsqueeze(2).to_broadcast([128, 8, 16 * JB]), op=ALU.is_ge), [sm, tau], [mk])
                        fw.op("dve", lambda e: e.tensor_tensor(out=Wp[:, t, :, :].rearrange("p (h a) j -> p h a j", a=16), in0=mk[:], in1=P[:], op=ALU.mult), [mk, P], [Wp])
                    return f
                items.append(rank(0)); items.append(rank(1))

                def transp(q):
                    def f():
                        pb = self.bank.next(); pv = self.bview(pb)
                        for jj in range(4):
                            for t in range(2):
                                c = jj * 2 + t
                                fw.op("pe", lambda e: e.transpose(out=pv[:, c * 128:(c + 1) * 128], in_=Wp[:, t, :, q * 4 + jj], identity=self.identb[:]), [Wp, self.identb], [pb])
                        fw.op("act", lambda e: e.activation(out=WT[:, q * 4:(q + 1) * 4, :].rearrange("p a b -> p (a b)"), in_=pv[:, 0:1024], func=AF.Copy), [pb], [WT])
                    return f
                for q in range(JB // 4):
                    items.append(transp(q))

                def scatter(nb):
                    def f():
                        if nb == 0:
                            Wl[jb] = Walls.next()
                        Wall = Wl[jb]
                        oh = ohs.next()
                        n0 = nb * 32
                        fw.op("dve", lambda e: e.tensor_tensor(out=oh[:], in0=idxT[:, n0:n0 + 32].unsqueeze(2).to_broadcast([128, 32, 128]),
                                                               in1=iota[:].unsqueeze(1).to_broadcast([128, 32, 128]), op=ALU.is_equal), [idxT, iota], [oh])
                        for half in range(2):
                            pb = self.bank.next()
                            for q in range(16):
                                n = n0 + half * 16 + q
                                fw.op("pe", lambda e: e.matmul(pb[:, q * JB:(q + 1) * JB], lhsT=oh[:, half * 16 + q, :], rhs=WT[:, :, n], start=True, stop=True), [oh, WT], [pb])
                            eng = "act" if half == 0 else "dve"
                            if eng == "act":
                                fw.op("act", lambda e: e.activation(out=Wall[:, n0 + half * 16:n0 + half * 16 + 16, :].rearrange("p a b -> p (a b)"), in_=pb[:, 0:16 * JB], func=AF.Copy), [pb], [Wall])
                            else:
                                fw.op("dve", lambda e: e.tensor_copy(out=Wall[:, n0 + half * 16:n0 + half * 16 + 16, :].rearrange("p a b -> p (a b)"), in_=pb[:, 0:16 * JB]), [pb], [Wall])
                    return f
                for nb in range(8):
                    items.append(scatter(nb))
                return items

            cur_ld = {}

            def front(jx):
                jb, jj = jx // JB, jx % JB
                Wall = Wls[g][jb]
                if jx % 2 == 0:
                    UTt = UTs.next(); Vt = Vs.next()
                    fw.dma("sp", UTt, UTt[:], self.UT, UTv[:, jx:jx + 2, :])
                    fw.dma("sp", Vt, Vt[:], self.Vb, Vbv[:, jx:jx + 2, :])
                    cur_ld["u"] = UTt; cur_ld["v"] = Vt
                UTt = cur_ld["u"]; Vt = cur_ld["v"]
                i2 = jx % 2
                pa = self.bank.next()
                for k in range(8):
                    fw.op("pe", lambda e: e.matmul(pa[:, 0:256], lhsT=UTt[:, i2, k * 128:(k + 1) * 128], rhs=xn2T[:, k, :], start=(k == 0), stop=(k == 7)), [UTt, xn2T], [pa])
                G = Gs.next()
                fw.op("act", lambda e: e.activation(out=G[:], in_=pa[:, 0:256], func=AF.Gelu), [pa], [G])
                WA = WAs.next()
                fw.op("dve", lambda e: e.tensor_tensor(out=WA[:], in0=G[:], in1=Wall[:, :, jj], op=ALU.mult), [G, Wall], [WA])

                def back():
                    for t in range(2):
                        for half in range(2):
                            ob = obanks[t * 2 + half]
                            fw.op("pe", lambda e: e.matmul(ob[:], lhsT=WA[:, t * 128:(t + 1) * 128], rhs=Vt[:, i2, half * 512:(half + 1) * 512], start=(jx == 0), stop=(jx == 127)), [WA, Vt], [ob])
                return back

            if not first_built:
                for it in build_items(0):
                    it()
                first_built = True
            backs = {0: front(0), 1: front(1)}
            pending = []
            side = prelude_items(gnext) if gnext is not None else []
            nside = len(side)
            for jx in range(128):
                jb, jj = jx // JB, jx % JB
                if jj == 0:
                    if jb + 1 < NJB:
                        pending = build_items(jb + 1)
                    elif gnext is not None:
                        while side:
                            side.pop(0)()
                        parn = gnext % 2
                        pending = build_items(0, g=gnext, m0b=m0bs[parn], s1c=s1cs[parn], tau=taus[parn], idxT=idxTs[parn])
                    else:
                        pending = []
                if pending and jj < JB - 2:
                    per = -(-len(pending) // max(1, (JB - 2 - jj)))
                    for _ in range(per):
                        if pending:
                            pending.pop(0)()
                if side and jb < NJB - 1:
                    steps_left = (NJB - 1) * JB - jx
                    per = -(-len(side) // max(1, steps_left))
                    for _ in range(per):
                        if side:
                            side.pop(0)()
                if jx + 2 < 128:
                    backs[jx + 2] = front(jx + 2)
                backs.pop(jx)()
            while pending:
                pending.pop(0)()
            for t in range(2):
                tile = 2 * g + t
                tmp = tmps.next(); ht = hts.next()
                src, sap = self.src_of(tile, hsrc)
                fw.dma("sp", ht, ht[:], src, sap)
                for half in range(2):
                    ob = obanks[t * 2 + half]
                    fw.op("dve", lambda e: e.tensor_tensor(out=tmp[:, half * 512:(half + 1) * 512], in0=ob[:], in1=gate[s][:, half * 512:(half + 1) * 512], op=ALU.mult), [ob, gate[s]], [tmp])
                fw.op("pool", lambda e: e.tensor_tensor(out=tmp[:], in0=tmp[:], in1=ht[:], op=ALU.add), [tmp, ht], [tmp])
                if final:
                    fw.dma("act", self.out, self.out.ap()[(tile - 2) * 128:(tile - 1) * 128, :], tmp, tmp[:])
                else:
                    fw.dma("act", hdst, hdst.ap()[tile * 128:(tile + 1) * 128, :], tmp, tmp[:])
        self.bank = Rot(self.banks)
        fw.pop()

    def l1_phaseA(self, hsrc):
        fw = self.fw
        I = self.I
        S = self.S1 = {}
        for nm in ("R", "KAP", "V", "G", "LW0", "LW1", "B0", "B1", "KD0", "KD1"):
            S[nm] = fw.dram("s1_" + nm, [TOK, D], F32, kind=("ExternalOutput" if self.dbg else "Internal"))
        S["BON"] = fw.dram("s1_BON", [TOK, 16], F32, kind=("ExternalOutput" if self.dbg else "Internal"))
        fw.push()
        GT, shT = self.modvecs(1, 0)
        self.make_normer()
        self.bank = Rot(self.banks)
        W = {}
        for nm in ("od_w_r", "od_w_k", "od_w_v"):
            W[nm] = fw.sb(nm, [128, 8, D], BF16)
            for k in range(8):
                fw.dma("pool", W[nm], W[nm][:, k, :], I[nm], I[nm].ap()[k * 128:(k + 1) * 128, :])
        g1 = fw.sb("g1", [128, 8, 128], BF16); g2 = fw.sb("g2", [128, D], BF16)
        fw.dma("pool", g1, g1[:], I["od_g1"], I["od_g1"].ap().rearrange("(k p) n -> p k n", p=128))
        fw.dma("pool", g2, g2[:], I["od_g2"], I["od_g2"].ap())
        w1c = fw.sb("w1c", [128, 8, 128], BF16); a1c = fw.sb("a1c", [128, 8, 128], BF16)
        for d in range(2):
            fw.dma("pool", w1c, w1c[:, :, d * 64:(d + 1) * 64], I["od_w1"], I["od_w1"].ap()[d].rearrange("(k p) n -> p k n", p=128))
            fw.dma("pool", a1c, a1c[:, :, d * 64:(d + 1) * 64], I["od_a1"], I["od_a1"].ap()[d].rearrange("(k p) n -> p k n", p=128))
        w2c = fw.sb("w2c", [128, D], BF16); a2c = fw.sb("a2c", [128, D], BF16)
        fw.dma("pool", w2c, w2c[:], I["od_w2"], I["od_w2"].ap().rearrange("d l n -> (d l) n"))
        fw.dma("pool", a2c, a2c[:], I["od_a2"], I["od_a2"].ap().rearrange("d l n -> (d l) n"))
        brow = fw.sb("brow", [1, 4, D], BF16)
        fw.dma("pool", brow, brow[:, 0:2, :], I["od_w0"], I["od_w0"].ap().rearrange("(o d) n -> o d n", o=1))
        fw.dma("pool", brow, brow[:, 2:4, :], I["od_a0"], I["od_a0"].ap().rearrange("(o d) n -> o d n", o=1))
        ones1 = fw.sb("ones1", [1, 128], BF16)
        fw.op("dve", lambda e: e.memset(ones1[:], 1.0), [], [ones1])
        kkb = fw.sb("kkb", [128, D]); kab = fw.sb("kab", [128, D]); rkb = fw.sb("rkb", [128, D])
        for t_, nm in ((kkb, "od_k_k"), (kab, "od_k_a"), (rkb, "od_r_k")):
            fw.dma("sp", t_, t_[:], I[nm], I[nm].ap().partition_broadcast(128))
        muT = fw.sb("muT", [128, 6, 8])
        for m in range(6):
            self.load_colT(muT, muT[:, m, :], I["od_mu"], I["od_mu"].ap()[m])
        NG = NT // 2
        win = [fw.sb("xnw", [128, 8, 256], BF16) for _ in range(4)]
        xxT = fw.sb("xxT", [128, 8, 256], BF16)
        xms = Rot([fw.sb("xm", [128, 8, 256], BF16) for _ in range(2)])
        tmpp = Rot([fw.sb("l1t", [128, D]) for _ in range(8)])
        ksb0 = fw.sb("ksb0", [128, D]); ksb1 = fw.sb("ksb1", [128, D])
        kap = fw.sb("kap", [128, D]); kka = fw.sb("kka", [128, D]); kmk = fw.sb("kmk", [128, D]); r_s = fw.sb("r_s", [128, D]); kd0 = fw.sb("kd0", [128, D])
        ss16 = Rot([fw.sb("ss16", [128, 16]) for _ in range(2)])
        lo = Rot([fw.sb("lo", [128, 128], BF16) for _ in range(3)])
        loT = Rot([fw.sb("loT", [128, 128], BF16) for _ in range(3)])

        def norm_group(g):
            s = 1 if g == 0 else 0
            for j in range(2):
                src, sap = self.src_of(2 * g + j, hsrc)
                self.norm_tile(src, sap, GT[s], shT[s], win[g % 4], slice(j * 128, (j + 1) * 128))

        def sub(out, a, b):
            fw.op("pool", lambda e: e.tensor_tensor(out=out, in0=a, in1=b, op=ALU.subtract), [win[0], win[1], win[2], win[3]], [xxT])

        def neg(out, a):
            fw.op("pool", lambda e: e.tensor_scalar(out=out, in0=a, scalar1=-1.0, scalar2=None, op0=ALU.mult), [win[0], win[1], win[2], win[3]], [xxT])

        def proj_tok(xm, t, Wt, nb=2):
            outs = []
            for half in range(nb):
                pb = self.bank.next()
                for k in range(8):
                    fw.op("pe", lambda e: e.matmul(pb[:], lhsT=xm[:, k, t * 128:(t + 1) * 128], rhs=Wt[:, k, half * 512:(half + 1) * 512], start=(k == 0), stop=(k == 7)), [xm, Wt], [pb])
                outs.append(pb)
            return outs

        def evac(dst, pbs):
            for half, pb in enumerate(pbs):
                fw.op("act", lambda e: e.activation(out=dst[:, half * 512:(half + 1) * 512], in_=pb[:], func=AF.Copy), [pb], [dst])

        def store(nm, tile, src, ap=None):
            fw.dma("sp", S[nm], S[nm].ap()[tile * 128:(tile + 1) * 128, :], src, src[:] if ap is None else ap)

        def lerp(m, cur):
            xm = xms.next()
            for k in range(8):
                fw.op("dve", lambda e: e.scalar_tensor_tensor(out=xm[:, k, :], in0=xxT[:, k, :], scalar=muT[:, m, k:k + 1], in1=cur[:, k, :], op0=ALU.mult, op1=ALU.add), [xxT, muT, cur], [xm])
            return xm

        def lora(xm, t, w1, w2, brow_i, func):
            pb = self.bank.next()
            for k in range(8):
                fw.op("pe", lambda e: e.matmul(pb[:, 0:128], lhsT=xm[:, k, t * 128:(t + 1) * 128], rhs=w1[:, k, :], start=(k == 0), stop=(k == 7)), [xm, w1], [pb])
            l_ = lo.next()
            fw.op("act", lambda e: e.activation(out=l_[:], in_=pb[:, 0:128], func=func), [pb], [l_])
            pt = self.bank.next(); pv = self.bview(pt)
            fw.op("pe", lambda e: e.transpose(out=pv[:, 0:128], in_=l_[:], identity=self.identb[:]), [l_, self.identb], [pt])
            lT = loT.next()
            fw.op("act", lambda e: e.activation(out=lT[:], in_=pv[:, 0:128], func=AF.Copy), [pt], [lT])
            res = []
            for d in range(2):
                outs = []
                for half in range(2):
                    po = self.bank.next()
                    fw.op("pe", lambda e: e.matmul(po[:], lhsT=ones1[:], rhs=brow[:, brow_i + d, half * 512:(half + 1) * 512], start=True, stop=False), [ones1, brow], [po])
                    fw.op("pe", lambda e: e.matmul(po[:], lhsT=lT[d * 64:(d + 1) * 64, :], rhs=w2[d * 64:(d + 1) * 64, half * 512:(half + 1) * 512], start=False, stop=True), [lT, w2], [po])
                    outs.append(po)
                res.append(outs)
            return res

        norm_group(0)
        import os
        for g in range(int(os.environ.get("L1A_NG", NG))):
            if g + 1 < NG:
                norm_group(g + 1)
            cur = win[g % 4]; prv = win[(g - 1) % 4]; nxt = win[(g + 1) % 4]
            for k in range(8):
                c = cur[:, k, :]; o = xxT[:, k, :]
                if g == 0:
                    if k < 4:
                        sub(o[:, 1:256], c[:, 0:255], c[:, 1:256]); neg(o[:, 0:1], c[:, 0:1])
                    else:
                        sub(o[:, 0:255], c[:, 1:256], c[:, 0:255]); neg(o[:, 255:256], c[:, 255:256])
                else:
                    c3 = c.rearrange("p (r c) -> p r c", c=64); o3 = o.rearrange("p (r c) -> p r c", c=64)
                    if k < 2:
                        sub(o3[:, :, 1:64], c3[:, :, 0:63], c3[:, :, 1:64]); neg(o3[:, :, 0:1], c3[:, :, 0:1])
                    elif k < 4:
                        sub(o3[:, :, 0:63], c3[:, :, 1:64], c3[:, :, 0:63]); neg(o3[:, :, 63:64], c3[:, :, 63:64])
                    elif k < 6:
                        sub(o[:, 64:256], c[:, 0:192], c[:, 64:256])
                        if g == 1:
                            neg(o[:, 0:64], c[:, 0:64])
                        else:
                            sub(o[:, 0:64], prv[:, k, 192:256], c[:, 0:64])
                    else:
                        sub(o[:, 0:192], c[:, 64:256], c[:, 0:192])
                        if g == NG - 1:
                            neg(o[:, 192:256], c[:, 192:256])
                        else:
                            sub(o[:, 192:256], nxt[:, k, 0:64], c[:, 192:256])
            xr = lerp(0, cur)
            for t in range(2):
                tile = 2 * g + t
                rr_ = tmpp.next()
                evac(rr_, proj_tok(xr, t, W["od_w_r"]))
                store("R", tile, rr_)
            xk = lerp(2, cur)
            ktiles = [ksb0, ksb1]
            for t in range(2):
                evac(ktiles[t], proj_tok(xk, t, W["od_w_k"]))
            xv = lerp(3, cur)
            for t in range(2):
                v_ = tmpp.next()
                evac(v_, proj_tok(xv, t, W["od_w_v"]))
                store("V", 2 * g + t, v_)
            xg = lerp(5, cur)
            for t in range(2):
                pb = self.bank.next()
                for k in range(8):
                    fw.op("pe", lambda e: e.matmul(pb[:, 0:128], lhsT=xg[:, k, t * 128:(t + 1) * 128], rhs=g1[:, k, :], start=(k == 0), stop=(k == 7)), [xg, g1], [pb])
                l_ = lo.next()
                fw.op("act", lambda e: e.activation(out=l_[:], in_=pb[:, 0:128], func=AF.Sigmoid), [pb], [l_])
                pt = self.bank.next(); pv = self.bview(pt)
                fw.op("pe", lambda e: e.transpose(out=pv[:, 0:128], in_=l_[:], identity=self.identb[:]), [l_, self.identb], [pt])
                lT = loT.next()
                fw.op("act", lambda e: e.activation(out=lT[:], in_=pv[:, 0:128], func=AF.Copy), [pt], [lT])
                g_ = tmpp.next()
                outs = []
                for half in range(2):
                    po = self.bank.next()
                    fw.op("pe", lambda e: e.matmul(po[:], lhsT=lT[:], rhs=g2[:, half * 512:(half + 1) * 512], start=True, stop=True), [lT, g2], [po])
                    outs.append(po)
                evac(g_, outs)
                store("G", 2 * g + t, g_)
            xw = lerp(1, cur)
            xa = lerp(4, cur)
            for t in range(2):
                tile = 2 * g + t
                ksb = ktiles[t]
                kkx = tmpp.next(); sq = tmpp.next()
                fw.op("pool", lambda e: e.tensor_tensor(out=kkx[:], in0=ksb[:], in1=kkb[:], op=ALU.mult), [ksb, kkb], [kkx])
                fw.op("pool", lambda e: e.tensor_tensor(out=sq[:], in0=kkx[:], in1=kkx[:], op=ALU.mult), [kkx], [sq])
                s16 = ss16.next()
                fw.op("dve", lambda e: e.reduce_sum(out=s16[:], in_=sq[:].rearrange("p (h j) -> p h j", j=64), axis=AX.X), [sq], [s16])
                fw.op("act", lambda e: e.activation(out=s16[:], in_=s16[:], func=AF.Sqrt, bias=1e-12, scale=1.0), [s16], [s16])
                fw.op("dve", lambda e: e.reciprocal(out=s16[:], in_=s16[:]), [s16], [s16])
                fw.op("dve", lambda e: e.tensor_tensor(out=kap[:].rearrange("p (h j) -> p h j", j=64), in0=kkx[:].rearrange("p (h j) -> p h j", j=64),
                                                       in1=s16[:].unsqueeze(2).to_broadcast([128, 16, 64]), op=ALU.mult), [kkx, s16], [kap])
                store("KAP", tile, kap)
                fw.op("pool", lambda e: e.tensor_tensor(out=kka[:], in0=ksb[:], in1=kab[:], op=ALU.mult), [ksb, kab], [kka])
                fw.op("pool", lambda e: e.tensor_tensor(out=kmk[:], in0=ksb[:], in1=kka[:], op=ALU.subtract), [ksb, kka], [kmk])
                wl = lora(xw, t, w1c, w2c, 0, AF.Tanh)
                for d in range(2):
                    lw = tmpp.next()
                    for half in range(2):
                        fw.op("act", lambda e: e.activation(out=lw[:, half * 512:(half + 1) * 512], in_=wl[d][half][:], func=AF.Sigmoid), [wl[d][half]], [lw])
                    fw.op("pool", lambda e: e.tensor_scalar(out=lw[:], in0=lw[:], scalar1=-0.6065306597126334, scalar2=None, op0=ALU.mult), [lw], [lw])
                    store("LW%d" % d, tile, lw)
                al = lora(xa, t, a1c, a2c, 2, AF.Copy)
                avs = []
                for d in range(2):
                    a_ = tmpp.next()
                    for half in range(2):
                        fw.op("act", lambda e: e.activation(out=a_[:, half * 512:(half + 1) * 512], in_=al[d][half][:], func=AF.Sigmoid), [al[d][half]], [a_])
                    avs.append(a_)
                kds = []
                for d in range(2):
                    a_ = avs[d]
                    b_ = tmpp.next()
                    fw.op("pool", lambda e: e.tensor_tensor(out=b_[:], in0=kap[:], in1=a_[:], op=ALU.mult), [kap, a_], [b_])
                    store("B%d" % d, tile, b_)
                    kd = kd0 if d == 0 else tmpp.next()
                    fw.op("pool", lambda e: e.tensor_tensor(out=kd[:], in0=kka[:], in1=a_[:], op=ALU.mult), [kka, a_], [kd])
                    fw.op("pool", lambda e: e.tensor_tensor(out=kd[:], in0=kd[:], in1=kmk[:], op=ALU.add), [kd, kmk], [kd])
                    store("KD%d" % d, tile, kd)
                    kds.append(kd)
                fw.dma("sp", r_s, r_s[:], S["R"], S["R"].ap()[tile * 128:(tile + 1) * 128, :])
                ks = tmpp.next()
                fw.op("pool", lambda e: e.tensor_tensor(out=ks[:], in0=kds[0][:], in1=kds[1][:], op=ALU.add), [kds[0], kds[1]], [ks])
                fw.op("pool", lambda e: e.tensor_tensor(out=ks[:], in0=ks[:], in1=rkb[:], op=ALU.mult), [ks, rkb], [ks])
                fw.op("pool", lambda e: e.tensor_tensor(out=ks[:], in0=ks[:], in1=r_s[:], op=ALU.mult), [ks, r_s], [ks])
                bon = ss16.next()
                fw.op("dve", lambda e: e.reduce_sum(out=bon[:], in_=ks[:].rearrange("p (h j) -> p h j", j=64), axis=AX.X), [ks], [bon])
                fw.dma("sp", S["BON"], S["BON"].ap()[tile * 128:(tile + 1) * 128, :], bon, bon[:])
        fw.pop()

    def l1_phaseB(self):
        fw = self.fw
        I = self.I
        S = self.S1
        k_ = "ExternalOutput" if self.dbg else "Internal"
        self.Y = [fw.dram("s1_Y%d" % d, [TOK, D], F32, kind=k_) for d in range(2)]
        fw.push()
        ybanks = self.banks[0:2]
        pbanks = Rot(self.banks[2:8])
        hb = []
        hbr = Rot([(self.banks[2 + (i % 6)], (i // 6) * 256) for i in range(12)])
        sml = hbr
        tm = fw.sb("tm", [128, 4, 128])
        fw.dma("sp", tm, tm[:], I["trimask"], I["trimask"].ap())
        ones = fw.sb("ones", [128, 128])
        fw.op("dve", lambda e: e.memset(ones[:], 1.0), [], [ones])
        MA = []; MB = []; MN = []; MC = []
        for d in range(2):
            mS, mI, mSn, mC = (0, 1, 2, 1) if d == 0 else (2, 3, 0, 3)
            a = fw.sb("MA", [128, 2, 128]); b = fw.sb("MB", [128, 2, 128]); n = fw.sb("MN", [128, 128])
            fw.op("dve", lambda e: e.tensor_scalar(out=a[:, 0, :], in0=tm[:, mS, :], scalar1=-1.0, scalar2=None, op0=ALU.mult), [tm], [a])
            fw.op("dve", lambda e: e.tensor_copy(out=a[:, 1, :], in_=tm[:, mI, :]), [tm], [a])
            fw.op("dve", lambda e: e.tensor_copy(out=b[:, 0, :], in_=tm[:, mS, :]), [tm], [b])
            fw.op("dve", lambda e: e.tensor_copy(out=b[:, 1, :], in_=tm[:, mI, :]), [tm], [b])
            fw.op("dve", lambda e: e.tensor_scalar(out=n[:], in0=tm[:, mSn, :], scalar1=-1.0, scalar2=None, op0=ALU.mult), [tm], [n])
            MA.append(a); MB.append(b); MN.append(n); MC.append(mC)
        H = fw.sb("H", [64, 16, 64])
        ld = {nm: Rot([fw.sb("ld" + nm, [128, D]) for _ in range(2)]) for nm in ("R", "KAP", "V", "LW", "B", "KD")}
        prep = {nm: fw.sb("pp" + nm, [128, D], BF16 if nm in ("rt", "kt", "bt", "kdt") else F32) for nm in ("rt", "kt", "bt", "kdt", "bh", "kh")}
        Vb = fw.sb("Vbf", [128, D], BF16); Hb = fw.sb("Hb", [64, 16, 64], BF16)
        cumS = fw.sb("cumS", [128, 512]); et = Rot([fw.sb("et", [128, 512]) for _ in range(3)])
        gC = fw.sb("gC", [64, 16])
        NH = 16
        TT = Rot([fw.sb("TT", [64, 4, 128], BF16) for _ in range(NH)])
        SC = Rot([fw.sb("SC", [128, 4, 128], BF16) for _ in range(NH)])
        NM = Rot([fw.sb("NM", [128, 2, 128], BF16) for _ in range(NH * 3)])
        XS = Rot([fw.sb("XS", [128, 64], BF16) for _ in range(NH * 3)])
        UF = Rot([fw.sb("UF", [128, 64]) for _ in range(NH)])
        ysb = Rot([fw.sb("ysb", [128, D]) for _ in range(2)])
        import os
        nchunks = int(os.environ.get("L1B_NC", NT))

        def head_gen(h, d, T_):
            R_, KAP_, V_, LW_, B_, KD_ = T_
            hs = slice(h * 64, (h + 1) * 64)
            tt = TT.next(); sc = SC.next()
            for pair, (x0, x1) in enumerate(((prep["kt"], prep["rt"]), (prep["bt"], prep["kdt"]))):
                bk, off = hbr.next()
                bv = self.bview(bk)
                fw.op("pe", lambda e: e.transpose(out=bv[0:64, 2 * off:2 * off + 128], in_=x0[:, hs], identity=self.identb[:]), [x0, self.identb], [bk])
                fw.op("pe", lambda e: e.transpose(out=bv[0:64, 2 * off + 128:2 * off + 256], in_=x1[:, hs], identity=self.identb[:]), [x1, self.identb], [bk])
                fw.op("act", lambda e: e.activation(out=tt[:, 2 * pair:2 * pair + 2, :].rearrange("p a b -> p (a b)"), in_=bv[0:64, 2 * off:2 * off + 256], func=AF.Copy), [bk], [tt])
            yield
            kT = tt[:, 0, :]; rT = tt[:, 1, :]; bT = tt[:, 2, :]; kdT = tt[:, 3, :]
            kr = tt[:, 0:2, :].rearrange("p a b -> p (a b)")
            bk, off = hbr.next()
            fw.op("pe", lambda e: e.matmul(bk[:, off:off + 256], lhsT=bT, rhs=kr, start=True, stop=True), [tt], [bk])
            fw.op("dve", lambda e: e.tensor_tensor(out=sc[:, 0:2, :].rearrange("p a b -> p (a b)"), in0=bk[:, off:off + 256], in1=MA[d][:].rearrange("p a b -> p (a b)"), op=ALU.mult), [bk, MA[d]], [sc])
            bk, off = hbr.next()
            fw.op("pe", lambda e: e.matmul(bk[:, off:off + 256], lhsT=kdT, rhs=kr, start=True, stop=True), [tt], [bk])
            fw.op("dve", lambda e: e.tensor_tensor(out=sc[:, 2:4, :].rearrange("p a b -> p (a b)"), in0=bk[:, off:off + 256], in1=MB[d][:].rearrange("p a b -> p (a b)"), op=ALU.mult), [bk, MB[d]], [sc])
            nm = NM.next()
            bk, off = hbr.next()
            fw.op("pe", lambda e: e.matmul(bk[:, off:off + 128], lhsT=kT, rhs=bT, start=True, stop=True), [tt], [bk])
            fw.op("dve", lambda e: e.tensor_tensor(out=nm[:, 0, :], in0=bk[:, off:off + 128], in1=MN[d][:], op=ALU.mult), [bk, MN[d]], [nm])
            fw.op("act", lambda e: e.activation(out=nm[:, 1, :], in_=sc[:, 0, :], func=AF.Copy), [sc], [nm])
            yield
            NT_ = sc[:, 0, :]; BrbT = sc[:, 1, :]; AkT = sc[:, 2, :]; BrkT = sc[:, 3, :]
            bk, off = sml.next()
            fw.op("pe", lambda e: e.matmul(bk[:, off:off + 64], lhsT=kT, rhs=Hb[:, h, :], start=True, stop=False), [tt, Hb], [bk])
            fw.op("pe", lambda e: e.matmul(bk[:, off:off + 64], lhsT=AkT, rhs=Vb[:, hs], start=False, stop=True), [sc, Vb], [bk])
            X = XS.next()
            fw.op("act", lambda e: e.activation(out=X[:], in_=bk[:, off:off + 64], func=AF.Copy, scale=-1.0), [bk], [X])
            yield
            cur = nm
            for lvl in range(7):
                bk, off = sml.next()
                fw.op("pe", lambda e: e.matmul(bk[:, off:off + 64], lhsT=cur[:, 1, :], rhs=X[:], start=True, stop=True), [cur, X], [bk])
                X2 = XS.next() if lvl < 6 else UF.next()
                fw.op("dve", lambda e: e.tensor_tensor(out=X2[:], in0=bk[:, off:off + 64], in1=X[:], op=ALU.add), [bk, X], [X2])
                X = X2
                if lvl < 6:
                    nxt = NM.next()
                    bk, off = hbr.next()
                    if lvl < 5:
                        fw.op("pe", lambda e: e.matmul(bk[:, off:off + 128], lhsT=cur[:, 1, :], rhs=cur[:, 0, :], start=True, stop=True), [cur], [bk])
                    fw.op("pe", lambda e: e.matmul(bk[:, off + 128:off + 256], lhsT=cur[:, 0, :], rhs=cur[:, 1, :], start=True, stop=True), [cur], [bk])
                    if lvl < 5:
                        fw.op("act", lambda e: e.activation(out=nxt[:].rearrange("p a b -> p (a b)"), in_=bk[:, off:off + 256], func=AF.Copy), [bk], [nxt])
                    else:
                        fw.op("act", lambda e: e.activation(out=nxt[:, 1, :], in_=bk[:, off + 128:off + 256], func=AF.Copy), [bk], [nxt])
                    cur = nxt
                yield
            U = X
            Ub = XS.next()
            fw.op("act", lambda e: e.activation(out=Ub[:], in_=U[:], func=AF.Copy), [U], [Ub])
            yb = ybanks[h // 8]; yo = (h % 8) * 64
            fw.op("pe", lambda e: e.matmul(yb[:, yo:yo + 64], lhsT=rT, rhs=Hb[:, h, :], start=True, stop=False), [tt, Hb], [yb])
            fw.op("pe", lambda e: e.matmul(yb[:, yo:yo + 64], lhsT=BrbT, rhs=Ub[:], start=False, stop=False), [sc, Ub], [yb])
            fw.op("pe", lambda e: e.matmul(yb[:, yo:yo + 64], lhsT=BrkT, rhs=Vb[:, hs], start=False, stop=True), [sc, Vb], [yb])
            bk, off = sml.next()
            fw.op("pe", lambda e: e.matmul(bk[0:64, off:off + 64], lhsT=prep["bh"][:, hs], rhs=U[:], start=True, stop=False), [prep["bh"], U], [bk])
            fw.op("pe", lambda e: e.matmul(bk[0:64, off:off + 64], lhsT=prep["kh"][:, hs], rhs=V_[:, hs], start=False, stop=True), [prep["kh"], V_], [bk])
            fw.op("dve", lambda e: e.scalar_tensor_tensor(out=H[:, h, :], in0=H[:, h, :], scalar=gC[:, h:h + 1], in1=bk[0:64, off:off + 64], op0=ALU.mult, op1=ALU.add), [H, gC, bk], [H])
            yield

        for d in range(2):
            fw.op("dve", lambda e: e.memset(H[:], 0.0), [], [H])
            fw.op("dve", lambda e: e.memset(Hb[:], 0.0), [], [Hb])
            order = list(range(NT)) if d == 0 else [1, 0] + list(range(NT - 1, 1, -1))
            for c in order[:nchunks]:
                rows = slice(c * 128, (c + 1) * 128)
                T_ = []
                for nm, key in (("R", "R"), ("KAP", "KAP"), ("V", "V"), ("LW", "LW%d" % d), ("B", "B%d" % d), ("KD", "KD%d" % d)):
                    t_ = ld[nm].next()
                    fw.dma("sp", t_, t_[:], S[key], S[key].ap()[rows, :])
                    T_.append(t_)
                R_, KAP_, V_, LW_, B_, KD_ = T_
                fw.op("pool", lambda e: e.tensor_copy(out=Vb[:], in_=V_[:]), [V_], [Vb])
                bk, off = sml.next()
                for h in range(16):
                    fw.op("pe", lambda e: e.matmul(bk[0:64, off + h:off + h + 1], lhsT=LW_[:, h * 64:(h + 1) * 64], rhs=ones[:, 0:1], start=True, stop=True), [LW_, ones], [bk])
                fw.op("act", lambda e: e.activation(out=gC[:], in_=bk[0:64, off:off + 16], func=AF.Exp), [bk], [gC])
                for half in range(2):
                    cs = slice(half * 512, (half + 1) * 512)
                    pc = pbanks.next(); ptot = pbanks.next()
                    fw.op("pe", lambda e: e.matmul(pc[:], lhsT=tm[:, MC[d], :], rhs=LW_[:, cs], start=True, stop=True), [tm, LW_], [pc])
                    fw.op("pe", lambda e: e.matmul(ptot[:], lhsT=ones[:], rhs=LW_[:, cs], start=True, stop=True), [ones, LW_], [ptot])
                    fw.op("act", lambda e: e.activation(out=cumS[:], in_=pc[:], func=AF.Copy), [pc], [cumS])
                    e1 = et.next()
                    fw.op("act", lambda e: e.activation(out=e1[:], in_=pc[:], func=AF.Exp), [pc], [e1])
                    fw.op("pool", lambda e: e.tensor_tensor(out=prep["rt"][:, cs], in0=R_[:, cs], in1=e1[:], op=ALU.mult), [R_, e1], [prep["rt"]])
                    e2 = et.next()
                    fw.op("act", lambda e: e.activation(out=e2[:], in_=pc[:], func=AF.Exp, scale=-1.0), [pc], [e2])
                    fw.op("pool", lambda e: e.tensor_tensor(out=prep["bt"][:, cs], in0=B_[:, cs], in1=e2[:], op=ALU.mult), [B_, e2], [prep["bt"]])
                    fw.op("pool", lambda e: e.tensor_tensor(out=prep["kdt"][:, cs], in0=KD_[:, cs], in1=e2[:], op=ALU.mult), [KD_, e2], [prep["kdt"]])
                    e3 = et.next()
                    fw.op("dve", lambda e: e.tensor_tensor(out=e3[:], in0=cumS[:], in1=LW_[:, cs], op=ALU.subtract), [cumS, LW_], [e3])
                    fw.op("act", lambda e: e.activation(out=e3[:], in_=e3[:], func=AF.Exp), [e3], [e3])
                    fw.op("pool", lambda e: e.tensor_tensor(out=prep["kt"][:, cs], in0=KAP_[:, cs], in1=e3[:], op=ALU.mult), [KAP_, e3], [prep["kt"]])
                    e4 = et.next()
                    fw.op("dve", lambda e: e.tensor_tensor(out=e4[:], in0=ptot[:], in1=cumS[:], op=ALU.subtract), [ptot, cumS], [e4])
                    fw.op("act", lambda e: e.activation(out=e4[:], in_=e4[:], func=AF.Exp), [e4], [e4])
                    fw.op("dve", lambda e: e.tensor_tensor(out=prep["bh"][:, cs], in0=B_[:, cs], in1=e4[:], op=ALU.mult), [B_, e4], [prep["bh"]])
                    fw.op("dve", lambda e: e.tensor_tensor(out=prep["kh"][:, cs], in0=KD_[:, cs], in1=e4[:], op=ALU.mult), [KD_, e4], [prep["kh"]])
                for h0 in range(0, 16, NH):
                    gens = [head_gen(h, d, T_) for h in range(h0, h0 + NH)]
                    alive = True
                    while alive:
                        alive = False
                        for gi in gens:
                            try:
                                next(gi)
                                alive = True
                            except StopIteration:
                                pass
                fw.op("act", lambda e: e.activation(out=Hb[:].rearrange("p a b -> p (a b)"), in_=H[:].rearrange("p a b -> p (a b)"), func=AF.Copy), [H], [Hb])
                ys = ysb.next()
                for half in range(2):
                    fw.op("act", lambda e: e.activation(out=ys[:, half * 512:(half + 1) * 512], in_=ybanks[half][:], func=AF.Copy), [ybanks[half]], [ys])
                fw.dma("act", self.Y[d], self.Y[d].ap()[rows, :], ys, ys[:])
        fw.pop()

    def l1_phaseC(self, hsrc, hdst):
        fw = self.fw
        I = self.I
        S = self.S1
        fw.push()
        self.bank = Rot(self.banks)
        gate = self.gate_bcast(1, 0)
        w_o = fw.sb("w_o", [128, 8, D], BF16)
        for k in range(8):
            fw.dma("pool", w_o, w_o[:, k, :], I["od_w_o"], I["od_w_o"].ap()[k * 128:(k + 1) * 128, :])
        gnw = fw.sb("gnw", [128, D]); gnb = fw.sb("gnb", [128, D])
        fw.dma("sp", gnw, gnw[:], I["od_gn_w"], I["od_gn_w"].ap().partition_broadcast(128))
        fw.dma("sp", gnb, gnb[:], I["od_gn_b"], I["od_gn_b"].ap().partition_broadcast(128))
        L = {nm: Rot([fw.sb("c" + nm, [128, D]) for _ in range(2)]) for nm in ("y0", "y1", "v", "g", "h")}
        bons = Rot([fw.sb("cbon", [128, 16]) for _ in range(2)])
        st = Rot([fw.sb("cst", [128, 16]) for _ in range(6)])
        wk = Rot([fw.sb("cwk", [128, D]) for _ in range(4)])
        obf = Rot([fw.sb("cob", [128, D], BF16) for _ in range(2)])
        oTs = Rot([fw.sb("coT", [128, 8, 128], BF16) for _ in range(2)])
        v3 = lambda t_: t_[:].rearrange("p (h j) -> p h j", j=64)
        b3 = lambda t_: t_[:].unsqueeze(2).to_broadcast([128, 16, 64])
        for tile in range(2, NT):
            rows = slice(tile * 128, (tile + 1) * 128)
            y0 = L["y0"].next(); y1 = L["y1"].next(); v = L["v"].next(); g = L["g"].next(); ht = L["h"].next(); bon = bons.next()
            fw.dma("sp", y0, y0[:], self.Y[0], self.Y[0].ap()[rows, :])
            fw.dma("sp", y1, y1[:], self.Y[1], self.Y[1].ap()[rows, :])
            fw.dma("sp", v, v[:], S["V"], S["V"].ap()[rows, :])
            fw.dma("sp", g, g[:], S["G"], S["G"].ap()[rows, :])
            fw.dma("sp", bon, bon[:], S["BON"], S["BON"].ap()[rows, :])
            src, sap = self.src_of(tile, hsrc)
            fw.dma("sp", ht, ht[:], src, sap)
            ysum = wk.next()
            fw.op("pool", lambda e: e.tensor_tensor(out=ysum[:], in0=y0[:], in1=y1[:], op=ALU.add), [y0, y1], [ysum])
            mean = st.next(); var = st.next()
            fw.op("dve", lambda e: e.reduce_sum(out=mean[:], in_=v3(ysum), axis=AX.X), [ysum], [mean])
            fw.op("dve", lambda e: e.tensor_scalar(out=mean[:], in0=mean[:], scalar1=1.0 / 64, scalar2=None, op0=ALU.mult), [mean], [mean])
            yc = wk.next()
            fw.op("dve", lambda e: e.tensor_tensor(out=v3(yc), in0=v3(ysum), in1=b3(mean), op=ALU.subtract), [ysum, mean], [yc])
            sq = wk.next()
            fw.op("pool", lambda e: e.tensor_tensor(out=sq[:], in0=yc[:], in1=yc[:], op=ALU.mult), [yc], [sq])
            fw.op("dve", lambda e: e.reduce_sum(out=var[:], in_=v3(sq), axis=AX.X), [sq], [var])
            fw.op("act", lambda e: e.activation(out=var[:], in_=var[:], func=AF.Sqrt, bias=64e-5, scale=1.0 / 64), [var], [var])
            fw.op("dve", lambda e: e.reciprocal(out=var[:], in_=var[:]), [var], [var])
            fw.op("dve", lambda e: e.tensor_tensor(out=v3(yc), in0=v3(yc), in1=b3(var), op=ALU.mult), [yc, var], [yc])
            fw.op("pool", lambda e: e.tensor_tensor(out=yc[:], in0=yc[:], in1=gnw[:], op=ALU.mult), [yc, gnw], [yc])
            fw.op("pool", lambda e: e.tensor_tensor(out=yc[:], in0=yc[:], in1=gnb[:], op=ALU.add), [yc, gnb], [yc])
            bv = wk.next()
            fw.op("dve", lambda e: e.tensor_tensor(out=v3(bv), in0=v3(v), in1=b3(bon), op=ALU.mult), [v, bon], [bv])
            fw.op("pool", lambda e: e.tensor_tensor(out=yc[:], in0=yc[:], in1=bv[:], op=ALU.add), [yc, bv], [yc])
            ob = obf.next()
            fw.op("dve", lambda e: e.tensor_tensor(out=ob[:], in0=yc[:], in1=g[:], op=ALU.mult), [yc, g], [ob])
            pb = self.bank.next(); pv = self.bview(pb)
            for k in range(8):
                fw.op("pe", lambda e: e.transpose(out=pv[:, k * 128:(k + 1) * 128], in_=ob[:, k * 128:(k + 1) * 128], identity=self.identb[:]), [ob, self.identb], [pb])
            oT = oTs.next()
            fw.op("act", lambda e: e.activation(out=oT[:].rearrange("p k n -> p (k n)"), in_=pv[:, 0:1024], func=AF.Copy), [pb], [oT])
            tmp = wk.next()
            for half in range(2):
                po = self.bank.next()
                for k in range(8):
                    fw.op("pe", lambda e: e.matmul(po[:], lhsT=oT[:, k, :], rhs=w_o[:, k, half * 512:(half + 1) * 512], start=(k == 0), stop=(k == 7)), [oT, w_o], [po])
                fw.op("dve", lambda e: e.tensor_tensor(out=tmp[:, half * 512:(half + 1) * 512], in0=po[:], in1=gate[0][:, half * 512:(half + 1) * 512], op=ALU.mult), [po, gate[0]], [tmp])
            fw.op("pool", lambda e: e.tensor_tensor(out=tmp[:], in0=tmp[:], in1=ht[:], op=ALU.add), [tmp, ht], [tmp])
            fw.dma("act", hdst, hdst.ap()[rows, :], tmp, tmp[:])
        fw.pop()

    def finish(self):
        fw = self.fw
        fw.barrier()
        fw.finish("sp", [self.out])
        print("ninst", fw.ninst, "nsem", fw.nsem)
        self.es.close()
        return self.nc


def host_consts():
    c = {}
    c["ident"] = np.eye(128, dtype=np.float32)
    t = np.arange(NLAT)
    row = (t // 64).astype(np.float32); col = (t % 64).astype(np.float32)
    inv = (10000.0 ** (-np.arange(0, 32, 2, dtype=np.float32) / 32)).astype(np.float32)
    ang = np.concatenate([row[:, None] * inv, col[:, None] * inv], axis=-1)
    cosl = np.cos(ang).astype(np.float32); sinl = np.sin(ang).astype(np.float32)
    cos = np.concatenate([np.ones((NCTX, 32), np.float32), cosl], 0)
    sin = np.concatenate([np.zeros((NCTX, 32), np.float32), sinl], 0)
    d = np.arange(128) % 64
    c["ropeC"] = np.ascontiguousarray(cos[:, d // 2].T)
    c["ropeS"] = np.ascontiguousarray(sin[:, d // 2].T)
    P = np.zeros((128, 128), np.float32)
    for i in range(64):
        P[2 * i + 1, 2 * i] = -1.0
        P[2 * i, 2 * i + 1] = 1.0
    c["ropeP"] = P
    b = np.zeros((128, 128), np.float32)
    b[:64, :64] = 1; b[64:, 64:] = 1
    c["blk64"] = b
    s_ = np.arange(128)[:, None]; t_ = np.arange(128)[None, :]
    tm = np.zeros((128, 4, 128), np.float32)
    tm[:, 0] = (s_ < t_); tm[:, 1] = (s_ <= t_); tm[:, 2] = (s_ > t_); tm[:, 3] = (s_ >= t_)
    c["trimask"] = tm
    c["iota128"] = np.tile(np.arange(128, dtype=np.float32)[None, :], (128, 1))
    return c


def shard_inputs(inputs):
    f = lambda a: np.ascontiguousarray(np.asarray(a, dtype=np.float32))
    w_in = f(inputs["ev_w_in"])[0]
    w_in_r = np.concatenate([w_in[:, 0:2048], w_in[:, 2048:2112], w_in[:, 2048:2112], w_in[:, 2112:2176], w_in[:, 2112:2176], w_in[:, 2176:2304]], axis=1)
    shared = dict(
        ada_w=f(inputs["ada_w"]), ada_b=f(inputs["ada_b"]), norm1_g=f(inputs["norm1_g"]), norm2_g=f(inputs["norm2_g"]),
        w_in=f(w_in_r), conv_w=f(inputs["ev_conv_w"])[0], qg2=f(np.tile(f(inputs["ev_q_gain"])[0], 2)), kg2=f(np.tile(f(inputs["ev_k_gain"])[0], 2)),
        w_out=f(inputs["ev_w_out"])[0], od_mu=f(inputs["od_mu"])[0], od_w_r=f(inputs["od_w_r"])[0], od_w_k=f(inputs["od_w_k"])[0],
        od_w_v=f(inputs["od_w_v"])[0], od_w_o=f(inputs["od_w_o"])[0], od_g1=f(inputs["od_g1"])[0], od_g2=f(inputs["od_g2"])[0],
        od_k_k=f(inputs["od_k_k"])[0], od_k_a=f(inputs["od_k_a"])[0], od_r_k=f(inputs["od_r_k"])[0].reshape(-1),
        od_w0=f(inputs["od_w0"])[0], od_w1=f(inputs["od_w1"])[0], od_w2=f(inputs["od_w2"])[0], od_a0=f(inputs["od_a0"])[0],
        od_a1=f(inputs["od_a1"])[0], od_a2=f(inputs["od_a2"])[0], od_gn_w=f(inputs["od_gn_w"])[0], od_gn_b=f(inputs["od_gn_b"])[0],
        peer_wq=f(inputs["peer_wq"]), peer_keys=f(inputs["peer_keys"]).reshape(2, 16, 128, 128), peer_u=f(inputs["peer_u"]), peer_v=f(inputs["peer_v"]),
    )
    shared.update(host_consts())
    x = f(inputs["x"]); ctx = f(inputs["ctx"]); c = f(inputs["c"]); cc = f(inputs["c_ctx"])
    maps = []
    for b in range(8):
        m = dict(shared)
        m["x"] = x[b]; m["ctx"] = ctx[b]; m["cvec"] = np.ascontiguousarray(np.stack([c[b], cc], 0))
        maps.append(m)
    return maps


def build(dbg=False, stop=None):
    p = Prog(dbg=dbg, stop=stop)
    p.phase0()
    if stop == "p0":
        return p
    if stop and stop.startswith("L1"):
        p.l1_phaseA(p.h2)
        if stop == "L1a":
            return p
        p.l1_phaseB()
        if stop == "L1b":
            return p
        p.l1_phaseC(p.h2, p.h3)
        return p
    p.l0_phaseA()
    if stop == "l0a":
        p.fw.pop()
        return p
    p.l0_phaseB()
    if stop == "l0":
        return p
    if os.environ.get("PEER_V", "2") == "2":
        p.peer_prep2(0)
        p.peer2(0, p.h1, p.h2)
    else:
        p.peer_prep(0)
        p.peer(0, p.h1, p.h2)
    if stop == "peer0":
        return p
    p.l1_phaseA(p.h2)
    if stop == "l1a":
        return p
    p.l1_phaseB()
    if stop == "l1b":
        return p
    p.l1_phaseC(p.h2, p.h3)
    if stop == "l1c":
        return p
    if os.environ.get("PEER_V", "2") == "2":
        p.peer_prep2(1)
        p.peer2(1, p.h3, None, final=True)
    else:
        p.peer_prep(1)
        p.peer(1, p.h3, None, final=True)
    return p


def kernel(**inputs):
    p = build()
    nc = p.finish()
    maps = shard_inputs(inputs)
    maps = [{k: v for k, v in m.items() if k in p.I} for m in maps]
    res = run_bass_kernel_spmd(nc, maps, core_ids=list(range(8)))
    return np.stack([r["out"] for r in res.results], 0)
```

```python
import os
import numpy as np
from contextlib import ExitStack
import concourse.bass as bass
import concourse.mybir as mybir
from concourse.bass_utils import run_bass_kernel_spmd

F32 = mybir.dt.float32
BF16 = mybir.dt.bfloat16
U32 = mybir.dt.uint32
ALU = mybir.AluOpType
AF = mybir.ActivationFunctionType
AX = mybir.AxisListType

D = 1024
NCTX = 256
NLAT = 4096
TOK = NCTX + NLAT
NT = TOK // 128
TOKP = TOK + 4
EPS = 1e-6
INPUT_SHAPES = {
    "x": [NLAT, D],
    "ctx": [NCTX, D],
    "cvec": [2, D],
    "ada_w": [2, D, 6 * D],
    "ada_b": [2, 6 * D],
    "norm1_g": [2, D],
    "norm2_g": [2, D],
    "w_in": [D, 2432],
    "conv_w": [3, 512],
    "qg2": [128],
    "kg2": [128],
    "w_out": [D, D],
    "od_mu": [6, D],
    "od_w_r": [D, D],
    "od_w_k": [D, D],
    "od_w_v": [D, D],
    "od_w_o": [D, D],
    "od_g1": [D, 128],
    "od_g2": [128, D],
    "od_k_k": [D],
    "od_k_a": [D],
    "od_r_k": [D],
    "od_w0": [2, D],
    "od_w1": [2, D, 64],
    "od_w2": [2, 64, D],
    "od_a0": [2, D],
    "od_a1": [2, D, 64],
    "od_a2": [2, 64, D],
    "od_gn_w": [D],
    "od_gn_b": [D],
    "peer_wq": [2, D, 2048],
    "peer_keys": [2, 16, 128, 128],
    "peer_u": [2, 16384, D],
    "peer_v": [2, 16384, D],
    "ident": [128, 128],
    "ropeC": [128, TOK],
    "ropeS": [128, TOK],
    "ropeP": [128, 128],
    "blk64": [128, 128],
    "trimask": [128, 4, 128],
    "iota128": [128, 128],
}


class T:
    __slots__ = ("t", "w", "r", "sem", "dcount", "name")

    def __init__(self, t, name):
        self.t = t
        self.name = name
        self.w = {}
        self.r = {}
        self.sem = None
        self.dcount = 0

    def __getitem__(self, k):
        return self.t[k]

    def ap(self):
        return self.t.ap()


class FW:
    def __init__(self, nc, es):
        self.nc = nc
        self.es = es
        self.root_es = es
        self.stack = []
        self.live = []
        self.drams = []
        self.sempool = []
        self.uid = 0
        self.eng = {"pe": nc.tensor, "act": nc.scalar, "dve": nc.vector, "pool": nc.gpsimd, "sp": nc.sync}
        self.esem = {}
        self.ecount = {}
        self.known = {}
        self.nsem = 0
        for k in self.eng:
            self.esem[k] = self.newsem("e_" + k)
            self.ecount[k] = 0
            self.known[k] = {}
        self.selfwait = {"pe": False, "act": True, "dve": True, "pool": True, "sp": False}
        self.ninst = 0

    def newsem(self, name):
        self.nsem += 1
        assert self.nsem < 200, "too many semaphores"
        return self.root_es.enter_context(self.nc.semaphore(name))

    def push(self):
        self.stack.append((self.es, self.live))
        self.es = ExitStack()
        self.live = []

    def pop(self):
        self.barrier()
        for t in self.live:
            if t.sem is not None:
                self.sempool.append((t.sem, t.dcount))
                t.sem = None
        self.es.close()
        self.es, self.live = self.stack.pop()

    def barrier(self):
        toks = {}
        for k in self.eng:
            if self.ecount[k] > 0:
                toks[self.esem[k]] = self.ecount[k]
        for _, live in self.stack + [(None, self.live)]:
            for t in live:
                if t.sem is not None and t.dcount > 0:
                    toks[t.sem] = 16 * t.dcount
        for t in self.drams:
            if t.sem is not None and t.dcount > 0:
                toks[t.sem] = 16 * t.dcount
        for e in self.eng:
            kn = self.known[e]
            for s_, v in toks.items():
                if s_ is self.esem[e]:
                    continue
                if kn.get(s_, 0) < v:
                    self.eng[e].wait_ge(s_, v)
                    kn[s_] = v
                    self.ninst += 1

    def sb(self, name, shape, dt=F32):
        self.uid += 1
        t = T(self.es.enter_context(self.nc.sbuf_tensor("%s_%d" % (name, self.uid), list(shape), dt)), name)
        self.live.append(t)
        return t

    def ps(self, name, shape, dt=F32):
        t = T(self.es.enter_context(self.nc.psum_tensor(name, list(shape), dt)), name)
        self.live.append(t)
        return t

    def dram(self, name, shape, dt=F32, kind="Internal"):
        t = T(self.nc.dram_tensor(name, list(shape), dt, kind=kind), name)
        self.drams.append(t)
        return t

    def _waits(self, e, reads, writes):
        toks = {}

        def add(tok):
            if tok is None:
                return
            s, v = tok
            if toks.get(s, 0) < v:
                toks[s] = v
        for r in reads:
            for s, v in r.w.items():
                add((s, v))
        for w in writes:
            for s, v in w.w.items():
                add((s, v))
            for s, v in w.r.items():
                add((s, v))
        eng = self.eng[e]
        kn = self.known[e]
        for s, v in toks.items():
            if s is self.esem[e] and not self.selfwait[e]:
                continue
            if kn.get(s, 0) < v:
                eng.wait_ge(s, v)
                kn[s] = v
                self.ninst += 1

    def op(self, e, fn, reads=(), writes=()):
        self._waits(e, reads, writes)
        inst = fn(self.eng[e])
        self.ecount[e] += 1
        self.ninst += 1
        s = self.esem[e]
        inst.then_inc(s, 1)
        tok = (s, self.ecount[e])
        for w in writes:
            w.w = {s: tok[1]}
            w.r = {}
        for r in reads:
            if r.r.get(s, 0) < tok[1]:
                r.r[s] = tok[1]
        return inst

    def dma(self, e, outT, out_ap, inT, in_ap, **kw):
        self._waits(e, [inT], [outT])
        own = inT if (outT in self.drams and inT not in self.drams) else outT
        if own.sem is None:
            if self.sempool:
                own.sem, own.dcount = self.sempool.pop()
            else:
                own.sem = self.newsem("d%d" % self.nsem)
        inst = self.eng[e].dma_start(out=out_ap, in_=in_ap, **kw)
        inst.then_inc(own.sem, 16)
        self.ninst += 1
        own.dcount += 1
        tok = (own.sem, 16 * own.dcount)
        if own is outT:
            outT.w = {tok[0]: tok[1]}
        else:
            outT.w[tok[0]] = tok[1]
        outT.r = {}
        if inT.r.get(tok[0], 0) < tok[1]:
            inT.r[tok[0]] = tok[1]
        return inst

    def finish(self, e, tiles):
        self._waits(e, tiles, [])


class Rot:
    def __init__(self, items):
        self.items = items
        self.i = -1

    def next(self):
        self.i = (self.i + 1) % len(self.items)
        return self.items[self.i]


def tokcol(tile):
    return 1 + tile * 128 if tile < 2 else 259 + (tile - 2) * 128


class Prog:
    def __init__(self, dbg=False, stop=None):
        self.dbg = dbg
        self.stop = stop
        self.nc = bass.Bass("TRN2", target_bir_lowering=False)
        self.es = ExitStack()
        self.fw = FW(self.nc, self.es)
        fw = self.fw

        prog = self

        class Lazy(dict):
            def __missing__(d, name):
                t = fw.dram(name, INPUT_SHAPES[name], F32, kind="ExternalInput")
                d[name] = t
                return t
        self.I = Lazy()
        self.out = fw.dram("out", [NLAT, D], F32, kind="ExternalOutput")
        k = "ExternalOutput" if dbg else "Internal"
        self.modrow = fw.dram("modrow", [2, 2, 6 * D], F32, kind=k)
        self.h1 = fw.dram("h1", [TOK, D], F32, kind=k)
        self.h2 = fw.dram("h2", [TOK, D], F32, kind=("ExternalInput" if stop and stop.startswith("L1") else k))
        self.h3 = fw.dram("h3", [TOK, D], F32, kind=k)
        self.uT = fw.dram("uT", [512, TOKP], BF16)
        if dbg:
            self.dza = fw.dram("dza", [TOK, 512], BF16, kind="ExternalOutput")
            self.dzc = fw.dram("dzc", [512, TOK], BF16, kind="ExternalOutput")
        self.gbT = fw.dram("gbT", [512, TOKP], BF16)
        self.UT = fw.dram("UT", [128, 128, 8, 128], BF16)
        self.Vb = fw.dram("Vb", [128, 128, D], BF16)
        self.banks = [fw.ps("bank%d" % i, [128, 512]) for i in range(8)]
        self.bank = Rot(self.banks)
        self.identf = fw.sb("identf", [128, 128])
        self.identb = fw.sb("identb", [128, 128], BF16)
        fw.dma("sp", self.identf, self.identf[:], self.I["ident"], self.I["ident"].ap())
        fw.dma("pool", self.identb, self.identb[:], self.I["ident"], self.I["ident"].ap())

    def bview(self, bank):
        return bank.ap().bitcast(BF16)

    def load_colT(self, dst, dst_ap, src, src_ap1d):
        self.fw.dma("sp", dst, dst_ap, src, src_ap1d.rearrange("(k p) -> p k", p=128), allow_slow_non_contiguous=True)

    def modvecs(self, layer, which):
        fw = self.fw
        I = self.I
        gname = "norm1_g" if which == 0 else "norm2_g"
        g = fw.sb("g", [128, 8])
        self.load_colT(g, g[:], I[gname], I[gname].ap()[layer])
        GT, shT = [], []
        for s in range(2):
            sc = fw.sb("scl", [128, 8]); sh = fw.sb("shf", [128, 8]); G = fw.sb("G", [128, 8])
            base = which * 3 * D
            self.load_colT(sh, sh[:], self.modrow, self.modrow.ap()[layer, s, base:base + D])
            self.load_colT(sc, sc[:], self.modrow, self.modrow.ap()[layer, s, base + D:base + 2 * D])
            fw.op("dve", lambda e: e.scalar_tensor_tensor(out=G[:], in0=sc[:], scalar=1.0, in1=g[:], op0=ALU.add, op1=ALU.mult), [sc, g], [G])
            GT.append(G); shT.append(sh)
        return GT, shT

    def gate_bcast(self, layer, which):
        fw = self.fw
        res = []
        for s in range(2):
            gt = fw.sb("gate", [128, D])
            off = which * 3 * D + 2 * D
            fw.dma("sp", gt, gt[:], self.modrow, self.modrow.ap()[layer, s, off:off + D].partition_broadcast(128))
            res.append(gt)
        return res

    def make_normer(self, nh=2):
        fw = self.fw
        self.nb_h = Rot([fw.sb("nh", [128, D]) for _ in range(nh)])
        self.nb_sq = fw.sb("nsq", [128, D], BF16)
        self.nb_ss = Rot([fw.sb("nss", [128, 1]) for _ in range(2)])
        self.nb_rs = Rot([fw.sb("nrs", [128, 1]) for _ in range(2)])
        self.nb_hn = Rot([fw.sb("nhn", [128, D], BF16) for _ in range(2)])

    def norm_tile(self, src, src_ap, G, sh, xnT, xnT_cols):
        fw = self.fw
        h = self.nb_h.next(); ss = self.nb_ss.next(); rs = self.nb_rs.next(); hn = self.nb_hn.next(); sq = self.nb_sq
        fw.dma("sp", h, h[:], src, src_ap)
        fw.op("act", lambda e: e.activation(out=sq[:], in_=h[:], func=AF.Square, scale=1.0 / 32, accum_out=ss[:]), [h], [sq, ss])
        fw.op("act", lambda e: e.activation(out=rs[:], in_=ss[:], func=AF.Sqrt, bias=EPS, scale=1.0), [ss], [rs])
        fw.op("dve", lambda e: e.reciprocal(out=rs[:], in_=rs[:]), [rs], [rs])
        fw.op("act", lambda e: e.activation(out=hn[:], in_=h[:], func=AF.Copy, scale=rs[:]), [h, rs], [hn])
        pb = self.bank.next()
        pv = self.bview(pb)
        for k in range(8):
            fw.op("pe", lambda e: e.transpose(out=pv[:, k * 128:(k + 1) * 128], in_=hn[:, k * 128:(k + 1) * 128], identity=self.identb[:]), [hn, self.identb], [pb])
        for k in range(8):
            fw.op("act", lambda e: e.activation(out=xnT[:, k, xnT_cols], in_=pv[:, k * 128:(k + 1) * 128], func=AF.Identity,
                                                scale=G[:, k:k + 1], bias=sh[:, k:k + 1]), [pb, G, sh], [xnT])
        return h

    def src_of(self, tile, hsrc):
        if hsrc is None:
            if tile < 2:
                return self.I["ctx"], self.I["ctx"].ap()[tile * 128:(tile + 1) * 128, :]
            return self.I["x"], self.I["x"].ap()[(tile - 2) * 128:(tile - 1) * 128, :]
        return hsrc, hsrc.ap()[tile * 128:(tile + 1) * 128, :]

    def phase0(self):
        fw = self.fw
        I = self.I
        fw.push()
        sc = fw.sb("sc", [128, 2, 8])
        for s_ in range(2):
            self.load_colT(sc, sc[:, s_, :], I["cvec"], I["cvec"].ap()[s_])
        fw.op("act", lambda e: e.activation(out=sc[:], in_=sc[:], func=AF.Silu), [sc], [sc])
        wb = Rot([fw.sb("adaw", [128, 8, 512]) for _ in range(2)])
        for i in range(2):
            bias = fw.sb("adab", [2, 6 * D])
            modr = fw.sb("modr", [2, 6 * D])
            fw.dma("sp", bias, bias[:], I["ada_b"], I["ada_b"].ap()[i].partition_broadcast(2))
            for cb in range(12):
                wt = wb.next()
                fw.dma("sp" if cb % 2 == 0 else "act", wt, wt[:], I["ada_w"], I["ada_w"].ap()[i, :, cb * 512:(cb + 1) * 512].rearrange("(k p) n -> p k n", p=128))
                pb = self.bank.next()
                for k in range(8):
                    fw.op("pe", lambda e: e.matmul(pb[0:2, :], lhsT=sc[:, :, k], rhs=wt[:, k, :], start=(k == 0), stop=(k == 7)), [sc, wt], [pb])
                fw.op("dve", lambda e: e.tensor_tensor(out=modr[:, cb * 512:(cb + 1) * 512], in0=pb[0:2, :], in1=bias[:, cb * 512:(cb + 1) * 512], op=ALU.add), [pb, bias], [modr])
            fw.dma("sp", self.modrow, self.modrow.ap()[i], modr, modr[:])
        fw.pop()

    def l0_phaseA(self):
        fw = self.fw
        I = self.I
        fw.push()
        self.qT = fw.sb("qT", [128, 4, TOK], BF16)
        self.kT2 = fw.sb("kT2", [128, 2, TOK], BF16)
        self.vext = fw.sb("vext", [128, NT, 2, 66], BF16)
        fw.op("pool", lambda e: e.memset(self.vext[:], 1.0), [], [self.vext])
        fw.push()
        GT, shT = self.modvecs(0, 0)
        self.make_normer()
        w_in = fw.sb("w_in", [128, 8, 2432], BF16)
        for k in range(8):
            fw.dma("pool", w_in, w_in[:, k, :], I["w_in"], I["w_in"].ap()[k * 128:(k + 1) * 128, :])
        rC = fw.sb("rC", [128, TOK], BF16); rS = fw.sb("rS", [128, TOK], BF16)
        fw.dma("pool", rC, rC[:], I["ropeC"], I["ropeC"].ap())
        fw.dma("pool", rS, rS[:], I["ropeS"], I["ropeS"].ap())
        ropeP = fw.sb("ropeP", [128, 128], BF16); blk64 = fw.sb("blk64", [128, 128], BF16)
        fw.dma("pool", ropeP, ropeP[:], I["ropeP"], I["ropeP"].ap())
        fw.dma("pool", blk64, blk64[:], I["blk64"], I["blk64"].ap())
        gains = fw.sb("gains", [128, 2])
        fw.dma("sp", gains, gains[:, 0:1], I["qg2"], I["qg2"].ap().rearrange("(p o) -> p o", o=1))
        fw.dma("sp", gains, gains[:, 1:2], I["kg2"], I["kg2"].ap().rearrange("(p o) -> p o", o=1))
        zt = fw.sb("zt", [128, 4, 2], BF16)
        fw.op("pool", lambda e: e.memset(zt[:], 0.0), [], [zt])
        for dst in (self.uT,):
            v = dst.ap().rearrange("(b p) c -> p b c", p=128)
            fw.dma("pool", dst, v[:, :, 0:1], zt, zt[:, :, 0:1], allow_slow_non_contiguous=True)
            fw.dma("pool", dst, v[:, :, 257:259], zt, zt[:, :, 0:2], allow_slow_non_contiguous=True)
            fw.dma("pool", dst, v[:, :, TOKP - 1:TOKP], zt, zt[:, :, 0:1], allow_slow_non_contiguous=True)
        xnTs = Rot([fw.sb("xnT", [128, 8, 256], BF16) for _ in range(2)])
        hTt = fw.sb("hTt", [128, 4, 256])
        gbst = Rot([fw.sb("gbst", [128, 4, 256], BF16) for _ in range(2)])
        ust = Rot([fw.sb("ust", [128, 4, 256], BF16) for _ in range(2)])
        qraw = Rot([fw.sb("qraw", [128, 256]) for _ in range(2)])
        sqb = Rot([fw.sb("sqb", [128, 256], BF16) for _ in range(2)])
        rsb = Rot([fw.sb("rsb", [128, 256]) for _ in range(2)])
        qnb = Rot([fw.sb("qnb", [128, 256], BF16) for _ in range(2)])
        t1b = Rot([fw.sb("t1b", [128, 256]) for _ in range(2)])
        t2b = Rot([fw.sb("t2b", [128, 256]) for _ in range(2)])
        uTv = self.uT.ap().rearrange("(b p) c -> p b c", p=128)
        gbTv = self.gbT.ap().rearrange("(b p) c -> p b c", p=128)
        import os
        skip = os.environ.get('L0A_SKIP', '').split(',')
        for g in range(int(os.environ.get('L0A_NG', NT // 2))):
            s = 1 if g == 0 else 0
            xnT = xnTs.next()
            for j in range(2):
                tile = 2 * g + j
                src, sap = self.src_of(tile, None)
                self.norm_tile(src, sap, GT[s], shT[s], xnT, slice(j * 128, (j + 1) * 128))
            tok0 = g * 256
            gb_s = gbst.next(); u_s = ust.next()
            for fb in range(18):
                if fb >= 12 and 'qk' in skip:
                    continue
                if fb < 12 and 'conv' in skip:
                    continue
                pb = self.bank.next()
                for k in range(8):
                    fw.op("pe", lambda e: e.matmul(pb[:, 0:256], lhsT=w_in[:, k, fb * 128:(fb + 1) * 128], rhs=xnT[:, k, :], start=(k == 0), stop=(k == 7)), [w_in, xnT], [pb])
                if fb < 4:
                    fw.op("act", lambda e: e.activation(out=hTt[:, fb, :], in_=pb[:, 0:256], func=AF.Copy), [pb], [hTt])
                elif fb < 8:
                    fw.op("act", lambda e: e.activation(out=gb_s[:, fb - 4, :], in_=pb[:, 0:256], func=AF.Copy), [pb], [gb_s])
                elif fb < 12:
                    fw.op("dve", lambda e: e.tensor_tensor(out=u_s[:, fb - 8, :], in0=pb[:, 0:256], in1=hTt[:, fb - 8, :], op=ALU.mult), [pb, hTt], [u_s])
                else:
                    isq = fb < 16
                    qr = qraw.next(); sq = sqb.next(); rs = rsb.next(); qn = qnb.next(); t1 = t1b.next(); t2 = t2b.next()
                    gcol = gains[:, 0:1] if isq else gains[:, 1:2]
                    fw.op("act", lambda e: e.activation(out=qr[:], in_=pb[:, 0:256], func=AF.Copy), [pb], [qr])
                    fw.op("act", lambda e: e.activation(out=sq[:], in_=pb[:, 0:256], func=AF.Square), [pb], [sq])
                    p2 = self.bank.next()
                    fw.op("pe", lambda e: e.matmul(p2[:, 0:256], lhsT=blk64[:], rhs=sq[:], start=True, stop=True), [blk64, sq], [p2])
                    fw.op("act", lambda e: e.activation(out=rs[:], in_=p2[:, 0:256], func=AF.Sqrt, bias=EPS, scale=1.0 / 64), [p2], [rs])
                    fw.op("dve", lambda e: e.reciprocal(out=rs[:], in_=rs[:]), [rs], [rs])
                    fw.op("dve", lambda e: e.scalar_tensor_tensor(out=qn[:], in0=qr[:], scalar=gcol, in1=rs[:], op0=ALU.mult, op1=ALU.mult), [qr, gains, rs], [qn])
                    p3 = self.bank.next()
                    fw.op("pe", lambda e: e.matmul(p3[:, 0:256], lhsT=ropeP[:], rhs=qn[:], start=True, stop=True), [ropeP, qn], [p3])
                    fw.op("dve", lambda e: e.tensor_tensor(out=t2[:], in0=p3[:, 0:256], in1=rS[:, tok0:tok0 + 256], op=ALU.mult), [p3, rS], [t2])
                    fw.op("pool", lambda e: e.tensor_tensor(out=t1[:], in0=qn[:], in1=rC[:, tok0:tok0 + 256], op=ALU.mult), [qn, rC], [t1])
                    dst = self.qT if isq else self.kT2
                    bi = fb - 12 if isq else fb - 16
                    fw.op("dve", lambda e: e.tensor_tensor(out=dst[:, bi, tok0:tok0 + 256], in0=t1[:], in1=t2[:], op=ALU.add), [t1, t2], [dst])
            for j in range(2 if 'v' not in skip else 0):
                tile = 2 * g + j
                pb = self.bank.next()
                for k in range(8):
                    fw.op("pe", lambda e: e.matmul(pb[:, 0:128], lhsT=xnT[:, k, j * 128:(j + 1) * 128], rhs=w_in[:, k, 2304:2432], start=(k == 0), stop=(k == 7)), [xnT, w_in], [pb])
                fw.op("act", lambda e: e.activation(out=self.vext[:, tile, :, 0:64], in_=pb[:, 0:128].rearrange("p (a d) -> p a d", a=2), func=AF.Copy), [pb], [self.vext])
            for j in range(2 if 'store' not in skip else 0):
                c0 = tokcol(2 * g + j)
                fw.dma("sp", self.uT, uTv[:, :, c0:c0 + 128], u_s, u_s[:, :, j * 128:(j + 1) * 128])
                fw.dma("sp", self.gbT, gbTv[:, :, c0:c0 + 128], gb_s, gb_s[:, :, j * 128:(j + 1) * 128])
        fw.pop()

    def l0_phaseB(self):
        fw = self.fw
        I = self.I
        fw.push()
        gate = self.gate_bcast(0, 0)
        w_out = fw.sb("w_out", [128, 8, D], BF16)
        for k in range(8):
            fw.dma("pool", w_out, w_out[:, k, :], I["w_out"], I["w_out"].ap()[k * 128:(k + 1) * 128, :])
        cw = fw.sb("cw", [128, 3, 4])
        for j_ in range(3):
            self.load_colT(cw, cw[:, j_, :], I["conv_w"], I["conv_w"].ap()[j_])
        PTs = Rot([fw.sb("PT", [128, NT, 512], BF16) for _ in range(2)])
        zatt = Rot([fw.sb("zatt", [128, 512], BF16) for _ in range(2)])
        rcb = Rot([fw.sb("rcb", [128, 4]) for _ in range(2)])
        zaTs = Rot([fw.sb("zaT", [128, 4, 128], BF16) for _ in range(2)])
        zcTs = Rot([fw.sb("zcT", [128, 4, 128], BF16) for _ in range(2)])
        uhs = Rot([fw.sb("uh", [128, 4, 130], BF16) for _ in range(2)])
        gbts = Rot([fw.sb("gbt", [128, 4, 128], BF16) for _ in range(2)])
        accs = Rot([fw.sb("acc", [128, 128]) for _ in range(3)])
        hts = Rot([fw.sb("ht", [128, D]) for _ in range(2)])
        tmps = Rot([fw.sb("tmp", [128, D]) for _ in range(2)])
        uTv = self.uT.ap().rearrange("(b p) c -> p b c", p=128)
        gbTv = self.gbT.ap().rearrange("(b p) c -> p b c", p=128)
        obanks = Rot(self.banks[0:2]); sbanks = Rot(self.banks[2:6]); self.bank = Rot(self.banks[6:8])
        import os
        skip = os.environ.get('L0B_SKIP', '').split(',')
        for qt in range(int(os.environ.get('L0B_NT', NT))):
            s = 1 if qt < 2 else 0
            nch = 2 if qt < 2 else NT
            ht = hts.next()
            src, sap = self.src_of(qt, None)
            fw.dma("sp", ht, ht[:], src, sap)
            uh = uhs.next(); gbt = gbts.next()
            c0 = tokcol(qt)
            fw.dma("sp", uh, uh[:], self.uT, uTv[:, :, c0 - 1:c0 + 129])
            fw.dma("sp", gbt, gbt[:], self.gbT, gbTv[:, :, c0:c0 + 128])
            za = zatt.next()
            for kv in range(2 if 'attn' not in skip else 0):
                PTall = PTs.next()
                for sc in range(nch):
                    sbs = [sbanks.next(), sbanks.next()]
                    for b in range(2):
                        for hf in range(2):
                            fw.op("pe", lambda e: e.matmul(sbs[hf][:, b * 128:(b + 1) * 128], lhsT=self.kT2[hf * 64:(hf + 1) * 64, kv, sc * 128:(sc + 1) * 128],
                                                            rhs=self.qT[hf * 64:(hf + 1) * 64, 2 * kv + b, qt * 128:(qt + 1) * 128], start=True, stop=True), [self.kT2, self.qT], [sbs[hf]])
                    PT4 = PTall[:, sc, :].rearrange("p (b h n) -> p b h n", b=2, h=2)
                    for hf in range(2):
                        fw.op("act", lambda e: e.activation(out=PT4[:, :, hf, :], in_=sbs[hf][:, 0:256].rearrange("p (b n) -> p b n", b=2), func=AF.Exp, scale=0.125), [sbs[hf]], [PTall])
                ob = obanks.next()
                O = ob[:, 0:260].rearrange("p (c e) -> p c e", e=65)
                for c in range(4):
                    for sc in range(nch):
                        fw.op("pe", lambda e: e.matmul(O[:, c, :], lhsT=PTall[:, sc, c * 128:(c + 1) * 128], rhs=self.vext[:, sc, kv, 0:65], start=(sc == 0), stop=(sc == nch - 1)), [PTall, self.vext], [ob])
                rc = rcb.next()
                fw.op("dve", lambda e: e.reciprocal(out=rc[:], in_=O[:, :, 64]), [ob], [rc])
                fw.op("dve", lambda e: e.tensor_tensor(out=za[:, kv * 256:(kv + 1) * 256].rearrange("p (c d) -> p c d", c=4), in0=O[:, :, 0:64],
                                                       in1=rc[:].unsqueeze(2).to_broadcast([128, 4, 64]), op=ALU.mult), [ob, rc], [za])
            pb = self.bank.next(); pv = self.bview(pb)
            for k in range(4):
                fw.op("pe", lambda e: e.transpose(out=pv[:, k * 128:(k + 1) * 128], in_=za[:, k * 128:(k + 1) * 128], identity=self.identb[:]), [za, self.identb], [pb])
            zaT = zaTs.next()
            fw.op("act", lambda e: e.activation(out=zaT[:].rearrange("p k n -> p (k n)"), in_=pv[:, 0:512], func=AF.Copy), [pb], [zaT])
            zcT = zcTs.next()
            for b in range(4 if 'conv' not in skip else 0):
                acc = accs.next()
                fw.op("pool", lambda e: e.tensor_scalar(out=acc[:], in0=uh[:, b, 0:128], scalar1=cw[:, 0, b:b + 1], scalar2=None, op0=ALU.mult), [uh, cw], [acc])
                for j_ in (1, 2):
                    t_ = accs.next()
                    fw.op("pool", lambda e: e.tensor_scalar(out=t_[:], in0=uh[:, b, j_:j_ + 128], scalar1=cw[:, j_, b:b + 1], scalar2=None, op0=ALU.mult), [uh, cw], [t_])
                    fw.op("pool", lambda e: e.tensor_tensor(out=acc[:], in0=acc[:], in1=t_[:], op=ALU.add), [acc, t_], [acc])
                fw.op("pool", lambda e: e.tensor_tensor(out=zcT[:, b, :], in0=acc[:], in1=gbt[:, b, :], op=ALU.mult), [acc, gbt], [zcT])
            if self.dbg:
                fw.dma("sp", self.dza, self.dza.ap()[qt * 128:(qt + 1) * 128, :], za, za[:])
                fw.dma("sp", self.dzc, self.dzc.ap().rearrange("(b p) c -> p b c", p=128)[:, :, qt * 128:(qt + 1) * 128], zcT, zcT[:])
            tmp = tmps.next()
            for half in range(2):
                pb = self.bank.next()
                for k in range(8):
                    lt = zcT[:, k, :] if k < 4 else zaT[:, k - 4, :]
                    fw.op("pe", lambda e: e.matmul(pb[:], lhsT=lt, rhs=w_out[:, k, half * 512:(half + 1) * 512], start=(k == 0), stop=(k == 7)), [zcT, zaT, w_out], [pb])
                fw.op("dve", lambda e: e.tensor_tensor(out=tmp[:, half * 512:(half + 1) * 512], in0=pb[:], in1=gate[s][:, half * 512:(half + 1) * 512], op=ALU.mult), [pb, gate[s]], [tmp])
            fw.op("pool", lambda e: e.tensor_tensor(out=tmp[:], in0=tmp[:], in1=ht[:], op=ALU.add), [tmp, ht], [tmp])
            fw.dma("act", self.h1, self.h1.ap()[qt * 128:(qt + 1) * 128, :], tmp, tmp[:])
        self.bank = Rot(self.banks)
        fw.pop()
        fw.pop()

    def peer_prep(self, layer):
        fw = self.fw
        I = self.I
        fw.push()
        banks = Rot(self.banks)
        Uv = I["peer_u"].ap()[layer].rearrange("(i j) d -> j i d", j=128)
        Vv = I["peer_v"].ap()[layer].rearrange("(i j) d -> j i d", j=128)
        UTv = self.UT.ap().rearrange("i p k j -> p i (k j)")
        Vbv = self.Vb.ap().rearrange("i j d -> j i d")
        ubs = Rot([fw.sb("ub", [128, 4, D], BF16) for _ in range(2)])
        vbs = Rot([fw.sb("vb", [128, 4, D], BF16) for _ in range(2)])
        uts = Rot([fw.sb("ut", [128, 4, 1024], BF16) for _ in range(2)])
        for c in range(32):
            ub = ubs.next(); vb = vbs.next(); ut = uts.next()
            fw.dma("pool", ub, ub[:], I["peer_u"], Uv[:, c * 4:(c + 1) * 4, :])
            fw.dma("pool", vb, vb[:], I["peer_v"], Vv[:, c * 4:(c + 1) * 4, :])
            fw.dma("sp", self.Vb, Vbv[:, c * 4:(c + 1) * 4, :], vb, vb[:])
            for ii in range(4):
                pb = banks.next(); pv = self.bview(pb)
                for k in range(8):
                    fw.op("pe", lambda e: e.transpose(out=pv[:, k * 128:(k + 1) * 128], in_=ub[:, ii, k * 128:(k + 1) * 128], identity=self.identb[:]), [ub, self.identb], [pb])
                if ii % 2 == 0:
                    fw.op("act", lambda e: e.activation(out=ut[:, ii, :], in_=pv[:, 0:1024], func=AF.Copy), [pb], [ut])
                else:
                    fw.op("dve", lambda e: e.tensor_copy(out=ut[:, ii, :], in_=pv[:, 0:1024]), [pb], [ut])
            fw.dma("sp", self.UT, UTv[:, c * 4:(c + 1) * 4, :], ut, ut[:])
        fw.pop()

    def peer(self, layer, hsrc, hdst, final=False):
        fw = self.fw
        I = self.I
        fw.push()
        GT, shT = self.modvecs(layer, 1)
        gate = self.gate_bcast(layer, 1)
        self.make_normer()
        obanks = self.banks[0:4]
        self.bank = Rot(self.banks[4:8])
        w_q = fw.sb("w_q", [128, 8, 2048], BF16)
        for k in range(8):
            fw.dma("pool", w_q, w_q[:, k, :], I["peer_wq"], I["peer_wq"].ap()[layer, k * 128:(k + 1) * 128, :])
        keyn = fw.sb("keyn", [128, 16, 128], BF16)
        fw.dma("pool", keyn, keyn[:], I["peer_keys"], I["peer_keys"].ap()[layer].rearrange("h k d -> k h d"))
        keysT = fw.sb("keysT", [128, 16, 128], BF16)
        for c in range(2):
            pb = self.bank.next(); pv = self.bview(pb)
            for q in range(8):
                fw.op("pe", lambda e: e.transpose(out=pv[:, q * 128:(q + 1) * 128], in_=keyn[:, c * 8 + q, :], identity=self.identb[:]), [keyn, self.identb], [pb])
            fw.op("act", lambda e: e.activation(out=keysT[:, c * 8:(c + 1) * 8, :].rearrange("p a b -> p (a b)"), in_=pv[:, 0:1024], func=AF.Copy), [pb], [keysT])
        xn2T = fw.sb("xn2T", [128, 8, 256], BF16)
        qT = fw.sb("pqT", [128, 16, 256], BF16)
        s_all = fw.sb("s_all", [128, 2, 16, 128])
        tau = fw.sb("tau", [128, 2, 8]); bias = fw.sb("pbias", [128, 2, 8])
        SB = [(fw.sb("m0", [128, 16]), fw.sb("m1", [128, 16]), fw.sb("scr", [128, 128]), fw.sb("cand", [128, 16, 16]), fw.sb("c24", [128, 24]),
               fw.sb("cscr", [128, 256]), fw.sb("st1", [128, 4]), fw.sb("e16", [128, 16])) for _ in range(4)]
        IB = 16
        Ws = Rot([fw.sb("W", [128, 2, IB, 128], BF16) for _ in range(2)])
        sums = Rot([fw.sb("sum", [128, IB, 128]) for _ in range(3)])
        Ps = Rot([fw.sb("P", [128, IB, 128], BF16) for _ in range(3)])
        Whs = Rot([fw.sb("Wh", [128, IB, 128], BF16) for _ in range(2)])
        Mks = Rot([fw.sb("Mk", [128, IB, 128], BF16) for _ in range(2)])
        UTs = Rot([fw.sb("UTs", [128, 2, 1024], BF16) for _ in range(2)])
        Vs = Rot([fw.sb("Vs", [128, 2, D], BF16) for _ in range(2)])
        Gs = Rot([fw.sb("Gs", [128, 256], BF16) for _ in range(4)])
        WAs = Rot([fw.sb("WAs", [128, 256], BF16) for _ in range(4)])
        tmps = Rot([fw.sb("ptmp", [128, D]) for _ in range(1)])
        hts = Rot([fw.sb("pht", [128, D]) for _ in range(2)])
        UTv = self.UT.ap().rearrange("i p k j -> p i (k j)")
        Vbv = self.Vb.ap().rearrange("i j d -> j i d")
        import os
        ng = int(os.environ.get("PEER_NG", NT // 2))
        for g in range(ng):
            if final and g == 0:
                continue
            s = 1 if g == 0 else 0
            for j in range(2):
                src, sap = self.src_of(2 * g + j, hsrc)
                self.norm_tile(src, sap, GT[s], shT[s], xn2T, slice(j * 128, (j + 1) * 128))
            for hp in range(16):
                pb = self.bank.next()
                for k in range(8):
                    fw.op("pe", lambda e: e.matmul(pb[:, 0:256], lhsT=w_q[:, k, hp * 128:(hp + 1) * 128], rhs=xn2T[:, k, :], start=(k == 0), stop=(k == 7)), [w_q, xn2T], [pb])
                fw.op("act", lambda e: e.activation(out=qT[:, hp, :], in_=pb[:, 0:256], func=AF.Copy), [pb], [qT])
            for t in range(2):
                for c in range(4):
                    pb = self.bank.next()
                    for q in range(4):
                        hp = c * 4 + q
                        fw.op("pe", lambda e: e.matmul(pb[:, q * 128:(q + 1) * 128], lhsT=qT[:, hp, t * 128:(t + 1) * 128], rhs=keysT[:, hp, :], start=True, stop=True), [qT, keysT], [pb])
                    fw.op("act", lambda e: e.activation(out=s_all[:, t, c * 4:(c + 1) * 4, :].rearrange("p a b -> p (a b)"), in_=pb[:], func=AF.Copy), [pb], [s_all])
            def stats_gen(t, h, B):
                m0, m1, scr, cand, c24, cscr, st1, e16 = B
                for (mm, src) in ((m0, s_all[:, t, 2 * h, :]), (m1, s_all[:, t, 2 * h + 1, :])):
                    fw.op("dve", lambda e: e.max(out=mm[:, 0:8], in_=src), [s_all], [mm]); yield
                    fw.op("dve", lambda e: e.match_replace(out=scr[:], in_to_replace=mm[:, 0:8], in_values=src, imm_value=-1e30), [s_all, mm], [scr]); yield
                    fw.op("dve", lambda e: e.max(out=mm[:, 8:16], in_=scr[:]), [scr], [mm]); yield
                fw.op("dve", lambda e: e.tensor_tensor(out=cand[:], in0=m0[:].unsqueeze(2).to_broadcast([128, 16, 16]), in1=m1[:].unsqueeze(1).to_broadcast([128, 16, 16]), op=ALU.add), [m0, m1], [cand]); yield
                cf = cand[:].rearrange("p a b -> p (a b)")
                fw.op("dve", lambda e: e.max(out=c24[:, 0:8], in_=cf), [cand], [c24]); yield
                fw.op("dve", lambda e: e.match_replace(out=cscr[:], in_to_replace=c24[:, 0:8], in_values=cf, imm_value=-1e30), [cand, c24], [cscr]); yield
                fw.op("dve", lambda e: e.max(out=c24[:, 8:16], in_=cscr[:]), [cscr], [c24]); yield
                fw.op("dve", lambda e: e.match_replace(out=cscr[:], in_to_replace=c24[:, 8:16], in_values=cscr[:], imm_value=-1e30), [cscr, c24], [cscr]); yield
                fw.op("dve", lambda e: e.max(out=c24[:, 16:24], in_=cscr[:]), [cscr], [c24]); yield
                fw.op("dve", lambda e: e.tensor_scalar(out=st1[:, 0:1], in0=c24[:, 16:17], scalar1=0.5, scalar2=None, op0=ALU.mult), [c24], [st1]); yield
                fw.op("dve", lambda e: e.scalar_tensor_tensor(out=tau[:, t, h:h + 1], in0=c24[:, 15:16], scalar=0.5, in1=st1[:, 0:1], op0=ALU.mult, op1=ALU.add), [c24, st1], [tau]); yield
                fw.op("dve", lambda e: e.tensor_scalar(out=st1[:, 1:2], in0=c24[:, 0:1], scalar1=-1.0, scalar2=None, op0=ALU.mult), [c24], [st1]); yield
                fw.op("act", lambda e: e.activation(out=e16[:], in_=c24[:, 0:16], func=AF.Exp, bias=st1[:, 1:2], scale=1.0, accum_out=st1[:, 2:3]), [c24, st1], [e16, st1]); yield
                fw.op("act", lambda e: e.activation(out=st1[:, 3:4], in_=st1[:, 2:3], func=AF.Ln), [st1], [st1]); yield
                fw.op("dve", lambda e: e.tensor_tensor(out=bias[:, t, h:h + 1], in0=st1[:, 1:2], in1=st1[:, 3:4], op=ALU.subtract), [st1], [bias]); yield

            for h0 in range(0, 8, 2):
                gens = [stats_gen(t, h0 + dh, SB[t * 2 + dh]) for t in range(2) for dh in range(2)]
                alive = True
                while alive:
                    alive = False
                    for gi in gens:
                        try:
                            next(gi); alive = True
                        except StopIteration:
                            pass
            NB = 128 // IB
            Wl = [None] * NB

            def build_unit(ib, u):
                i0 = ib * IB
                if u == 0:
                    Wl[ib] = Ws.next()
                W = Wl[ib]
                t, h = u // 8, u % 8
                sm = sums.next(); P = Ps.next()
                fw.op("dve", lambda e: e.tensor_tensor(out=sm[:], in0=s_all[:, t, 2 * h, i0:i0 + IB].unsqueeze(2).to_broadcast([128, IB, 128]),
                                                       in1=s_all[:, t, 2 * h + 1, :].unsqueeze(1).to_broadcast([128, IB, 128]), op=ALU.add), [s_all], [sm])
                def emitB():
                    fw.op("act", lambda e: e.activation(out=P[:], in_=sm[:], func=AF.Exp, bias=bias[:, t, h:h + 1], scale=1.0), [sm, bias], [P])

                def fin():
                    mk = Mks.next()
                    fw.op("dve", lambda e: e.tensor_scalar(out=mk[:], in0=sm[:], scalar1=tau[:, t, h:h + 1], scalar2=None, op0=ALU.is_ge), [sm, tau], [mk])
                    if h == 0:
                        fw.op("dve", lambda e: e.tensor_tensor(out=W[:, t, :, :], in0=mk[:], in1=P[:], op=ALU.mult), [mk, P], [W])
                    else:
                        Wh = Whs.next()
                        fw.op("dve", lambda e: e.tensor_tensor(out=Wh[:], in0=mk[:], in1=P[:], op=ALU.mult), [mk, P], [Wh])
                        eng = "pool" if h % 2 == 1 else "dve"
                        fw.op(eng, lambda e: e.tensor_tensor(out=W[:, t, :, :], in0=W[:, t, :, :], in1=Wh[:], op=ALU.add), [W, Wh], [W])
                return emitB, fin

            cur_ld = {}
            carry = {}

            def front(i):
                ib, iw = i // IB, i % IB
                W = Wl[ib]
                if i % 2 == 0:
                    UTt = UTs.next(); Vt = Vs.next()
                    fw.dma("sp", UTt, UTt[:], self.UT, UTv[:, i:i + 2, :])
                    fw.dma("sp", Vt, Vt[:], self.Vb, Vbv[:, i:i + 2, :])
                    cur_ld["u"] = UTt; cur_ld["v"] = Vt
                UTt = cur_ld["u"]; Vt = cur_ld["v"]
                i2 = i % 2
                pa = self.bank.next()
                for k in range(8):
                    fw.op("pe", lambda e: e.matmul(pa[:, 0:256], lhsT=UTt[:, i2, k * 128:(k + 1) * 128], rhs=xn2T[:, k, :], start=(k == 0), stop=(k == 7)), [UTt, xn2T], [pa])
                G = Gs.next()
                fw.op("act", lambda e: e.activation(out=G[:], in_=pa[:, 0:256], func=AF.Gelu), [pa], [G])
                pw = self.bank.next(); pwv = self.bview(pw)
                for t in range(2):
                    fw.op("pe", lambda e: e.transpose(out=pwv[:, t * 128:(t + 1) * 128], in_=W[:, t, iw, :], identity=self.identb[:]), [W, self.identb], [pw])
                WA = WAs.next()
                fw.op("dve", lambda e: e.tensor_tensor(out=WA[:], in0=G[:], in1=pwv[:, 0:256], op=ALU.mult), [G, pw], [WA])

                def back():
                    for t in range(2):
                        for half in range(2):
                            ob = obanks[t * 2 + half]
                            fw.op("pe", lambda e: e.matmul(ob[:], lhsT=WA[:, t * 128:(t + 1) * 128], rhs=Vt[:, i2, half * 512:(half + 1) * 512], start=(i == 0), stop=(i == 127)), [WA, Vt], [ob])
                return back

            pend = None
            for u in range(16):
                eb, f = build_unit(0, u)
                eb()
                if pend is not None:
                    pend()
                pend = f
            pend()
            backs = {0: front(0), 1: front(1)}
            sched = {0: [0, 1], 1: [2, 3]}
            for j_ in range(2, 14):
                sched[j_] = [j_ + 2]
            for i in range(128):
                ib, j = i // IB, i % IB
                units = sched.get(j, []) if ib + 1 < NB else []
                ebs = []; fins = []
                for u in units:
                    eb, f = build_unit(ib + 1, u)
                    ebs.append(eb); fins.append(f)
                if i + 2 < 128:
                    backs[i + 2] = front(i + 2)
                for eb in ebs:
                    eb()
                cp = carry.pop("f", None)
                if cp is not None:
                    cp()
                for f in fins[:-1]:
                    f()
                if fins:
                    if j == 13:
                        fins[-1]()
                    else:
                        carry["f"] = fins[-1]
                backs.pop(i)()
            for t in range(2):
                tile = 2 * g + t
                tmp = tmps.next(); ht = hts.next()
                src, sap = self.src_of(tile, hsrc)
                fw.dma("sp", ht, ht[:], src, sap)
                for half in range(2):
                    ob = obanks[t * 2 + half]
                    fw.op("dve", lambda e: e.tensor_tensor(out=tmp[:, half * 512:(half + 1) * 512], in0=ob[:], in1=gate[s][:, half * 512:(half + 1) * 512], op=ALU.mult), [ob, gate[s]], [tmp])
                fw.op("pool", lambda e: e.tensor_tensor(out=tmp[:], in0=tmp[:], in1=ht[:], op=ALU.add), [tmp, ht], [tmp])
                if final:
                    fw.dma("act", self.out, self.out.ap()[(tile - 2) * 128:(tile - 1) * 128, :], tmp, tmp[:])
                else:
                    fw.dma("act", hdst, hdst.ap()[tile * 128:(tile + 1) * 128, :], tmp, tmp[:])
        self.bank = Rot(self.banks)
        fw.pop()

    def peer_prep2(self, layer):
        fw = self.fw
        I = self.I
        fw.push()
        banks = Rot(self.banks)
        Uv = I["peer_u"].ap()[layer].rearrange("(i j) d -> i j d", j=128)
        Vv = I["peer_v"].ap()[layer].rearrange("(i j) d -> i j d", j=128)
        UTv = self.UT.ap().rearrange("j p k i -> p j (k i)")
        Vbv = self.Vb.ap().rearrange("j i d -> i j d")
        ubs = Rot([fw.sb("ub", [128, 4, D], BF16) for _ in range(2)])
        vbs = Rot([fw.sb("vb", [128, 4, D], BF16) for _ in range(2)])
        uts = Rot([fw.sb("ut", [128, 4, 1024], BF16) for _ in range(2)])
        for c in range(32):
            ub = ubs.next(); vb = vbs.next(); ut = uts.next()
            fw.dma("pool", ub, ub[:], I["peer_u"], Uv[:, c * 4:(c + 1) * 4, :])
            fw.dma("pool", vb, vb[:], I["peer_v"], Vv[:, c * 4:(c + 1) * 4, :])
            fw.dma("sp", self.Vb, Vbv[:, c * 4:(c + 1) * 4, :], vb, vb[:])
            for jj in range(4):
                pb = banks.next(); pv = self.bview(pb)
                for k in range(8):
                    fw.op("pe", lambda e: e.transpose(out=pv[:, k * 128:(k + 1) * 128], in_=ub[:, jj, k * 128:(k + 1) * 128], identity=self.identb[:]), [ub, self.identb], [pb])
                if jj % 2 == 0:
                    fw.op("act", lambda e: e.activation(out=ut[:, jj, :], in_=pv[:, 0:1024], func=AF.Copy), [pb], [ut])
                else:
                    fw.op("dve", lambda e: e.tensor_copy(out=ut[:, jj, :], in_=pv[:, 0:1024]), [pb], [ut])
            fw.dma("sp", self.UT, UTv[:, c * 4:(c + 1) * 4, :], ut, ut[:])
        fw.pop()

    def peer2(self, layer, hsrc, hdst, final=False):
        fw = self.fw
        I = self.I
        fw.push()
        GT, shT = self.modvecs(layer, 1)
        gate = self.gate_bcast(layer, 1)
        self.make_normer(nh=1)
        obanks = self.banks[0:4]
        self.bank = Rot(self.banks[4:8])
        w_q = fw.sb("w_q", [128, 8, 2048], BF16)
        for k in range(8):
            fw.dma("pool", w_q, w_q[:, k, :], I["peer_wq"], I["peer_wq"].ap()[layer, k * 128:(k + 1) * 128, :])
        keysT = fw.sb("keysT", [128, 16, 128], BF16)
        fw.push()
        keyn = fw.sb("keyn", [128, 16, 128], BF16)
        fw.dma("pool", keyn, keyn[:], I["peer_keys"], I["peer_keys"].ap()[layer].rearrange("h k d -> k h d"))
        for c in range(2):
            pb = self.bank.next(); pv = self.bview(pb)
            for q in range(8):
                fw.op("pe", lambda e: e.transpose(out=pv[:, q * 128:(q + 1) * 128], in_=keyn[:, c * 8 + q, :], identity=self.identb[:]), [keyn, self.identb], [pb])
            fw.op("act", lambda e: e.activation(out=keysT[:, c * 8:(c + 1) * 8, :].rearrange("p a b -> p (a b)"), in_=pv[:, 0:1024], func=AF.Copy), [pb], [keysT])
        fw.pop()
        iota = fw.sb("iota", [128, 128], BF16)
        fw.dma("pool", iota, iota[:], I["iota128"], I["iota128"].ap())
        xn2Ts = [fw.sb("xn2T", [128, 8, 256], BF16) for _ in range(2)]
        qT = fw.sb("pqT", [128, 16, 256], BF16)
        s_all = fw.sb("s_all", [128, 2, 16, 128])
        s1cs = [fw.sb("s1c", [128, 2, 8, 128]) for _ in range(2)]
        taus = [fw.sb("tau", [128, 2, 8]) for _ in range(2)]; bias = fw.sb("pbias", [128, 2, 8])
        m0bs = [fw.sb("m0b", [128, 2, 8, 16]) for _ in range(2)]
        idxa = fw.sb("idxa", [128, 2, 128])
        idxTs = [fw.sb("idxT", [128, 256], BF16) for _ in range(2)]
        SB = [(fw.sb("m0", [128, 16]), fw.sb("m1", [128, 16]), fw.sb("scr", [128, 128]), fw.sb("cand", [128, 16, 16]), fw.sb("c24", [128, 24]),
               fw.sb("cscr", [128, 256]), fw.sb("st1", [128, 4]), fw.sb("e16", [128, 16]), fw.sb("ix", [128, 16], U32)) for _ in range(3)]
        JB = 16
        sms = Rot([fw.sb("sm", [128, 8, 16, JB]) for _ in range(1)])
        Pss = Rot([fw.sb("P", [128, 8, 16, JB], BF16) for _ in range(1)])
        mks = Rot([fw.sb("mk", [128, 8, 16, JB], BF16) for _ in range(1)])
        Wp = fw.sb("Wp", [128, 2, 128, JB], BF16)
        WT = fw.sb("WT", [128, JB, 256], BF16)
        Walls = Rot([fw.sb("Wall", [128, 256, JB], BF16) for _ in range(2)])
        ohs = Rot([fw.sb("oh", [128, 32, 128], BF16) for _ in range(2)])
        UTs = Rot([fw.sb("UTs", [128, 2, 1024], BF16) for _ in range(2)])
        Vs = Rot([fw.sb("Vs", [128, 2, D], BF16) for _ in range(2)])
        Gs = Rot([fw.sb("Gs", [128, 256], BF16) for _ in range(3)])
        WAs = Rot([fw.sb("WAs", [128, 256], BF16) for _ in range(3)])
        tmps = Rot([fw.sb("ptmp", [128, D]) for _ in range(1)])
        hts = Rot([fw.sb("pht", [128, D]) for _ in range(1)])
        UTv = self.UT.ap().rearrange("j p k i -> p j (k i)")
        Vbv = self.Vb.ap().rearrange("j i d -> i j d")
        import os
        ng = int(os.environ.get("PEER_NG", NT // 2))
        NJB = 128 // JB
        glist = [g for g in range(ng) if not (final and g == 0)]
        Wls = {}

        def prelude_items(g):
            par = g % 2
            xn2T = xn2Ts[par]; s1c = s1cs[par]; tau = taus[par]; m0b = m0bs[par]; idxT = idxTs[par]
            s = 1 if g == 0 else 0
            items = []

            def normit(j):
                def f():
                    src, sap = self.src_of(2 * g + j, hsrc)
                    self.norm_tile(src, sap, GT[s], shT[s], xn2T, slice(j * 128, (j + 1) * 128))
                return f
            items += [normit(0), normit(1)]

            def qit(hp):
                def f():
                    pb = self.bank.next()
                    for k in range(8):
                        fw.op("pe", lambda e: e.matmul(pb[:, 0:256], lhsT=w_q[:, k, hp * 128:(hp + 1) * 128], rhs=xn2T[:, k, :], start=(k == 0), stop=(k == 7)), [w_q, xn2T], [pb])
                    fw.op("act", lambda e: e.activation(out=qT[:, hp, :], in_=pb[:, 0:256], func=AF.Copy), [pb], [qT])
                return f
            items += [qit(hp) for hp in range(16)]

            def scit(t, c):
                def f():
                    pb = self.bank.next()
                    for q in range(4):
                        hp = c * 4 + q
                        fw.op("pe", lambda e: e.matmul(pb[:, q * 128:(q + 1) * 128], lhsT=qT[:, hp, t * 128:(t + 1) * 128], rhs=keysT[:, hp, :], start=True, stop=True), [qT, keysT], [pb])
                    fw.op("act", lambda e: e.activation(out=s_all[:, t, c * 4:(c + 1) * 4, :].rearrange("p a b -> p (a b)"), in_=pb[:], func=AF.Copy), [pb], [s_all])
                    if c == 3:
                        fw.op("act", lambda e: e.activation(out=s1c[:, t, :, :], in_=s_all[:, t, :, :].rearrange("p (h two) j -> p h two j", two=2)[:, :, 1, :], func=AF.Copy), [s_all], [s1c])
                return f
            items += [scit(t, c) for t in range(2) for c in range(4)]

            def stats_gen(t, h, B):
                m0, m1, scr, cand, c24, cscr, st1, e16, ix = B
                s0 = s_all[:, t, 2 * h, :]; s1 = s_all[:, t, 2 * h + 1, :]
                fw.op("dve", lambda e: e.max(out=m0[:, 0:8], in_=s0), [s_all], [m0]); yield
                fw.op("dve", lambda e: e.max_index(out=ix[:, 0:8], in_max=m0[:, 0:8], in_values=s0), [s_all, m0], [ix]); yield
                fw.op("dve", lambda e: e.match_replace(out=scr[:], in_to_replace=m0[:, 0:8], in_values=s0, imm_value=-1e30), [s_all, m0], [scr]); yield
                fw.op("dve", lambda e: e.max(out=m0[:, 8:16], in_=scr[:]), [scr], [m0]); yield
                fw.op("dve", lambda e: e.max_index(out=ix[:, 8:16], in_max=m0[:, 8:16], in_values=scr[:]), [scr, m0], [ix]); yield
                fw.op("dve", lambda e: e.tensor_copy(out=idxa[:, t, h * 16:(h + 1) * 16], in_=ix[:]), [ix], [idxa]); yield
                fw.op("dve", lambda e: e.max(out=m1[:, 0:8], in_=s1), [s_all], [m1]); yield
                fw.op("dve", lambda e: e.match_replace(out=scr[:], in_to_replace=m1[:, 0:8], in_values=s1, imm_value=-1e30), [s_all, m1], [scr]); yield
                fw.op("dve", lambda e: e.max(out=m1[:, 8:16], in_=scr[:]), [scr], [m1]); yield
                fw.op("dve", lambda e: e.tensor_tensor(out=cand[:], in0=m0[:].unsqueeze(2).to_broadcast([128, 16, 16]), in1=m1[:].unsqueeze(1).to_broadcast([128, 16, 16]), op=ALU.add), [m0, m1], [cand]); yield
                cf = cand[:].rearrange("p a b -> p (a b)")
                fw.op("dve", lambda e: e.max(out=c24[:, 0:8], in_=cf), [cand], [c24]); yield
                fw.op("dve", lambda e: e.match_replace(out=cscr[:], in_to_replace=c24[:, 0:8], in_values=cf, imm_value=-1e30), [cand, c24], [cscr]); yield
                fw.op("dve", lambda e: e.max(out=c24[:, 8:16], in_=cscr[:]), [cscr], [c24]); yield
                fw.op("dve", lambda e: e.match_replace(out=cscr[:], in_to_replace=c24[:, 8:16], in_values=cscr[:], imm_value=-1e30), [cscr, c24], [cscr]); yield
                fw.op("dve", lambda e: e.max(out=c24[:, 16:24], in_=cscr[:]), [cscr], [c24]); yield
                fw.op("dve", lambda e: e.tensor_scalar(out=st1[:, 0:1], in0=c24[:, 16:17], scalar1=0.5, scalar2=None, op0=ALU.mult), [c24], [st1]); yield
                fw.op("dve", lambda e: e.scalar_tensor_tensor(out=st1[:, 0:1], in0=c24[:, 15:16], scalar=0.5, in1=st1[:, 0:1], op0=ALU.mult, op1=ALU.add), [c24, st1], [st1]); yield
                fw.op("dve", lambda e: e.tensor_scalar(out=st1[:, 1:2], in0=c24[:, 0:1], scalar1=-1.0, scalar2=None, op0=ALU.mult), [c24], [st1]); yield
                fw.op("act", lambda e: e.activation(out=e16[:], in_=c24[:, 0:16], func=AF.Exp, bias=st1[:, 1:2], scale=1.0, accum_out=st1[:, 2:3]), [c24, st1], [e16, st1]); yield
                fw.op("act", lambda e: e.activation(out=st1[:, 3:4], in_=st1[:, 2:3], func=AF.Ln), [st1], [st1]); yield
                fw.op("dve", lambda e: e.tensor_tensor(out=bias[:, t, h:h + 1], in0=st1[:, 1:2], in1=st1[:, 3:4], op=ALU.subtract), [st1], [bias]); yield
                fw.op("dve", lambda e: e.tensor_scalar(out=m0b[:, t, h, :], in0=m0[:], scalar1=bias[:, t, h:h + 1], scalar2=None, op0=ALU.add), [m0, bias], [m0b]); yield
                fw.op("dve", lambda e: e.tensor_tensor(out=tau[:, t, h:h + 1], in0=st1[:, 0:1], in1=bias[:, t, h:h + 1], op=ALU.add), [st1, bias], [tau]); yield

            def statit(ths):
                def f():
                    gens = [stats_gen(t, h, SB[i]) for i, (t, h) in enumerate(ths)]
                    alive = True
                    while alive:
                        alive = False
                        for gi in gens:
                            try:
                                next(gi); alive = True
                            except StopIteration:
                                pass
                return f
            allth = [(t, h) for h in range(8) for t in range(2)]
            for q in range(0, 16, 3):
                items.append(statit(allth[q:q + 3]))

            def idxit(t):
                def f():
                    pb = self.bank.next()
                    fw.op("pe", lambda e: e.transpose(out=pb[:, 0:128], in_=idxa[:, t, :], identity=self.identf[:]), [idxa, self.identf], [pb])
                    fw.op("act", lambda e: e.activation(out=idxT[:, t * 128:(t + 1) * 128], in_=pb[:, 0:128], func=AF.Copy), [pb], [idxT])
                return f
            items += [idxit(0), idxit(1)]
            return items

        for it in prelude_items(glist[0]):
            it()
        first_built = False
        for gi_, g in enumerate(glist):
            par = g % 2
            xn2T = xn2Ts[par]; s1c = s1cs[par]; tau = taus[par]; m0b = m0bs[par]; idxT = idxTs[par]
            s = 1 if g == 0 else 0
            gnext = glist[gi_ + 1] if gi_ + 1 < len(glist) else None
            def build_items(jb, g=g, m0b=m0b, s1c=s1c, tau=tau, idxT=idxT):
                Wl = Wls.setdefault(g, [None] * NJB)
                j0 = jb * JB
                items = []

                def rank(t):
                    def f():
                        sm = sms.next(); P = Pss.next(); mk = mks.next()
                        fw.op("dve", lambda e: e.tensor_tensor(out=sm[:].rearrange("p h a j -> p h (a j)").rearrange("p h (a j) -> p h a j", a=16),
                                                               in0=m0b[:, t, :, :].unsqueeze(3).to_broadcast([128, 8, 16, JB]),
                                                               in1=s1c[:, t, :, j0:j0 + JB].unsqueeze(2).to_broadcast([128, 8, 16, JB]),
                                                               op=ALU.add), [m0b, s1c], [sm])
                        fw.op("act", lambda e: e.activation(out=P[:].rearrange("p h a j -> p (h a j)"), in_=sm[:].rearrange("p h a j -> p (h a j)"), func=AF.Exp), [sm], [P])
                        fw.op("dve", lambda e: e.tensor_tensor(out=mk[:].rearrange("p h a j -> p h (a j)"), in0=sm[:].rearrange("p h a j -> p h (a j)"),
                                                               in1=tau[:, t, :].unsqueeze(2).to_broadcast([128, 8, 16 * JB]), op=ALU.is_ge), [sm, tau], [mk])
                        fw.op("dve", lambda e: e.tensor_tensor(out=Wp[:, t, :, :].rearrange("p (h a) j -> p h a j", a=16), in0=mk[:], in1=P[:], op=ALU.mult), [mk, P], [Wp])
                    return f
                items.append(rank(0)); items.append(rank(1))

                def transp(q):
                    def f():
                        pb = self.bank.next(); pv = self.bview(pb)
                        for jj in range(4):
                            for t in range(2):
                                c = jj * 2 + t
                                fw.op("pe", lambda e: e.transpose(out=pv[:, c * 128:(c + 1) * 128], in_=Wp[:, t, :, q * 4 + jj], identity=self.identb[:]), [Wp, self.identb], [pb])
                        fw.op("act", lambda e: e.activation(out=WT[:, q * 4:(q + 1) * 4, :].rearrange("p a b -> p (a b)"), in_=pv[:, 0:1024], func=AF.Copy), [pb], [WT])
                    return f
                for q in range(JB // 4):
                    items.append(transp(q))

                ohl = {}

                def ohA(nb):
                    oh = ohs.next()
                    ohl[nb] = oh
                    n0 = nb * 32
                    fw.op("dve", lambda e: e.tensor_tensor(out=oh[:], in0=idxT[:, n0:n0 + 32].unsqueeze(2).to_broadcast([128, 32, 128]),
                                                                                 in1=iota[:].unsqueeze(1).to_broadcast([128, 32, 128]), op=ALU.is_equal), [idxT, iota], [oh])

                def scatter(nb):
                    def f():
                        if nb == 0:
                            Wl[jb] = Walls.next()
                        Wall = Wl[jb]
                        oh = ohl[nb]
                        n0 = nb * 32
                        for half in range(2):
                            pb = self.bank.next()
                            for q in range(16):
                                n = n0 + half * 16 + q
                                fw.op("pe", lambda e: e.matmul(pb[:, q * JB:(q + 1) * JB], lhsT=oh[:, half * 16 + q, :], rhs=WT[:, :, n], start=True, stop=True), [oh, WT], [pb])
                            if half == 0:
                                fw.op("act", lambda e: e.activation(out=Wall[:, n0 + half * 16:n0 + half * 16 + 16, :].rearrange("p a b -> p (a b)"), in_=pb[:, 0:16 * JB], func=AF.Copy), [pb], [Wall])
                            else:
                                fw.op("dve", lambda e: e.tensor_copy(out=Wall[:, n0 + half * 16:n0 + half * 16 + 16, :].rearrange("p a b -> p (a b)"), in_=pb[:, 0:16 * JB]), [pb], [Wall])
                        if nb + 1 < 8:
                            ohA(nb + 1)
                    return f
                last_tr = items[-1]

                def tr_and_oh():
                    last_tr()
                    ohA(0)
                items[-1] = tr_and_oh
                for nb in range(8):
                    items.append(scatter(nb))
                return items

            cur_ld = {}

            def front(jx):
                jb, jj = jx // JB, jx % JB
                Wall = Wls[g][jb]
                if jx % 2 == 0:
                    UTt = UTs.next(); Vt = Vs.next()
                    fw.dma("sp", UTt, UTt[:], self.UT, UTv[:, jx:jx + 2, :])
                    fw.dma("sp", Vt, Vt[:], self.Vb, Vbv[:, jx:jx + 2, :])
                    cur_ld["u"] = UTt; cur_ld["v"] = Vt
                UTt = cur_ld["u"]; Vt = cur_ld["v"]
                i2 = jx % 2
                pa = self.bank.next()
                for k in range(8):
                    fw.op("pe", lambda e: e.matmul(pa[:, 0:256], lhsT=UTt[:, i2, k * 128:(k + 1) * 128], rhs=xn2T[:, k, :], start=(k == 0), stop=(k == 7)), [UTt, xn2T], [pa])
                G = Gs.next()
                fw.op("act", lambda e: e.activation(out=G[:], in_=pa[:, 0:256], func=AF.Gelu), [pa], [G])
                WA = WAs.next()
                fw.op("dve", lambda e: e.tensor_tensor(out=WA[:], in0=G[:], in1=Wall[:, :, jj], op=ALU.mult), [G, Wall], [WA])

                def back():
                    for t in range(2):
                        for half in range(2):
                            ob = obanks[t * 2 + half]
                            fw.op("pe", lambda e: e.matmul(ob[:], lhsT=WA[:, t * 128:(t + 1) * 128], rhs=Vt[:, i2, half * 512:(half + 1) * 512], start=(jx == 0), stop=(jx == 127)), [WA, Vt], [ob])
                return back

            if not first_built:
                for it in build_items(0):
                    it()
                first_built = True
            backs = {0: front(0), 1: front(1)}
            pending = []
            side = prelude_items(gnext) if gnext is not None else []
            nside = len(side)
            for jx in range(128):
                jb, jj = jx // JB, jx % JB
                if jj == 0:
                    if jb + 1 < NJB:
                        pending = build_items(jb + 1)
                    elif gnext is not None:
                        while side:
                            side.pop(0)()
                        parn = gnext % 2
                        pending = build_items(0, g=gnext, m0b=m0bs[parn], s1c=s1cs[parn], tau=taus[parn], idxT=idxTs[parn])
                    else:
                        pending = []
                if pending and jj < JB - 2:
                    per = -(-len(pending) // max(1, (JB - 2 - jj)))
                    for _ in range(per):
                        if pending:
                            pending.pop(0)()
                if side and jb < NJB - 1:
                    steps_left = (NJB - 1) * JB - jx
                    per = -(-len(side) // max(1, steps_left))
                    for _ in range(per):
                        if side:
                            side.pop(0)()
                if jx + 2 < 128:
                    backs[jx + 2] = front(jx + 2)
                backs.pop(jx)()
            while pending:
                pending.pop(0)()
            for t in range(2):
                tile = 2 * g + t
                tmp = tmps.next(); ht = hts.next()
                src, sap = self.src_of(tile, hsrc)
                fw.dma("sp", ht, ht[:], src, sap)
                for half in range(2):
                    ob = obanks[t * 2 + half]
                    fw.op("dve", lambda e: e.tensor_tensor(out=tmp[:, half * 512:(half + 1) * 512], in0=ob[:], in1=gate[s][:, half * 512:(half + 1) * 512], op=ALU.mult), [ob, gate[s]], [tmp])
                fw.op("pool", lambda e: e.tensor_tensor(out=tmp[:], in0=tmp[:], in1=ht[:], op=ALU.add), [tmp, ht], [tmp])
                if final:
                    fw.dma("act", self.out, self.out.ap()[(tile - 2) * 128:(tile - 1) * 128, :], tmp, tmp[:])
                else:
                    fw.dma("act", hdst, hdst.ap()[tile * 128:(tile + 1) * 128, :], tmp, tmp[:])
        self.bank = Rot(self.banks)
        fw.pop()

    def l1_phaseA(self, hsrc):
        fw = self.fw
        I = self.I
        S = self.S1 = {}
        for nm in ("R", "KAP", "V", "G", "LW0", "LW1", "B0", "B1", "KD0", "KD1"):
            S[nm] = fw.dram("s1_" + nm, [TOK, D], F32, kind=("ExternalOutput" if self.dbg else "Internal"))
        S["BON"] = fw.dram("s1_BON", [TOK, 16], F32, kind=("ExternalOutput" if self.dbg else "Internal"))
        fw.push()
        GT, shT = self.modvecs(1, 0)
        self.make_normer()
        self.bank = Rot(self.banks)
        W = {}
        for nm in ("od_w_r", "od_w_k", "od_w_v"):
            W[nm] = fw.sb(nm, [128, 8, D], BF16)
            for k in range(8):
                fw.dma("pool", W[nm], W[nm][:, k, :], I[nm], I[nm].ap()[k * 128:(k + 1) * 128, :])
        g1 = fw.sb("g1", [128, 8, 128], BF16); g2 = fw.sb("g2", [128, D], BF16)
        fw.dma("pool", g1, g1[:], I["od_g1"], I["od_g1"].ap().rearrange("(k p) n -> p k n", p=128))
        fw.dma("pool", g2, g2[:], I["od_g2"], I["od_g2"].ap())
        w1c = fw.sb("w1c", [128, 8, 128], BF16); a1c = fw.sb("a1c", [128, 8, 128], BF16)
        for d in range(2):
            fw.dma("pool", w1c, w1c[:, :, d * 64:(d + 1) * 64], I["od_w1"], I["od_w1"].ap()[d].rearrange("(k p) n -> p k n", p=128))
            fw.dma("pool", a1c, a1c[:, :, d * 64:(d + 1) * 64], I["od_a1"], I["od_a1"].ap()[d].rearrange("(k p) n -> p k n", p=128))
        w2c = fw.sb("w2c", [128, D], BF16); a2c = fw.sb("a2c", [128, D], BF16)
        fw.dma("pool", w2c, w2c[:], I["od_w2"], I["od_w2"].ap().rearrange("d l n -> (d l) n"))
        fw.dma("pool", a2c, a2c[:], I["od_a2"], I["od_a2"].ap().rearrange("d l n -> (d l) n"))
        brow = fw.sb("brow", [1, 4, D], BF16)
        fw.dma("pool", brow, brow[:, 0:2, :], I["od_w0"], I["od_w0"].ap().rearrange("(o d) n -> o d n", o=1))
        fw.dma("pool", brow, brow[:, 2:4, :], I["od_a0"], I["od_a0"].ap().rearrange("(o d) n -> o d n", o=1))
        ones1 = fw.sb("ones1", [1, 128], BF16)
        fw.op("dve", lambda e: e.memset(ones1[:], 1.0), [], [ones1])
        kkb = fw.sb("kkb", [128, D]); kab = fw.sb("kab", [128, D]); rkb = fw.sb("rkb", [128, D])
        for t_, nm in ((kkb, "od_k_k"), (kab, "od_k_a"), (rkb, "od_r_k")):
            fw.dma("sp", t_, t_[:], I[nm], I[nm].ap().partition_broadcast(128))
        muT = fw.sb("muT", [128, 6, 8])
        for m in range(6):
            self.load_colT(muT, muT[:, m, :], I["od_mu"], I["od_mu"].ap()[m])
        NG = NT // 2
        win = [fw.sb("xnw", [128, 8, 256], BF16) for _ in range(4)]
        xxT = fw.sb("xxT", [128, 8, 256], BF16)
        xms = Rot([fw.sb("xm", [128, 8, 256], BF16) for _ in range(2)])
        tmpp = Rot([fw.sb("l1t", [128, D]) for _ in range(8)])
        ksb0 = fw.sb("ksb0", [128, D]); ksb1 = fw.sb("ksb1", [128, D])
        kap = fw.sb("kap", [128, D]); kka = fw.sb("kka", [128, D]); kmk = fw.sb("kmk", [128, D]); r_s = fw.sb("r_s", [128, D]); kd0 = fw.sb("kd0", [128, D])
        ss16 = Rot([fw.sb("ss16", [128, 16]) for _ in range(2)])
        lo = Rot([fw.sb("lo", [128, 128], BF16) for _ in range(3)])
        loT = Rot([fw.sb("loT", [128, 128], BF16) for _ in range(3)])

        def norm_group(g):
            s = 1 if g == 0 else 0
            for j in range(2):
                src, sap = self.src_of(2 * g + j, hsrc)
                self.norm_tile(src, sap, GT[s], shT[s], win[g % 4], slice(j * 128, (j + 1) * 128))

        def sub(out, a, b):
            fw.op("pool", lambda e: e.tensor_tensor(out=out, in0=a, in1=b, op=ALU.subtract), [win[0], win[1], win[2], win[3]], [xxT])

        def neg(out, a):
            fw.op("pool", lambda e: e.tensor_scalar(out=out, in0=a, scalar1=-1.0, scalar2=None, op0=ALU.mult), [win[0], win[1], win[2], win[3]], [xxT])

        def proj_tok(xm, t, Wt, nb=2):
            outs = []
            for half in range(nb):
                pb = self.bank.next()
                for k in range(8):
                    fw.op("pe", lambda e: e.matmul(pb[:], lhsT=xm[:, k, t * 128:(t + 1) * 128], rhs=Wt[:, k, half * 512:(half + 1) * 512], start=(k == 0), stop=(k == 7)), [xm, Wt], [pb])
                outs.append(pb)
            return outs

        def evac(dst, pbs):
            for half, pb in enumerate(pbs):
                fw.op("act", lambda e: e.activation(out=dst[:, half * 512:(half + 1) * 512], in_=pb[:], func=AF.Copy), [pb], [dst])

        def store(nm, tile, src, ap=None):
            fw.dma("sp", S[nm], S[nm].ap()[tile * 128:(tile + 1) * 128, :], src, src[:] if ap is None else ap)

        def lerp(m, cur):
            xm = xms.next()
            for k in range(8):
                fw.op("dve", lambda e: e.scalar_tensor_tensor(out=xm[:, k, :], in0=xxT[:, k, :], scalar=muT[:, m, k:k + 1], in1=cur[:, k, :], op0=ALU.mult, op1=ALU.add), [xxT, muT, cur], [xm])
            return xm

        def lora(xm, t, w1, w2, brow_i, func):
            pb = self.bank.next()
            for k in range(8):
                fw.op("pe", lambda e: e.matmul(pb[:, 0:128], lhsT=xm[:, k, t * 128:(t + 1) * 128], rhs=w1[:, k, :], start=(k == 0), stop=(k == 7)), [xm, w1], [pb])
            l_ = lo.next()
            fw.op("act", lambda e: e.activation(out=l_[:], in_=pb[:, 0:128], func=func), [pb], [l_])
            pt = self.bank.next(); pv = self.bview(pt)
            fw.op("pe", lambda e: e.transpose(out=pv[:, 0:128], in_=l_[:], identity=self.identb[:]), [l_, self.identb], [pt])
            lT = loT.next()
            fw.op("act", lambda e: e.activation(out=lT[:], in_=pv[:, 0:128], func=AF.Copy), [pt], [lT])
            res = []
            for d in range(2):
                outs = []
                for half in range(2):
                    po = self.bank.next()
                    fw.op("pe", lambda e: e.matmul(po[:], lhsT=ones1[:], rhs=brow[:, brow_i + d, half * 512:(half + 1) * 512], start=True, stop=False), [ones1, brow], [po])
                    fw.op("pe", lambda e: e.matmul(po[:], lhsT=lT[d * 64:(d + 1) * 64, :], rhs=w2[d * 64:(d + 1) * 64, half * 512:(half + 1) * 512], start=False, stop=True), [lT, w2], [po])
                    outs.append(po)
                res.append(outs)
            return res

        norm_group(0)
        import os
        for g in range(int(os.environ.get("L1A_NG", NG))):
            if g + 1 < NG:
                norm_group(g + 1)
            cur = win[g % 4]; prv = win[(g - 1) % 4]; nxt = win[(g + 1) % 4]
            for k in range(8):
                c = cur[:, k, :]; o = xxT[:, k, :]
                if g == 0:
                    if k < 4:
                        sub(o[:, 1:256], c[:, 0:255], c[:, 1:256]); neg(o[:, 0:1], c[:, 0:1])
                    else:
                        sub(o[:, 0:255], c[:, 1:256], c[:, 0:255]); neg(o[:, 255:256], c[:, 255:256])
                else:
                    c3 = c.rearrange("p (r c) -> p r c", c=64); o3 = o.rearrange("p (r c) -> p r c", c=64)
                    if k < 2:
                        sub(o3[:, :, 1:64], c3[:, :, 0:63], c3[:, :, 1:64]); neg(o3[:, :, 0:1], c3[:, :, 0:1])
                    elif k < 4:
                        sub(o3[:, :, 0:63], c3[:, :, 1:64], c3[:, :, 0:63]); neg(o3[:, :, 63:64], c3[:, :, 63:64])
                    elif k < 6:
                        sub(o[:, 64:256], c[:, 0:192], c[:, 64:256])
                        if g == 1:
                            neg(o[:, 0:64], c[:, 0:64])
                        else:
                            sub(o[:, 0:64], prv[:, k, 192:256], c[:, 0:64])
                    else:
                        sub(o[:, 0:192], c[:, 64:256], c[:, 0:192])
                        if g == NG - 1:
                            neg(o[:, 192:256], c[:, 192:256])
                        else:
                            sub(o[:, 192:256], nxt[:, k, 0:64], c[:, 192:256])
            xr = lerp(0, cur)
            for t in range(2):
                tile = 2 * g + t
                rr_ = tmpp.next()
                evac(rr_, proj_tok(xr, t, W["od_w_r"]))
                store("R", tile, rr_)
            xk = lerp(2, cur)
            ktiles = [ksb0, ksb1]
            for t in range(2):
                evac(ktiles[t], proj_tok(xk, t, W["od_w_k"]))
            xv = lerp(3, cur)
            for t in range(2):
                v_ = tmpp.next()
                evac(v_, proj_tok(xv, t, W["od_w_v"]))
                store("V", 2 * g + t, v_)
            xg = lerp(5, cur)
            for t in range(2):
                pb = self.bank.next()
                for k in range(8):
                    fw.op("pe", lambda e: e.matmul(pb[:, 0:128], lhsT=xg[:, k, t * 128:(t + 1) * 128], rhs=g1[:, k, :], start=(k == 0), stop=(k == 7)), [xg, g1], [pb])
                l_ = lo.next()
                fw.op("act", lambda e: e.activation(out=l_[:], in_=pb[:, 0:128], func=AF.Sigmoid), [pb], [l_])
                pt = self.bank.next(); pv = self.bview(pt)
                fw.op("pe", lambda e: e.transpose(out=pv[:, 0:128], in_=l_[:], identity=self.identb[:]), [l_, self.identb], [pt])
                lT = loT.next()
                fw.op("act", lambda e: e.activation(out=lT[:], in_=pv[:, 0:128], func=AF.Copy), [pt], [lT])
                g_ = tmpp.next()
                outs = []
                for half in range(2):
                    po = self.bank.next()
                    fw.op("pe", lambda e: e.matmul(po[:], lhsT=lT[:], rhs=g2[:, half * 512:(half + 1) * 512], start=True, stop=True), [lT, g2], [po])
                    outs.append(po)
                evac(g_, outs)
                store("G", 2 * g + t, g_)
            xw = lerp(1, cur)
            xa = lerp(4, cur)
            for t in range(2):
                tile = 2 * g + t
                ksb = ktiles[t]
                kkx = tmpp.next(); sq = tmpp.next()
                fw.op("dve", lambda e: e.tensor_tensor(out=kkx[:], in0=ksb[:], in1=kkb[:], op=ALU.mult), [ksb, kkb], [kkx])
                fw.op("dve", lambda e: e.tensor_tensor(out=sq[:], in0=kkx[:], in1=kkx[:], op=ALU.mult), [kkx], [sq])
                s16 = ss16.next()
                fw.op("dve", lambda e: e.reduce_sum(out=s16[:], in_=sq[:].rearrange("p (h j) -> p h j", j=64), axis=AX.X), [sq], [s16])
                fw.op("act", lambda e: e.activation(out=s16[:], in_=s16[:], func=AF.Sqrt, bias=1e-12, scale=1.0), [s16], [s16])
                fw.op("dve", lambda e: e.reciprocal(out=s16[:], in_=s16[:]), [s16], [s16])
                fw.op("dve", lambda e: e.tensor_tensor(out=kap[:].rearrange("p (h j) -> p h j", j=64), in0=kkx[:].rearrange("p (h j) -> p h j", j=64),
                                                       in1=s16[:].unsqueeze(2).to_broadcast([128, 16, 64]), op=ALU.mult), [kkx, s16], [kap])
                store("KAP", tile, kap)
                fw.op("pool", lambda e: e.tensor_tensor(out=kka[:], in0=ksb[:], in1=kab[:], op=ALU.mult), [ksb, kab], [kka])
                fw.op("pool", lambda e: e.tensor_tensor(out=kmk[:], in0=ksb[:], in1=kka[:], op=ALU.subtract), [ksb, kka], [kmk])
                wl = lora(xw, t, w1c, w2c, 0, AF.Tanh)
                for d in range(2):
                    lw = tmpp.next()
                    for half in range(2):
                        fw.op("act", lambda e: e.activation(out=lw[:, half * 512:(half + 1) * 512], in_=wl[d][half][:], func=AF.Sigmoid), [wl[d][half]], [lw])
                    fw.op("dve", lambda e: e.tensor_scalar(out=lw[:], in0=lw[:], scalar1=-0.6065306597126334, scalar2=None, op0=ALU.mult), [lw], [lw])
                    store("LW%d" % d, tile, lw)
                al = lora(xa, t, a1c, a2c, 2, AF.Copy)
                avs = []
                for d in range(2):
                    a_ = tmpp.next()
                    for half in range(2):
                        fw.op("act", lambda e: e.activation(out=a_[:, half * 512:(half + 1) * 512], in_=al[d][half][:], func=AF.Sigmoid), [al[d][half]], [a_])
                    avs.append(a_)
                kds = []
                for d in range(2):
                    a_ = avs[d]
                    b_ = tmpp.next()
                    fw.op("dve", lambda e: e.tensor_tensor(out=b_[:], in0=kap[:], in1=a_[:], op=ALU.mult), [kap, a_], [b_])
                    store("B%d" % d, tile, b_)
                    kd = kd0 if d == 0 else tmpp.next()
                    fw.op("dve", lambda e: e.tensor_tensor(out=kd[:], in0=kka[:], in1=a_[:], op=ALU.mult), [kka, a_], [kd])
                    fw.op("dve", lambda e: e.tensor_tensor(out=kd[:], in0=kd[:], in1=kmk[:], op=ALU.add), [kd, kmk], [kd])
                    store("KD%d" % d, tile, kd)
                    kds.append(kd)
                fw.dma("sp", r_s, r_s[:], S["R"], S["R"].ap()[tile * 128:(tile + 1) * 128, :])
                ks = tmpp.next()
                fw.op("dve", lambda e: e.tensor_tensor(out=ks[:], in0=kds[0][:], in1=kds[1][:], op=ALU.add), [kds[0], kds[1]], [ks])
                fw.op("dve", lambda e: e.tensor_tensor(out=ks[:], in0=ks[:], in1=rkb[:], op=ALU.mult), [ks, rkb], [ks])
                fw.op("dve", lambda e: e.tensor_tensor(out=ks[:], in0=ks[:], in1=r_s[:], op=ALU.mult), [ks, r_s], [ks])
                bon = ss16.next()
                fw.op("dve", lambda e: e.reduce_sum(out=bon[:], in_=ks[:].rearrange("p (h j) -> p h j", j=64), axis=AX.X), [ks], [bon])
                fw.dma("sp", S["BON"], S["BON"].ap()[tile * 128:(tile + 1) * 128, :], bon, bon[:])
        fw.pop()

    def l1_phaseB(self):
        fw = self.fw
        I = self.I
        S = self.S1
        k_ = "ExternalOutput" if self.dbg else "Internal"
        self.Y = [fw.dram("s1_Y%d" % d, [TOK, D], F32, kind=k_) for d in range(2)]
        fw.push()
        ybanks = self.banks[0:2]
        pbanks = Rot(self.banks[2:8])
        hb = []
        hbr = Rot([(self.banks[2 + (i % 6)], (i // 6) * 256) for i in range(12)])
        sml = hbr
        tm = fw.sb("tm", [128, 4, 128])
        fw.dma("sp", tm, tm[:], I["trimask"], I["trimask"].ap())
        ones = fw.sb("ones", [128, 128])
        fw.op("dve", lambda e: e.memset(ones[:], 1.0), [], [ones])
        MA = []; MB = []; MN = []; MC = []
        for d in range(2):
            mS, mI, mSn, mC = (0, 1, 2, 1) if d == 0 else (2, 3, 0, 3)
            a = fw.sb("MA", [128, 2, 128]); b = fw.sb("MB", [128, 2, 128]); n = fw.sb("MN", [128, 128])
            fw.op("dve", lambda e: e.tensor_scalar(out=a[:, 0, :], in0=tm[:, mS, :], scalar1=-1.0, scalar2=None, op0=ALU.mult), [tm], [a])
            fw.op("dve", lambda e: e.tensor_copy(out=a[:, 1, :], in_=tm[:, mI, :]), [tm], [a])
            fw.op("dve", lambda e: e.tensor_copy(out=b[:, 0, :], in_=tm[:, mS, :]), [tm], [b])
            fw.op("dve", lambda e: e.tensor_copy(out=b[:, 1, :], in_=tm[:, mI, :]), [tm], [b])
            fw.op("dve", lambda e: e.tensor_scalar(out=n[:], in0=tm[:, mSn, :], scalar1=-1.0, scalar2=None, op0=ALU.mult), [tm], [n])
            MA.append(a); MB.append(b); MN.append(n); MC.append(mC)
        H = fw.sb("H", [64, 16, 64])
        ld = {nm: Rot([fw.sb("ld" + nm, [128, D]) for _ in range(2)]) for nm in ("R", "KAP", "V", "LW", "B", "KD")}
        prep = {nm: fw.sb("pp" + nm, [128, D], BF16 if nm in ("rt", "kt", "bt", "kdt") else F32) for nm in ("rt", "kt", "bt", "kdt", "bh", "kh")}
        Vb = fw.sb("Vbf", [128, D], BF16); Hb = fw.sb("Hb", [64, 16, 64], BF16)
        cumS = fw.sb("cumS", [128, 512]); et = Rot([fw.sb("et", [128, 512]) for _ in range(3)])
        gC = fw.sb("gC", [64, 16])
        NH = 16
        TT = Rot([fw.sb("TT", [64, 4, 128], BF16) for _ in range(NH)])
        SC = Rot([fw.sb("SC", [128, 4, 128], BF16) for _ in range(NH)])
        NM = Rot([fw.sb("NM", [128, 2, 128], BF16) for _ in range(NH * 3)])
        XS = Rot([fw.sb("XS", [128, 64], BF16) for _ in range(NH * 3)])
        UF = Rot([fw.sb("UF", [128, 64]) for _ in range(NH)])
        ysb = Rot([fw.sb("ysb", [128, D]) for _ in range(2)])
        import os
        nchunks = int(os.environ.get("L1B_NC", NT))

        def head_gen(h, d, T_):
            R_, KAP_, V_, LW_, B_, KD_ = T_
            hs = slice(h * 64, (h + 1) * 64)
            tt = TT.next(); sc = SC.next()
            for pair, (x0, x1) in enumerate(((prep["kt"], prep["rt"]), (prep["bt"], prep["kdt"]))):
                bk, off = hbr.next()
                bv = self.bview(bk)
                fw.op("pe", lambda e: e.transpose(out=bv[0:64, 2 * off:2 * off + 128], in_=x0[:, hs], identity=self.identb[:]), [x0, self.identb], [bk])
                fw.op("pe", lambda e: e.transpose(out=bv[0:64, 2 * off + 128:2 * off + 256], in_=x1[:, hs], identity=self.identb[:]), [x1, self.identb], [bk])
                fw.op("act", lambda e: e.activation(out=tt[:, 2 * pair:2 * pair + 2, :].rearrange("p a b -> p (a b)"), in_=bv[0:64, 2 * off:2 * off + 256], func=AF.Copy), [bk], [tt])
            yield
            kT = tt[:, 0, :]; rT = tt[:, 1, :]; bT = tt[:, 2, :]; kdT = tt[:, 3, :]
            kr = tt[:, 0:2, :].rearrange("p a b -> p (a b)")
            bk, off = hbr.next()
            fw.op("pe", lambda e: e.matmul(bk[:, off:off + 256], lhsT=bT, rhs=kr, start=True, stop=True), [tt], [bk])
            fw.op("dve", lambda e: e.tensor_tensor(out=sc[:, 0:2, :].rearrange("p a b -> p (a b)"), in0=bk[:, off:off + 256], in1=MA[d][:].rearrange("p a b -> p (a b)"), op=ALU.mult), [bk, MA[d]], [sc])
            bk, off = hbr.next()
            fw.op("pe", lambda e: e.matmul(bk[:, off:off + 256], lhsT=kdT, rhs=kr, start=True, stop=True), [tt], [bk])
            fw.op("dve", lambda e: e.tensor_tensor(out=sc[:, 2:4, :].rearrange("p a b -> p (a b)"), in0=bk[:, off:off + 256], in1=MB[d][:].rearrange("p a b -> p (a b)"), op=ALU.mult), [bk, MB[d]], [sc])
            nm = NM.next()
            bk, off = hbr.next()
            fw.op("pe", lambda e: e.matmul(bk[:, off:off + 128], lhsT=kT, rhs=bT, start=True, stop=True), [tt], [bk])
            fw.op("dve", lambda e: e.tensor_tensor(out=nm[:, 0, :], in0=bk[:, off:off + 128], in1=MN[d][:], op=ALU.mult), [bk, MN[d]], [nm])
            fw.op("act", lambda e: e.activation(out=nm[:, 1, :], in_=sc[:, 0, :], func=AF.Copy), [sc], [nm])
            yield
            NT_ = sc[:, 0, :]; BrbT = sc[:, 1, :]; AkT = sc[:, 2, :]; BrkT = sc[:, 3, :]
            bk, off = sml.next()
            fw.op("pe", lambda e: e.matmul(bk[:, off:off + 64], lhsT=kT, rhs=Hb[:, h, :], start=True, stop=False), [tt, Hb], [bk])
            fw.op("pe", lambda e: e.matmul(bk[:, off:off + 64], lhsT=AkT, rhs=Vb[:, hs], start=False, stop=True), [sc, Vb], [bk])
            X = XS.next()
            fw.op("act", lambda e: e.activation(out=X[:], in_=bk[:, off:off + 64], func=AF.Copy, scale=-1.0), [bk], [X])
            yield
            cur = nm
            for lvl in range(7):
                bk, off = sml.next()
                fw.op("pe", lambda e: e.matmul(bk[:, off:off + 64], lhsT=cur[:, 1, :], rhs=X[:], start=True, stop=True), [cur, X], [bk])
                X2 = XS.next() if lvl < 6 else UF.next()
                fw.op("dve", lambda e: e.tensor_tensor(out=X2[:], in0=bk[:, off:off + 64], in1=X[:], op=ALU.add), [bk, X], [X2])
                X = X2
                if lvl < 6:
                    nxt = NM.next()
                    bk, off = hbr.next()
                    if lvl < 5:
                        fw.op("pe", lambda e: e.matmul(bk[:, off:off + 128], lhsT=cur[:, 1, :], rhs=cur[:, 0, :], start=True, stop=True), [cur], [bk])
                    fw.op("pe", lambda e: e.matmul(bk[:, off + 128:off + 256], lhsT=cur[:, 0, :], rhs=cur[:, 1, :], start=True, stop=True), [cur], [bk])
                    if lvl < 5:
                        fw.op("act", lambda e: e.activation(out=nxt[:].rearrange("p a b -> p (a b)"), in_=bk[:, off:off + 256], func=AF.Copy), [bk], [nxt])
                    else:
                        fw.op("act", lambda e: e.activation(out=nxt[:, 1, :], in_=bk[:, off + 128:off + 256], func=AF.Copy), [bk], [nxt])
                    cur = nxt
                yield
            U = X
            Ub = XS.next()
            fw.op("act", lambda e: e.activation(out=Ub[:], in_=U[:], func=AF.Copy), [U], [Ub])
            yb = ybanks[h // 8]; yo = (h % 8) * 64
            fw.op("pe", lambda e: e.matmul(yb[:, yo:yo + 64], lhsT=rT, rhs=Hb[:, h, :], start=True, stop=False), [tt, Hb], [yb])
            fw.op("pe", lambda e: e.matmul(yb[:, yo:yo + 64], lhsT=BrbT, rhs=Ub[:], start=False, stop=False), [sc, Ub], [yb])
            fw.op("pe", lambda e: e.matmul(yb[:, yo:yo + 64], lhsT=BrkT, rhs=Vb[:, hs], start=False, stop=True), [sc, Vb], [yb])
            bk, off = sml.next()
            fw.op("pe", lambda e: e.matmul(bk[0:64, off:off + 64], lhsT=prep["bh"][:, hs], rhs=U[:], start=True, stop=False), [prep["bh"], U], [bk])
            fw.op("pe", lambda e: e.matmul(bk[0:64, off:off + 64], lhsT=prep["kh"][:, hs], rhs=V_[:, hs], start=False, stop=True), [prep["kh"], V_], [bk])
            fw.op("dve", lambda e: e.scalar_tensor_tensor(out=H[:, h, :], in0=H[:, h, :], scalar=gC[:, h:h + 1], in1=bk[0:64, off:off + 64], op0=ALU.mult, op1=ALU.add), [H, gC, bk], [H])
            yield

        for d in range(2):
            fw.op("dve", lambda e: e.memset(H[:], 0.0), [], [H])
            fw.op("dve", lambda e: e.memset(Hb[:], 0.0), [], [Hb])
            order = list(range(NT)) if d == 0 else [1, 0] + list(range(NT - 1, 1, -1))
            for c in order[:nchunks]:
                rows = slice(c * 128, (c + 1) * 128)
                T_ = []
                for nm, key in (("R", "R"), ("KAP", "KAP"), ("V", "V"), ("LW", "LW%d" % d), ("B", "B%d" % d), ("KD", "KD%d" % d)):
                    t_ = ld[nm].next()
                    fw.dma("sp", t_, t_[:], S[key], S[key].ap()[rows, :])
                    T_.append(t_)
                R_, KAP_, V_, LW_, B_, KD_ = T_
                fw.op("pool", lambda e: e.tensor_copy(out=Vb[:], in_=V_[:]), [V_], [Vb])
                bk, off = sml.next()
                for h in range(16):
                    fw.op("pe", lambda e: e.matmul(bk[0:64, off + h:off + h + 1], lhsT=LW_[:, h * 64:(h + 1) * 64], rhs=ones[:, 0:1], start=True, stop=True), [LW_, ones], [bk])
                fw.op("act", lambda e: e.activation(out=gC[:], in_=bk[0:64, off:off + 16], func=AF.Exp), [bk], [gC])
                for half in range(2):
                    cs = slice(half * 512, (half + 1) * 512)
                    pc = pbanks.next(); ptot = pbanks.next()
                    fw.op("pe", lambda e: e.matmul(pc[:], lhsT=tm[:, MC[d], :], rhs=LW_[:, cs], start=True, stop=True), [tm, LW_], [pc])
                    fw.op("pe", lambda e: e.matmul(ptot[:], lhsT=ones[:], rhs=LW_[:, cs], start=True, stop=True), [ones, LW_], [ptot])
                    fw.op("act", lambda e: e.activation(out=cumS[:], in_=pc[:], func=AF.Copy), [pc], [cumS])
                    e1 = et.next()
                    fw.op("act", lambda e: e.activation(out=e1[:], in_=pc[:], func=AF.Exp), [pc], [e1])
                    fw.op("pool", lambda e: e.tensor_tensor(out=prep["rt"][:, cs], in0=R_[:, cs], in1=e1[:], op=ALU.mult), [R_, e1], [prep["rt"]])
                    e2 = et.next()
                    fw.op("act", lambda e: e.activation(out=e2[:], in_=pc[:], func=AF.Exp, scale=-1.0), [pc], [e2])
                    fw.op("pool", lambda e: e.tensor_tensor(out=prep["bt"][:, cs], in0=B_[:, cs], in1=e2[:], op=ALU.mult), [B_, e2], [prep["bt"]])
                    fw.op("dve", lambda e: e.tensor_tensor(out=prep["kdt"][:, cs], in0=KD_[:, cs], in1=e2[:], op=ALU.mult), [KD_, e2], [prep["kdt"]])
                    e3 = et.next()
                    fw.op("dve", lambda e: e.tensor_tensor(out=e3[:], in0=cumS[:], in1=LW_[:, cs], op=ALU.subtract), [cumS, LW_], [e3])
                    fw.op("act", lambda e: e.activation(out=e3[:], in_=e3[:], func=AF.Exp), [e3], [e3])
                    fw.op("dve", lambda e: e.tensor_tensor(out=prep["kt"][:, cs], in0=KAP_[:, cs], in1=e3[:], op=ALU.mult), [KAP_, e3], [prep["kt"]])
                    e4 = et.next()
                    fw.op("dve", lambda e: e.tensor_tensor(out=e4[:], in0=ptot[:], in1=cumS[:], op=ALU.subtract), [ptot, cumS], [e4])
                    fw.op("act", lambda e: e.activation(out=e4[:], in_=e4[:], func=AF.Exp), [e4], [e4])
                    fw.op("dve", lambda e: e.tensor_tensor(out=prep["bh"][:, cs], in0=B_[:, cs], in1=e4[:], op=ALU.mult), [B_, e4], [prep["bh"]])
                    fw.op("dve", lambda e: e.tensor_tensor(out=prep["kh"][:, cs], in0=KD_[:, cs], in1=e4[:], op=ALU.mult), [KD_, e4], [prep["kh"]])
                for h0 in range(0, 16, NH):
                    gens = [head_gen(h, d, T_) for h in range(h0, h0 + NH)]
                    alive = True
                    while alive:
                        alive = False
                        for gi in gens:
                            try:
                                next(gi)
                                alive = True
                            except StopIteration:
                                pass
                fw.op("act", lambda e: e.activation(out=Hb[:].rearrange("p a b -> p (a b)"), in_=H[:].rearrange("p a b -> p (a b)"), func=AF.Copy), [H], [Hb])
                ys = ysb.next()
                for half in range(2):
                    fw.op("act", lambda e: e.activation(out=ys[:, half * 512:(half + 1) * 512], in_=ybanks[half][:], func=AF.Copy), [ybanks[half]], [ys])
                fw.dma("act", self.Y[d], self.Y[d].ap()[rows, :], ys, ys[:])
        fw.pop()

    def l1_phaseC(self, hsrc, hdst):
        fw = self.fw
        I = self.I
        S = self.S1
        fw.push()
        self.bank = Rot(self.banks)
        gate = self.gate_bcast(1, 0)
        w_o = fw.sb("w_o", [128, 8, D], BF16)
        for k in range(8):
            fw.dma("pool", w_o, w_o[:, k, :], I["od_w_o"], I["od_w_o"].ap()[k * 128:(k + 1) * 128, :])
        gnw = fw.sb("gnw", [128, D]); gnb = fw.sb("gnb", [128, D])
        fw.dma("sp", gnw, gnw[:], I["od_gn_w"], I["od_gn_w"].ap().partition_broadcast(128))
        fw.dma("sp", gnb, gnb[:], I["od_gn_b"], I["od_gn_b"].ap().partition_broadcast(128))
        L = {nm: Rot([fw.sb("c" + nm, [128, D]) for _ in range(2)]) for nm in ("y0", "y1", "v", "g", "h")}
        bons = Rot([fw.sb("cbon", [128, 16]) for _ in range(2)])
        st = Rot([fw.sb("cst", [128, 16]) for _ in range(6)])
        wk = Rot([fw.sb("cwk", [128, D]) for _ in range(4)])
        obf = Rot([fw.sb("cob", [128, D], BF16) for _ in range(2)])
        oTs = Rot([fw.sb("coT", [128, 8, 128], BF16) for _ in range(2)])
        v3 = lambda t_: t_[:].rearrange("p (h j) -> p h j", j=64)
        b3 = lambda t_: t_[:].unsqueeze(2).to_broadcast([128, 16, 64])
        for tile in range(2, NT):
            rows = slice(tile * 128, (tile + 1) * 128)
            y0 = L["y0"].next(); y1 = L["y1"].next(); v = L["v"].next(); g = L["g"].next(); ht = L["h"].next(); bon = bons.next()
            fw.dma("sp", y0, y0[:], self.Y[0], self.Y[0].ap()[rows, :])
            fw.dma("sp", y1, y1[:], self.Y[1], self.Y[1].ap()[rows, :])
            fw.dma("sp", v, v[:], S["V"], S["V"].ap()[rows, :])
            fw.dma("sp", g, g[:], S["G"], S["G"].ap()[rows, :])
            fw.dma("sp", bon, bon[:], S["BON"], S["BON"].ap()[rows, :])
            src, sap = self.src_of(tile, hsrc)
            fw.dma("sp", ht, ht[:], src, sap)
            ysum = wk.next()
            fw.op("dve", lambda e: e.tensor_tensor(out=ysum[:], in0=y0[:], in1=y1[:], op=ALU.add), [y0, y1], [ysum])
            mean = st.next(); var = st.next()
            fw.op("dve", lambda e: e.reduce_sum(out=mean[:], in_=v3(ysum), axis=AX.X), [ysum], [mean])
            fw.op("dve", lambda e: e.tensor_scalar(out=mean[:], in0=mean[:], scalar1=1.0 / 64, scalar2=None, op0=ALU.mult), [mean], [mean])
            yc = wk.next()
            fw.op("dve", lambda e: e.tensor_tensor(out=v3(yc), in0=v3(ysum), in1=b3(mean), op=ALU.subtract), [ysum, mean], [yc])
            sq = wk.next()
            fw.op("dve", lambda e: e.tensor_tensor(out=sq[:], in0=yc[:], in1=yc[:], op=ALU.mult), [yc], [sq])
            fw.op("dve", lambda e: e.reduce_sum(out=var[:], in_=v3(sq), axis=AX.X), [sq], [var])
            fw.op("act", lambda e: e.activation(out=var[:], in_=var[:], func=AF.Sqrt, bias=64e-5, scale=1.0 / 64), [var], [var])
            fw.op("dve", lambda e: e.reciprocal(out=var[:], in_=var[:]), [var], [var])
            fw.op("dve", lambda e: e.tensor_tensor(out=v3(yc), in0=v3(yc), in1=b3(var), op=ALU.mult), [yc, var], [yc])
            fw.op("dve", lambda e: e.tensor_tensor(out=yc[:], in0=yc[:], in1=gnw[:], op=ALU.mult), [yc, gnw], [yc])
            fw.op("dve", lambda e: e.tensor_tensor(out=yc[:], in0=yc[:], in1=gnb[:], op=ALU.add), [yc, gnb], [yc])
            bv = wk.next()
            fw.op("dve", lambda e: e.tensor_tensor(out=v3(bv), in0=v3(v), in1=b3(bon), op=ALU.mult), [v, bon], [bv])
            fw.op("dve", lambda e: e.tensor_tensor(out=yc[:], in0=yc[:], in1=bv[:], op=ALU.add), [yc, bv], [yc])
            ob = obf.next()
            fw.op("dve", lambda e: e.tensor_tensor(out=ob[:], in0=yc[:], in1=g[:], op=ALU.mult), [yc, g], [ob])
            pb = self.bank.next(); pv = self.bview(pb)
            for k in range(8):
                fw.op("pe", lambda e: e.transpose(out=pv[:, k * 128:(k + 1) * 128], in_=ob[:, k * 128:(k + 1) * 128], identity=self.identb[:]), [ob, self.identb], [pb])
            oT = oTs.next()
            fw.op("act", lambda e: e.activation(out=oT[:].rearrange("p k n -> p (k n)"), in_=pv[:, 0:1024], func=AF.Copy), [pb], [oT])
            tmp = wk.next()
            for half in range(2):
                po = self.bank.next()
                for k in range(8):
                    fw.op("pe", lambda e: e.matmul(po[:], lhsT=oT[:, k, :], rhs=w_o[:, k, half * 512:(half + 1) * 512], start=(k == 0), stop=(k == 7)), [oT, w_o], [po])
                fw.op("dve", lambda e: e.tensor_tensor(out=tmp[:, half * 512:(half + 1) * 512], in0=po[:], in1=gate[0][:, half * 512:(half + 1) * 512], op=ALU.mult), [po, gate[0]], [tmp])
            fw.op("pool", lambda e: e.tensor_tensor(out=tmp[:], in0=tmp[:], in1=ht[:], op=ALU.add), [tmp, ht], [tmp])
            fw.dma("act", hdst, hdst.ap()[rows, :], tmp, tmp[:])
        fw.pop()

    def finish(self):
        fw = self.fw
        fw.barrier()
        fw.finish("sp", [self.out])
        print("ninst", fw.ninst, "nsem", fw.nsem)
        self.es.close()
        return self.nc


def host_consts():
    c = {}
    c["ident"] = np.eye(128, dtype=np.float32)
    t = np.arange(NLAT)
    row = (t // 64).astype(np.float32); col = (t % 64).astype(np.float32)
    inv = (10000.0 ** (-np.arange(0, 32, 2, dtype=np.float32) / 32)).astype(np.float32)
    ang = np.concatenate([row[:, None] * inv, col[:, None] * inv], axis=-1)
    cosl = np.cos(ang).astype(np.float32); sinl = np.sin(ang).astype(np.float32)
    cos = np.concatenate([np.ones((NCTX, 32), np.float32), cosl], 0)
    sin = np.concatenate([np.zeros((NCTX, 32), np.float32), sinl], 0)
    d = np.arange(128) % 64
    c["ropeC"] = np.ascontiguousarray(cos[:, d // 2].T)
    c["ropeS"] = np.ascontiguousarray(sin[:, d // 2].T)
    P = np.zeros((128, 128), np.float32)
    for i in range(64):
        P[2 * i + 1, 2 * i] = -1.0
        P[2 * i, 2 * i + 1] = 1.0
    c["ropeP"] = P
    b = np.zeros((128, 128), np.float32)
    b[:64, :64] = 1; b[64:, 64:] = 1
    c["blk64"] = b
    s_ = np.arange(128)[:, None]; t_ = np.arange(128)[None, :]
    tm = np.zeros((128, 4, 128), np.float32)
    tm[:, 0] = (s_ < t_); tm[:, 1] = (s_ <= t_); tm[:, 2] = (s_ > t_); tm[:, 3] = (s_ >= t_)
    c["trimask"] = tm
    c["iota128"] = np.tile(np.arange(128, dtype=np.float32)[None, :], (128, 1))
    return c


def shard_inputs(inputs):
    f = lambda a: np.ascontiguousarray(np.asarray(a, dtype=np.float32))
    w_in = f(inputs["ev_w_in"])[0]
    w_in_r = np.concatenate([w_in[:, 0:2048], w_in[:, 2048:2112], w_in[:, 2048:2112], w_in[:, 2112:2176], w_in[:, 2112:2176], w_in[:, 2176:2304]], axis=1)
    shared = dict(
        ada_w=f(inputs["ada_w"]), ada_b=f(inputs["ada_b"]), norm1_g=f(inputs["norm1_g"]), norm2_g=f(inputs["norm2_g"]),
        w_in=f(w_in_r), conv_w=f(inputs["ev_conv_w"])[0], qg2=f(np.tile(f(inputs["ev_q_gain"])[0], 2)), kg2=f(np.tile(f(inputs["ev_k_gain"])[0], 2)),
        w_out=f(inputs["ev_w_out"])[0], od_mu=f(inputs["od_mu"])[0], od_w_r=f(inputs["od_w_r"])[0], od_w_k=f(inputs["od_w_k"])[0],
        od_w_v=f(inputs["od_w_v"])[0], od_w_o=f(inputs["od_w_o"])[0], od_g1=f(inputs["od_g1"])[0], od_g2=f(inputs["od_g2"])[0],
        od_k_k=f(inputs["od_k_k"])[0], od_k_a=f(inputs["od_k_a"])[0], od_r_k=f(inputs["od_r_k"])[0].reshape(-1),
        od_w0=f(inputs["od_w0"])[0], od_w1=f(inputs["od_w1"])[0], od_w2=f(inputs["od_w2"])[0], od_a0=f(inputs["od_a0"])[0],
        od_a1=f(inputs["od_a1"])[0], od_a2=f(inputs["od_a2"])[0], od_gn_w=f(inputs["od_gn_w"])[0], od_gn_b=f(inputs["od_gn_b"])[0],
        peer_wq=f(inputs["peer_wq"]), peer_keys=f(inputs["peer_keys"]).reshape(2, 16, 128, 128), peer_u=f(inputs["peer_u"]), peer_v=f(inputs["peer_v"]),
    )
    shared.update(host_consts())
    x = f(inputs["x"]); ctx = f(inputs["ctx"]); c = f(inputs["c"]); cc = f(inputs["c_ctx"])
    maps = []
    for b in range(8):
        m = dict(shared)
        m["x"] = x[b]; m["ctx"] = ctx[b]; m["cvec"] = np.ascontiguousarray(np.stack([c[b], cc], 0))
        maps.append(m)
    return maps


def build(dbg=False, stop=None):
    p = Prog(dbg=dbg, stop=stop)
    p.phase0()
    if stop == "p0":
        return p
    if stop and stop.startswith("L1"):
        p.l1_phaseA(p.h2)
        if stop == "L1a":
            return p
        p.l1_phaseB()
        if stop == "L1b":
            return p
        p.l1_phaseC(p.h2, p.h3)
        return p
    p.l0_phaseA()
    if stop == "l0a":
        p.fw.pop()
        return p
    p.l0_phaseB()
    if stop == "l0":
        return p
    if os.environ.get("PEER_V", "2") == "2":
        p.peer_prep2(0)
        p.peer2(0, p.h1, p.h2)
    else:
        p.peer_prep(0)
        p.peer(0, p.h1, p.h2)
    if stop == "peer0":
        return p
    p.l1_phaseA(p.h2)
    if stop == "l1a":
        return p
    p.l1_phaseB()
    if stop == "l1b":
        return p
    p.l1_phaseC(p.h2, p.h3)
    if stop == "l1c":
        return p
    if os.environ.get("PEER_V", "2") == "2":
        p.peer_prep2(1)
        p.peer2(1, p.h3, None, final=True)
    else:
        p.peer_prep(1)
        p.peer(1, p.h3, None, final=True)
    return p


def kernel(**inputs):
    p = build()
    nc = p.finish()
    maps = shard_inputs(inputs)
    maps = [{k: v for k, v in m.items() if k in p.I} for m in maps]
    res = run_bass_kernel_spmd(nc, maps, core_ids=list(range(8)))
    return np.stack([r["out"] for r in res.results], 0)
```

```python
import os
import numpy as np
from contextlib import ExitStack
import concourse.bass as bass
import concourse.mybir as mybir
from concourse.bass_utils import run_bass_kernel_spmd

F32 = mybir.dt.float32
BF16 = mybir.dt.bfloat16
U32 = mybir.dt.uint32
ALU = mybir.AluOpType
AF = mybir.ActivationFunctionType
AX = mybir.AxisListType

D = 1024
NCTX = 256
NLAT = 4096
TOK = NCTX + NLAT
NT = TOK // 128
TOKP = TOK + 4
EPS = 1e-6
INPUT_SHAPES = {
    "x": [NLAT, D],
    "ctx": [NCTX, D],
    "cvec": [2, D],
    "ada_w": [2, D, 6 * D],
    "ada_b": [2, 6 * D],
    "norm1_g": [2, D],
    "norm2_g": [2, D],
    "w_in": [D, 2432],
    "conv_w": [3, 512],
    "qg2": [128],
    "kg2": [128],
    "w_out": [D, D],
    "od_mu": [6, D],
    "od_w_r": [D, D],
    "od_w_k": [D, D],
    "od_w_v": [D, D],
    "od_w_o": [D, D],
    "od_g1": [D, 128],
    "od_g2": [128, D],
    "od_k_k": [D],
    "od_k_a": [D],
    "od_r_k": [D],
    "od_w0": [2, D],
    "od_w1": [2, D, 64],
    "od_w2": [2, 64, D],
    "od_a0": [2, D],
    "od_a1": [2, D, 64],
    "od_a2": [2, 64, D],
    "od_gn_w": [D],
    "od_gn_b": [D],
    "peer_wq": [2, D, 2048],
    "peer_keys": [2, 16, 128, 128],
    "peer_u": [2, 16384, D],
    "peer_v": [2, 16384, D],
    "ident": [128, 128],
    "ropeC": [128, TOK],
    "ropeS": [128, TOK],
    "ropeP": [128, 128],
    "blk64": [128, 128],
    "trimask": [128, 4, 128],
    "iota128": [128, 128],
}


class T:
    __slots__ = ("t", "w", "r", "sem", "dcount", "name")

    def __init__(self, t, name):
        self.t = t
        self.name = name
        self.w = {}
        self.r = {}
        self.sem = None
        self.dcount = 0

    def __getitem__(self, k):
        return self.t[k]

    def ap(self):
        return self.t.ap()


class FW:
    def __init__(self, nc, es):
        self.nc = nc
        self.es = es
        self.root_es = es
        self.stack = []
        self.live = []
        self.drams = []
        self.sempool = []
        self.uid = 0
        self.eng = {"pe": nc.tensor, "act": nc.scalar, "dve": nc.vector, "pool": nc.gpsimd, "sp": nc.sync}
        self.esem = {}
        self.ecount = {}
        self.known = {}
        self.nsem = 0
        for k in self.eng:
            self.esem[k] = self.newsem("e_" + k)
            self.ecount[k] = 0
            self.known[k] = {}
        self.selfwait = {"pe": False, "act": True, "dve": True, "pool": True, "sp": False}
        self.ninst = 0

    def newsem(self, name):
        self.nsem += 1
        assert self.nsem < 200, "too many semaphores"
        return self.root_es.enter_context(self.nc.semaphore(name))

    def push(self):
        self.stack.append((self.es, self.live))
        self.es = ExitStack()
        self.live = []

    def pop(self):
        self.barrier()
        for t in self.live:
            if t.sem is not None:
                self.sempool.append((t.sem, t.dcount))
                t.sem = None
        self.es.close()
        self.es, self.live = self.stack.pop()

    def barrier(self):
        toks = {}
        for k in self.eng:
            if self.ecount[k] > 0:
                toks[self.esem[k]] = self.ecount[k]
        for _, live in self.stack + [(None, self.live)]:
            for t in live:
                if t.sem is not None and t.dcount > 0:
                    toks[t.sem] = 16 * t.dcount
        for t in self.drams:
            if t.sem is not None and t.dcount > 0:
                toks[t.sem] = 16 * t.dcount
        for e in self.eng:
            kn = self.known[e]
            for s_, v in toks.items():
                if s_ is self.esem[e]:
                    continue
                if kn.get(s_, 0) < v:
                    self.eng[e].wait_ge(s_, v)
                    kn[s_] = v
                    self.ninst += 1

    def sb(self, name, shape, dt=F32):
        self.uid += 1
        t = T(self.es.enter_context(self.nc.sbuf_tensor("%s_%d" % (name, self.uid), list(shape), dt)), name)
        self.live.append(t)
        return t

    def ps(self, name, shape, dt=F32):
        t = T(self.es.enter_context(self.nc.psum_tensor(name, list(shape), dt)), name)
        self.live.append(t)
        return t

    def dram(self, name, shape, dt=F32, kind="Internal"):
        t = T(self.nc.dram_tensor(name, list(shape), dt, kind=kind), name)
        self.drams.append(t)
        return t

    def _waits(self, e, reads, writes):
        toks = {}

        def add(tok):
            if tok is None:
                return
            s, v = tok
            if toks.get(s, 0) < v:
                toks[s] = v
        for r in reads:
            for s, v in r.w.items():
                add((s, v))
        for w in writes:
            for s, v in w.w.items():
                add((s, v))
            for s, v in w.r.items():
                add((s, v))
        eng = self.eng[e]
        kn = self.known[e]
        for s, v in toks.items():
            if s is self.esem[e] and not self.selfwait[e]:
                continue
            if kn.get(s, 0) < v:
                eng.wait_ge(s, v)
                kn[s] = v
                self.ninst += 1

    def op(self, e, fn, reads=(), writes=()):
        self._waits(e, reads, writes)
        inst = fn(self.eng[e])
        self.ecount[e] += 1
        self.ninst += 1
        s = self.esem[e]
        inst.then_inc(s, 1)
        tok = (s, self.ecount[e])
        for w in writes:
            w.w = {s: tok[1]}
            w.r = {}
        for r in reads:
            if r.r.get(s, 0) < tok[1]:
                r.r[s] = tok[1]
        return inst

    def dma(self, e, outT, out_ap, inT, in_ap, **kw):
        self._waits(e, [inT], [outT])
        own = inT if (outT in self.drams and inT not in self.drams) else outT
        if own.sem is None:
            if self.sempool:
                own.sem, own.dcount = self.sempool.pop()
            else:
                own.sem = self.newsem("d%d" % self.nsem)
        inst = self.eng[e].dma_start(out=out_ap, in_=in_ap, **kw)
        inst.then_inc(own.sem, 16)
        self.ninst += 1
        own.dcount += 1
        tok = (own.sem, 16 * own.dcount)
        if own is outT:
            outT.w = {tok[0]: tok[1]}
        else:
            outT.w[tok[0]] = tok[1]
        outT.r = {}
        if inT.r.get(tok[0], 0) < tok[1]:
            inT.r[tok[0]] = tok[1]
        return inst

    def finish(self, e, tiles):
        self._waits(e, tiles, [])


class Rot:
    def __init__(self, items):
        self.items = items
        self.i = -1

    def next(self):
        self.i = (self.i + 1) % len(self.items)
        return self.items[self.i]


def tokcol(tile):
    return 1 + tile * 128 if tile < 2 else 259 + (tile - 2) * 128


class Prog:
    def __init__(self, dbg=False, stop=None):
        self.dbg = dbg
        self.stop = stop
        self.nc = bass.Bass("TRN2", target_bir_lowering=False)
        self.es = ExitStack()
        self.fw = FW(self.nc, self.es)
        fw = self.fw

        prog = self

        class Lazy(dict):
            def __missing__(d, name):
                t = fw.dram(name, INPUT_SHAPES[name], F32, kind="ExternalInput")
                d[name] = t
                return t
        self.I = Lazy()
        self.out = fw.dram("out", [NLAT, D], F32, kind="ExternalOutput")
        k = "ExternalOutput" if dbg else "Internal"
        self.modrow = fw.dram("modrow", [2, 2, 6 * D], F32, kind=k)
        self.h1 = fw.dram("h1", [TOK, D], F32, kind=k)
        self.h2 = fw.dram("h2", [TOK, D], F32, kind=("ExternalInput" if stop and stop.startswith("L1") else k))
        self.h3 = fw.dram("h3", [TOK, D], F32, kind=k)
        self.uT = fw.dram("uT", [512, TOKP], BF16)
        if dbg:
            self.dza = fw.dram("dza", [TOK, 512], BF16, kind="ExternalOutput")
            self.dzc = fw.dram("dzc", [512, TOK], BF16, kind="ExternalOutput")
        self.gbT = fw.dram("gbT", [512, TOKP], BF16)
        self.UT = fw.dram("UT", [128, 128, 8, 128], BF16)
        self.Vb = fw.dram("Vb", [128, 128, D], BF16)
        self.banks = [fw.ps("bank%d" % i, [128, 512]) for i in range(8)]
        self.bank = Rot(self.banks)
        self.identf = fw.sb("identf", [128, 128])
        self.identb = fw.sb("identb", [128, 128], BF16)
        fw.dma("sp", self.identf, self.identf[:], self.I["ident"], self.I["ident"].ap())
        fw.dma("pool", self.identb, self.identb[:], self.I["ident"], self.I["ident"].ap())

    def bview(self, bank):
        return bank.ap().bitcast(BF16)

    def load_colT(self, dst, dst_ap, src, src_ap1d):
        self.fw.dma("sp", dst, dst_ap, src, src_ap1d.rearrange("(k p) -> p k", p=128), allow_slow_non_contiguous=True)

    def modvecs(self, layer, which):
        fw = self.fw
        I = self.I
        gname = "norm1_g" if which == 0 else "norm2_g"
        g = fw.sb("g", [128, 8])
        self.load_colT(g, g[:], I[gname], I[gname].ap()[layer])
        GT, shT = [], []
        for s in range(2):
            sc = fw.sb("scl", [128, 8]); sh = fw.sb("shf", [128, 8]); G = fw.sb("G", [128, 8])
            base = which * 3 * D
            self.load_colT(sh, sh[:], self.modrow, self.modrow.ap()[layer, s, base:base + D])
            self.load_colT(sc, sc[:], self.modrow, self.modrow.ap()[layer, s, base + D:base + 2 * D])
            fw.op("dve", lambda e: e.scalar_tensor_tensor(out=G[:], in0=sc[:], scalar=1.0, in1=g[:], op0=ALU.add, op1=ALU.mult), [sc, g], [G])
            GT.append(G); shT.append(sh)
        return GT, shT

    def gate_bcast(self, layer, which):
        fw = self.fw
        res = []
        for s in range(2):
            gt = fw.sb("gate", [128, D])
            off = which * 3 * D + 2 * D
            fw.dma("sp", gt, gt[:], self.modrow, self.modrow.ap()[layer, s, off:off + D].partition_broadcast(128))
            res.append(gt)
        return res

    def make_normer(self, nh=2):
        fw = self.fw
        self.nb_h = Rot([fw.sb("nh", [128, D]) for _ in range(nh)])
        self.nb_sq = fw.sb("nsq", [128, D], BF16)
        self.nb_ss = Rot([fw.sb("nss", [128, 1]) for _ in range(2)])
        self.nb_rs = Rot([fw.sb("nrs", [128, 1]) for _ in range(2)])
        self.nb_hn = Rot([fw.sb("nhn", [128, D], BF16) for _ in range(2)])

    def norm_tile(self, src, src_ap, G, sh, xnT, xnT_cols):
        fw = self.fw
        h = self.nb_h.next(); ss = self.nb_ss.next(); rs = self.nb_rs.next(); hn = self.nb_hn.next(); sq = self.nb_sq
        fw.dma("sp", h, h[:], src, src_ap)
        fw.op("act", lambda e: e.activation(out=sq[:], in_=h[:], func=AF.Square, scale=1.0 / 32, accum_out=ss[:]), [h], [sq, ss])
        fw.op("act", lambda e: e.activation(out=rs[:], in_=ss[:], func=AF.Sqrt, bias=EPS, scale=1.0), [ss], [rs])
        fw.op("dve", lambda e: e.reciprocal(out=rs[:], in_=rs[:]), [rs], [rs])
        fw.op("act", lambda e: e.activation(out=hn[:], in_=h[:], func=AF.Copy, scale=rs[:]), [h, rs], [hn])
        pb = self.bank.next()
        pv = self.bview(pb)
        for k in range(8):
            fw.op("pe", lambda e: e.transpose(out=pv[:, k * 128:(k + 1) * 128], in_=hn[:, k * 128:(k + 1) * 128], identity=self.identb[:]), [hn, self.identb], [pb])
        for k in range(8):
            fw.op("act", lambda e: e.activation(out=xnT[:, k, xnT_cols], in_=pv[:, k * 128:(k + 1) * 128], func=AF.Identity,
                                                scale=G[:, k:k + 1], bias=sh[:, k:k + 1]), [pb, G, sh], [xnT])
        return h

    def src_of(self, tile, hsrc):
        if hsrc is None:
            if tile < 2:
                return self.I["ctx"], self.I["ctx"].ap()[tile * 128:(tile + 1) * 128, :]
            return self.I["x"], self.I["x"].ap()[(tile - 2) * 128:(tile - 1) * 128, :]
        return hsrc, hsrc.ap()[tile * 128:(tile + 1) * 128, :]

    def phase0(self):
        fw = self.fw
        I = self.I
        fw.push()
        sc = fw.sb("sc", [128, 2, 8])
        for s_ in range(2):
            self.load_colT(sc, sc[:, s_, :], I["cvec"], I["cvec"].ap()[s_])
        fw.op("act", lambda e: e.activation(out=sc[:], in_=sc[:], func=AF.Silu), [sc], [sc])
        wb = Rot([fw.sb("adaw", [128, 8, 512]) for _ in range(2)])
        for i in range(2):
            bias = fw.sb("adab", [2, 6 * D])
            modr = fw.sb("modr", [2, 6 * D])
            fw.dma("sp", bias, bias[:], I["ada_b"], I["ada_b"].ap()[i].partition_broadcast(2))
            for cb in range(12):
                wt = wb.next()
                fw.dma("sp" if cb % 2 == 0 else "act", wt, wt[:], I["ada_w"], I["ada_w"].ap()[i, :, cb * 512:(cb + 1) * 512].rearrange("(k p) n -> p k n", p=128))
                pb = self.bank.next()
                for k in range(8):
                    fw.op("pe", lambda e: e.matmul(pb[0:2, :], lhsT=sc[:, :, k], rhs=wt[:, k, :], start=(k == 0), stop=(k == 7)), [sc, wt], [pb])
                fw.op("dve", lambda e: e.tensor_tensor(out=modr[:, cb * 512:(cb + 1) * 512], in0=pb[0:2, :], in1=bias[:, cb * 512:(cb + 1) * 512], op=ALU.add), [pb, bias], [modr])
            fw.dma("sp", self.modrow, self.modrow.ap()[i], modr, modr[:])
        fw.pop()

    def l0_phaseA(self):
        fw = self.fw
        I = self.I
        fw.push()
        self.qT = fw.sb("qT", [128, 4, TOK], BF16)
        self.kT2 = fw.sb("kT2", [128, 2, TOK], BF16)
        self.vext = fw.sb("vext", [128, NT, 2, 66], BF16)
        fw.op("pool", lambda e: e.memset(self.vext[:], 1.0), [], [self.vext])
        fw.push()
        GT, shT = self.modvecs(0, 0)
        self.make_normer()
        w_in = fw.sb("w_in", [128, 8, 2432], BF16)
        for k in range(8):
            fw.dma("pool", w_in, w_in[:, k, :], I["w_in"], I["w_in"].ap()[k * 128:(k + 1) * 128, :])
        rC = fw.sb("rC", [128, TOK], BF16); rS = fw.sb("rS", [128, TOK], BF16)
        fw.dma("pool", rC, rC[:], I["ropeC"], I["ropeC"].ap())
        fw.dma("pool", rS, rS[:], I["ropeS"], I["ropeS"].ap())
        ropeP = fw.sb("ropeP", [128, 128], BF16); blk64 = fw.sb("blk64", [128, 128], BF16)
        fw.dma("pool", ropeP, ropeP[:], I["ropeP"], I["ropeP"].ap())
        fw.dma("pool", blk64, blk64[:], I["blk64"], I["blk64"].ap())
        gains = fw.sb("gains", [128, 2])
        fw.dma("sp", gains, gains[:, 0:1], I["qg2"], I["qg2"].ap().rearrange("(p o) -> p o", o=1))
        fw.dma("sp", gains, gains[:, 1:2], I["kg2"], I["kg2"].ap().rearrange("(p o) -> p o", o=1))
        zt = fw.sb("zt", [128, 4, 2], BF16)
        fw.op("pool", lambda e: e.memset(zt[:], 0.0), [], [zt])
        for dst in (self.uT,):
            v = dst.ap().rearrange("(b p) c -> p b c", p=128)
            fw.dma("pool", dst, v[:, :, 0:1], zt, zt[:, :, 0:1], allow_slow_non_contiguous=True)
            fw.dma("pool", dst, v[:, :, 257:259], zt, zt[:, :, 0:2], allow_slow_non_contiguous=True)
            fw.dma("pool", dst, v[:, :, TOKP - 1:TOKP], zt, zt[:, :, 0:1], allow_slow_non_contiguous=True)
        xnTs = Rot([fw.sb("xnT", [128, 8, 256], BF16) for _ in range(2)])
        hTt = fw.sb("hTt", [128, 4, 256])
        gbst = Rot([fw.sb("gbst", [128, 4, 256], BF16) for _ in range(2)])
        ust = Rot([fw.sb("ust", [128, 4, 256], BF16) for _ in range(2)])
        qraw = Rot([fw.sb("qraw", [128, 256]) for _ in range(2)])
        sqb = Rot([fw.sb("sqb", [128, 256], BF16) for _ in range(2)])
        rsb = Rot([fw.sb("rsb", [128, 256]) for _ in range(2)])
        qnb = Rot([fw.sb("qnb", [128, 256], BF16) for _ in range(2)])
        t1b = Rot([fw.sb("t1b", [128, 256]) for _ in range(2)])
        t2b = Rot([fw.sb("t2b", [128, 256]) for _ in range(2)])
        uTv = self.uT.ap().rearrange("(b p) c -> p b c", p=128)
        gbTv = self.gbT.ap().rearrange("(b p) c -> p b c", p=128)
        import os
        skip = os.environ.get('L0A_SKIP', '').split(',')
        for g in range(int(os.environ.get('L0A_NG', NT // 2))):
            s = 1 if g == 0 else 0
            xnT = xnTs.next()
            for j in range(2):
                tile = 2 * g + j
                src, sap = self.src_of(tile, None)
                self.norm_tile(src, sap, GT[s], shT[s], xnT, slice(j * 128, (j + 1) * 128))
            tok0 = g * 256
            gb_s = gbst.next(); u_s = ust.next()
            for fb in range(18):
                if fb >= 12 and 'qk' in skip:
                    continue
                if fb < 12 and 'conv' in skip:
                    continue
                pb = self.bank.next()
                for k in range(8):
                    fw.op("pe", lambda e: e.matmul(pb[:, 0:256], lhsT=w_in[:, k, fb * 128:(fb + 1) * 128], rhs=xnT[:, k, :], start=(k == 0), stop=(k == 7)), [w_in, xnT], [pb])
                if fb < 4:
                    fw.op("act", lambda e: e.activation(out=hTt[:, fb, :], in_=pb[:, 0:256], func=AF.Copy), [pb], [hTt])
                elif fb < 8:
                    fw.op("act", lambda e: e.activation(out=gb_s[:, fb - 4, :], in_=pb[:, 0:256], func=AF.Copy), [pb], [gb_s])
                elif fb < 12:
                    fw.op("dve", lambda e: e.tensor_tensor(out=u_s[:, fb - 8, :], in0=pb[:, 0:256], in1=hTt[:, fb - 8, :], op=ALU.mult), [pb, hTt], [u_s])
                else:
                    isq = fb < 16
                    qr = qraw.next(); sq = sqb.next(); rs = rsb.next(); qn = qnb.next(); t1 = t1b.next(); t2 = t2b.next()
                    gcol = gains[:, 0:1] if isq else gains[:, 1:2]
                    fw.op("act", lambda e: e.activation(out=qr[:], in_=pb[:, 0:256], func=AF.Copy), [pb], [qr])
                    fw.op("act", lambda e: e.activation(out=sq[:], in_=pb[:, 0:256], func=AF.Square), [pb], [sq])
                    p2 = self.bank.next()
                    fw.op("pe", lambda e: e.matmul(p2[:, 0:256], lhsT=blk64[:], rhs=sq[:], start=True, stop=True), [blk64, sq], [p2])
                    fw.op("act", lambda e: e.activation(out=rs[:], in_=p2[:, 0:256], func=AF.Sqrt, bias=EPS, scale=1.0 / 64), [p2], [rs])
                    fw.op("dve", lambda e: e.reciprocal(out=rs[:], in_=rs[:]), [rs], [rs])
                    fw.op("dve", lambda e: e.scalar_tensor_tensor(out=qn[:], in0=qr[:], scalar=gcol, in1=rs[:], op0=ALU.mult, op1=ALU.mult), [qr, gains, rs], [qn])
                    p3 = self.bank.next()
                    fw.op("pe", lambda e: e.matmul(p3[:, 0:256], lhsT=ropeP[:], rhs=qn[:], start=True, stop=True), [ropeP, qn], [p3])
                    fw.op("dve", lambda e: e.tensor_tensor(out=t2[:], in0=p3[:, 0:256], in1=rS[:, tok0:tok0 + 256], op=ALU.mult), [p3, rS], [t2])
                    fw.op("pool", lambda e: e.tensor_tensor(out=t1[:], in0=qn[:], in1=rC[:, tok0:tok0 + 256], op=ALU.mult), [qn, rC], [t1])
                    dst = self.qT if isq else self.kT2
                    bi = fb - 12 if isq else fb - 16
                    fw.op("dve", lambda e: e.tensor_tensor(out=dst[:, bi, tok0:tok0 + 256], in0=t1[:], in1=t2[:], op=ALU.add), [t1, t2], [dst])
            for j in range(2 if 'v' not in skip else 0):
                tile = 2 * g + j
                pb = self.bank.next()
                for k in range(8):
                    fw.op("pe", lambda e: e.matmul(pb[:, 0:128], lhsT=xnT[:, k, j * 128:(j + 1) * 128], rhs=w_in[:, k, 2304:2432], start=(k == 0), stop=(k == 7)), [xnT, w_in], [pb])
                fw.op("act", lambda e: e.activation(out=self.vext[:, tile, :, 0:64], in_=pb[:, 0:128].rearrange("p (a d) -> p a d", a=2), func=AF.Copy), [pb], [self.vext])
            for j in range(2 if 'store' not in skip else 0):
                c0 = tokcol(2 * g + j)
                fw.dma("sp", self.uT, uTv[:, :, c0:c0 + 128], u_s, u_s[:, :, j * 128:(j + 1) * 128])
                fw.dma("sp", self.gbT, gbTv[:, :, c0:c0 + 128], gb_s, gb_s[:, :, j * 128:(j + 1) * 128])
        fw.pop()

    def l0_phaseB(self):
        fw = self.fw
        I = self.I
        fw.push()
        gate = self.gate_bcast(0, 0)
        w_out = fw.sb("w_out", [128, 8, D], BF16)
        for k in range(8):
            fw.dma("pool", w_out, w_out[:, k, :], I["w_out"], I["w_out"].ap()[k * 128:(k + 1) * 128, :])
        cw = fw.sb("cw", [128, 3, 4])
        for j_ in range(3):
            self.load_colT(cw, cw[:, j_, :], I["conv_w"], I["conv_w"].ap()[j_])
        PTs = Rot([fw.sb("PT", [128, NT, 512], BF16) for _ in range(2)])
        zatt = Rot([fw.sb("zatt", [128, 512], BF16) for _ in range(2)])
        rcb = Rot([fw.sb("rcb", [128, 4]) for _ in range(2)])
        zaTs = Rot([fw.sb("zaT", [128, 4, 128], BF16) for _ in range(2)])
        zcTs = Rot([fw.sb("zcT", [128, 4, 128], BF16) for _ in range(2)])
        uhs = Rot([fw.sb("uh", [128, 4, 130], BF16) for _ in range(2)])
        gbts = Rot([fw.sb("gbt", [128, 4, 128], BF16) for _ in range(2)])
        accs = Rot([fw.sb("acc", [128, 128]) for _ in range(3)])
        hts = Rot([fw.sb("ht", [128, D]) for _ in range(2)])
        tmps = Rot([fw.sb("tmp", [128, D]) for _ in range(2)])
        uTv = self.uT.ap().rearrange("(b p) c -> p b c", p=128)
        gbTv = self.gbT.ap().rearrange("(b p) c -> p b c", p=128)
        obanks = Rot(self.banks[0:2]); sbanks = Rot(self.banks[2:6]); self.bank = Rot(self.banks[6:8])
        import os
        skip = os.environ.get('L0B_SKIP', '').split(',')
        for qt in range(int(os.environ.get('L0B_NT', NT))):
            s = 1 if qt < 2 else 0
            nch = 2 if qt < 2 else NT
            ht = hts.next()
            src, sap = self.src_of(qt, None)
            fw.dma("sp", ht, ht[:], src, sap)
            uh = uhs.next(); gbt = gbts.next()
            c0 = tokcol(qt)
            fw.dma("sp", uh, uh[:], self.uT, uTv[:, :, c0 - 1:c0 + 129])
            fw.dma("sp", gbt, gbt[:], self.gbT, gbTv[:, :, c0:c0 + 128])
            za = zatt.next()
            for kv in range(2 if 'attn' not in skip else 0):
                PTall = PTs.next()
                for sc in range(nch):
                    sbs = [sbanks.next(), sbanks.next()]
                    for b in range(2):
                        for hf in range(2):
                            fw.op("pe", lambda e: e.matmul(sbs[hf][:, b * 128:(b + 1) * 128], lhsT=self.kT2[hf * 64:(hf + 1) * 64, kv, sc * 128:(sc + 1) * 128],
                                                            rhs=self.qT[hf * 64:(hf + 1) * 64, 2 * kv + b, qt * 128:(qt + 1) * 128], start=True, stop=True), [self.kT2, self.qT], [sbs[hf]])
                    PT4 = PTall[:, sc, :].rearrange("p (b h n) -> p b h n", b=2, h=2)
                    for hf in range(2):
                        fw.op("act", lambda e: e.activation(out=PT4[:, :, hf, :], in_=sbs[hf][:, 0:256].rearrange("p (b n) -> p b n", b=2), func=AF.Exp, scale=0.125), [sbs[hf]], [PTall])
                ob = obanks.next()
                O = ob[:, 0:260].rearrange("p (c e) -> p c e", e=65)
                for c in range(4):
                    for sc in range(nch):
                        fw.op("pe", lambda e: e.matmul(O[:, c, :], lhsT=PTall[:, sc, c * 128:(c + 1) * 128], rhs=self.vext[:, sc, kv, 0:65], start=(sc == 0), stop=(sc == nch - 1)), [PTall, self.vext], [ob])
                rc = rcb.next()
                fw.op("dve", lambda e: e.reciprocal(out=rc[:], in_=O[:, :, 64]), [ob], [rc])
                fw.op("dve", lambda e: e.tensor_tensor(out=za[:, kv * 256:(kv + 1) * 256].rearrange("p (c d) -> p c d", c=4), in0=O[:, :, 0:64],
                                                       in1=rc[:].unsqueeze(2).to_broadcast([128, 4, 64]), op=ALU.mult), [ob, rc], [za])
            pb = self.bank.next(); pv = self.bview(pb)
            for k in range(4):
                fw.op("pe", lambda e: e.transpose(out=pv[:, k * 128:(k + 1) * 128], in_=za[:, k * 128:(k + 1) * 128], identity=self.identb[:]), [za, self.identb], [pb])
            zaT = zaTs.next()
            fw.op("act", lambda e: e.activation(out=zaT[:].rearrange("p k n -> p (k n)"), in_=pv[:, 0:512], func=AF.Copy), [pb], [zaT])
            zcT = zcTs.next()
            for b in range(4 if 'conv' not in skip else 0):
                acc = accs.next()
                fw.op("pool", lambda e: e.tensor_scalar(out=acc[:], in0=uh[:, b, 0:128], scalar1=cw[:, 0, b:b + 1], scalar2=None, op0=ALU.mult), [uh, cw], [acc])
                for j_ in (1, 2):
                    t_ = accs.next()
                    fw.op("pool", lambda e: e.tensor_scalar(out=t_[:], in0=uh[:, b, j_:j_ + 128], scalar1=cw[:, j_, b:b + 1], scalar2=None, op0=ALU.mult), [uh, cw], [t_])
                    fw.op("pool", lambda e: e.tensor_tensor(out=acc[:], in0=acc[:], in1=t_[:], op=ALU.add), [acc, t_], [acc])
                fw.op("pool", lambda e: e.tensor_tensor(out=zcT[:, b, :], in0=acc[:], in1=gbt[:, b, :], op=ALU.mult), [acc, gbt], [zcT])
            if self.dbg:
                fw.dma("sp", self.dza, self.dza.ap()[qt * 128:(qt + 1) * 128, :], za, za[:])
                fw.dma("sp", self.dzc, self.dzc.ap().rearrange("(b p) c -> p b c", p=128)[:, :, qt * 128:(qt + 1) * 128], zcT, zcT[:])
            tmp = tmps.next()
            for half in range(2):
                pb = self.bank.next()
                for k in range(8):
                    lt = zcT[:, k, :] if k < 4 else zaT[:, k - 4, :]
                    fw.op("pe", lambda e: e.matmul(pb[:], lhsT=lt, rhs=w_out[:, k, half * 512:(half + 1) * 512], start=(k == 0), stop=(k == 7)), [zcT, zaT, w_out], [pb])
                fw.op("dve", lambda e: e.tensor_tensor(out=tmp[:, half * 512:(half + 1) * 512], in0=pb[:], in1=gate[s][:, half * 512:(half + 1) * 512], op=ALU.mult), [pb, gate[s]], [tmp])
            fw.op("pool", lambda e: e.tensor_tensor(out=tmp[:], in0=tmp[:], in1=ht[:], op=ALU.add), [tmp, ht], [tmp])
            fw.dma("act", self.h1, self.h1.ap()[qt * 128:(qt + 1) * 128, :], tmp, tmp[:])
        self.bank = Rot(self.banks)
        fw.pop()
        fw.pop()

    def peer_prep(self, layer):
        fw = self.fw
        I = self.I
        fw.push()
        banks = Rot(self.banks)
        Uv = I["peer_u"].ap()[layer].rearrange("(i j) d -> j i d", j=128)
        Vv = I["peer_v"].ap()[layer].rearrange("(i j) d -> j i d", j=128)
        UTv = self.UT.ap().rearrange("i p k j -> p i (k j)")
        Vbv = self.Vb.ap().rearrange("i j d -> j i d")
        ubs = Rot([fw.sb("ub", [128, 4, D], BF16) for _ in range(2)])
        vbs = Rot([fw.sb("vb", [128, 4, D], BF16) for _ in range(2)])
        uts = Rot([fw.sb("ut", [128, 4, 1024], BF16) for _ in range(2)])
        for c in range(32):
            ub = ubs.next(); vb = vbs.next(); ut = uts.next()
            fw.dma("pool", ub, ub[:], I["peer_u"], Uv[:, c * 4:(c + 1) * 4, :])
            fw.dma("pool", vb, vb[:], I["peer_v"], Vv[:, c * 4:(c + 1) * 4, :])
            fw.dma("sp", self.Vb, Vbv[:, c * 4:(c + 1) * 4, :], vb, vb[:])
            for ii in range(4):
                pb = banks.next(); pv = self.bview(pb)
                for k in range(8):
                    fw.op("pe", lambda e: e.transpose(out=pv[:, k * 128:(k + 1) * 128], in_=ub[:, ii, k * 128:(k + 1) * 128], identity=self.identb[:]), [ub, self.identb], [pb])
                if ii % 2 == 0:
                    fw.op("act", lambda e: e.activation(out=ut[:, ii, :], in_=pv[:, 0:1024], func=AF.Copy), [pb], [ut])
                else:
                    fw.op("dve", lambda e: e.tensor_copy(out=ut[:, ii, :], in_=pv[:, 0:1024]), [pb], [ut])
            fw.dma("sp", self.UT, UTv[:, c * 4:(c + 1) * 4, :], ut, ut[:])
        fw.pop()

    def peer(self, layer, hsrc, hdst, final=False):
        fw = self.fw
        I = self.I
        fw.push()
        GT, shT = self.modvecs(layer, 1)
        gate = self.gate_bcast(layer, 1)
        self.make_normer()
        obanks = self.banks[0:4]
        self.bank = Rot(self.banks[4:8])
        w_q = fw.sb("w_q", [128, 8, 2048], BF16)
        for k in range(8):
            fw.dma("pool", w_q, w_q[:, k, :], I["peer_wq"], I["peer_wq"].ap()[layer, k * 128:(k + 1) * 128, :])
        keyn = fw.sb("keyn", [128, 16, 128], BF16)
        fw.dma("pool", keyn, keyn[:], I["peer_keys"], I["peer_keys"].ap()[layer].rearrange("h k d -> k h d"))
        keysT = fw.sb("keysT", [128, 16, 128], BF16)
        for c in range(2):
            pb = self.bank.next(); pv = self.bview(pb)
            for q in range(8):
                fw.op("pe", lambda e: e.transpose(out=pv[:, q * 128:(q + 1) * 128], in_=keyn[:, c * 8 + q, :], identity=self.identb[:]), [keyn, self.identb], [pb])
            fw.op("act", lambda e: e.activation(out=keysT[:, c * 8:(c + 1) * 8, :].rearrange("p a b -> p (a b)"), in_=pv[:, 0:1024], func=AF.Copy), [pb], [keysT])
        xn2T = fw.sb("xn2T", [128, 8, 256], BF16)
        qT = fw.sb("pqT", [128, 16, 256], BF16)
        s_all = fw.sb("s_all", [128, 2, 16, 128])
        tau = fw.sb("tau", [128, 2, 8]); bias = fw.sb("pbias", [128, 2, 8])
        SB = [(fw.sb("m0", [128, 16]), fw.sb("m1", [128, 16]), fw.sb("scr", [128, 128]), fw.sb("cand", [128, 16, 16]), fw.sb("c24", [128, 24]),
               fw.sb("cscr", [128, 256]), fw.sb("st1", [128, 4]), fw.sb("e16", [128, 16])) for _ in range(4)]
        IB = 16
        Ws = Rot([fw.sb("W", [128, 2, IB, 128], BF16) for _ in range(2)])
        sums = Rot([fw.sb("sum", [128, IB, 128]) for _ in range(3)])
        Ps = Rot([fw.sb("P", [128, IB, 128], BF16) for _ in range(3)])
        Whs = Rot([fw.sb("Wh", [128, IB, 128], BF16) for _ in range(2)])
        Mks = Rot([fw.sb("Mk", [128, IB, 128], BF16) for _ in range(2)])
        UTs = Rot([fw.sb("UTs", [128, 2, 1024], BF16) for _ in range(2)])
        Vs = Rot([fw.sb("Vs", [128, 2, D], BF16) for _ in range(2)])
        Gs = Rot([fw.sb("Gs", [128, 256], BF16) for _ in range(4)])
        WAs = Rot([fw.sb("WAs", [128, 256], BF16) for _ in range(4)])
        tmps = Rot([fw.sb("ptmp", [128, D]) for _ in range(1)])
        hts = Rot([fw.sb("pht", [128, D]) for _ in range(2)])
        UTv = self.UT.ap().rearrange("i p k j -> p i (k j)")
        Vbv = self.Vb.ap().rearrange("i j d -> j i d")
        import os
        ng = int(os.environ.get("PEER_NG", NT // 2))
        for g in range(ng):
            if final and g == 0:
                continue
            s = 1 if g == 0 else 0
            for j in range(2):
                src, sap = self.src_of(2 * g + j, hsrc)
                self.norm_tile(src, sap, GT[s], shT[s], xn2T, slice(j * 128, (j + 1) * 128))
            for hp in range(16):
                pb = self.bank.next()
                for k in range(8):
                    fw.op("pe", lambda e: e.matmul(pb[:, 0:256], lhsT=w_q[:, k, hp * 128:(hp + 1) * 128], rhs=xn2T[:, k, :], start=(k == 0), stop=(k == 7)), [w_q, xn2T], [pb])
                fw.op("act", lambda e: e.activation(out=qT[:, hp, :], in_=pb[:, 0:256], func=AF.Copy), [pb], [qT])
            for t in range(2):
                for c in range(4):
                    pb = self.bank.next()
                    for q in range(4):
                        hp = c * 4 + q
                        fw.op("pe", lambda e: e.matmul(pb[:, q * 128:(q + 1) * 128], lhsT=qT[:, hp, t * 128:(t + 1) * 128], rhs=keysT[:, hp, :], start=True, stop=True), [qT, keysT], [pb])
                    fw.op("act", lambda e: e.activation(out=s_all[:, t, c * 4:(c + 1) * 4, :].rearrange("p a b -> p (a b)"), in_=pb[:], func=AF.Copy), [pb], [s_all])
            def stats_gen(t, h, B):
                m0, m1, scr, cand, c24, cscr, st1, e16 = B
                for (mm, src) in ((m0, s_all[:, t, 2 * h, :]), (m1, s_all[:, t, 2 * h + 1, :])):
                    fw.op("dve", lambda e: e.max(out=mm[:, 0:8], in_=src), [s_all], [mm]); yield
                    fw.op("dve", lambda e: e.match_replace(out=scr[:], in_to_replace=mm[:, 0:8], in_values=src, imm_value=-1e30), [s_all, mm], [scr]); yield
                    fw.op("dve", lambda e: e.max(out=mm[:, 8:16], in_=scr[:]), [scr], [mm]); yield
                fw.op("dve", lambda e: e.tensor_tensor(out=cand[:], in0=m0[:].unsqueeze(2).to_broadcast([128, 16, 16]), in1=m1[:].unsqueeze(1).to_broadcast([128, 16, 16]), op=ALU.add), [m0, m1], [cand]); yield
                cf = cand[:].rearrange("p a b -> p (a b)")
                fw.op("dve", lambda e: e.max(out=c24[:, 0:8], in_=cf), [cand], [c24]); yield
                fw.op("dve", lambda e: e.match_replace(out=cscr[:], in_to_replace=c24[:, 0:8], in_values=cf, imm_value=-1e30), [cand, c24], [cscr]); yield
                fw.op("dve", lambda e: e.max(out=c24[:, 8:16], in_=cscr[:]), [cscr], [c24]); yield
                fw.op("dve", lambda e: e.match_replace(out=cscr[:], in_to_replace=c24[:, 8:16], in_values=cscr[:], imm_value=-1e30), [cscr, c24], [cscr]); yield
                fw.op("dve", lambda e: e.max(out=c24[:, 16:24], in_=cscr[:]), [cscr], [c24]); yield
                fw.op("dve", lambda e: e.tensor_scalar(out=st1[:, 0:1], in0=c24[:, 16:17], scalar1=0.5, scalar2=None, op0=ALU.mult), [c24], [st1]); yield
                fw.op("dve", lambda e: e.scalar_tensor_tensor(out=tau[:, t, h:h + 1], in0=c24[:, 15:16], scalar=0.5, in1=st1[:, 0:1], op0=ALU.mult, op1=ALU.add), [c24, st1], [tau]); yield
                fw.op("dve", lambda e: e.tensor_scalar(out=st1[:, 1:2], in0=c24[:, 0:1], scalar1=-1.0, scalar2=None, op0=ALU.mult), [c24], [st1]); yield
                fw.op("act", lambda e: e.activation(out=e16[:], in_=c24[:, 0:16], func=AF.Exp, bias=st1[:, 1:2], scale=1.0, accum_out=st1[:, 2:3]), [c24, st1], [e16, st1]); yield
                fw.op("act", lambda e: e.activation(out=st1[:, 3:4], in_=st1[:, 2:3], func=AF.Ln), [st1], [st1]); yield
                fw.op("dve", lambda e: e.tensor_tensor(out=bias[:, t, h:h + 1], in0=st1[:, 1:2], in1=st1[:, 3:4], op=ALU.subtract), [st1], [bias]); yield

            for h0 in range(0, 8, 2):
                gens = [stats_gen(t, h0 + dh, SB[t * 2 + dh]) for t in range(2) for dh in range(2)]
                alive = True
                while alive:
                    alive = False
                    for gi in gens:
                        try:
                            next(gi); alive = True
                        except StopIteration:
                            pass
            NB = 128 // IB
            Wl = [None] * NB

            def build_unit(ib, u):
                i0 = ib * IB
                if u == 0:
                    Wl[ib] = Ws.next()
                W = Wl[ib]
                t, h = u // 8, u % 8
                sm = sums.next(); P = Ps.next()
                fw.op("dve", lambda e: e.tensor_tensor(out=sm[:], in0=s_all[:, t, 2 * h, i0:i0 + IB].unsqueeze(2).to_broadcast([128, IB, 128]),
                                                       in1=s_all[:, t, 2 * h + 1, :].unsqueeze(1).to_broadcast([128, IB, 128]), op=ALU.add), [s_all], [sm])
                def emitB():
                    fw.op("act", lambda e: e.activation(out=P[:], in_=sm[:], func=AF.Exp, bias=bias[:, t, h:h + 1], scale=1.0), [sm, bias], [P])

                def fin():
                    mk = Mks.next()
                    fw.op("dve", lambda e: e.tensor_scalar(out=mk[:], in0=sm[:], scalar1=tau[:, t, h:h + 1], scalar2=None, op0=ALU.is_ge), [sm, tau], [mk])
                    if h == 0:
                        fw.op("dve", lambda e: e.tensor_tensor(out=W[:, t, :, :], in0=mk[:], in1=P[:], op=ALU.mult), [mk, P], [W])
                    else:
                        Wh = Whs.next()
                        fw.op("dve", lambda e: e.tensor_tensor(out=Wh[:], in0=mk[:], in1=P[:], op=ALU.mult), [mk, P], [Wh])
                        eng = "pool" if h % 2 == 1 else "dve"
                        fw.op(eng, lambda e: e.tensor_tensor(out=W[:, t, :, :], in0=W[:, t, :, :], in1=Wh[:], op=ALU.add), [W, Wh], [W])
                return emitB, fin

            cur_ld = {}
            carry = {}

            def front(i):
                ib, iw = i // IB, i % IB
                W = Wl[ib]
                if i % 2 == 0:
                    UTt = UTs.next(); Vt = Vs.next()
                    fw.dma("sp", UTt, UTt[:], self.UT, UTv[:, i:i + 2, :])
                    fw.dma("sp", Vt, Vt[:], self.Vb, Vbv[:, i:i + 2, :])
                    cur_ld["u"] = UTt; cur_ld["v"] = Vt
                UTt = cur_ld["u"]; Vt = cur_ld["v"]
                i2 = i % 2
                pa = self.bank.next()
                for k in range(8):
                    fw.op("pe", lambda e: e.matmul(pa[:, 0:256], lhsT=UTt[:, i2, k * 128:(k + 1) * 128], rhs=xn2T[:, k, :], start=(k == 0), stop=(k == 7)), [UTt, xn2T], [pa])
                G = Gs.next()
                fw.op("act", lambda e: e.activation(out=G[:], in_=pa[:, 0:256], func=AF.Gelu), [pa], [G])
                pw = self.bank.next(); pwv = self.bview(pw)
                for t in range(2):
                    fw.op("pe", lambda e: e.transpose(out=pwv[:, t * 128:(t + 1) * 128], in_=W[:, t, iw, :], identity=self.identb[:]), [W, self.identb], [pw])
                WA = WAs.next()
                fw.op("dve", lambda e: e.tensor_tensor(out=WA[:], in0=G[:], in1=pwv[:, 0:256], op=ALU.mult), [G, pw], [WA])

                def back():
                    for t in range(2):
                        for half in range(2):
                            ob = obanks[t * 2 + half]
                            fw.op("pe", lambda e: e.matmul(ob[:], lhsT=WA[:, t * 128:(t + 1) * 128], rhs=Vt[:, i2, half * 512:(half + 1) * 512], start=(i == 0), stop=(i == 127)), [WA, Vt], [ob])
                return back

            pend = None
            for u in range(16):
                eb, f = build_unit(0, u)
                eb()
                if pend is not None:
                    pend()
                pend = f
            pend()
            backs = {0: front(0), 1: front(1)}
            sched = {0: [0, 1], 1: [2, 3]}
            for j_ in range(2, 14):
                sched[j_] = [j_ + 2]
            for i in range(128):
                ib, j = i // IB, i % IB
                units = sched.get(j, []) if ib + 1 < NB else []
                ebs = []; fins = []
                for u in units:
                    eb, f = build_unit(ib + 1, u)
                    ebs.append(eb); fins.append(f)
                if i + 2 < 128:
                    backs[i + 2] = front(i + 2)
                for eb in ebs:
                    eb()
                cp = carry.pop("f", None)
                if cp is not None:
                    cp()
                for f in fins[:-1]:
                    f()
                if fins:
                    if j == 13:
                        fins[-1]()
                    else:
                        carry["f"] = fins[-1]
                backs.pop(i)()
            for t in range(2):
                tile = 2 * g + t
                tmp = tmps.next(); ht = hts.next()
                src, sap = self.src_of(tile, hsrc)
                fw.dma("sp", ht, ht[:], src, sap)
                for half in range(2):
                    ob = obanks[t * 2 + half]
                    fw.op("dve", lambda e: e.tensor_tensor(out=tmp[:, half * 512:(half + 1) * 512], in0=ob[:], in1=gate[s][:, half * 512:(half + 1) * 512], op=ALU.mult), [ob, gate[s]], [tmp])
                fw.op("pool", lambda e: e.tensor_tensor(out=tmp[:], in0=tmp[:], in1=ht[:], op=ALU.add), [tmp, ht], [tmp])
                if final:
                    fw.dma("act", self.out, self.out.ap()[(tile - 2) * 128:(tile - 1) * 128, :], tmp, tmp[:])
                else:
                    fw.dma("act", hdst, hdst.ap()[tile * 128:(tile + 1) * 128, :], tmp, tmp[:])
        self.bank = Rot(self.banks)
        fw.pop()

    def peer_prep2(self, layer):
        fw = self.fw
        I = self.I
        fw.push()
        banks = Rot(self.banks)
        Uv = I["peer_u"].ap()[layer].rearrange("(i j) d -> i j d", j=128)
        Vv = I["peer_v"].ap()[layer].rearrange("(i j) d -> i j d", j=128)
        UTv = self.UT.ap().rearrange("j p k i -> p j (k i)")
        Vbv = self.Vb.ap().rearrange("j i d -> i j d")
        ubs = Rot([fw.sb("ub", [128, 4, D], BF16) for _ in range(2)])
        vbs = Rot([fw.sb("vb", [128, 4, D], BF16) for _ in range(2)])
        uts = Rot([fw.sb("ut", [128, 4, 1024], BF16) for _ in range(2)])
        for c in range(32):
            ub = ubs.next(); vb = vbs.next(); ut = uts.next()
            fw.dma("pool", ub, ub[:], I["peer_u"], Uv[:, c * 4:(c + 1) * 4, :])
            fw.dma("pool", vb, vb[:], I["peer_v"], Vv[:, c * 4:(c + 1) * 4, :])
            fw.dma("sp", self.Vb, Vbv[:, c * 4:(c + 1) * 4, :], vb, vb[:])
            for jj in range(4):
                pb = banks.next(); pv = self.bview(pb)
                for k in range(8):
                    fw.op("pe", lambda e: e.transpose(out=pv[:, k * 128:(k + 1) * 128], in_=ub[:, jj, k * 128:(k + 1) * 128], identity=self.identb[:]), [ub, self.identb], [pb])
                if jj % 2 == 0:
                    fw.op("act", lambda e: e.activation(out=ut[:, jj, :], in_=pv[:, 0:1024], func=AF.Copy), [pb], [ut])
                else:
                    fw.op("dve", lambda e: e.tensor_copy(out=ut[:, jj, :], in_=pv[:, 0:1024]), [pb], [ut])
            fw.dma("sp", self.UT, UTv[:, c * 4:(c + 1) * 4, :], ut, ut[:])
        fw.pop()

    def peer2(self, layer, hsrc, hdst, final=False):
        fw = self.fw
        I = self.I
        fw.push()
        GT, shT = self.modvecs(layer, 1)
        gate = self.gate_bcast(layer, 1)
        self.make_normer(nh=1)
        obanks = self.banks[0:4]
        self.bank = Rot(self.banks[4:8])
        w_q = fw.sb("w_q", [128, 8, 2048], BF16)
        for k in range(8):
            fw.dma("pool", w_q, w_q[:, k, :], I["peer_wq"], I["peer_wq"].ap()[layer, k * 128:(k + 1) * 128, :])
        keysT = fw.sb("keysT", [128, 16, 128], BF16)
        fw.push()
        keyn = fw.sb("keyn", [128, 16, 128], BF16)
        fw.dma("pool", keyn, keyn[:], I["peer_keys"], I["peer_keys"].ap()[layer].rearrange("h k d -> k h d"))
        for c in range(2):
            pb = self.bank.next(); pv = self.bview(pb)
            for q in range(8):
                fw.op("pe", lambda e: e.transpose(out=pv[:, q * 128:(q + 1) * 128], in_=keyn[:, c * 8 + q, :], identity=self.identb[:]), [keyn, self.identb], [pb])
            fw.op("act", lambda e: e.activation(out=keysT[:, c * 8:(c + 1) * 8, :].rearrange("p a b -> p (a b)"), in_=pv[:, 0:1024], func=AF.Copy), [pb], [keysT])
        fw.pop()
        iota = fw.sb("iota", [128, 128], BF16)
        fw.dma("pool", iota, iota[:], I["iota128"], I["iota128"].ap())
        xn2Ts = [fw.sb("xn2T", [128, 8, 256], BF16) for _ in range(2)]
        qT = fw.sb("pqT", [128, 16, 256], BF16)
        s_all = fw.sb("s_all", [128, 2, 16, 128])
        s1cs = [fw.sb("s1c", [128, 2, 8, 128]) for _ in range(2)]
        taus = [fw.sb("tau", [128, 2, 8]) for _ in range(2)]; bias = fw.sb("pbias", [128, 2, 8])
        m0bs = [fw.sb("m0b", [128, 2, 8, 16]) for _ in range(2)]
        idxa = fw.sb("idxa", [128, 2, 128])
        idxTs = [fw.sb("idxT", [128, 256], BF16) for _ in range(2)]
        SB = [(fw.sb("m0", [128, 16]), fw.sb("m1", [128, 16]), fw.sb("scr", [128, 128]), fw.sb("cand", [128, 16, 16]), fw.sb("c24", [128, 24]),
               fw.sb("cscr", [128, 256]), fw.sb("st1", [128, 4]), fw.sb("e16", [128, 16]), fw.sb("ix", [128, 16], U32)) for _ in range(3)]
        JB = 16
        sms = Rot([fw.sb("sm", [128, 8, 16, JB]) for _ in range(1)])
        Pss = Rot([fw.sb("P", [128, 8, 16, JB], BF16) for _ in range(1)])
        mks = Rot([fw.sb("mk", [128, 8, 16, JB], BF16) for _ in range(1)])
        Wp = fw.sb("Wp", [128, 2, 128, JB], BF16)
        WT = fw.sb("WT", [128, JB, 256], BF16)
        Walls = Rot([fw.sb("Wall", [128, 256, JB], BF16) for _ in range(2)])
        ohs = Rot([fw.sb("oh", [128, 32, 128], BF16) for _ in range(2)])
        UTs = Rot([fw.sb("UTs", [128, 2, 1024], BF16) for _ in range(2)])
        Vs = Rot([fw.sb("Vs", [128, 2, D], BF16) for _ in range(2)])
        Gs = Rot([fw.sb("Gs", [128, 256], BF16) for _ in range(3)])
        WAs = Rot([fw.sb("WAs", [128, 256], BF16) for _ in range(3)])
        tmps = Rot([fw.sb("ptmp", [128, D]) for _ in range(1)])
        hts = Rot([fw.sb("pht", [128, D]) for _ in range(1)])
        UTv = self.UT.ap().rearrange("j p k i -> p j (k i)")
        Vbv = self.Vb.ap().rearrange("j i d -> i j d")
        import os
        ng = int(os.environ.get("PEER_NG", NT // 2))
        NJB = 128 // JB
        glist = [g for g in range(ng) if not (final and g == 0)]
        Wls = {}

        def prelude_items(g):
            par = g % 2
            xn2T = xn2Ts[par]; s1c = s1cs[par]; tau = taus[par]; m0b = m0bs[par]; idxT = idxTs[par]
            s = 1 if g == 0 else 0
            items = []

            def normit(j):
                def f():
                    src, sap = self.src_of(2 * g + j, hsrc)
                    self.norm_tile(src, sap, GT[s], shT[s], xn2T, slice(j * 128, (j + 1) * 128))
                return f
            items += [normit(0), normit(1)]

            def qit(hp):
                def f():
                    pb = self.bank.next()
                    for k in range(8):
                        fw.op("pe", lambda e: e.matmul(pb[:, 0:256], lhsT=w_q[:, k, hp * 128:(hp + 1) * 128], rhs=xn2T[:, k, :], start=(k == 0), stop=(k == 7)), [w_q, xn2T], [pb])
                    fw.op("act", lambda e: e.activation(out=qT[:, hp, :], in_=pb[:, 0:256], func=AF.Copy), [pb], [qT])
                return f
            items += [qit(hp) for hp in range(16)]

            def scit(t, c):
                def f():
                    pb = self.bank.next()
                    for q in range(4):
                        hp = c * 4 + q
                        fw.op("pe", lambda e: e.matmul(pb[:, q * 128:(q + 1) * 128], lhsT=qT[:, hp, t * 128:(t + 1) * 128], rhs=keysT[:, hp, :], start=True, stop=True), [qT, keysT], [pb])
                    fw.op("act", lambda e: e.activation(out=s_all[:, t, c * 4:(c + 1) * 4, :].rearrange("p a b -> p (a b)"), in_=pb[:], func=AF.Copy), [pb], [s_all])
                    if c == 3:
                        fw.op("act", lambda e: e.activation(out=s1c[:, t, :, :], in_=s_all[:, t, :, :].rearrange("p (h two) j -> p h two j", two=2)[:, :, 1, :], func=AF.Copy), [s_all], [s1c])
                return f
            items += [scit(t, c) for t in range(2) for c in range(4)]

            def stats_gen(t, h, B):
                m0, m1, scr, cand, c24, cscr, st1, e16, ix = B
                s0 = s_all[:, t, 2 * h, :]; s1 = s_all[:, t, 2 * h + 1, :]
                fw.op("dve", lambda e: e.max(out=m0[:, 0:8], in_=s0), [s_all], [m0]); yield
                fw.op("dve", lambda e: e.max_index(out=ix[:, 0:8], in_max=m0[:, 0:8], in_values=s0), [s_all, m0], [ix]); yield
                fw.op("dve", lambda e: e.match_replace(out=scr[:], in_to_replace=m0[:, 0:8], in_values=s0, imm_value=-1e30), [s_all, m0], [scr]); yield
                fw.op("dve", lambda e: e.max(out=m0[:, 8:16], in_=scr[:]), [scr], [m0]); yield
                fw.op("dve", lambda e: e.max_index(out=ix[:, 8:16], in_max=m0[:, 8:16], in_values=scr[:]), [scr, m0], [ix]); yield
                fw.op("dve", lambda e: e.tensor_copy(out=idxa[:, t, h * 16:(h + 1) * 16], in_=ix[:]), [ix], [idxa]); yield
                fw.op("dve", lambda e: e.max(out=m1[:, 0:8], in_=s1), [s_all], [m1]); yield
                fw.op("dve", lambda e: e.match_replace(out=scr[:], in_to_replace=m1[:, 0:8], in_values=s1, imm_value=-1e30), [s_all, m1], [scr]); yield
                fw.op("dve", lambda e: e.max(out=m1[:, 8:16], in_=scr[:]), [scr], [m1]); yield
                fw.op("dve", lambda e: e.tensor_tensor(out=cand[:], in0=m0[:].unsqueeze(2).to_broadcast([128, 16, 16]), in1=m1[:].unsqueeze(1).to_broadcast([128, 16, 16]), op=ALU.add), [m0, m1], [cand]); yield
                cf = cand[:].rearrange("p a b -> p (a b)")
                fw.op("dve", lambda e: e.max(out=c24[:, 0:8], in_=cf), [cand], [c24]); yield
                fw.op("dve", lambda e: e.match_replace(out=cscr[:], in_to_replace=c24[:, 0:8], in_values=cf, imm_value=-1e30), [cand, c24], [cscr]); yield
                fw.op("dve", lambda e: e.max(out=c24[:, 8:16], in_=cscr[:]), [cscr], [c24]); yield
                fw.op("dve", lambda e: e.match_replace(out=cscr[:], in_to_replace=c24[:, 8:16], in_values=cscr[:], imm_value=-1e30), [cscr, c24], [cscr]); yield
                fw.op("dve", lambda e: e.max(out=c24[:, 16:24], in_=cscr[:]), [cscr], [c24]); yield
                fw.op("dve", lambda e: e.tensor_scalar(out=st1[:, 0:1], in0=c24[:, 16:17], scalar1=0.5, scalar2=None, op0=ALU.mult), [c24], [st1]); yield
                fw.op("dve", lambda e: e.scalar_tensor_tensor(out=st1[:, 0:1], in0=c24[:, 15:16], scalar=0.5, in1=st1[:, 0:1], op0=ALU.mult, op1=ALU.add), [c24, st1], [st1]); yield
                fw.op("dve", lambda e: e.tensor_scalar(out=st1[:, 1:2], in0=c24[:, 0:1], scalar1=-1.0, scalar2=None, op0=ALU.mult), [c24], [st1]); yield
                fw.op("act", lambda e: e.activation(out=e16[:], in_=c24[:, 0:16], func=AF.Exp, bias=st1[:, 1:2], scale=1.0, accum_out=st1[:, 2:3]), [c24, st1], [e16, st1]); yield
                fw.op("act", lambda e: e.activation(out=st1[:, 3:4], in_=st1[:, 2:3], func=AF.Ln), [st1], [st1]); yield
                fw.op("dve", lambda e: e.tensor_tensor(out=bias[:, t, h:h + 1], in0=st1[:, 1:2], in1=st1[:, 3:4], op=ALU.subtract), [st1], [bias]); yield
                fw.op("dve", lambda e: e.tensor_scalar(out=m0b[:, t, h, :], in0=m0[:], scalar1=bias[:, t, h:h + 1], scalar2=None, op0=ALU.add), [m0, bias], [m0b]); yield
                fw.op("dve", lambda e: e.tensor_tensor(out=tau[:, t, h:h + 1], in0=st1[:, 0:1], in1=bias[:, t, h:h + 1], op=ALU.add), [st1, bias], [tau]); yield

            def statit(ths):
                def f():
                    gens = [stats_gen(t, h, SB[i]) for i, (t, h) in enumerate(ths)]
                    alive = True
                    while alive:
                        alive = False
                        for gi in gens:
                            try:
                                next(gi); alive = True
                            except StopIteration:
                                pass
                return f
            allth = [(t, h) for h in range(8) for t in range(2)]
            for q in range(0, 16, 3):
                items.append(statit(allth[q:q + 3]))

            def idxit(t):
                def f():
                    pb = self.bank.next()
                    fw.op("pe", lambda e: e.transpose(out=pb[:, 0:128], in_=idxa[:, t, :], identity=self.identf[:]), [idxa, self.identf], [pb])
                    fw.op("act", lambda e: e.activation(out=idxT[:, t * 128:(t + 1) * 128], in_=pb[:, 0:128], func=AF.Copy), [pb], [idxT])
                return f
            items += [idxit(0), idxit(1)]
            return items

        for it in prelude_items(glist[0]):
            it()
        first_built = False
        for gi_, g in enumerate(glist):
            par = g % 2
            xn2T = xn2Ts[par]; s1c = s1cs[par]; tau = taus[par]; m0b = m0bs[par]; idxT = idxTs[par]
            s = 1 if g == 0 else 0
            gnext = glist[gi_ + 1] if gi_ + 1 < len(glist) else None
            def build_items(jb, g=g, m0b=m0b, s1c=s1c, tau=tau, idxT=idxT):
                Wl = Wls.setdefault(g, [None] * NJB)
                j0 = jb * JB
                items = []

                def rank(t):
                    def f():
                        sm = sms.next(); P = Pss.next(); mk = mks.next()
                        fw.op("pool", lambda e: e.tensor_tensor(out=sm[:].rearrange("p h a j -> p h (a j)").rearrange("p h (a j) -> p h a j", a=16),
                                                               in0=m0b[:, t, :, :].unsqueeze(3).to_broadcast([128, 8, 16, JB]),
                                                               in1=s1c[:, t, :, j0:j0 + JB].unsqueeze(2).to_broadcast([128, 8, 16, JB]),
                                                               op=ALU.add), [m0b, s1c], [sm])
                        fw.op("act", lambda e: e.activation(out=P[:].rearrange("p h a j -> p (h a j)"), in_=sm[:].rearrange("p h a j -> p (h a j)"), func=AF.Exp), [sm], [P])
                        fw.op("dve", lambda e: e.tensor_tensor(out=mk[:].rearrange("p h a j -> p h (a j)"), in0=sm[:].rearrange("p h a j -> p h (a j)"),
                                                               in1=tau[:, t, :].unsqueeze(2).to_broadcast([128, 8, 16 * JB]), op=ALU.is_ge), [sm, tau], [mk])
                        fw.op("pool", lambda e: e.tensor_tensor(out=Wp[:, t, :, :].rearrange("p (h a) j -> p h a j", a=16), in0=mk[:], in1=P[:], op=ALU.mult), [mk, P], [Wp])
                    return f
                items.append(rank(0)); items.append(rank(1))

                def transp(q):
                    def f():
                        pb = self.bank.next(); pv = self.bview(pb)
                        for jj in range(4):
                            for t in range(2):
                                c = jj * 2 + t
                                fw.op("pe", lambda e: e.transpose(out=pv[:, c * 128:(c + 1) * 128], in_=Wp[:, t, :, q * 4 + jj], identity=self.identb[:]), [Wp, self.identb], [pb])
                        fw.op("act", lambda e: e.activation(out=WT[:, q * 4:(q + 1) * 4, :].rearrange("p a b -> p (a b)"), in_=pv[:, 0:1024], func=AF.Copy), [pb], [WT])
                    return f
                for q in range(JB // 4):
                    items.append(transp(q))

                ohl = {}

                def ohA(nb):
                    oh = ohs.next()
                    ohl[nb] = oh
                    n0 = nb * 32
                    fw.op("dve", lambda e: e.tensor_tensor(out=oh[:], in0=idxT[:, n0:n0 + 32].unsqueeze(2).to_broadcast([128, 32, 128]),
                                                                                 in1=iota[:].unsqueeze(1).to_broadcast([128, 32, 128]), op=ALU.is_equal), [idxT, iota], [oh])

                def scatter(nb):
                    def f():
                        if nb == 0:
                            Wl[jb] = Walls.next()
                        Wall = Wl[jb]
                        oh = ohl[nb]
                        n0 = nb * 32
                        for half in range(2):
                            pb = self.bank.next()
                            for q in range(16):
                                n = n0 + half * 16 + q
                                fw.op("pe", lambda e: e.matmul(pb[:, q * JB:(q + 1) * JB], lhsT=oh[:, half * 16 + q, :], rhs=WT[:, :, n], start=True, stop=True), [oh, WT], [pb])
                            if half == 0:
                                fw.op("act", lambda e: e.activation(out=Wall[:, n0 + half * 16:n0 + half * 16 + 16, :].rearrange("p a b -> p (a b)"), in_=pb[:, 0:16 * JB], func=AF.Copy), [pb], [Wall])
                            else:
                                fw.op("act", lambda e: e.activation(out=Wall[:, n0 + half * 16:n0 + half * 16 + 16, :].rearrange("p a b -> p (a b)"), in_=pb[:, 0:16 * JB], func=AF.Copy), [pb], [Wall])
                        if nb + 1 < 8:
                            ohA(nb + 1)
                    return f
                last_tr = items[-1]

                def tr_and_oh():
                    last_tr()
                    ohA(0)
                items[-1] = tr_and_oh
                for nb in range(8):
                    items.append(scatter(nb))
                return items

            cur_ld = {}

            def front(jx):
                jb, jj = jx // JB, jx % JB
                Wall = Wls[g][jb]
                if jx % 2 == 0:
                    UTt = UTs.next(); Vt = Vs.next()
                    fw.dma("sp", UTt, UTt[:], self.UT, UTv[:, jx:jx + 2, :])
                    fw.dma("sp", Vt, Vt[:], self.Vb, Vbv[:, jx:jx + 2, :])
                    cur_ld["u"] = UTt; cur_ld["v"] = Vt
                UTt = cur_ld["u"]; Vt = cur_ld["v"]
                i2 = jx % 2
                pa = self.bank.next()
                for k in range(8):
                    fw.op("pe", lambda e: e.matmul(pa[:, 0:256], lhsT=UTt[:, i2, k * 128:(k + 1) * 128], rhs=xn2T[:, k, :], start=(k == 0), stop=(k == 7)), [UTt, xn2T], [pa])
                G = Gs.next()
                fw.op("act", lambda e: e.activation(out=G[:], in_=pa[:, 0:256], func=AF.Gelu), [pa], [G])
                WA = WAs.next()
                fw.op("pool", lambda e: e.tensor_tensor(out=WA[:], in0=G[:], in1=Wall[:, :, jj], op=ALU.mult), [G, Wall], [WA])

                def back():
                    for t in range(2):
                        for half in range(2):
                            ob = obanks[t * 2 + half]
                            fw.op("pe", lambda e: e.matmul(ob[:], lhsT=WA[:, t * 128:(t + 1) * 128], rhs=Vt[:, i2, half * 512:(half + 1) * 512], start=(jx == 0), stop=(jx == 127)), [WA, Vt], [ob])
                return back

            if not first_built:
                for it in build_items(0):
                    it()
                first_built = True
            backs = {0: front(0), 1: front(1)}
            pending = []
            side = prelude_items(gnext) if gnext is not None else []
            nside = len(side)
            for jx in range(128):
                jb, jj = jx // JB, jx % JB
                if jj == 0:
                    if jb + 1 < NJB:
                        pending = build_items(jb + 1)
                    elif gnext is not None:
                        while side:
                            side.pop(0)()
                        parn = gnext % 2
                        pending = build_items(0, g=gnext, m0b=m0bs[parn], s1c=s1cs[parn], tau=taus[parn], idxT=idxTs[parn])
                    else:
                        pending = []
                if pending and jj < JB - 2:
                    per = -(-len(pending) // max(1, (JB - 2 - jj)))
                    for _ in range(per):
                        if pending:
                            pending.pop(0)()
                if side and jb < NJB - 1:
                    steps_left = (NJB - 1) * JB - jx
                    per = -(-len(side) // max(1, steps_left))
                    for _ in range(per):
                        if side:
                            side.pop(0)()
                if jx + 2 < 128:
                    backs[jx + 2] = front(jx + 2)
                backs.pop(jx)()
            while pending:
                pending.pop(0)()
            for t in range(2):
                tile = 2 * g + t
                tmp = tmps.next(); ht = hts.next()
                src, sap = self.src_of(tile, hsrc)
                fw.dma("sp", ht, ht[:], src, sap)
                for half in range(2):
                    ob = obanks[t * 2 + half]
                    fw.op("dve", lambda e: e.tensor_tensor(out=tmp[:, half * 512:(half + 1) * 512], in0=ob[:], in1=gate[s][:, half * 512:(half + 1) * 512], op=ALU.mult), [ob, gate[s]], [tmp])
                fw.op("pool", lambda e: e.tensor_tensor(out=tmp[:], in0=tmp[:], in1=ht[:], op=ALU.add), [tmp, ht], [tmp])
                if final:
                    fw.dma("act", self.out, self.out.ap()[(tile - 2) * 128:(tile - 1) * 128, :], tmp, tmp[:])
                else:
                    fw.dma("act", hdst, hdst.ap()[tile * 128:(tile + 1) * 128, :], tmp, tmp[:])
        self.bank = Rot(self.banks)
        fw.pop()

    def l1_phaseA(self, hsrc):
        fw = self.fw
        I = self.I
        S = self.S1 = {}
        for nm in ("R", "KAP", "V", "G", "LW0", "LW1", "B0", "B1", "KD0", "KD1"):
            S[nm] = fw.dram("s1_" + nm, [TOK, D], F32, kind=("ExternalOutput" if self.dbg else "Internal"))
        S["BON"] = fw.dram("s1_BON", [TOK, 16], F32, kind=("ExternalOutput" if self.dbg else "Internal"))
        fw.push()
        GT, shT = self.modvecs(1, 0)
        self.make_normer()
        self.bank = Rot(self.banks)
        W = {}
        for nm in ("od_w_r", "od_w_k", "od_w_v"):
            W[nm] = fw.sb(nm, [128, 8, D], BF16)
            for k in range(8):
                fw.dma("pool", W[nm], W[nm][:, k, :], I[nm], I[nm].ap()[k * 128:(k + 1) * 128, :])
        g1 = fw.sb("g1", [128, 8, 128], BF16); g2 = fw.sb("g2", [128, D], BF16)
        fw.dma("pool", g1, g1[:], I["od_g1"], I["od_g1"].ap().rearrange("(k p) n -> p k n", p=128))
        fw.dma("pool", g2, g2[:], I["od_g2"], I["od_g2"].ap())
        w1c = fw.sb("w1c", [128, 8, 128], BF16); a1c = fw.sb("a1c", [128, 8, 128], BF16)
        for d in range(2):
            fw.dma("pool", w1c, w1c[:, :, d * 64:(d + 1) * 64], I["od_w1"], I["od_w1"].ap()[d].rearrange("(k p) n -> p k n", p=128))
            fw.dma("pool", a1c, a1c[:, :, d * 64:(d + 1) * 64], I["od_a1"], I["od_a1"].ap()[d].rearrange("(k p) n -> p k n", p=128))
        w2c = fw.sb("w2c", [128, D], BF16); a2c = fw.sb("a2c", [128, D], BF16)
        fw.dma("pool", w2c, w2c[:], I["od_w2"], I["od_w2"].ap().rearrange("d l n -> (d l) n"))
        fw.dma("pool", a2c, a2c[:], I["od_a2"], I["od_a2"].ap().rearrange("d l n -> (d l) n"))
        brow = fw.sb("brow", [1, 4, D], BF16)
        fw.dma("pool", brow, brow[:, 0:2, :], I["od_w0"], I["od_w0"].ap().rearrange("(o d) n -> o d n", o=1))
        fw.dma("pool", brow, brow[:, 2:4, :], I["od_a0"], I["od_a0"].ap().rearrange("(o d) n -> o d n", o=1))
        ones1 = fw.sb("ones1", [1, 128], BF16)
        fw.op("dve", lambda e: e.memset(ones1[:], 1.0), [], [ones1])
        kkb = fw.sb("kkb", [128, D]); kab = fw.sb("kab", [128, D]); rkb = fw.sb("rkb", [128, D])
        for t_, nm in ((kkb, "od_k_k"), (kab, "od_k_a"), (rkb, "od_r_k")):
            fw.dma("sp", t_, t_[:], I[nm], I[nm].ap().partition_broadcast(128))
        muT = fw.sb("muT", [128, 6, 8])
        for m in range(6):
            self.load_colT(muT, muT[:, m, :], I["od_mu"], I["od_mu"].ap()[m])
        NG = NT // 2
        win = [fw.sb("xnw", [128, 8, 256], BF16) for _ in range(4)]
        xxT = fw.sb("xxT", [128, 8, 256], BF16)
        xms = Rot([fw.sb("xm", [128, 8, 256], BF16) for _ in range(2)])
        tmpp = Rot([fw.sb("l1t", [128, D]) for _ in range(8)])
        ksb0 = fw.sb("ksb0", [128, D]); ksb1 = fw.sb("ksb1", [128, D])
        kap = fw.sb("kap", [128, D]); kka = fw.sb("kka", [128, D]); kmk = fw.sb("kmk", [128, D]); r_s = fw.sb("r_s", [128, D]); kd0 = fw.sb("kd0", [128, D])
        ss16 = Rot([fw.sb("ss16", [128, 16]) for _ in range(2)])
        lo = Rot([fw.sb("lo", [128, 128], BF16) for _ in range(3)])
        loT = Rot([fw.sb("loT", [128, 128], BF16) for _ in range(3)])

        def norm_group(g):
            s = 1 if g == 0 else 0
            for j in range(2):
                src, sap = self.src_of(2 * g + j, hsrc)
                self.norm_tile(src, sap, GT[s], shT[s], win[g % 4], slice(j * 128, (j + 1) * 128))

        def sub(out, a, b):
            fw.op("pool", lambda e: e.tensor_tensor(out=out, in0=a, in1=b, op=ALU.subtract), [win[0], win[1], win[2], win[3]], [xxT])

        def neg(out, a):
            fw.op("pool", lambda e: e.tensor_scalar(out=out, in0=a, scalar1=-1.0, scalar2=None, op0=ALU.mult), [win[0], win[1], win[2], win[3]], [xxT])

        def proj_tok(xm, t, Wt, nb=2):
            outs = []
            for half in range(nb):
                pb = self.bank.next()
                for k in range(8):
                    fw.op("pe", lambda e: e.matmul(pb[:], lhsT=xm[:, k, t * 128:(t + 1) * 128], rhs=Wt[:, k, half * 512:(half + 1) * 512], start=(k == 0), stop=(k == 7)), [xm, Wt], [pb])
                outs.append(pb)
            return outs

        def evac(dst, pbs):
            for half, pb in enumerate(pbs):
                fw.op("act", lambda e: e.activation(out=dst[:, half * 512:(half + 1) * 512], in_=pb[:], func=AF.Copy), [pb], [dst])

        def store(nm, tile, src, ap=None):
            fw.dma("sp", S[nm], S[nm].ap()[tile * 128:(tile + 1) * 128, :], src, src[:] if ap is None else ap)

        def lerp(m, cur):
            xm = xms.next()
            for k in range(8):
                fw.op("dve", lambda e: e.scalar_tensor_tensor(out=xm[:, k, :], in0=xxT[:, k, :], scalar=muT[:, m, k:k + 1], in1=cur[:, k, :], op0=ALU.mult, op1=ALU.add), [xxT, muT, cur], [xm])
            return xm

        def lora(xm, t, w1, w2, brow_i, func):
            pb = self.bank.next()
            for k in range(8):
                fw.op("pe", lambda e: e.matmul(pb[:, 0:128], lhsT=xm[:, k, t * 128:(t + 1) * 128], rhs=w1[:, k, :], start=(k == 0), stop=(k == 7)), [xm, w1], [pb])
            l_ = lo.next()
            fw.op("act", lambda e: e.activation(out=l_[:], in_=pb[:, 0:128], func=func), [pb], [l_])
            pt = self.bank.next(); pv = self.bview(pt)
            fw.op("pe", lambda e: e.transpose(out=pv[:, 0:128], in_=l_[:], identity=self.identb[:]), [l_, self.identb], [pt])
            lT = loT.next()
            fw.op("act", lambda e: e.activation(out=lT[:], in_=pv[:, 0:128], func=AF.Copy), [pt], [lT])
            res = []
            for d in range(2):
                outs = []
                for half in range(2):
                    po = self.bank.next()
                    fw.op("pe", lambda e: e.matmul(po[:], lhsT=ones1[:], rhs=brow[:, brow_i + d, half * 512:(half + 1) * 512], start=True, stop=False), [ones1, brow], [po])
                    fw.op("pe", lambda e: e.matmul(po[:], lhsT=lT[d * 64:(d + 1) * 64, :], rhs=w2[d * 64:(d + 1) * 64, half * 512:(half + 1) * 512], start=False, stop=True), [lT, w2], [po])
                    outs.append(po)
                res.append(outs)
            return res

        norm_group(0)
        import os
        for g in range(int(os.environ.get("L1A_NG", NG))):
            if g + 1 < NG:
                norm_group(g + 1)
            cur = win[g % 4]; prv = win[(g - 1) % 4]; nxt = win[(g + 1) % 4]
            for k in range(8):
                c = cur[:, k, :]; o = xxT[:, k, :]
                if g == 0:
                    if k < 4:
                        sub(o[:, 1:256], c[:, 0:255], c[:, 1:256]); neg(o[:, 0:1], c[:, 0:1])
                    else:
                        sub(o[:, 0:255], c[:, 1:256], c[:, 0:255]); neg(o[:, 255:256], c[:, 255:256])
                else:
                    c3 = c.rearrange("p (r c) -> p r c", c=64); o3 = o.rearrange("p (r c) -> p r c", c=64)
                    if k < 2:
                        sub(o3[:, :, 1:64], c3[:, :, 0:63], c3[:, :, 1:64]); neg(o3[:, :, 0:1], c3[:, :, 0:1])
                    elif k < 4:
                        sub(o3[:, :, 0:63], c3[:, :, 1:64], c3[:, :, 0:63]); neg(o3[:, :, 63:64], c3[:, :, 63:64])
                    elif k < 6:
                        sub(o[:, 64:256], c[:, 0:192], c[:, 64:256])
                        if g == 1:
                            neg(o[:, 0:64], c[:, 0:64])
                        else:
                            sub(o[:, 0:64], prv[:, k, 192:256], c[:, 0:64])
                    else:
                        sub(o[:, 0:192], c[:, 64:256], c[:, 0:192])
                        if g == NG - 1:
                            neg(o[:, 192:256], c[:, 192:256])
                        else:
                            sub(o[:, 192:256], nxt[:, k, 0:64], c[:, 192:256])
            xr = lerp(0, cur)
            for t in range(2):
                tile = 2 * g + t
                rr_ = tmpp.next()
                evac(rr_, proj_tok(xr, t, W["od_w_r"]))
                store("R", tile, rr_)
            xk = lerp(2, cur)
            ktiles = [ksb0, ksb1]
            for t in range(2):
                evac(ktiles[t], proj_tok(xk, t, W["od_w_k"]))
            xv = lerp(3, cur)
            for t in range(2):
                v_ = tmpp.next()
                evac(v_, proj_tok(xv, t, W["od_w_v"]))
                store("V", 2 * g + t, v_)
            xg = lerp(5, cur)
            for t in range(2):
                pb = self.bank.next()
                for k in range(8):
                    fw.op("pe", lambda e: e.matmul(pb[:, 0:128], lhsT=xg[:, k, t * 128:(t + 1) * 128], rhs=g1[:, k, :], start=(k == 0), stop=(k == 7)), [xg, g1], [pb])
                l_ = lo.next()
                fw.op("act", lambda e: e.activation(out=l_[:], in_=pb[:, 0:128], func=AF.Sigmoid), [pb], [l_])
                pt = self.bank.next(); pv = self.bview(pt)
                fw.op("pe", lambda e: e.transpose(out=pv[:, 0:128], in_=l_[:], identity=self.identb[:]), [l_, self.identb], [pt])
                lT = loT.next()
                fw.op("act", lambda e: e.activation(out=lT[:], in_=pv[:, 0:128], func=AF.Copy), [pt], [lT])
                g_ = tmpp.next()
                outs = []
                for half in range(2):
                    po = self.bank.next()
                    fw.op("pe", lambda e: e.matmul(po[:], lhsT=lT[:], rhs=g2[:, half * 512:(half + 1) * 512], start=True, stop=True), [lT, g2], [po])
                    outs.append(po)
                evac(g_, outs)
                store("G", 2 * g + t, g_)
            xw = lerp(1, cur)
            xa = lerp(4, cur)
            for t in range(2):
                tile = 2 * g + t
                ksb = ktiles[t]
                kkx = tmpp.next(); sq = tmpp.next()
                fw.op("dve", lambda e: e.tensor_tensor(out=kkx[:], in0=ksb[:], in1=kkb[:], op=ALU.mult), [ksb, kkb], [kkx])
                fw.op("dve", lambda e: e.tensor_tensor(out=sq[:], in0=kkx[:], in1=kkx[:], op=ALU.mult), [kkx], [sq])
                s16 = ss16.next()
                fw.op("dve", lambda e: e.reduce_sum(out=s16[:], in_=sq[:].rearrange("p (h j) -> p h j", j=64), axis=AX.X), [sq], [s16])
                fw.op("act", lambda e: e.activation(out=s16[:], in_=s16[:], func=AF.Sqrt, bias=1e-12, scale=1.0), [s16], [s16])
                fw.op("dve", lambda e: e.reciprocal(out=s16[:], in_=s16[:]), [s16], [s16])
                fw.op("dve", lambda e: e.tensor_tensor(out=kap[:].rearrange("p (h j) -> p h j", j=64), in0=kkx[:].rearrange("p (h j) -> p h j", j=64),
                                                       in1=s16[:].unsqueeze(2).to_broadcast([128, 16, 64]), op=ALU.mult), [kkx, s16], [kap])
                store("KAP", tile, kap)
                fw.op("pool", lambda e: e.tensor_tensor(out=kka[:], in0=ksb[:], in1=kab[:], op=ALU.mult), [ksb, kab], [kka])
                fw.op("pool", lambda e: e.tensor_tensor(out=kmk[:], in0=ksb[:], in1=kka[:], op=ALU.subtract), [ksb, kka], [kmk])
                wl = lora(xw, t, w1c, w2c, 0, AF.Tanh)
                for d in range(2):
                    lw = tmpp.next()
                    for half in range(2):
                        fw.op("act", lambda e: e.activation(out=lw[:, half * 512:(half + 1) * 512], in_=wl[d][half][:], func=AF.Sigmoid), [wl[d][half]], [lw])
                    fw.op("dve", lambda e: e.tensor_scalar(out=lw[:], in0=lw[:], scalar1=-0.6065306597126334, scalar2=None, op0=ALU.mult), [lw], [lw])
                    store("LW%d" % d, tile, lw)
                al = lora(xa, t, a1c, a2c, 2, AF.Copy)
                avs = []
                for d in range(2):
                    a_ = tmpp.next()
                    for half in range(2):
                        fw.op("act", lambda e: e.activation(out=a_[:, half * 512:(half + 1) * 512], in_=al[d][half][:], func=AF.Sigmoid), [al[d][half]], [a_])
                    avs.append(a_)
                kds = []
                for d in range(2):
                    a_ = avs[d]
                    b_ = tmpp.next()
                    fw.op("dve", lambda e: e.tensor_tensor(out=b_[:], in0=kap[:], in1=a_[:], op=ALU.mult), [kap, a_], [b_])
                    store("B%d" % d, tile, b_)
                    kd = kd0 if d == 0 else tmpp.next()
                    fw.op("dve", lambda e: e.tensor_tensor(out=kd[:], in0=kka[:], in1=a_[:], op=ALU.mult), [kka, a_], [kd])
                    fw.op("dve", lambda e: e.tensor_tensor(out=kd[:], in0=kd[:], in1=kmk[:], op=ALU.add), [kd, kmk], [kd])
                    store("KD%d" % d, tile, kd)
                    kds.append(kd)
                fw.dma("sp", r_s, r_s[:], S["R"], S["R"].ap()[tile * 128:(tile + 1) * 128, :])
                ks = tmpp.next()
                fw.op("dve", lambda e: e.tensor_tensor(out=ks[:], in0=kds[0][:], in1=kds[1][:], op=ALU.add), [kds[0], kds[1]], [ks])
                fw.op("dve", lambda e: e.tensor_tensor(out=ks[:], in0=ks[:], in1=rkb[:], op=ALU.mult), [ks, rkb], [ks])
                fw.op("dve", lambda e: e.tensor_tensor(out=ks[:], in0=ks[:], in1=r_s[:], op=ALU.mult), [ks, r_s], [ks])
                bon = ss16.next()
                fw.op("dve", lambda e: e.reduce_sum(out=bon[:], in_=ks[:].rearrange("p (h j) -> p h j", j=64), axis=AX.X), [ks], [bon])
                fw.dma("sp", S["BON"], S["BON"].ap()[tile * 128:(tile + 1) * 128, :], bon, bon[:])
        fw.pop()

    def l1_phaseB(self):
        fw = self.fw
        I = self.I
        S = self.S1
        k_ = "ExternalOutput" if self.dbg else "Internal"
        self.Y = [fw.dram("s1_Y%d" % d, [TOK, D], F32, kind=k_) for d in range(2)]
        fw.push()
        ybanks = self.banks[0:2]
        pbanks = Rot(self.banks[2:8])
        hb = []
        hbr = Rot([(self.banks[2 + (i % 6)], (i // 6) * 256) for i in range(12)])
        sml = hbr
        tm = fw.sb("tm", [128, 4, 128])
        fw.dma("sp", tm, tm[:], I["trimask"], I["trimask"].ap())
        ones = fw.sb("ones", [128, 128])
        fw.op("dve", lambda e: e.memset(ones[:], 1.0), [], [ones])
        MA = []; MB = []; MN = []; MC = []
        for d in range(2):
            mS, mI, mSn, mC = (0, 1, 2, 1) if d == 0 else (2, 3, 0, 3)
            a = fw.sb("MA", [128, 2, 128]); b = fw.sb("MB", [128, 2, 128]); n = fw.sb("MN", [128, 128])
            fw.op("dve", lambda e: e.tensor_scalar(out=a[:, 0, :], in0=tm[:, mS, :], scalar1=-1.0, scalar2=None, op0=ALU.mult), [tm], [a])
            fw.op("dve", lambda e: e.tensor_copy(out=a[:, 1, :], in_=tm[:, mI, :]), [tm], [a])
            fw.op("dve", lambda e: e.tensor_copy(out=b[:, 0, :], in_=tm[:, mS, :]), [tm], [b])
            fw.op("dve", lambda e: e.tensor_copy(out=b[:, 1, :], in_=tm[:, mI, :]), [tm], [b])
            fw.op("dve", lambda e: e.tensor_scalar(out=n[:], in0=tm[:, mSn, :], scalar1=-1.0, scalar2=None, op0=ALU.mult), [tm], [n])
            MA.append(a); MB.append(b); MN.append(n); MC.append(mC)
        H = fw.sb("H", [64, 16, 64])
        ld = {nm: Rot([fw.sb("ld" + nm, [128, D]) for _ in range(2)]) for nm in ("R", "KAP", "V", "LW", "B", "KD")}
        prep = {nm: fw.sb("pp" + nm, [128, D], BF16 if nm in ("rt", "kt", "bt", "kdt") else F32) for nm in ("rt", "kt", "bt", "kdt", "bh", "kh")}
        Vb = fw.sb("Vbf", [128, D], BF16); Hb = fw.sb("Hb", [64, 16, 64], BF16)
        cumS = fw.sb("cumS", [128, 512]); et = Rot([fw.sb("et", [128, 512]) for _ in range(3)])
        gC = fw.sb("gC", [64, 16])
        NH = 16
        TT = Rot([fw.sb("TT", [64, 4, 128], BF16) for _ in range(NH)])
        SC = Rot([fw.sb("SC", [128, 4, 128], BF16) for _ in range(NH)])
        NM = Rot([fw.sb("NM", [128, 2, 128], BF16) for _ in range(NH * 3)])
        XS = Rot([fw.sb("XS", [128, 64], BF16) for _ in range(NH * 3)])
        UF = Rot([fw.sb("UF", [128, 64]) for _ in range(NH)])
        ysb = Rot([fw.sb("ysb", [128, D]) for _ in range(2)])
        import os
        nchunks = int(os.environ.get("L1B_NC", NT))

        def head_gen(h, d, T_):
            R_, KAP_, V_, LW_, B_, KD_ = T_
            hs = slice(h * 64, (h + 1) * 64)
            tt = TT.next(); sc = SC.next()
            for pair, (x0, x1) in enumerate(((prep["kt"], prep["rt"]), (prep["bt"], prep["kdt"]))):
                bk, off = hbr.next()
                bv = self.bview(bk)
                fw.op("pe", lambda e: e.transpose(out=bv[0:64, 2 * off:2 * off + 128], in_=x0[:, hs], identity=self.identb[:]), [x0, self.identb], [bk])
                fw.op("pe", lambda e: e.transpose(out=bv[0:64, 2 * off + 128:2 * off + 256], in_=x1[:, hs], identity=self.identb[:]), [x1, self.identb], [bk])
                fw.op("act", lambda e: e.activation(out=tt[:, 2 * pair:2 * pair + 2, :].rearrange("p a b -> p (a b)"), in_=bv[0:64, 2 * off:2 * off + 256], func=AF.Copy), [bk], [tt])
            yield
            kT = tt[:, 0, :]; rT = tt[:, 1, :]; bT = tt[:, 2, :]; kdT = tt[:, 3, :]
            kr = tt[:, 0:2, :].rearrange("p a b -> p (a b)")
            bk, off = hbr.next()
            fw.op("pe", lambda e: e.matmul(bk[:, off:off + 256], lhsT=bT, rhs=kr, start=True, stop=True), [tt], [bk])
            fw.op("dve", lambda e: e.tensor_tensor(out=sc[:, 0:2, :].rearrange("p a b -> p (a b)"), in0=bk[:, off:off + 256], in1=MA[d][:].rearrange("p a b -> p (a b)"), op=ALU.mult), [bk, MA[d]], [sc])
            bk, off = hbr.next()
            fw.op("pe", lambda e: e.matmul(bk[:, off:off + 256], lhsT=kdT, rhs=kr, start=True, stop=True), [tt], [bk])
            fw.op("dve", lambda e: e.tensor_tensor(out=sc[:, 2:4, :].rearrange("p a b -> p (a b)"), in0=bk[:, off:off + 256], in1=MB[d][:].rearrange("p a b -> p (a b)"), op=ALU.mult), [bk, MB[d]], [sc])
            nm = NM.next()
            bk, off = hbr.next()
            fw.op("pe", lambda e: e.matmul(bk[:, off:off + 128], lhsT=kT, rhs=bT, start=True, stop=True), [tt], [bk])
            fw.op("dve", lambda e: e.tensor_tensor(out=nm[:, 0, :], in0=bk[:, off:off + 128], in1=MN[d][:], op=ALU.mult), [bk, MN[d]], [nm])
            fw.op("act", lambda e: e.activation(out=nm[:, 1, :], in_=sc[:, 0, :], func=AF.Copy), [sc], [nm])
            yield
            NT_ = sc[:, 0, :]; BrbT = sc[:, 1, :]; AkT = sc[:, 2, :]; BrkT = sc[:, 3, :]
            bk, off = sml.next()
            fw.op("pe", lambda e: e.matmul(bk[:, off:off + 64], lhsT=kT, rhs=Hb[:, h, :], start=True, stop=False), [tt, Hb], [bk])
            fw.op("pe", lambda e: e.matmul(bk[:, off:off + 64], lhsT=AkT, rhs=Vb[:, hs], start=False, stop=True), [sc, Vb], [bk])
            X = XS.next()
            fw.op("act", lambda e: e.activation(out=X[:], in_=bk[:, off:off + 64], func=AF.Copy, scale=-1.0), [bk], [X])
            yield
            cur = nm
            for lvl in range(7):
                bk, off = sml.next()
                fw.op("pe", lambda e: e.matmul(bk[:, off:off + 64], lhsT=cur[:, 1, :], rhs=X[:], start=True, stop=True), [cur, X], [bk])
                X2 = XS.next() if lvl < 6 else UF.next()
                fw.op("dve", lambda e: e.tensor_tensor(out=X2[:], in0=bk[:, off:off + 64], in1=X[:], op=ALU.add), [bk, X], [X2])
                X = X2
                if lvl < 6:
                    nxt = NM.next()
                    bk, off = hbr.next()
                    if lvl < 5:
                        fw.op("pe", lambda e: e.matmul(bk[:, off:off + 128], lhsT=cur[:, 1, :], rhs=cur[:, 0, :], start=True, stop=True), [cur], [bk])
                    fw.op("pe", lambda e: e.matmul(bk[:, off + 128:off + 256], lhsT=cur[:, 0, :], rhs=cur[:, 1, :], start=True, stop=True), [cur], [bk])
                    if lvl < 5:
                        fw.op("act", lambda e: e.activation(out=nxt[:].rearrange("p a b -> p (a b)"), in_=bk[:, off:off + 256], func=AF.Copy), [bk], [nxt])
                    else:
                        fw.op("act", lambda e: e.activation(out=nxt[:, 1, :], in_=bk[:, off + 128:off + 256], func=AF.Copy), [bk], [nxt])
                    cur = nxt
                yield
            U = X
            Ub = XS.next()
            fw.op("act", lambda e: e.activation(out=Ub[:], in_=U[:], func=AF.Copy), [U], [Ub])
            yb = ybanks[h // 8]; yo = (h % 8) * 64
            fw.op("pe", lambda e: e.matmul(yb[:, yo:yo + 64], lhsT=rT, rhs=Hb[:, h, :], start=True, stop=False), [tt, Hb], [yb])
            fw.op("pe", lambda e: e.matmul(yb[:, yo:yo + 64], lhsT=BrbT, rhs=Ub[:], start=False, stop=False), [sc, Ub], [yb])
            fw.op("pe", lambda e: e.matmul(yb[:, yo:yo + 64], lhsT=BrkT, rhs=Vb[:, hs], start=False, stop=True), [sc, Vb], [yb])
            bk, off = sml.next()
            fw.op("pe", lambda e: e.matmul(bk[0:64, off:off + 64], lhsT=prep["bh"][:, hs], rhs=U[:], start=True, stop=False), [prep["bh"], U], [bk])
            fw.op("pe", lambda e: e.matmul(bk[0:64, off:off + 64], lhsT=prep["kh"][:, hs], rhs=V_[:, hs], start=False, stop=True), [prep["kh"], V_], [bk])
            fw.op("dve", lambda e: e.scalar_tensor_tensor(out=H[:, h, :], in0=H[:, h, :], scalar=gC[:, h:h + 1], in1=bk[0:64, off:off + 64], op0=ALU.mult, op1=ALU.add), [H, gC, bk], [H])
            yield

        for d in range(2):
            fw.op("dve", lambda e: e.memset(H[:], 0.0), [], [H])
            fw.op("dve", lambda e: e.memset(Hb[:], 0.0), [], [Hb])
            order = list(range(NT)) if d == 0 else [1, 0] + list(range(NT - 1, 1, -1))
            for c in order[:nchunks]:
                rows = slice(c * 128, (c + 1) * 128)
                T_ = []
                for nm, key in (("R", "R"), ("KAP", "KAP"), ("V", "V"), ("LW", "LW%d" % d), ("B", "B%d" % d), ("KD", "KD%d" % d)):
                    t_ = ld[nm].next()
                    fw.dma("sp", t_, t_[:], S[key], S[key].ap()[rows, :])
                    T_.append(t_)
                R_, KAP_, V_, LW_, B_, KD_ = T_
                fw.op("pool", lambda e: e.tensor_copy(out=Vb[:], in_=V_[:]), [V_], [Vb])
                bk, off = sml.next()
                for h in range(16):
                    fw.op("pe", lambda e: e.matmul(bk[0:64, off + h:off + h + 1], lhsT=LW_[:, h * 64:(h + 1) * 64], rhs=ones[:, 0:1], start=True, stop=True), [LW_, ones], [bk])
                fw.op("act", lambda e: e.activation(out=gC[:], in_=bk[0:64, off:off + 16], func=AF.Exp), [bk], [gC])
                for half in range(2):
                    cs = slice(half * 512, (half + 1) * 512)
                    pc = pbanks.next(); ptot = pbanks.next()
                    fw.op("pe", lambda e: e.matmul(pc[:], lhsT=tm[:, MC[d], :], rhs=LW_[:, cs], start=True, stop=True), [tm, LW_], [pc])
                    fw.op("pe", lambda e: e.matmul(ptot[:], lhsT=ones[:], rhs=LW_[:, cs], start=True, stop=True), [ones, LW_], [ptot])
                    fw.op("act", lambda e: e.activation(out=cumS[:], in_=pc[:], func=AF.Copy), [pc], [cumS])
                    e1 = et.next()
                    fw.op("act", lambda e: e.activation(out=e1[:], in_=pc[:], func=AF.Exp), [pc], [e1])
                    fw.op("pool", lambda e: e.tensor_tensor(out=prep["rt"][:, cs], in0=R_[:, cs], in1=e1[:], op=ALU.mult), [R_, e1], [prep["rt"]])
                    e2 = et.next()
                    fw.op("act", lambda e: e.activation(out=e2[:], in_=pc[:], func=AF.Exp, scale=-1.0), [pc], [e2])
                    fw.op("pool", lambda e: e.tensor_tensor(out=prep["bt"][:, cs], in0=B_[:, cs], in1=e2[:], op=ALU.mult), [B_, e2], [prep["bt"]])
                    fw.op("dve", lambda e: e.tensor_tensor(out=prep["kdt"][:, cs], in0=KD_[:, cs], in1=e2[:], op=ALU.mult), [KD_, e2], [prep["kdt"]])
                    e3 = et.next()
                    fw.op("dve", lambda e: e.tensor_tensor(out=e3[:], in0=cumS[:], in1=LW_[:, cs], op=ALU.subtract), [cumS, LW_], [e3])
                    fw.op("act", lambda e: e.activation(out=e3[:], in_=e3[:], func=AF.Exp), [e3], [e3])
                    fw.op("dve", lambda e: e.tensor_tensor(out=prep["kt"][:, cs], in0=KAP_[:, cs], in1=e3[:], op=ALU.mult), [KAP_, e3], [prep["kt"]])
                    e4 = et.next()
                    fw.op("dve", lambda e: e.tensor_tensor(out=e4[:], in0=ptot[:], in1=cumS[:], op=ALU.subtract), [ptot, cumS], [e4])
                    fw.op("act", lambda e: e.activation(out=e4[:], in_=e4[:], func=AF.Exp), [e4], [e4])
                    fw.op("dve", lambda e: e.tensor_tensor(out=prep["bh"][:, cs], in0=B_[:, cs], in1=e4[:], op=ALU.mult), [B_, e4], [prep["bh"]])
                    fw.op("dve", lambda e: e.tensor_tensor(out=prep["kh"][:, cs], in0=KD_[:, cs], in1=e4[:], op=ALU.mult), [KD_, e4], [prep["kh"]])
                for h0 in range(0, 16, NH):
                    gens = [head_gen(h, d, T_) for h in range(h0, h0 + NH)]
                    alive = True
                    while alive:
                        alive = False
                        for gi in gens:
                            try:
                                next(gi)
                                alive = True
                            except StopIteration:
                                pass
                fw.op("act", lambda e: e.activation(out=Hb[:].rearrange("p a b -> p (a b)"), in_=H[:].rearrange("p a b -> p (a b)"), func=AF.Copy), [H], [Hb])
                ys = ysb.next()
                for half in range(2):
                    fw.op("act", lambda e: e.activation(out=ys[:, half * 512:(half + 1) * 512], in_=ybanks[half][:], func=AF.Copy), [ybanks[half]], [ys])
                fw.dma("act", self.Y[d], self.Y[d].ap()[rows, :], ys, ys[:])
        fw.pop()

    def l1_phaseC(self, hsrc, hdst):
        fw = self.fw
        I = self.I
        S = self.S1
        fw.push()
        self.bank = Rot(self.banks)
        gate = self.gate_bcast(1, 0)
        w_o = fw.sb("w_o", [128, 8, D], BF16)
        for k in range(8):
            fw.dma("pool", w_o, w_o[:, k, :], I["od_w_o"], I["od_w_o"].ap()[k * 128:(k + 1) * 128, :])
        gnw = fw.sb("gnw", [128, D]); gnb = fw.sb("gnb", [128, D])
        fw.dma("sp", gnw, gnw[:], I["od_gn_w"], I["od_gn_w"].ap().partition_broadcast(128))
        fw.dma("sp", gnb, gnb[:], I["od_gn_b"], I["od_gn_b"].ap().partition_broadcast(128))
        L = {nm: Rot([fw.sb("c" + nm, [128, D]) for _ in range(2)]) for nm in ("y0", "y1", "v", "g", "h")}
        bons = Rot([fw.sb("cbon", [128, 16]) for _ in range(2)])
        st = Rot([fw.sb("cst", [128, 16]) for _ in range(6)])
        wk = Rot([fw.sb("cwk", [128, D]) for _ in range(4)])
        obf = Rot([fw.sb("cob", [128, D], BF16) for _ in range(2)])
        oTs = Rot([fw.sb("coT", [128, 8, 128], BF16) for _ in range(2)])
        v3 = lambda t_: t_[:].rearrange("p (h j) -> p h j", j=64)
        b3 = lambda t_: t_[:].unsqueeze(2).to_broadcast([128, 16, 64])
        for tile in range(2, NT):
            rows = slice(tile * 128, (tile + 1) * 128)
            y0 = L["y0"].next(); y1 = L["y1"].next(); v = L["v"].next(); g = L["g"].next(); ht = L["h"].next(); bon = bons.next()
            fw.dma("sp", y0, y0[:], self.Y[0], self.Y[0].ap()[rows, :])
            fw.dma("sp", y1, y1[:], self.Y[1], self.Y[1].ap()[rows, :])
            fw.dma("sp", v, v[:], S["V"], S["V"].ap()[rows, :])
            fw.dma("sp", g, g[:], S["G"], S["G"].ap()[rows, :])
            fw.dma("sp", bon, bon[:], S["BON"], S["BON"].ap()[rows, :])
            src, sap = self.src_of(tile, hsrc)
            fw.dma("sp", ht, ht[:], src, sap)
            ysum = wk.next()
            fw.op("dve", lambda e: e.tensor_tensor(out=ysum[:], in0=y0[:], in1=y1[:], op=ALU.add), [y0, y1], [ysum])
            mean = st.next(); var = st.next()
            fw.op("dve", lambda e: e.reduce_sum(out=mean[:], in_=v3(ysum), axis=AX.X), [ysum], [mean])
            fw.op("dve", lambda e: e.tensor_scalar(out=mean[:], in0=mean[:], scalar1=1.0 / 64, scalar2=None, op0=ALU.mult), [mean], [mean])
            yc = wk.next()
            fw.op("dve", lambda e: e.tensor_tensor(out=v3(yc), in0=v3(ysum), in1=b3(mean), op=ALU.subtract), [ysum, mean], [yc])
            sq = wk.next()
            fw.op("dve", lambda e: e.tensor_tensor(out=sq[:], in0=yc[:], in1=yc[:], op=ALU.mult), [yc], [sq])
            fw.op("dve", lambda e: e.reduce_sum(out=var[:], in_=v3(sq), axis=AX.X), [sq], [var])
            fw.op("act", lambda e: e.activation(out=var[:], in_=var[:], func=AF.Sqrt, bias=64e-5, scale=1.0 / 64), [var], [var])
            fw.op("dve", lambda e: e.reciprocal(out=var[:], in_=var[:]), [var], [var])
            fw.op("dve", lambda e: e.tensor_tensor(out=v3(yc), in0=v3(yc), in1=b3(var), op=ALU.mult), [yc, var], [yc])
            fw.op("dve", lambda e: e.tensor_tensor(out=yc[:], in0=yc[:], in1=gnw[:], op=ALU.mult), [yc, gnw], [yc])
            fw.op("dve", lambda e: e.tensor_tensor(out=yc[:], in0=yc[:], in1=gnb[:], op=ALU.add), [yc, gnb], [yc])
            bv = wk.next()
            fw.op("dve", lambda e: e.tensor_tensor(out=v3(bv), in0=v3(v), in1=b3(bon), op=ALU.mult), [v, bon], [bv])
            fw.op("dve", lambda e: e.tensor_tensor(out=yc[:], in0=yc[:], in1=bv[:], op=ALU.add), [yc, bv], [yc])
            ob = obf.next()
            fw.op("dve", lambda e: e.tensor_tensor(out=ob[:], in0=yc[:], in1=g[:], op=ALU.mult), [yc, g], [ob])
            pb = self.bank.next(); pv = self.bview(pb)
            for k in range(8):
                fw.op("pe", lambda e: e.transpose(out=pv[:, k * 128:(k + 1) * 128], in_=ob[:, k * 128:(k + 1) * 128], identity=self.identb[:]), [ob, self.identb], [pb])
            oT = oTs.next()
            fw.op("act", lambda e: e.activation(out=oT[:].rearrange("p k n -> p (k n)"), in_=pv[:, 0:1024], func=AF.Copy), [pb], [oT])
            tmp = wk.next()
            for half in range(2):
                po = self.bank.next()
                for k in range(8):
                    fw.op("pe", lambda e: e.matmul(po[:], lhsT=oT[:, k, :], rhs=w_o[:, k, half * 512:(half + 1) * 512], start=(k == 0), stop=(k == 7)), [oT, w_o], [po])
                fw.op("dve", lambda e: e.tensor_tensor(out=tmp[:, half * 512:(half + 1) * 512], in0=po[:], in1=gate[0][:, half * 512:(half + 1) * 512], op=ALU.mult), [po, gate[0]], [tmp])
            fw.op("pool", lambda e: e.tensor_tensor(out=tmp[:], in0=tmp[:], in1=ht[:], op=ALU.add), [tmp, ht], [tmp])
            fw.dma("act", hdst, hdst.ap()[rows, :], tmp, tmp[:])
        fw.pop()

    def finish(self):
        fw = self.fw
        fw.barrier()
        fw.finish("sp", [self.out])
        print("ninst", fw.ninst, "nsem", fw.nsem)
        self.es.close()
        return self.nc


def host_consts():
    c = {}
    c["ident"] = np.eye(128, dtype=np.float32)
    t = np.arange(NLAT)
    row = (t // 64).astype(np.float32); col = (t % 64).astype(np.float32)
    inv = (10000.0 ** (-np.arange(0, 32, 2, dtype=np.float32) / 32)).astype(np.float32)
    ang = np.concatenate([row[:, None] * inv, col[:, None] * inv], axis=-1)
    cosl = np.cos(ang).astype(np.float32); sinl = np.sin(ang).astype(np.float32)
    cos = np.concatenate([np.ones((NCTX, 32), np.float32), cosl], 0)
    sin = np.concatenate([np.zeros((NCTX, 32), np.float32), sinl], 0)
    d = np.arange(128) % 64
    c["ropeC"] = np.ascontiguousarray(cos[:, d // 2].T)
    c["ropeS"] = np.ascontiguousarray(sin[:, d // 2].T)
    P = np.zeros((128, 128), np.float32)
    for i in range(64):
        P[2 * i + 1, 2 * i] = -1.0
        P[2 * i, 2 * i + 1] = 1.0
    c["ropeP"] = P
    b = np.zeros((128, 128), np.float32)
    b[:64, :64] = 1; b[64:, 64:] = 1
    c["blk64"] = b
    s_ = np.arange(128)[:, None]; t_ = np.arange(128)[None, :]
    tm = np.zeros((128, 4, 128), np.float32)
    tm[:, 0] = (s_ < t_); tm[:, 1] = (s_ <= t_); tm[:, 2] = (s_ > t_); tm[:, 3] = (s_ >= t_)
    c["trimask"] = tm
    c["iota128"] = np.tile(np.arange(128, dtype=np.float32)[None, :], (128, 1))
    return c


def shard_inputs(inputs):
    f = lambda a: np.ascontiguousarray(np.asarray(a, dtype=np.float32))
    w_in = f(inputs["ev_w_in"])[0]
    w_in_r = np.concatenate([w_in[:, 0:2048], w_in[:, 2048:2112], w_in[:, 2048:2112], w_in[:, 2112:2176], w_in[:, 2112:2176], w_in[:, 2176:2304]], axis=1)
    shared = dict(
        ada_w=f(inputs["ada_w"]), ada_b=f(inputs["ada_b"]), norm1_g=f(inputs["norm1_g"]), norm2_g=f(inputs["norm2_g"]),
        w_in=f(w_in_r), conv_w=f(inputs["ev_conv_w"])[0], qg2=f(np.tile(f(inputs["ev_q_gain"])[0], 2)), kg2=f(np.tile(f(inputs["ev_k_gain"])[0], 2)),
        w_out=f(inputs["ev_w_out"])[0], od_mu=f(inputs["od_mu"])[0], od_w_r=f(inputs["od_w_r"])[0], od_w_k=f(inputs["od_w_k"])[0],
        od_w_v=f(inputs["od_w_v"])[0], od_w_o=f(inputs["od_w_o"])[0], od_g1=f(inputs["od_g1"])[0], od_g2=f(inputs["od_g2"])[0],
        od_k_k=f(inputs["od_k_k"])[0], od_k_a=f(inputs["od_k_a"])[0], od_r_k=f(inputs["od_r_k"])[0].reshape(-1),
        od_w0=f(inputs["od_w0"])[0], od_w1=f(inputs["od_w1"])[0], od_w2=f(inputs["od_w2"])[0], od_a0=f(inputs["od_a0"])[0],
        od_a1=f(inputs["od_a1"])[0], od_a2=f(inputs["od_a2"])[0], od_gn_w=f(inputs["od_gn_w"])[0], od_gn_b=f(inputs["od_gn_b"])[0],
        peer_wq=f(inputs["peer_wq"]), peer_keys=f(inputs["peer_keys"]).reshape(2, 16, 128, 128), peer_u=f(inputs["peer_u"]), peer_v=f(inputs["peer_v"]),
    )
    shared.update(host_consts())
    x = f(inputs["x"]); ctx = f(inputs["ctx"]); c = f(inputs["c"]); cc = f(inputs["c_ctx"])
    maps = []
    for b in range(8):
        m = dict(shared)
        m["x"] = x[b]; m["ctx"] = ctx[b]; m["cvec"] = np.ascontiguousarray(np.stack([c[b], cc], 0))
        maps.append(m)
    return maps


def build(dbg=False, stop=None):
    p = Prog(dbg=dbg, stop=stop)
    p.phase0()
    if stop == "p0":
        return p
    if stop and stop.startswith("L1"):
        p.l1_phaseA(p.h2)
        if stop == "L1a":
            return p
        p.l1_phaseB()
        if stop == "L1b":
            return p
        p.l1_phaseC(p.h2, p.h3)
        return p
    p.l0_phaseA()
    if stop == "l0a":
        p.fw.pop()
        return p
    p.l0_phaseB()
    if stop == "l0":
        return p
    if os.environ.get("PEER_V", "2") == "2":
        p.peer_prep2(0)
        p.peer2(0, p.h1, p.h2)
    else:
        p.peer_prep(0)
        p.peer(0, p.h1, p.h2)
    if stop == "peer0":
        return p
    p.l1_phaseA(p.h2)
    if stop == "l1a":
        return p
    p.l1_phaseB()
    if stop == "l1b":
        return p
    p.l1_phaseC(p.h2, p.h3)
    if stop == "l1c":
        return p
    if os.environ.get("PEER_V", "2") == "2":
        p.peer_prep2(1)
        p.peer2(1, p.h3, None, final=True)
    else:
        p.peer_prep(1)
        p.peer(1, p.h3, None, final=True)
    return p


def kernel(**inputs):
    p = build()
    nc = p.finish()
    maps = shard_inputs(inputs)
    maps = [{k: v for k, v in m.items() if k in p.I} for m in maps]
    res = run_bass_kernel_spmd(nc, maps, core_ids=list(range(8)))
    return np.stack([r["out"] for r in res.results], 0)
```
